# Optimizing a Trainium2 kernel written in Bass

```python
import math
import jax, jax.numpy as jnp
from jax import lax
import numpy as np

D_MODEL = 1024
BATCH = 8
SEQ = 2048
DEPTH = 2

N_MIXERS = 2
D_FF = 4 * D_MODEL
NORM_EPS = 1e-6
GLA_HEADS = 4
GLA_DK = D_MODEL // 2 // GLA_HEADS
GLA_DV = D_MODEL // GLA_HEADS
GLA_GATE_RANK = 16
GLA_TAU = 16.0
GLA_CHUNK = 64
GLA_IN = 2 * GLA_HEADS * GLA_DK + 2 * D_MODEL + GLA_GATE_RANK
DIFF_HEADS = 8
DIFF_DH = D_MODEL // DIFF_HEADS // 2
DIFF_DV = 2 * DIFF_DH
DIFF_IN = 3 * D_MODEL
ROPE_THETA = 500000.0
ROPE_DIM = DIFF_DH // 4
Q_BLOCK = 128
N_GLA = (DEPTH + 1) // 2
N_DIFF = DEPTH // 2

kernel_name = "hybrid_gla_diffattn_adaln_trunk"


def rmsnorm(x, g):
    xf = x.astype(jnp.float32)
    y = xf * lax.rsqrt(jnp.mean(xf * xf, axis=-1, keepdims=True) + NORM_EPS)
    return (y * g.astype(jnp.float32)).astype(x.dtype)


def rope_partial(x, cos, sin):
    half = ROPE_DIM // 2
    cos = cos.astype(x.dtype)
    sin = sin.astype(x.dtype)
    x1 = x[..., :half]
    x2 = x[..., half:ROPE_DIM]
    rest = x[..., ROPE_DIM:]
    return jnp.concatenate([x1 * cos - x2 * sin, x2 * cos + x1 * sin, rest], axis=-1)


def gla_mixer(h, w_in, w_a2, b_a, b_r, norm_g, w_o):
    B, T, _ = h.shape
    H, dk, dv, C = GLA_HEADS, GLA_DK, GLA_DV, GLA_CHUNK
    n = T // C
    hk = H * dk
    f32 = jnp.float32
    q, k, v, r, a_lr = jnp.split(h @ w_in, [hk, 2 * hk, 2 * hk + D_MODEL, 2 * hk + 2 * D_MODEL], axis=-1)
    log_a = jax.nn.log_sigmoid((a_lr @ w_a2 + b_a).astype(f32)) / GLA_TAU

    def chunks(t, d):
        return t.reshape(B, n, C, H, d)

    q = chunks(q.astype(f32), dk) * (dk ** -0.5)
    k = chunks(k.astype(f32), dk)
    v = chunks(v.astype(f32), dv)
    b = jnp.cumsum(chunks(log_a, dk), axis=2)
    b_last = b[:, :, -1:]
    q_dec = q * jnp.exp(b)
    k_intra = k * jnp.exp(-b)
    k_state = k * jnp.exp(b_last - b)

    causal = jnp.tril(jnp.ones((C, C), dtype=bool))
    scores = jnp.einsum('bnchk,bnshk->bnhcs', q_dec, k_intra)
    scores = jnp.where(causal, scores, 0.0)
    o_intra = jnp.einsum('bnhcs,bnshv->bnchv', scores, v)

    def step(state, inp):
        q_c, k_c, v_c, decay = inp
        o_c = jnp.einsum('bchk,bhkv->bchv', q_c, state)
        state = state * jnp.exp(decay)[..., None] + jnp.einsum('bchk,bchv->bhkv', k_c, v_c)
        return state, o_c

    xs = (jnp.moveaxis(q_dec, 1, 0), jnp.moveaxis(k_state, 1, 0),
          jnp.moveaxis(v, 1, 0), jnp.moveaxis(b_last[:, :, 0], 1, 0))
    s0 = jnp.zeros((B, H, dk, dv), f32)
    _, o_inter = lax.scan(step, s0, xs)
    o = o_intra + jnp.moveaxis(o_inter, 0, 1)
    o = rmsnorm(o, norm_g).reshape(B, T, D_MODEL)
    o = o * jax.nn.silu((r + b_r).astype(f32))
    return o.astype(h.dtype) @ w_o


def diff_mixer(h, cos, sin, w_in, lam_vecs, subln_g, w_o, lambda_init):
    B, T, _ = h.shape
    H, dh, dv = DIFF_HEADS, DIFF_DH, DIFF_DV
    f32 = jnp.float32
    q, k, v = jnp.split(h @ w_in, 3, axis=-1)
    q = rope_partial(q.reshape(B, T, H, 2, dh), cos, sin) * (dh ** -0.5)
    k = rope_partial(k.reshape(B, T, H, 2, dh), cos, sin)
    q = q.transpose(0, 2, 3, 1, 4)
    k = k.transpose(0, 2, 3, 1, 4)
    v = v.reshape(B, T, H, dv).transpose(0, 2, 1, 3)
    lf = lam_vecs.astype(f32)
    lam = jnp.exp(jnp.sum(lf[0] * lf[1])) - jnp.exp(jnp.sum(lf[2] * lf[3])) + lambda_init
    outs = []
    for i in range(T // Q_BLOCK):
        L = (i + 1) * Q_BLOCK
        qb = q[:, :, :, i * Q_BLOCK:L]
        s = jnp.einsum('bhiqd,bhikd->bhiqk', qb, k[:, :, :, :L]).astype(f32)
        q_idx = i * Q_BLOCK + jnp.arange(Q_BLOCK)
        mask = jnp.arange(L)[None, :] <= q_idx[:, None]
        p = jax.nn.softmax(jnp.where(mask, s, -jnp.inf), axis=-1)
        a = p[:, :, 0] - lam * p[:, :, 1]
        outs.append(jnp.einsum('bhqk,bhkv->bhqv', a.astype(v.dtype), v[:, :, :L]))
    o = jnp.concatenate(outs, axis=2)
    o = rmsnorm(o, subln_g) * (1.0 - lambda_init)
    o = o.transpose(0, 2, 1, 3).reshape(B, T, D_MODEL)
    return o @ w_o


def setup_inputs(seed: int = 0) -> dict:
    key = jax.random.key(seed)
    ks = jax.random.split(key, 24)
    D = D_MODEL
    nrm = jax.random.normal
    x = nrm(ks[0], (BATCH, SEQ, D), jnp.float32)
    c = nrm(ks[1], (BATCH, D), jnp.float32)
    offsets = jax.random.randint(ks[2], (BATCH, 1), 0, 4096, dtype=jnp.int32)
    positions = (offsets + jnp.arange(SEQ, dtype=jnp.int32)[None, :]).astype(jnp.int32)
    return {
        "x": x,
        "c": c,
        "positions": positions,
        "ada_w": nrm(ks[3], (DEPTH, D, 6 * D), jnp.float32) * (0.5 * D ** -0.5),
        "ada_b": nrm(ks[4], (DEPTH, 6 * D), jnp.float32) * 0.02,
        "norm_g": 1.0 + 0.05 * nrm(ks[5], (DEPTH, 2, D), jnp.float32),
        "mlp_w1": nrm(ks[6], (DEPTH, D, D_FF), jnp.float32) * D ** -0.5,
        "mlp_w2": nrm(ks[7], (DEPTH, D_FF, D), jnp.float32) * D_FF ** -0.5,
        "gla_w_in": nrm(ks[8], (N_GLA, D, GLA_IN), jnp.float32) * D ** -0.5,
        "gla_w_a2": nrm(ks[9], (N_GLA, GLA_GATE_RANK, GLA_HEADS * GLA_DK), jnp.float32) * GLA_GATE_RANK ** -0.5,
        "gla_b_a": 0.5 + 0.1 * nrm(ks[10], (N_GLA, GLA_HEADS * GLA_DK), jnp.float32),
        "gla_b_r": 0.02 * nrm(ks[11], (N_GLA, D), jnp.float32),
        "gla_norm_g": 1.0 + 0.05 * nrm(ks[12], (N_GLA, GLA_DV), jnp.float32),
        "gla_w_o": nrm(ks[13], (N_GLA, D, D), jnp.float32) * D ** -0.5,
        "diff_w_in": nrm(ks[14], (N_DIFF, D, DIFF_IN), jnp.float32) * D ** -0.5,
        "diff_lambda": 0.1 * nrm(ks[15], (N_DIFF, 4, DIFF_DH), jnp.float32),
        "diff_subln_g": 1.0 + 0.05 * nrm(ks[16], (N_DIFF, DIFF_DV), jnp.float32),
        "diff_w_o": nrm(ks[17], (N_DIFF, D, D), jnp.float32) * D ** -0.5,
        "final_g": 1.0 + 0.05 * nrm(ks[18], (D,), jnp.float32),
    }


def reference(x, c, positions, ada_w, ada_b, norm_g, mlp_w1, mlp_w2,
              gla_w_in, gla_w_a2, gla_b_a, gla_b_r, gla_norm_g, gla_w_o,
              diff_w_in, diff_lambda, diff_subln_g, diff_w_o, final_g):
    f32 = jnp.float32
    inv_freq = ROPE_THETA ** (-jnp.arange(0, ROPE_DIM, 2, dtype=f32) / ROPE_DIM)
    ang = positions.astype(f32)[..., None] * inv_freq
    cos = jnp.cos(ang)[:, :, None, None, :]
    sin = jnp.sin(ang)[:, :, None, None, :]
    c_act = jax.nn.silu(c)
    for i in range(DEPTH):
        mod = c_act @ ada_w[i] + ada_b[i]
        shift_t, scale_t, gate_t, shift_c, scale_c, gate_c = [
            m[:, None, :] for m in jnp.split(mod, 6, axis=-1)]
        h = rmsnorm(x, norm_g[i, 0]) * (1.0 + scale_t) + shift_t
        j = i // N_MIXERS
        if i % N_MIXERS == 0:
            y = gla_mixer(h, gla_w_in[j], gla_w_a2[j], gla_b_a[j], gla_b_r[j],
                          gla_norm_g[j], gla_w_o[j])
        else:
            lambda_init = 0.8 - 0.6 * math.exp(-0.3 * i)
            y = diff_mixer(h, cos, sin, diff_w_in[j], diff_lambda[j], diff_subln_g[j],
                           diff_w_o[j], lambda_init)
        x = x + gate_t * y
        h = rmsnorm(x, norm_g[i, 1]) * (1.0 + scale_c) + shift_c
        y = jnp.square(jax.nn.relu(h @ mlp_w1[i])) @ mlp_w2[i]
        x = x + gate_c * y
    return rmsnorm(x, final_g)
```

```python
import contextlib
import os
import math
import numpy as np
import concourse.bass as bass
import concourse.mybir as mybir
from concourse.bass_utils import run_bass_kernel_spmd

F32 = mybir.dt.float32
BF16 = mybir.dt.bfloat16
I32 = mybir.dt.int32
AF = mybir.ActivationFunctionType
ALU = mybir.AluOpType
AX = mybir.AxisListType

D = 1024
T = 2048
NB = 8
DFF = 4096
EPS = 1e-6
LAMBDA_INIT = 0.8 - 0.6 * math.exp(-0.3 * 1)
NBLK = 36
NSLOT = 3
TWO_PI = float(2.0 * np.pi)

SM_NG = 0
SM_FG = 32
SM_AB = 40
SM_BR = 136
SM_GNG = 144
SM_SLG = 146
SM_INVF = 147
SM_C = 148
NSM = 160


class Res:
    __slots__ = ("w", "r", "excl")

    def __init__(self):
        self.w = None
        self.r = []
        self.excl = False


class Op:
    __slots__ = ("eng", "fn", "deps", "eidx", "signal", "seq", "dma", "dsem", "dval")


class Prog:
    NDMA = 24

    def __init__(self, nc, same_engine_sync=True):
        self.nc = nc
        self.engs = {"pe": nc.tensor, "act": nc.scalar, "dve": nc.vector,
                     "pool": nc.gpsimd, "sp": nc.sync}
        self.ops = []
        self.ecount = {k: 0 for k in self.engs}
        self.known = {k: {} for k in self.engs}
        self.kdma = {k: {} for k in self.engs}
        self.ndma = 0
        self.dma_ops = []
        self.same_engine_sync = same_engine_sync
        self.res = {}
        self.last = {}

    def R(self, *key):
        r = self.res.get(key)
        if r is None:
            r = Res()
            r.excl = (key[0] == "ps")
            self.res[key] = r
        return r

    def fence(self):
        last = dict(self.last)
        for eng in self.engs:
            self.add(eng, lambda e: None, extra=[o for k, o in last.items() if k != eng], noinst=True)

    def add(self, eng, fn, reads=(), writes=(), dma=False, extra=(), noinst=False):
        op = Op()
        op.eng = eng
        op.fn = fn
        op.dma = dma
        op.signal = False
        op.seq = None
        op.eidx = self.ecount[eng]
        self.ecount[eng] += 1
        if any(r.excl for r in reads):
            writes = list(writes) + [r for r in reads if r.excl]
            reads = [r for r in reads if not r.excl]
        deps = list(extra)
        for r in reads:
            if r.w is not None:
                deps.append(r.w)
        for r in writes:
            if r.w is not None:
                deps.append(r.w)
            deps.extend(r.r)
        if dma:
            i = self.ndma
            self.ndma += 1
            op.dsem = i % self.NDMA
            op.dval = 16 * (i // self.NDMA + 1)
            if i >= self.NDMA:
                deps.append(self.dma_ops[i - self.NDMA])
            self.dma_ops.append(op)
        best = {}
        for d in deps:
            if d is op:
                continue
            if d.dma:
                if self.kdma[eng].get(d.dsem, 0) >= d.dval:
                    continue
                key = ("dma", d.dsem)
                if key not in best or best[key].dval < d.dval:
                    best[key] = d
            else:
                if d.eng == eng and (eng == "pe" or not self.same_engine_sync):
                    continue
                if self.known[eng].get(d.eng, -1) >= d.eidx:
                    continue
                key = d.eng
                if key not in best or best[key].eidx < d.eidx:
                    best[key] = d
        final = []
        for key, d in best.items():
            final.append(d)
            if d.dma:
                self.kdma[eng][d.dsem] = d.dval
            else:
                d.signal = True
                self.known[eng][d.eng] = d.eidx
        op.deps = final
        for r in reads:
            r.r.append(op)
        for r in writes:
            r.w = op
            r.r = []
        self.ops.append(op)
        if not dma and not noinst:
            self.last[eng] = op
        return op

    def emit(self, sems, dsems):
        cnt = {k: 0 for k in self.engs}
        for op in self.ops:
            if not op.dma and op.signal:
                cnt[op.eng] += 1
                op.seq = cnt[op.eng]
        for op in self.ops:
            e = self.engs[op.eng]
            for d in op.deps:
                if d.dma:
                    e.wait_ge(dsems[d.dsem], d.dval)
                else:
                    e.wait_ge(sems[d.eng], d.seq)
            ins = op.fn(e)
            if ins is None:
                continue
            if op.dma:
                ins.then_inc(dsems[op.dsem], 16)
            elif op.signal:
                ins.then_inc(sems[op.eng], 1)
        return cnt


def build_nc(stage=4, debug=False):
    nc = bass.Bass("TRN2", target_bir_lowering=False)
    xT_d = nc.dram_tensor("xT", [D, T], F32, kind="ExternalInput").ap()
    pos_d = nc.dram_tensor("pos", [1, T], F32, kind="ExternalInput").ap()
    sm_d = nc.dram_tensor("sm", [128, NSM], F32, kind="ExternalInput").ap()
    rw_d = nc.dram_tensor("rw", [1, 256], F32, kind="ExternalInput").ap()
    wa_d = nc.dram_tensor("wa", [128, 8 * 48], F32, kind="ExternalInput").ap()
    wa2_d = nc.dram_tensor("wa2", [48, 512], F32, kind="ExternalInput").ap()
    cst_d = nc.dram_tensor("cst", [128, 512], F32, kind="ExternalInput").ap()
    wblk_d = nc.dram_tensor("wblk", [NBLK, 128, 8192], F32, kind="ExternalInput").ap()
    out_d = nc.dram_tensor("outT", [D, T], F32, kind="ExternalOutput").ap()
    if debug:
        dbgf_d = nc.dram_tensor("dbgf", [16, 128, 2048], F32, kind="ExternalOutput").ap()
        dbgb_d = nc.dram_tensor("dbgb", [24, 128, 2048], BF16, kind="ExternalOutput").ap()

    es = contextlib.ExitStack()
    with es:
        def sb(name, shape, dt):
            return es.enter_context(nc.sbuf_tensor(name, shape, dt))

        xT = sb("xTs", [128, 8, T], F32)
        hT = sb("hTs", [128, 8, T], BF16)
        wsl = [sb("wsl%d" % i, [128, 8192], BF16) for i in range(NSLOT)]
        cstf = sb("cstf", [128, 512], F32)
        cstb = sb("cstb", [128, 512], BF16)
        onesb = sb("onesb", [128, 128], BF16)
        onesf = sb("onesf", [128, 128], F32)
        sm = sb("sms", [128, NSM], F32)
        modv = sb("modv", [128, 96], F32)
        dv = sb("dvs", [128, 64], F32)
        cact = sb("cact", [128, 8], BF16)
        lamv = sb("lamv", [128, 8], F32)
        lrow_t = sb("lrow", [1, 256], F32)
        U = sb("U", [128, 29400], BF16)
        PS = [es.enter_context(nc.psum_tensor("ps%d" % i, [128, 512], F32)) for i in range(8)]
        sems = {k: es.enter_context(nc.semaphore("s_" + k)) for k in ["pe", "act", "dve", "pool", "sp"]}
        dsems = [es.enter_context(nc.semaphore("d%d" % i)) for i in range(Prog.NDMA)]

        P = Prog(nc)
        R = P.R

        identf = cstf[:, 0:128]
        trinegf = cstf[:, 256:384]
        identb = cstb[:, 0:128]
        trib = cstb[:, 128:256]
        permb = cstb[:, 384:512]

        def Ub(off, n):
            return U[:, off:off + n]

        def Uf(off, n):
            return U[:, off:off + 2 * n].bitcast(F32)

        def dump(kind, idx, ap, reads, npart=128):
            if not debug:
                return
            n = ap.shape[-1]
            dst = (dbgf_d if kind == "f" else dbgb_d)[idx, 0:npart, 0:n]
            P.add("sp", lambda e: e.dma_start(out=dst, in_=ap), reads=reads, writes=[R("dbg", kind, idx)], dma=True)

        wstate = {"next": 0}

        def w_issue():
            i = wstate["next"]
            if i >= NBLK:
                return
            wstate["next"] = i + 1
            s = i % NSLOT
            P.add("pool", lambda e, i=i, s=s: e.dma_start(out=wsl[s][:], in_=wblk_d[i]),
                  writes=[R("w", s)], dma=True)

        def w_slot(i):
            return wsl[i % NSLOT], R("w", i % NSLOT)

        prot = {"i": 0}

        def ps_next(banks):
            b = banks[prot["i"] % len(banks)]
            prot["i"] += 1
            return PS[b], R("ps", b)

        P.add("sp", lambda e: e.dma_start(out=sm[:], in_=sm_d), writes=[R("sm")], dma=True)
        P.add("sp", lambda e: e.dma_start(out=cstf[:], in_=cst_d), writes=[R("cstf")], dma=True)
        for i in range(NSLOT):
            w_issue()
        xv = xT_d.rearrange("(k p) t -> p k t", p=128)
        for k in range(8):
            P.add("sp", lambda e, k=k: e.dma_start(out=xT[:, k, :], in_=xv[:, k, :]),
                  writes=[R("x", k, tb) for tb in range(4)], dma=True)
        P.add("dve", lambda e: e.tensor_copy(cstb[:], cstf[:]), reads=[R("cstf")], writes=[R("cstb")])
        P.add("dve", lambda e: e.memset(onesb[:], 1.0), writes=[R("onesb")])
        P.add("dve", lambda e: e.memset(onesf[:], 1.0), writes=[R("onesf")])
        P.add("act", lambda e: e.activation(cact[:], sm[:, SM_C:SM_C + 8], AF.Silu),
              reads=[R("sm")], writes=[R("cact")])

        def adaln(l, blk0):
            P.fence()
            modrow = U[0:1, 0:12288].bitcast(F32)
            for nb_ in range(6):
                wt, wr = w_slot(blk0 + nb_)
                for half in range(2):
                    n0 = half * 512
                    ps, psr = ps_next([0, 1])
                    for k in range(8):
                        P.add("pe", lambda e, ps=ps, wt=wt, k=k, n0=n0: e.matmul(
                            ps[0:1, :], cact[:, k:k + 1], wt[:, k * 1024 + n0:k * 1024 + n0 + 512],
                            start=(k == 0), stop=(k == 7)),
                            reads=[wr, R("cact")], writes=[psr])
                    c0 = nb_ * 1024 + n0
                    P.add("act", lambda e, ps=ps, c0=c0: e.activation(modrow[0:1, c0:c0 + 512], ps[0:1, :], AF.Copy),
                          reads=[psr], writes=[R("modrow")])
                w_issue()
            ps, psr = ps_next([0, 1])
            for m in range(48):
                P.add("pe", lambda e, ps=ps, m=m: e.matmul(
                    ps[:, m:m + 1], modrow[0:1, m * 128:(m + 1) * 128], onesf[0:1, 0:1], start=True, stop=True),
                    reads=[R("modrow"), R("onesf")], writes=[psr])
            P.add("dve", lambda e, ps=ps, l=l: e.tensor_tensor(
                modv[:, l * 48:(l + 1) * 48], ps[:, 0:48], sm[:, SM_AB + l * 48:SM_AB + (l + 1) * 48], ALU.add),
                reads=[psr, R("sm")], writes=[R("modv")])
            mo = l * 48
            P.add("dve", lambda e: e.scalar_tensor_tensor(
                dv[:, 0:8], modv[:, mo + 8:mo + 16], 1.0, sm[:, SM_NG + (2 * l) * 8:SM_NG + (2 * l) * 8 + 8],
                ALU.add, ALU.mult), reads=[R("modv"), R("sm")], writes=[R("dv")])
            P.add("dve", lambda e: e.scalar_tensor_tensor(
                dv[:, 8:16], modv[:, mo + 32:mo + 40], 1.0, sm[:, SM_NG + (2 * l + 1) * 8:SM_NG + (2 * l + 1) * 8 + 8],
                ALU.add, ALU.mult), reads=[R("modv"), R("sm")], writes=[R("dv")])

        NO = 29400 - 5120
        UEND = 29400

        def norm(gs_ap, sh_ap, final=False):
            P.fence()
            for tb in range(4):
                ts = slice(tb * 512, (tb + 1) * 512)
                pss, pssr = ps_next([0, 1])
                for k in range(8):
                    sq = Ub(NO + (k % 2) * 512, 512)
                    sqr = R("n_sq", k % 2)
                    P.add("act", lambda e, sq=sq, k=k, ts=ts: e.activation(sq, xT[:, k, ts], AF.Square),
                          reads=[R("x", k, tb)], writes=[sqr])
                    P.add("pe", lambda e, pss=pss, sq=sq, k=k: e.matmul(pss[:], onesb[:], sq, start=(k == 0), stop=(k == 7)),
                          reads=[sqr, R("onesb")], writes=[pssr])
                lnt = Uf(NO + 1024, 512)
                rstd = Uf(NO + 2048, 512)
                P.add("act", lambda e, pss=pss, lnt=lnt: e.activation(lnt, pss[:], AF.Ln, bias=EPS, scale=1.0 / D),
                      reads=[pssr], writes=[R("n_ln")])
                P.add("act", lambda e, lnt=lnt, rstd=rstd: e.activation(rstd, lnt, AF.Exp, scale=-0.5),
                      reads=[R("n_ln")], writes=[R("n_rstd")])
                for k in range(8):
                    if final:
                        P.add("dve", lambda e, k=k, ts=ts, rstd=rstd: e.scalar_tensor_tensor(
                            xT[:, k, ts], xT[:, k, ts], gs_ap[:, k:k + 1], rstd, ALU.mult, ALU.mult),
                            reads=[R("n_rstd"), R("sm")], writes=[R("x", k, tb)])
                    else:
                        tmp = Uf(NO + 3072 + (k % 2) * 1024, 512)
                        tr = R("n_tmp", k % 2)
                        P.add("dve", lambda e, k=k, ts=ts, rstd=rstd, tmp=tmp: e.scalar_tensor_tensor(
                            tmp, xT[:, k, ts], gs_ap[:, k:k + 1], rstd, ALU.mult, ALU.mult),
                            reads=[R("n_rstd"), R("x", k, tb), R("dv")], writes=[tr])
                        P.add("act", lambda e, k=k, ts=ts, tmp=tmp: e.activation(
                            hT[:, k, ts], tmp, AF.Identity, bias=sh_ap[:, k:k + 1]),
                            reads=[tr, R("modv")], writes=[R("h", k, tb)])

        def mlp(l, blk0):
            P.fence()
            mo = l * 48
            gate = modv[:, mo + 40:mo + 48]
            hid = U[:, 0:16384].rearrange("p (j t) -> p j t", j=8)
            for g in range(4):
                w1, w1r = w_slot(blk0 + 2 * g)
                for tb in range(4):
                    ts = slice(tb * 512, (tb + 1) * 512)
                    for j in range(8):
                        ps, psr = ps_next([0, 1, 2, 3])
                        for k in range(8):
                            P.add("pe", lambda e, ps=ps, w1=w1, k=k, j=j, ts=ts: e.matmul(
                                ps[:], w1[:, k * 1024 + j * 128:k * 1024 + (j + 1) * 128], hT[:, k, ts],
                                start=(k == 0), stop=(k == 7)),
                                reads=[w1r, R("h", k, tb)], writes=[psr])
                        sq = Uf(16384 + (j % 2) * 1024, 512)
                        sqr = R("m_sq", j % 2)
                        P.add("act", lambda e, ps=ps, sq=sq: e.activation(sq, ps[:], AF.Square),
                              reads=[psr], writes=[sqr])
                        P.add("dve", lambda e, ps=ps, sq=sq, j=j, ts=ts: e.scalar_tensor_tensor(
                            hid[:, j, ts], ps[:], 0.0, sq, ALU.is_gt, ALU.mult),
                            reads=[psr, sqr], writes=[R("hid", j, tb)])
                w_issue()
                w2, w2r = w_slot(blk0 + 2 * g + 1)
                for tb in range(4):
                    ts = slice(tb * 512, (tb + 1) * 512)
                    for dch in range(8):
                        ps, psr = ps_next([0, 1, 2, 3])
                        for j in range(8):
                            P.add("pe", lambda e, ps=ps, w2=w2, j=j, dch=dch, ts=ts: e.matmul(
                                ps[:], w2[:, j * 1024 + dch * 128:j * 1024 + (dch + 1) * 128], hid[:, j, ts],
                                start=(j == 0), stop=(j == 7)),
                                reads=[w2r, R("hid", j, tb)], writes=[psr])
                        P.add("dve", lambda e, ps=ps, dch=dch, ts=ts: e.scalar_tensor_tensor(
                            xT[:, dch, ts], ps[:], gate[:, dch:dch + 1], xT[:, dch, ts], ALU.mult, ALU.add),
                            reads=[psr, R("modv"), R("x", dch, tb)], writes=[R("x", dch, tb)])
                w_issue()

        def gla(blk0):
            P.fence()
            gate = modv[:, 16:24]
            o = 0
            a_aug = U[0:48, o:o + 2048]; o += 2048
            wab = Ub(o, 384); o += 384
            wa2b = U[0:48, o:o + 512]; o += 512
            waf = Uf(o, 384); o += 768
            wa2f = U[0:48, o:o + 1024].bitcast(F32); o += 1024
            spf = [Uf(o + i * 512, 256) for i in range(2)]; o += 1024
            ebf = [Uf(o + i * 1024, 512) for i in range(2)]; o += 2048
            enbf = [Uf(o + i * 1024, 512) for i in range(2)]; o += 2048
            qd = U[:, o:o + 1024].rearrange("p (h t) -> p h t", h=2); o += 1024
            ki = U[:, o:o + 1024].rearrange("p (h t) -> p h t", h=2); o += 1024
            ktok = U[:, o:o + 1024].rearrange("p (c h f) -> p c h f", c=4, h=2); o += 1024
            vtok = U[:, o:o + 2048].rearrange("p (c f) -> p c f", c=4); o += 2048
            rg = U[:, o:o + 2048].rearrange("p (j t) -> p j t", j=4); o += 2048
            og = U[:, o:o + 2048].rearrange("p (j t) -> p j t", j=4); o += 2048
            Sf = U[:, o:o + 1024].bitcast(F32).rearrange("p (h f) -> p h f", h=2); o += 1024
            Sb = U[:, o:o + 512].rearrange("p (h f) -> p h f", h=2); o += 512
            sTb = [Ub(o + i * 128, 128) for i in range(2)]; o += 256
            tmpf = [Uf(o + i * 1024, 512) for i in range(2)]; o += 2048
            stmp = [Uf(o + i * 512, 256) for i in range(2)]; o += 1024
            etmp = Uf(o, 256); o += 512
            sqb = [Ub(o + i * 512, 512) for i in range(2)]; o += 1024
            assert o <= UEND, o

            P.add("sp", lambda e: e.dma_start(out=waf, in_=wa_d), writes=[R("waf"), R("modrow")], dma=True)
            P.add("sp", lambda e: e.dma_start(out=wa2f, in_=wa2_d), writes=[R("wa2f"), R("modrow")], dma=True)
            P.add("dve", lambda e: e.tensor_copy(wab, waf), reads=[R("waf")], writes=[R("wab")])
            P.add("dve", lambda e: e.tensor_copy(wa2b, wa2f), reads=[R("wa2f")], writes=[R("wa2b")])
            for tb in range(4):
                ts = slice(tb * 512, (tb + 1) * 512)
                ps, psr = ps_next([0, 1])
                for k in range(8):
                    P.add("pe", lambda e, ps=ps, k=k, ts=ts: e.matmul(
                        ps[0:48, :], wab[:, k * 48:(k + 1) * 48], hT[:, k, ts], start=(k == 0), stop=(k == 7)),
                        reads=[R("wab"), R("h", k, tb)], writes=[psr])
                P.add("act", lambda e, ps=ps, ts=ts: e.activation(a_aug[0:48, ts], ps[0:48, :], AF.Copy),
                      reads=[psr], writes=[R("a_aug", tb)])
                P.add("dve", lambda e, ts=ts: e.memset(a_aug[0:1, ts], 1.0),
                      reads=[R("a_aug", tb)], writes=[R("a_aug", tb)])

            for g in range(2):
                wA, wAr = w_slot(blk0 + 2 * g)
                wB, wBr = w_slot(blk0 + 2 * g + 1)
                if g == 0:
                    dump("b", 7, wsl[0][:, 0:2048], [R("w", 0)])
                    dump("b", 16, wsl[1][:, 0:2048], [R("w", 1)])
                    dump("b", 17, wsl[2][:, 0:2048], [R("w", 2)])
                P.add("dve", lambda e: e.memset(Sf[:, :, :], 0.0), writes=[R("Sf", 0), R("Sf", 1)])
                P.add("dve", lambda e: e.memset(Sb[:, :, :], 0.0), writes=[R("Sb", 0), R("Sb", 1)])
                for tb in range(4):
                    ts = slice(tb * 512, (tb + 1) * 512)
                    pb = [(PS[2], R("ps", 2)), (PS[3], R("ps", 3))]
                    for c in range(4):
                        tok = slice(tb * 512 + c * 128, tb * 512 + (c + 1) * 128)
                        ps, psr = ps_next([0, 1])
                        P.add("pe", lambda e, wA=wA, wB=wB, ps=ps, tok=tok, g=g: e.matmul(
                            ps[:, 0:256], a_aug[0:48, tok], wa2b[0:48, g * 256:(g + 1) * 256], start=True, stop=True),
                            reads=[R("a_aug", tb), R("wa2b")], writes=[psr])
                        P.add("act", lambda e, wA=wA, wB=wB, ps=ps: e.activation(etmp, ps[:, 0:256], AF.Exp, scale=-1.0),
                              reads=[psr], writes=[R("etmp")])
                        sp_ = spf[c % 2]
                        P.add("act", lambda e, wA=wA, wB=wB, sp_=sp_: e.activation(sp_, etmp, AF.Ln, bias=1.0),
                              reads=[R("etmp")], writes=[R("sp", c % 2)])
                        for hh in range(2):
                            P.add("pe", lambda e, wA=wA, wB=wB, hh=hh, c=c, sp_=sp_, pb=pb: e.matmul(
                                pb[hh][0][:, c * 128:(c + 1) * 128], sp_[:, hh * 128:(hh + 1) * 128], trinegf,
                                start=True, stop=True),
                                reads=[R("sp", c % 2), R("cstf")],
                                writes=[pb[hh][1]])
                    for hh in range(2):
                        P.add("act", lambda e, wA=wA, wB=wB, hh=hh, pb=pb: e.activation(ebf[hh], pb[hh][0][:], AF.Exp),
                              reads=[pb[hh][1]], writes=[R("eb", hh)])
                        P.add("act", lambda e, wA=wA, wB=wB, hh=hh, pb=pb: e.activation(enbf[hh], pb[hh][0][:], AF.Exp, scale=-1.0),
                              reads=[pb[hh][1]], writes=[R("enb", hh)])
                    for hh in range(2):
                        ps, psr = ps_next([0, 1])
                        for k in range(8):
                            P.add("pe", lambda e, wA=wA, wB=wB, ps=ps, k=k, hh=hh, ts=ts: e.matmul(
                                ps[:], wA[:, k * 1024 + hh * 128:k * 1024 + (hh + 1) * 128], hT[:, k, ts],
                                start=(k == 0), stop=(k == 7)), reads=[wAr, R("h", k, tb)], writes=[psr])
                        P.add("dve", lambda e, wA=wA, wB=wB, ps=ps, hh=hh: e.scalar_tensor_tensor(
                            qd[:, hh, :], ps[:], 128.0 ** -0.5, ebf[hh], ALU.mult, ALU.mult),
                            reads=[psr, R("eb", hh)], writes=[R("qd", hh)])
                        ps, psr = ps_next([0, 1])
                        for k in range(8):
                            P.add("pe", lambda e, wA=wA, wB=wB, ps=ps, k=k, hh=hh, ts=ts: e.matmul(
                                ps[:], wA[:, k * 1024 + 256 + hh * 128:k * 1024 + 256 + (hh + 1) * 128], hT[:, k, ts],
                                start=(k == 0), stop=(k == 7)), reads=[wAr, R("h", k, tb)], writes=[psr])
                        P.add("dve", lambda e, wA=wA, wB=wB, ps=ps, hh=hh: e.tensor_tensor(ki[:, hh, :], ps[:], enbf[hh], ALU.mult),
                              reads=[psr, R("enb", hh)], writes=[R("ki", hh)])
                    for c in range(4):
                        tok = slice(tb * 512 + c * 128, tb * 512 + (c + 1) * 128)
                        ps, psr = ps_next([0, 1])
                        for k in range(8):
                            P.add("pe", lambda e, wA=wA, wB=wB, ps=ps, k=k, tok=tok: e.matmul(
                                ps[:], hT[:, k, tok], wA[:, k * 1024 + 512:k * 1024 + 1024],
                                start=(k == 0), stop=(k == 7)), reads=[wAr, R("h", k, tb)], writes=[psr])
                        P.add("act", lambda e, wA=wA, wB=wB, ps=ps, c=c: e.activation(vtok[:, c, :], ps[:], AF.Copy),
                              reads=[psr], writes=[R("vtok", c)])
                    for c in range(4):
                        pt, ptr = PS[4 + (c % 2)], R("ps", 4 + (c % 2))
                        ptb = pt[:].bitcast(BF16)
                        for hh in range(2):
                            P.add("pe", lambda e, wA=wA, wB=wB, ptb=ptb, c=c, hh=hh: e.transpose(
                                ptb[:, hh * 128:(hh + 1) * 128], ki[:, hh, c * 128:(c + 1) * 128], identb),
                                reads=[R("ki", hh), R("cstb")], writes=[ptr])
                        P.add("act", lambda e, wA=wA, wB=wB, ptb=ptb, c=c: e.activation(
                            ktok[:, c, :, :].rearrange("p h f -> p (h f)"), ptb[:, 0:256], AF.Copy),
                            reads=[ptr], writes=[R("ktok", c)])
                    for jj in range(4):
                        ps, psr = ps_next([0, 1])
                        for k in range(8):
                            P.add("pe", lambda e, wA=wA, wB=wB, ps=ps, k=k, jj=jj, ts=ts: e.matmul(
                                ps[:], wB[:, k * 512 + jj * 128:k * 512 + (jj + 1) * 128], hT[:, k, ts],
                                start=(k == 0), stop=(k == 7)), reads=[wBr, R("h", k, tb)], writes=[psr])
                        bc = SM_BR + g * 4 + jj
                        P.add("act", lambda e, wA=wA, wB=wB, ps=ps, jj=jj, bc=bc: e.activation(
                            rg[:, jj, :], ps[:], AF.Silu, bias=sm[:, bc:bc + 1]),
                            reads=[psr, R("sm")], writes=[R("rg", jj)])
                    if g == 0 and tb == 0:
                        dump("f", 0, ebf[0], [R("eb", 0)])
                        dump("f", 1, enbf[0], [R("enb", 0)])
                        dump("f", 2, spf[1], [R("sp", 1)])
                        dump("b", 0, qd[:, :, :].rearrange("p h t -> p (h t)"), [R("qd", 0), R("qd", 1)])
                        dump("b", 1, ki[:, :, :].rearrange("p h t -> p (h t)"), [R("ki", 0), R("ki", 1)])
                        dump("b", 2, vtok[:, :, :].rearrange("p c f -> p (c f)"), [R("vtok", c) for c in range(4)])
                        dump("b", 3, ktok[:, :, :, :].rearrange("p c h f -> p (c h f)"), [R("ktok", c) for c in range(4)])
                        dump("b", 4, rg[:, :, :].rearrange("p j t -> p (j t)"), [R("rg", j) for j in range(4)])
                        dump("b", 6, a_aug, [R("a_aug", t_) for t_ in range(4)], npart=48)
                    po = [(PS[4 + i], R("ps", 4 + i)) for i in range(4)]
                    for c in range(4):
                        cs = slice(c * 128, (c + 1) * 128)
                        for hh in range(2):
                            pss, pssr = PS[2 + hh], R("ps", 2 + hh)
                            sl = (c % 4) * 128
                            P.add("pe", lambda e, wA=wA, wB=wB, pss=pss, sl=sl, hh=hh, cs=cs: e.matmul(
                                pss[:, sl:sl + 128], ki[:, hh, cs], qd[:, hh, cs], start=True, stop=True),
                                reads=[R("ki", hh), R("qd", hh)], writes=[pssr])
                            sT = sTb[hh]
                            P.add("dve", lambda e, wA=wA, wB=wB, pss=pss, sl=sl, sT=sT: e.tensor_tensor(
                                sT, pss[:, sl:sl + 128], trib, ALU.mult),
                                reads=[pssr, R("cstb")], writes=[R("sT", hh)])
                            for jj in range(2):
                                pot, potr = po[hh * 2 + jj]
                                P.add("pe", lambda e, wA=wA, wB=wB, pot=pot, c=c, hh=hh, jj=jj, sT=sT, cs=cs: e.matmul(
                                    pot[:, cs], vtok[:, c, hh * 256 + jj * 128:hh * 256 + (jj + 1) * 128], sT,
                                    start=True, stop=False),
                                    reads=[R("vtok", c), R("sT", hh)], writes=[potr])
                                P.add("pe", lambda e, wA=wA, wB=wB, pot=pot, hh=hh, jj=jj, cs=cs: e.matmul(
                                    pot[:, cs], Sb[:, hh, jj * 128:(jj + 1) * 128], qd[:, hh, cs],
                                    start=False, stop=True),
                                    reads=[R("Sb", hh), R("qd", hh)], writes=[potr])
                            psp, pspr = ps_next([0, 1])
                            pl = 0
                            P.add("pe", lambda e, wA=wA, wB=wB, psp=psp, pl=pl, c=c, hh=hh: e.matmul(
                                psp[:, pl:pl + 256], ktok[:, c, hh, :], vtok[:, c, hh * 256:(hh + 1) * 256],
                                start=True, stop=True),
                                reads=[R("ktok", c), R("vtok", c)], writes=[pspr])
                            st = stmp[hh]
                            P.add("dve", lambda e, wA=wA, wB=wB, psp=psp, pl=pl, st=st, hh=hh: e.tensor_tensor(
                                st, psp[:, pl:pl + 256], Sf[:, hh, :], ALU.add),
                                reads=[pspr, R("Sf", hh)], writes=[R("stmp", hh)])
                            ecol = ebf[hh][:, c * 128 + 127:c * 128 + 128]
                            P.add("dve", lambda e, wA=wA, wB=wB, st=st, hh=hh, ecol=ecol: e.tensor_scalar(
                                Sf[:, hh, :], st, ecol, None, ALU.mult),
                                reads=[R("stmp", hh), R("eb", hh)], writes=[R("Sf", hh)])
                            P.add("act", lambda e, wA=wA, wB=wB, st=st, hh=hh, ecol=ecol: e.activation(
                                Sb[:, hh, :], st, AF.Copy, scale=ecol),
                                reads=[R("stmp", hh), R("eb", hh)], writes=[R("Sb", hh)])
                    for hh in range(2):
                        pst, pstr = ps_next([0, 1])
                        for jj in range(2):
                            pot, potr = po[hh * 2 + jj]
                            sq = sqb[jj]
                            P.add("act", lambda e, wA=wA, wB=wB, pot=pot, sq=sq: e.activation(sq, pot[:], AF.Square),
                                  reads=[potr], writes=[R("g_sq", jj)])
                            P.add("pe", lambda e, wA=wA, wB=wB, pst=pst, sq=sq, jj=jj: e.matmul(
                                pst[:], onesb[:], sq, start=(jj == 0), stop=(jj == 1)),
                                reads=[R("g_sq", jj), R("onesb")], writes=[pstr])
                        lnt = tmpf[0]
                        rstd = tmpf[1]
                        P.add("act", lambda e, wA=wA, wB=wB, pst=pst, lnt=lnt: e.activation(lnt, pst[:], AF.Ln, bias=EPS, scale=1.0 / 256),
                              reads=[pstr], writes=[R("g_ln")])
                        P.add("act", lambda e, wA=wA, wB=wB, lnt=lnt, rstd=rstd: e.activation(rstd, lnt, AF.Exp, scale=-0.5),
                              reads=[R("g_ln")], writes=[R("g_rstd")])
                        for jj in range(2):
                            pot, potr = po[hh * 2 + jj]
                            t2 = ebf[jj]
                            P.add("dve", lambda e, wA=wA, wB=wB, pot=pot, jj=jj, rstd=rstd, t2=t2: e.scalar_tensor_tensor(
                                t2, pot[:], sm[:, SM_GNG + jj:SM_GNG + jj + 1], rstd, ALU.mult, ALU.mult),
                                reads=[potr, R("g_rstd"), R("sm"), R("eb", jj)], writes=[R("eb", jj)])
                            P.add("dve", lambda e, wA=wA, wB=wB, hh=hh, jj=jj, t2=t2: e.tensor_tensor(
                                og[:, hh * 2 + jj, :], t2, rg[:, hh * 2 + jj, :], ALU.mult),
                                reads=[R("eb", jj), R("rg", hh * 2 + jj)], writes=[R("og", hh * 2 + jj)])
                    if g == 0 and tb == 0:
                        dump("b", 5, og[:, :, :].rearrange("p j t -> p (j t)"), [R("og", j) for j in range(4)])
                        dump("f", 3, Sf[:, :, :].rearrange("p h f -> p (h f)"), [R("Sf", 0), R("Sf", 1)])
                    for dch in range(8):
                        ps, psr = ps_next([0, 1])
                        for j4 in range(4):
                            P.add("pe", lambda e, wA=wA, wB=wB, ps=ps, j4=j4, dch=dch: e.matmul(
                                ps[:], wB[:, 4096 + j4 * 1024 + dch * 128:4096 + j4 * 1024 + (dch + 1) * 128],
                                og[:, j4, :], start=(j4 == 0), stop=(j4 == 3)),
                                reads=[wBr, R("og", j4)], writes=[psr])
                        P.add("dve", lambda e, wA=wA, wB=wB, ps=ps, dch=dch, ts=ts: e.scalar_tensor_tensor(
                            xT[:, dch, ts], ps[:], gate[:, dch:dch + 1], xT[:, dch, ts], ALU.mult, ALU.add),
                            reads=[psr, R("modv"), R("x", dch, tb)], writes=[R("x", dch, tb)])
                w_issue()
                w_issue()

        def att(blk0):
            P.fence()
            gate = modv[:, 48 + 16:48 + 24]
            o = 0
            qT = U[:, o:o + 4096].rearrange("p (h t) -> p h t", h=2); o += 4096
            kT = U[:, o:o + 4096].rearrange("p (h t) -> p h t", h=2); o += 4096
            vaug = U[:, o:o + 16 * 2 * 130].rearrange("p (c h f) -> p c h f", c=16, h=2); o += 4160
            oT = U[:, o:o + 4096].rearrange("p (h t) -> p h t", h=2); o += 4096
            Cb = Ub(o, 2048); o += 2048
            Sb_ = Ub(o, 2048); o += 2048
            pTb = [Ub(o + i * 512, 512) for i in range(3)]; o += 1536
            qnb = [Ub(o + i * 512, 512) for i in range(2)]; o += 1024
            t1f = [Uf(o + i * 1024, 512) for i in range(2)]; o += 2048
            t2f = [Uf(o + i * 1024, 512) for i in range(2)]; o += 2048
            taf = [Uf(o + i * 256, 128) for i in range(4)]; o += 1024
            tdf = [Uf(o + i * 256, 128) for i in range(2)]; o += 512
            dnb = [Ub(o + i * 128, 128) for i in range(2)]; o += 256
            colf = Uf(o, 32); o += 64
            assert o <= UEND, o
            posi = U[:, 0:4096].bitcast(I32)
            angf = Uf(4096, 2048)
            kf = Uf(8192, 2048)
            ki32 = U[:, 12288:16384].bitcast(I32)

            lrow = lrow_t[0:1, :]
            lsum = lamv[0:1, 2:8]
            P.add("sp", lambda e: e.dma_start(out=lrow, in_=rw_d), writes=[R("lrow")], dma=True)
            P.add("dve", lambda e: e.tensor_tensor(lrow[0:1, 0:64], lrow[0:1, 0:64], lrow[0:1, 64:128], ALU.mult),
                  reads=[R("lrow")], writes=[R("lrow")])
            P.add("dve", lambda e: e.tensor_tensor(lrow[0:1, 128:192], lrow[0:1, 128:192], lrow[0:1, 192:256], ALU.mult),
                  reads=[R("lrow")], writes=[R("lrow")])
            P.add("dve", lambda e: e.tensor_reduce(
                lsum[0:1, 0:2], lrow.rearrange("p (a b) -> p a b", a=2)[:, :, 0:64], AX.X, ALU.add),
                reads=[R("lrow")], writes=[R("lamv")])
            P.add("act", lambda e: e.activation(lsum[0:1, 2:4], lsum[0:1, 0:2], AF.Exp),
                  reads=[R("lamv")], writes=[R("lamv")])
            P.add("dve", lambda e: e.scalar_tensor_tensor(
                lsum[0:1, 4:5], lsum[0:1, 3:4], -LAMBDA_INIT, lsum[0:1, 2:3], ALU.add, ALU.subtract),
                reads=[R("lamv")], writes=[R("lamv")])
            ps, psr = ps_next([0, 1])
            P.add("pe", lambda e, ps=ps: e.matmul(ps[:, 0:1], onesf[0:1, :], lsum[0:1, 4:5], start=True, stop=True),
                  reads=[R("lamv"), R("onesf")], writes=[psr])
            P.add("dve", lambda e, ps=ps: e.tensor_copy(lamv[:, 0:1], ps[:, 0:1]), reads=[psr, R("lamv")], writes=[R("lamv")])
            P.add("dve", lambda e: e.tensor_scalar(lamv[:, 1:2], sm[:, SM_SLG:SM_SLG + 1], 1.0 - LAMBDA_INIT, None, ALU.mult),
                  reads=[R("sm"), R("lamv")], writes=[R("lamv")])

            posi1 = U[0:1, 12288:16384].bitcast(I32)
            posf1 = U[0:1, 16448:20544].bitcast(F32)
            P.add("sp", lambda e: e.dma_start(out=posi1.bitcast(F32), in_=pos_d), writes=[R("posi")], dma=True)
            P.add("dve", lambda e: e.tensor_copy(posf1, posi1), reads=[R("posi")], writes=[R("posf")])
            for tb in range(4):
                ts = slice(tb * 512, (tb + 1) * 512)
                ps, psr = ps_next([0, 1])
                P.add("pe", lambda e, ps=ps, ts=ts: e.matmul(ps[:], onesf[0:1, :], posf1[0:1, ts], start=True, stop=True),
                      reads=[R("posf"), R("onesf")], writes=[psr])
                P.add("dve", lambda e, ps=ps, ts=ts: e.tensor_scalar(angf[:, ts], ps[:], sm[:, SM_INVF:SM_INVF + 1], None, ALU.mult),
                      reads=[psr, R("sm")], writes=[R("angf")])
            dump('f', 1, angf, [R('angf')])
            dump('f', 6, posi1.bitcast(F32), [R('posi')], npart=1)
            dump('f', 7, posf1, [R('posf')], npart=1)
            for which, dst, off in ((0, Sb_, 0.0), (1, Cb, float(np.pi / 2))):
                if which == 1:
                    P.add("dve", lambda e, off=off: e.tensor_scalar(angf, angf, off, None, ALU.add),
                          reads=[R("angf")], writes=[R("angf")])
                P.add("dve", lambda e: e.tensor_scalar(kf, angf, 1.0 / TWO_PI, None, ALU.mult),
                      reads=[R("angf")], writes=[R("kf")])
                P.add("dve", lambda e: e.tensor_copy(ki32, kf), reads=[R("kf")], writes=[R("ki32")])
                P.add("dve", lambda e: e.tensor_copy(kf, ki32), reads=[R("ki32")], writes=[R("kf")])
                P.add("dve", lambda e: e.scalar_tensor_tensor(kf, kf, -TWO_PI, angf, ALU.mult, ALU.add),
                      reads=[R("kf"), R("angf")], writes=[R("kf")])
                P.add("dve", lambda e: e.tensor_scalar(kf, kf, float(np.pi), float(-np.pi), ALU.min, ALU.max),
                      reads=[R("kf")], writes=[R("kf")])
                P.add("act", lambda e, dst=dst: e.activation(dst, kf, AF.Sin), reads=[R("kf")], writes=[R("rope", which)])
                dump('f', 2 + which, kf, [R("kf")])
            P.add("dve", lambda e: e.memset(vaug[:, :, :, 128:129], 1.0), writes=[R("vone"), R("kf"), R("ki32"), R("angf")])
            P.add("dve", lambda e: e.memset(vaug[:, :, :, 129:130], 0.0), writes=[R("vpad"), R("kf"), R("ki32"), R("angf")])

            ALV = int(os.environ.get('ATT_LEVEL', '9'))
            for pr in range(4 if ALV > 0 else 0):
                wt, wr = w_slot(blk0 + pr)
                for (dst, coff, nm) in (((qT, 0, "qT"), (kT, 256, "kT")) if not os.environ.get("SKIPQK") else ()):
                    for hh in range(2):
                        for tb in range(4):
                            ts = slice(tb * 512, (tb + 1) * 512)
                            ps, psr = ps_next([0, 1])
                            for k in range(8):
                                P.add("pe", lambda e, wt=wt, ps=ps, k=k, hh=hh, ts=ts, coff=coff: e.matmul(
                                    ps[:], wt[:, k * 768 + coff + hh * 128:k * 768 + coff + (hh + 1) * 128], hT[:, k, ts],
                                    start=(k == 0), stop=(k == 7)), reads=[wr, R("h", k, tb)], writes=[psr])
                            i2 = (hh * 4 + tb) % 2
                            qn = qnb[i2]
                            P.add("act", lambda e, wt=wt, ps=ps, qn=qn: e.activation(qn, ps[:], AF.Copy),
                                  reads=[psr], writes=[R("qn", i2)])
                            ps2, ps2r = ps_next([2, 3])
                            P.add("pe", lambda e, wt=wt, ps2=ps2, qn=qn: e.matmul(ps2[:], permb, qn, start=True, stop=True),
                                  reads=[R("qn", i2), R("cstb")], writes=[ps2r])
                            t1 = t1f[i2]
                            t2 = t2f[i2]
                            P.add("dve", lambda e, wt=wt, ps=ps, t1=t1, ts=ts: e.tensor_tensor(t1, ps[:], Cb[:, ts], ALU.mult),
                                  reads=[psr, R("rope", 1)], writes=[R("t1", i2)])
                            P.add("dve", lambda e, wt=wt, ps2=ps2, t2=t2, ts=ts: e.tensor_tensor(t2, ps2[:], Sb_[:, ts], ALU.mult),
                                  reads=[ps2r, R("rope", 0)], writes=[R("t2", i2)])
                            P.add(os.environ.get("ROPE_ENG", "pool"), lambda e, wt=wt, dst=dst, hh=hh, ts=ts, t1=t1, t2=t2: e.tensor_tensor(
                                dst[:, hh, ts], t1, t2, ALU.add),
                                reads=[R("t1", i2), R("t2", i2)], writes=[R(nm, hh, tb)])
                for tt in range(16 if not os.environ.get("SKIPV") else 0):
                    tok = slice(tt * 128, (tt + 1) * 128)
                    ps, psr = ps_next([0, 1])
                    for k in range(8):
                        P.add("pe", lambda e, wt=wt, ps=ps, k=k, tok=tok: e.matmul(
                            ps[:, 0:256], hT[:, k, tok], wt[:, k * 768 + 512:k * 768 + 768],
                            start=(k == 0), stop=(k == 7)), reads=[wr, R("h", k, tt // 4)], writes=[psr])
                    P.add("act", lambda e, wt=wt, ps=ps, tt=tt: e.activation(
                        vaug[:, tt, :, 0:128], ps[:, 0:256].rearrange("p (h f) -> p h f", h=2), AF.Copy),
                        reads=[psr], writes=[R("vaug", tt)])
                if pr == 0:
                    for hh_ in range(2):
                        dump("b", 0 + hh_, qT[:, hh_, :], [R("qT", hh_, t_) for t_ in range(4)])
                        dump("b", 2 + hh_, kT[:, hh_, :], [R("kT", hh_, t_) for t_ in range(4)])
                    dump("b", 6, Cb, [R("rope", 1)])
                    dump("b", 7, Sb_, [R("rope", 0)])
                    dump("f", 0, lamv[:, :], [R("lamv")])
                    dump("b", 18, vaug[:, 0:7, :, :].rearrange("p c h f -> p (c h f)"), [R("vaug", t_) for t_ in range(16)] + [R("vone"), R("vpad")])
                for hh in range(2 if ALV > 1 else 0):
                    for j in range(4):
                        accs = [(PS[4 + qt], R("ps", 4 + qt)) for qt in range(4)]
                        for i in range(2):
                            prt = slice(64 * i, 64 * i + 64)
                            for kt in range(4 * j + 4):
                                r = kt - 4 * j
                                n0 = 128 * r if r > 0 else 0
                                ps, psr = ps_next([2, 3])
                                P.add("pe", lambda e, wt=wt, ps=ps, prt=prt, hh=hh, kt=kt, j=j, n0=n0: e.matmul(
                                    ps[:, n0:512], kT[prt, hh, kt * 128:(kt + 1) * 128],
                                    qT[prt, hh, j * 512 + n0:(j + 1) * 512], start=True, stop=True),
                                    reads=[R("kT", hh, kt // 4), R("qT", hh, j)], writes=[psr])
                                pi = prot["i"] % 3
                                pT = pTb[pi]
                                pTr = R("pT", pi)
                                P.add("act", lambda e, wt=wt, ps=ps, pT=pT, n0=n0: e.activation(
                                    pT[:, n0:512], ps[:, n0:512], AF.Exp, scale=0.125),
                                    reads=[psr], writes=[pTr])
                                if r >= 0:
                                    P.add("pool", lambda e, wt=wt, pT=pT, n0=n0: e.tensor_tensor(
                                        pT[:, n0:n0 + 128], pT[:, n0:n0 + 128], trib, ALU.mult),
                                        reads=[pTr, R("cstb")], writes=[pTr])
                                for qt in range(n0 // 128, 4):
                                    pa, par = accs[qt]
                                    Q = 4 * j + qt
                                    P.add("pe", lambda e, wt=wt, pa=pa, pT=pT, qt=qt, kt=kt, hh=hh, Q=Q: e.matmul(
                                        pa[:, 0:130], pT[:, qt * 128:(qt + 1) * 128], vaug[:, kt, hh, :],
                                        start=(kt == 0), stop=(kt == Q)),
                                        reads=[pTr, R("vaug", kt), R("vone"), R("vpad")], writes=[par])
                            for qt in range(4):
                                Q = 4 * j + qt
                                pa, par = accs[qt]
                                rl = colf[:, qt * 8:qt * 8 + 8]
                                rlr = R("colf", qt)
                                ta = taf[qt]
                                tar = R("ta", qt)
                                if i == 0:
                                    P.add("dve", lambda e, wt=wt, rl=rl, pa=pa: e.reciprocal(rl[:, 0:1], pa[:, 128:129]),
                                          reads=[par], writes=[rlr])
                                    P.add("dve", lambda e, wt=wt, ta=ta, pa=pa, rl=rl: e.tensor_scalar(
                                        ta, pa[:, 0:128], rl[:, 0:1], None, ALU.mult),
                                        reads=[par, rlr], writes=[tar])
                                    continue
                                td = tdf[Q % 2]
                                tdr = R("td", Q % 2)
                                P.add("dve", lambda e, wt=wt, rl=rl, pa=pa: e.reciprocal(rl[:, 1:2], pa[:, 128:129]),
                                      reads=[par, rlr], writes=[rlr])
                                P.add("dve", lambda e, wt=wt, rl=rl: e.tensor_tensor(rl[:, 2:3], rl[:, 1:2], lamv[:, 0:1], ALU.mult),
                                      reads=[rlr, R("lamv")], writes=[rlr])
                                P.add("dve", lambda e, wt=wt, ta=ta, td=td, pa=pa, rl=rl: e.scalar_tensor_tensor(
                                    td, pa[:, 0:128], rl[:, 2:3], ta, ALU.mult, ALU.add),
                                    reads=[par, rlr, tar], writes=[tdr])
                                P.add("act", lambda e, wt=wt, ta=ta, td=td, rl=rl: e.activation(ta, td, AF.Square, accum_out=rl[:, 3:4]),
                                      reads=[tdr, rlr, tar], writes=[tar, rlr])
                                P.add("act", lambda e, wt=wt, rl=rl: e.activation(rl[:, 4:5], rl[:, 3:4], AF.Ln, bias=EPS, scale=1.0 / 128),
                                      reads=[rlr], writes=[rlr])
                                P.add("act", lambda e, wt=wt, rl=rl: e.activation(rl[:, 5:6], rl[:, 4:5], AF.Exp, scale=-0.5),
                                      reads=[rlr], writes=[rlr])
                                dn = dnb[Q % 2]
                                dnr = R("dn", Q % 2)
                                P.add("dve", lambda e, wt=wt, dn=dn, td=td, rl=rl: e.tensor_scalar(dn, td, rl[:, 5:6], None, ALU.mult),
                                      reads=[tdr, rlr], writes=[dnr])
                                pt, ptr = ps_next([0, 1])
                                ptb = pt[:].bitcast(BF16)
                                P.add("pe", lambda e, wt=wt, ptb=ptb, dn=dn: e.transpose(ptb[:, 0:128], dn, identb),
                                      reads=[dnr, R("cstb")], writes=[ptr])
                                P.add("act", lambda e, wt=wt, ptb=ptb, hh=hh, Q=Q: e.activation(
                                    oT[:, hh, Q * 128:(Q + 1) * 128], ptb[:, 0:128], AF.Copy, scale=lamv[:, 1:2]),
                                    reads=[ptr, R("lamv")], writes=[R("oT", hh, Q // 4)])
                if pr == 0:
                    for hh_ in range(2):
                        dump("b", 4 + hh_, oT[:, hh_, :], [R("oT", hh_, t_) for t_ in range(4)])
                for tb in range(4 if ALV > 2 else 0):
                    ts = slice(tb * 512, (tb + 1) * 512)
                    for dch in range(8):
                        ps, psr = ps_next([0, 1])
                        for hh in range(2):
                            P.add("pe", lambda e, wt=wt, ps=ps, hh=hh, dch=dch, ts=ts: e.matmul(
                                ps[:], wt[:, 6144 + hh * 1024 + dch * 128:6144 + hh * 1024 + (dch + 1) * 128],
                                oT[:, hh, ts], start=(hh == 0), stop=(hh == 1)),
                                reads=[wr, R("oT", hh, tb)], writes=[psr])
                        P.add("dve", lambda e, wt=wt, ps=ps, dch=dch, ts=ts: e.scalar_tensor_tensor(
                            xT[:, dch, ts], ps[:], gate[:, dch:dch + 1], xT[:, dch, ts], ALU.mult, ALU.add),
                            reads=[psr, R("modv"), R("x", dch, tb)], writes=[R("x", dch, tb)])
                w_issue()

        adaln(0, 0)
        norm(dv[:, 0:8], modv[:, 0:8])
        for k in range(8):
            dump("b", 8 + k, hT[:, k, :], [R("h", k, tb) for tb in range(4)])
        dump("f", 4, modv[:, :], [R("modv")])
        dump("f", 5, dv[:, :], [R("dv")])
        gla(6)
        if stage >= 2:
            norm(dv[:, 8:16], modv[:, 24:32])
            mlp(0, 10)
        if stage >= 3:
            adaln(1, 18)
            norm(dv[:, 0:8], modv[:, 48:56])
            att(24)
        if stage >= 4:
            norm(dv[:, 8:16], modv[:, 48 + 24:48 + 32])
            mlp(1, 28)
            norm(sm[:, SM_FG:SM_FG + 8], None, final=True)
        ov = out_d.rearrange("(k p) t -> p k t", p=128)
        outr = []
        for k in range(8):
            rr = R("out", k)
            outr.append(rr)
            P.add("sp", lambda e, k=k: e.dma_start(out=ov[:, k, :], in_=xT[:, k, :]),
                  reads=[R("x", k, tb) for tb in range(4)], writes=[rr], dma=True)
        P.add("sp", lambda e: None, reads=outr, noinst=True)
        P.emit(sems, dsems)
    return nc


def _kmaj(Wc):
    K = Wc.shape[0] // 128
    return np.ascontiguousarray(Wc.reshape(K, 128, -1).transpose(1, 0, 2)).reshape(128, -1)


def _prep_shared(inp):
    f = np.float32
    blocks = np.zeros((NBLK, 128, 8192), f)
    bi = 0
    ada_w = np.asarray(inp["ada_w"], f)
    mlp_w1 = np.asarray(inp["mlp_w1"], f)
    mlp_w2 = np.asarray(inp["mlp_w2"], f)
    gw = np.asarray(inp["gla_w_in"], f)[0]
    gwo = np.asarray(inp["gla_w_o"], f)[0]
    dw = np.asarray(inp["diff_w_in"], f)[0]
    dwo = np.asarray(inp["diff_w_o"], f)[0]
    for nb_ in range(6):
        blocks[bi] = _kmaj(ada_w[0][:, nb_ * 1024:(nb_ + 1) * 1024]); bi += 1
    for g in range(2):
        cols = np.concatenate([np.arange(2 * g * 128, 2 * g * 128 + 256),
                               512 + np.arange(2 * g * 128, 2 * g * 128 + 256),
                               1024 + np.arange(2 * g * 256, 2 * g * 256 + 512)])
        blocks[bi] = _kmaj(gw[:, cols]); bi += 1
        rc = 2048 + np.arange(2 * g * 256, 2 * g * 256 + 512)
        blocks[bi][:, 0:4096] = _kmaj(gw[:, rc])
        blocks[bi][:, 4096:8192] = _kmaj(gwo[g * 512:(g + 1) * 512, :]); bi += 1
    for g in range(4):
        blocks[bi] = _kmaj(mlp_w1[0][:, g * 1024:(g + 1) * 1024]); bi += 1
        blocks[bi] = _kmaj(mlp_w2[0][g * 1024:(g + 1) * 1024, :]); bi += 1
    for nb_ in range(6):
        blocks[bi] = _kmaj(ada_w[1][:, nb_ * 1024:(nb_ + 1) * 1024]); bi += 1
    for pr in range(4):
        cols = np.concatenate([np.arange(pr * 256, pr * 256 + 256),
                               1024 + np.arange(pr * 256, pr * 256 + 256),
                               2048 + np.arange(pr * 256, pr * 256 + 256)])
        blocks[bi][:, 0:6144] = _kmaj(dw[:, cols])
        blocks[bi][:, 6144:8192] = _kmaj(dwo[pr * 256:(pr + 1) * 256, :]); bi += 1
    for g in range(4):
        blocks[bi] = _kmaj(mlp_w1[1][:, g * 1024:(g + 1) * 1024]); bi += 1
        blocks[bi] = _kmaj(mlp_w2[1][g * 1024:(g + 1) * 1024, :]); bi += 1
    assert bi == NBLK

    def pk(v):
        return np.asarray(v, f).reshape(-1, 128).T

    sm = np.zeros((128, NSM), f)
    ng = np.asarray(inp["norm_g"], f)
    for l in range(2):
        for j in range(2):
            sm[:, SM_NG + (2 * l + j) * 8:SM_NG + (2 * l + j) * 8 + 8] = pk(ng[l, j])
    sm[:, SM_FG:SM_FG + 8] = pk(inp["final_g"])
    ab = np.asarray(inp["ada_b"], f)
    for l in range(2):
        sm[:, SM_AB + l * 48:SM_AB + (l + 1) * 48] = pk(ab[l])
    sm[:, SM_BR:SM_BR + 8] = pk(np.asarray(inp["gla_b_r"], f)[0])
    sm[:, SM_GNG:SM_GNG + 2] = pk(np.asarray(inp["gla_norm_g"], f)[0])
    sm[:, SM_SLG:SM_SLG + 1] = pk(np.asarray(inp["diff_subln_g"], f)[0])
    inv_freq = (np.float32(500000.0) ** (-(np.arange(0, 16, 2, dtype=np.float32)) / np.float32(16))).astype(f)
    invf = np.zeros(128, f)
    for p in range(128):
        d = p % 64
        if d < 16:
            invf[p] = inv_freq[d % 8]
    sm[:, SM_INVF] = invf

    rw = np.asarray(inp["diff_lambda"], f)[0].reshape(1, 256)
    wa = np.zeros((128, 8, 48), f)
    wa[:, :, 32:48] = gw[:, 3072:3088].reshape(8, 128, 16).transpose(1, 0, 2)
    wa = wa.reshape(128, 8 * 48)
    wa2 = np.zeros((48, 512), f)
    wa2[0] = np.asarray(inp["gla_b_a"], f)[0]
    wa2[32:48] = np.asarray(inp["gla_w_a2"], f)[0]
    cst = np.zeros((128, 512), f)
    cst[:, 0:128] = np.eye(128, dtype=f)
    tri = (np.arange(128)[None, :] >= np.arange(128)[:, None]).astype(f)
    cst[:, 128:256] = tri
    cst[:, 256:384] = tri * f(-1.0 / 16.0)
    perm = np.zeros((128, 128), f)
    for b_ in range(2):
        base = 64 * b_
        for d in range(8):
            perm[base + d + 8, base + d] = -1.0
        for d in range(8, 16):
            perm[base + d - 8, base + d] = 1.0
    cst[:, 384:512] = perm
    return dict(wblk=blocks, sm=sm, rw=rw, wa=wa, wa2=wa2, cst=cst)


_NC_CACHE = {}


def kernel(x, c, positions, ada_w, ada_b, norm_g, mlp_w1, mlp_w2,
           gla_w_in, gla_w_a2, gla_b_a, gla_b_r, gla_norm_g, gla_w_o,
           diff_w_in, diff_lambda, diff_subln_g, diff_w_o, final_g, _stage=4, _debug=False, _ncores=NB):
    inp = dict(ada_w=ada_w, ada_b=ada_b, norm_g=norm_g, mlp_w1=mlp_w1, mlp_w2=mlp_w2,
               gla_w_in=gla_w_in, gla_w_a2=gla_w_a2, gla_b_a=gla_b_a, gla_b_r=gla_b_r,
               gla_norm_g=gla_norm_g, gla_w_o=gla_w_o, diff_w_in=diff_w_in, diff_lambda=diff_lambda,
               diff_subln_g=diff_subln_g, diff_w_o=diff_w_o, final_g=final_g)
    sh = _prep_shared(inp)
    x = np.asarray(x, np.float32)
    c = np.asarray(c, np.float32)
    positions = np.asarray(positions, np.int32)
    in_maps = []
    for b in range(NB):
        sm = sh["sm"].copy()
        sm[:, SM_C:SM_C + 8] = c[b].reshape(8, 128).T
        in_maps.append(dict(xT=np.ascontiguousarray(x[b].T), pos=np.ascontiguousarray(positions[b:b + 1]).view(np.float32),
                            sm=sm, rw=sh["rw"], wa=sh["wa"], wa2=sh["wa2"], cst=sh["cst"], wblk=sh["wblk"]))
    key = (_stage, _debug)
    if key not in _NC_CACHE:
        _NC_CACHE[key] = build_nc(_stage, _debug)
    nc = _NC_CACHE[key]
    res = run_bass_kernel_spmd(nc, in_maps[:_ncores], core_ids=list(range(_ncores)))
    out = np.stack([np.ascontiguousarray(r["outT"].T) for r in res.results], axis=0)
    if _debug:
        return out.astype(np.float32), res.results[0]["dbgf"], res.results[0]["dbgb"]
    return out.astype(np.float32)
```

```python
import contextlib
import os
import math
import numpy as np
import concourse.bass as bass
import concourse.mybir as mybir
from concourse.bass_utils import run_bass_kernel_spmd

F32 = mybir.dt.float32
BF16 = mybir.dt.bfloat16
I32 = mybir.dt.int32
AF = mybir.ActivationFunctionType
ALU = mybir.AluOpType
AX = mybir.AxisListType

D = 1024
T = 2048
NB = 8
DFF = 4096
EPS = 1e-6
LAMBDA_INIT = 0.8 - 0.6 * math.exp(-0.3 * 1)
NBLK = 36
NSLOT = 3
TWO_PI = float(2.0 * np.pi)

SM_NG = 0
SM_FG = 32
SM_AB = 40
SM_BR = 136
SM_GNG = 144
SM_SLG = 146
SM_INVF = 147
SM_C = 148
NSM = 160


class Res:
    __slots__ = ("w", "r", "excl")

    def __init__(self):
        self.w = None
        self.r = []
        self.excl = False


class Op:
    __slots__ = ("eng", "fn", "deps", "eidx", "signal", "seq", "dma", "dsem", "dval")


class Prog:
    NDMA = 24

    def __init__(self, nc, same_engine_sync=True):
        self.nc = nc
        self.engs = {"pe": nc.tensor, "act": nc.scalar, "dve": nc.vector,
                     "pool": nc.gpsimd, "sp": nc.sync}
        self.ops = []
        self.ecount = {k: 0 for k in self.engs}
        self.known = {k: {} for k in self.engs}
        self.kdma = {k: {} for k in self.engs}
        self.ndma = {"sp": 0, "pool": 0}
        self.dma_ops = {"sp": [], "pool": []}
        self.dbase = {"sp": 0, "pool": self.NDMA // 2}
        self.same_engine_sync = same_engine_sync
        self.res = {}
        self.last = {}

    def R(self, *key):
        r = self.res.get(key)
        if r is None:
            r = Res()
            r.excl = (key[0] == "ps")
            self.res[key] = r
        return r

    def fence(self):
        last = dict(self.last)
        for eng in self.engs:
            self.add(eng, lambda e: None, extra=[o for k, o in last.items() if k != eng], noinst=True)

    def add(self, eng, fn, reads=(), writes=(), dma=False, extra=(), noinst=False):
        op = Op()
        op.eng = eng
        op.fn = fn
        op.dma = dma
        op.signal = False
        op.seq = None
        op.eidx = self.ecount[eng]
        self.ecount[eng] += 1
        if any(r.excl for r in reads):
            writes = list(writes) + [r for r in reads if r.excl]
            reads = [r for r in reads if not r.excl]
        deps = list(extra)
        for r in reads:
            if r.w is not None:
                deps.append(r.w)
        for r in writes:
            if r.w is not None:
                deps.append(r.w)
            deps.extend(r.r)
        if dma:
            i = self.ndma[eng]
            self.ndma[eng] += 1
            h = self.NDMA // 2
            op.dsem = self.dbase[eng] + i % h
            op.dval = 16 * (i // h + 1)
            if i >= h:
                deps.append(self.dma_ops[eng][i - h])
            self.dma_ops[eng].append(op)
        best = {}
        for d in deps:
            if d is op:
                continue
            if d.dma:
                if self.kdma[eng].get(d.dsem, 0) >= d.dval:
                    continue
                key = ("dma", d.dsem)
                if key not in best or best[key].dval < d.dval:
                    best[key] = d
            else:
                if d.eng == eng and (eng == "pe" or not self.same_engine_sync):
                    continue
                if self.known[eng].get(d.eng, -1) >= d.eidx:
                    continue
                key = d.eng
                if key not in best or best[key].eidx < d.eidx:
                    best[key] = d
        final = []
        for key, d in best.items():
            final.append(d)
            if d.dma:
                self.kdma[eng][d.dsem] = d.dval
            else:
                d.signal = True
                self.known[eng][d.eng] = d.eidx
        op.deps = final
        for r in reads:
            r.r.append(op)
        for r in writes:
            r.w = op
            r.r = []
        self.ops.append(op)
        if not dma and not noinst:
            self.last[eng] = op
        return op

    def emit(self, sems, dsems):
        cnt = {k: 0 for k in self.engs}
        for op in self.ops:
            if not op.dma and op.signal:
                cnt[op.eng] += 1
                op.seq = cnt[op.eng]
        for op in self.ops:
            e = self.engs[op.eng]
            for d in op.deps:
                if d.dma:
                    e.wait_ge(dsems[d.dsem], d.dval)
                else:
                    e.wait_ge(sems[d.eng], d.seq)
            ins = op.fn(e)
            if ins is None:
                continue
            if op.dma:
                ins.then_inc(dsems[op.dsem], 16)
            elif op.signal:
                ins.then_inc(sems[op.eng], 1)
        return cnt


def build_nc(stage=4, debug=False):
    nc = bass.Bass("TRN2", target_bir_lowering=False)
    xT_d = nc.dram_tensor("xT", [D, T], F32, kind="ExternalInput").ap()
    pos_d = nc.dram_tensor("pos", [1, T], F32, kind="ExternalInput").ap()
    sm_d = nc.dram_tensor("sm", [128, NSM], F32, kind="ExternalInput").ap()
    rw_d = nc.dram_tensor("rw", [1, 256], F32, kind="ExternalInput").ap()
    wa_d = nc.dram_tensor("wa", [128, 8 * 48], F32, kind="ExternalInput").ap()
    wa2_d = nc.dram_tensor("wa2", [48, 512], F32, kind="ExternalInput").ap()
    cst_d = nc.dram_tensor("cst", [128, 512], F32, kind="ExternalInput").ap()
    wblk_d = nc.dram_tensor("wblk", [NBLK, 128, 8192], F32, kind="ExternalInput").ap()
    out_d = nc.dram_tensor("outT", [D, T], F32, kind="ExternalOutput").ap()
    if debug:
        dbgf_d = nc.dram_tensor("dbgf", [16, 128, 2048], F32, kind="ExternalOutput").ap()
        dbgb_d = nc.dram_tensor("dbgb", [24, 128, 2048], BF16, kind="ExternalOutput").ap()

    es = contextlib.ExitStack()
    with es:
        def sb(name, shape, dt):
            return es.enter_context(nc.sbuf_tensor(name, shape, dt))

        xT = sb("xTs", [128, 8, T], F32)
        hT = sb("hTs", [128, 8, T], BF16)
        wsl = [sb("wsl%d" % i, [128, 8192], BF16) for i in range(NSLOT)]
        cstf = sb("cstf", [128, 512], F32)
        cstb = sb("cstb", [128, 512], BF16)
        onesb = sb("onesb", [128, 128], BF16)
        onesf = sb("onesf", [128, 128], F32)
        sm = sb("sms", [128, NSM], F32)
        modv = sb("modv", [128, 96], F32)
        dv = sb("dvs", [128, 64], F32)
        cact = sb("cact", [128, 8], BF16)
        lamv = sb("lamv", [128, 8], F32)
        lrow_t = sb("lrow", [1, 256], F32)
        U = sb("U", [128, 29400], BF16)
        PS = [es.enter_context(nc.psum_tensor("ps%d" % i, [128, 512], F32)) for i in range(8)]
        sems = {k: es.enter_context(nc.semaphore("s_" + k)) for k in ["pe", "act", "dve", "pool", "sp"]}
        dsems = [es.enter_context(nc.semaphore("d%d" % i)) for i in range(Prog.NDMA)]

        P = Prog(nc)
        R = P.R

        identf = cstf[:, 0:128]
        trinegf = cstf[:, 256:384]
        identb = cstb[:, 0:128]
        trib = cstb[:, 128:256]
        permb = cstb[:, 384:512]
        negmb = cstb[:, 256:384]

        def Ub(off, n):
            return U[:, off:off + n]

        def Uf(off, n):
            return U[:, off:off + 2 * n].bitcast(F32)

        def dump(kind, idx, ap, reads, npart=128):
            if not debug:
                return
            n = ap.shape[-1]
            dst = (dbgf_d if kind == "f" else dbgb_d)[idx, 0:npart, 0:n]
            P.add("sp", lambda e: e.dma_start(out=dst, in_=ap), reads=reads, writes=[R("dbg", kind, idx)], dma=True)

        wstate = {"next": 0}

        def w_issue():
            i = wstate["next"]
            if i >= NBLK:
                return
            wstate["next"] = i + 1
            s = i % NSLOT
            P.add("pool", lambda e, i=i, s=s: e.dma_start(out=wsl[s][:], in_=wblk_d[i]),
                  writes=[R("w", s)], dma=True)

        def w_slot(i):
            return wsl[i % NSLOT], R("w", i % NSLOT)

        prot = {"i": 0}

        def ps_next(banks):
            b = banks[prot["i"] % len(banks)]
            prot["i"] += 1
            return PS[b], R("ps", b)

        P.add("sp", lambda e: e.dma_start(out=sm[:], in_=sm_d), writes=[R("sm")], dma=True)
        P.add("sp", lambda e: e.dma_start(out=cstf[:], in_=cst_d), writes=[R("cstf")], dma=True)
        for i in range(NSLOT):
            w_issue()
        xv = xT_d.rearrange("(k p) t -> p k t", p=128)
        for k in range(8):
            P.add("sp", lambda e, k=k: e.dma_start(out=xT[:, k, :], in_=xv[:, k, :]),
                  writes=[R("x", k, tb) for tb in range(4)], dma=True)
        P.add("dve", lambda e: e.tensor_copy(cstb[:], cstf[:]), reads=[R("cstf")], writes=[R("cstb")])
        P.add("dve", lambda e: e.tensor_scalar(cstb[:, 256:384], cstf[:, 128:256], -1.0, 30000.0, ALU.add, ALU.mult),
              reads=[R("cstf"), R("cstb")], writes=[R("cstb")])
        P.add("dve", lambda e: e.memset(onesb[:], 1.0), writes=[R("onesb")])
        P.add("dve", lambda e: e.memset(onesf[:], 1.0), writes=[R("onesf")])
        P.add("act", lambda e: e.activation(cact[:], sm[:, SM_C:SM_C + 8], AF.Silu),
              reads=[R("sm")], writes=[R("cact")])

        def adaln(l, blk0):
            P.fence()
            modrow = U[0:1, 0:12288].bitcast(F32)
            for nb_ in range(6):
                wt, wr = w_slot(blk0 + nb_)
                for half in range(2):
                    n0 = half * 512
                    ps, psr = ps_next([0, 1])
                    for k in range(8):
                        P.add("pe", lambda e, ps=ps, wt=wt, k=k, n0=n0: e.matmul(
                            ps[0:1, :], cact[:, k:k + 1], wt[:, k * 1024 + n0:k * 1024 + n0 + 512],
                            start=(k == 0), stop=(k == 7)),
                            reads=[wr, R("cact")], writes=[psr])
                    c0 = nb_ * 1024 + n0
                    P.add("act", lambda e, ps=ps, c0=c0: e.activation(modrow[0:1, c0:c0 + 512], ps[0:1, :], AF.Copy),
                          reads=[psr], writes=[R("modrow")])
                w_issue()
            ps, psr = ps_next([0, 1])
            for m in range(48):
                P.add("pe", lambda e, ps=ps, m=m: e.matmul(
                    ps[:, m:m + 1], modrow[0:1, m * 128:(m + 1) * 128], onesf[0:1, 0:1], start=True, stop=True),
                    reads=[R("modrow"), R("onesf")], writes=[psr])
            P.add("dve", lambda e, ps=ps, l=l: e.tensor_tensor(
                modv[:, l * 48:(l + 1) * 48], ps[:, 0:48], sm[:, SM_AB + l * 48:SM_AB + (l + 1) * 48], ALU.add),
                reads=[psr, R("sm")], writes=[R("modv")])
            mo = l * 48
            P.add("dve", lambda e: e.scalar_tensor_tensor(
                dv[:, 0:8], modv[:, mo + 8:mo + 16], 1.0, sm[:, SM_NG + (2 * l) * 8:SM_NG + (2 * l) * 8 + 8],
                ALU.add, ALU.mult), reads=[R("modv"), R("sm")], writes=[R("dv")])
            P.add("dve", lambda e: e.scalar_tensor_tensor(
                dv[:, 8:16], modv[:, mo + 32:mo + 40], 1.0, sm[:, SM_NG + (2 * l + 1) * 8:SM_NG + (2 * l + 1) * 8 + 8],
                ALU.add, ALU.mult), reads=[R("modv"), R("sm")], writes=[R("dv")])

        NO = 29400 - 5120
        UEND = 29400

        def norm(gs_ap, sh_ap, final=False):
            P.fence()
            for tb in range(4):
                ts = slice(tb * 512, (tb + 1) * 512)
                pss, pssr = ps_next([0, 1])
                for k in range(8):
                    sq = Ub(NO + (k % 2) * 512, 512)
                    sqr = R("n_sq", k % 2)
                    P.add("act", lambda e, sq=sq, k=k, ts=ts: e.activation(sq, xT[:, k, ts], AF.Square),
                          reads=[R("x", k, tb)], writes=[sqr])
                    P.add("pe", lambda e, pss=pss, sq=sq, k=k: e.matmul(pss[:], onesb[:], sq, start=(k == 0), stop=(k == 7)),
                          reads=[sqr, R("onesb")], writes=[pssr])
                lnt = Uf(NO + 1024, 512)
                rstd = Uf(NO + 2048, 512)
                P.add("act", lambda e, pss=pss, lnt=lnt: e.activation(lnt, pss[:], AF.Ln, bias=EPS, scale=1.0 / D),
                      reads=[pssr], writes=[R("n_ln")])
                P.add("act", lambda e, lnt=lnt, rstd=rstd: e.activation(rstd, lnt, AF.Exp, scale=-0.5),
                      reads=[R("n_ln")], writes=[R("n_rstd")])
                for k in range(8):
                    if final:
                        P.add("dve", lambda e, k=k, ts=ts, rstd=rstd: e.scalar_tensor_tensor(
                            xT[:, k, ts], xT[:, k, ts], gs_ap[:, k:k + 1], rstd, ALU.mult, ALU.mult),
                            reads=[R("n_rstd"), R("sm")], writes=[R("x", k, tb)])
                    else:
                        tmp = Uf(NO + 3072 + (k % 2) * 1024, 512)
                        tr = R("n_tmp", k % 2)
                        P.add("dve", lambda e, k=k, ts=ts, rstd=rstd, tmp=tmp: e.scalar_tensor_tensor(
                            tmp, xT[:, k, ts], gs_ap[:, k:k + 1], rstd, ALU.mult, ALU.mult),
                            reads=[R("n_rstd"), R("x", k, tb), R("dv")], writes=[tr])
                        P.add("act", lambda e, k=k, ts=ts, tmp=tmp: e.activation(
                            hT[:, k, ts], tmp, AF.Identity, bias=sh_ap[:, k:k + 1]),
                            reads=[tr, R("modv")], writes=[R("h", k, tb)])

        def mlp(l, blk0):
            P.fence()
            mo = l * 48
            gate = modv[:, mo + 40:mo + 48]
            hid = U[:, 0:16384].rearrange("p (j t) -> p j t", j=8)
            for g in range(4):
                w1, w1r = w_slot(blk0 + 2 * g)
                for tb in range(4):
                    ts = slice(tb * 512, (tb + 1) * 512)
                    for j in range(8):
                        ps, psr = ps_next([0, 1, 2, 3])
                        for k in range(8):
                            P.add("pe", lambda e, ps=ps, w1=w1, k=k, j=j, ts=ts: e.matmul(
                                ps[:], w1[:, k * 1024 + j * 128:k * 1024 + (j + 1) * 128], hT[:, k, ts],
                                start=(k == 0), stop=(k == 7)),
                                reads=[w1r, R("h", k, tb)], writes=[psr])
                        sq = Uf(16384 + (j % 2) * 1024, 512)
                        sqr = R("m_sq", j % 2)
                        P.add("act", lambda e, ps=ps, sq=sq: e.activation(sq, ps[:], AF.Square),
                              reads=[psr], writes=[sqr])
                        P.add("dve", lambda e, ps=ps, sq=sq, j=j, ts=ts: e.scalar_tensor_tensor(
                            hid[:, j, ts], ps[:], 0.0, sq, ALU.is_gt, ALU.mult),
                            reads=[psr, sqr], writes=[R("hid", j, tb)])
                w_issue()
                w2, w2r = w_slot(blk0 + 2 * g + 1)
                for tb in range(4):
                    ts = slice(tb * 512, (tb + 1) * 512)
                    for dch in range(8):
                        ps, psr = ps_next([0, 1, 2, 3])
                        for j in range(8):
                            P.add("pe", lambda e, ps=ps, w2=w2, j=j, dch=dch, ts=ts: e.matmul(
                                ps[:], w2[:, j * 1024 + dch * 128:j * 1024 + (dch + 1) * 128], hid[:, j, ts],
                                start=(j == 0), stop=(j == 7)),
                                reads=[w2r, R("hid", j, tb)], writes=[psr])
                        P.add("dve", lambda e, ps=ps, dch=dch, ts=ts: e.scalar_tensor_tensor(
                            xT[:, dch, ts], ps[:], gate[:, dch:dch + 1], xT[:, dch, ts], ALU.mult, ALU.add),
                            reads=[psr, R("modv"), R("x", dch, tb)], writes=[R("x", dch, tb)])
                w_issue()

        def gla(blk0):
            P.fence()
            gate = modv[:, 16:24]
            o = 0
            a_aug = U[0:48, o:o + 2048]; o += 2048
            wab = Ub(o, 384); o += 384
            wa2b = U[0:48, o:o + 512]; o += 512
            waf = Uf(o, 384); o += 768
            wa2f = U[0:48, o:o + 1024].bitcast(F32); o += 1024
            spf = [Uf(o + i * 512, 256) for i in range(2)]; o += 1024
            ebf = [Uf(o + i * 1024, 512) for i in range(2)]; o += 2048
            enbf = [Uf(o + i * 1024, 512) for i in range(2)]; o += 2048
            qd = U[:, o:o + 1024].rearrange("p (h t) -> p h t", h=2); o += 1024
            ki = U[:, o:o + 1024].rearrange("p (h t) -> p h t", h=2); o += 1024
            ktok = U[:, o:o + 1024].rearrange("p (c h f) -> p c h f", c=4, h=2); o += 1024
            vtok = U[:, o:o + 2048].rearrange("p (c f) -> p c f", c=4); o += 2048
            rg = U[:, o:o + 2048].rearrange("p (j t) -> p j t", j=4); o += 2048
            og = U[:, o:o + 2048].rearrange("p (j t) -> p j t", j=4); o += 2048
            Sf = U[:, o:o + 1024].bitcast(F32).rearrange("p (h f) -> p h f", h=2); o += 1024
            Sb2 = [U[:, o + i * 512:o + (i + 1) * 512].rearrange("p (h f) -> p h f", h=2) for i in range(2)]; o += 1024
            sTb = [Ub(o + i * 128, 128) for i in range(2)]; o += 256
            tmpf = [Uf(o + i * 1024, 512) for i in range(2)]; o += 2048
            stmp = [Uf(o + i * 512, 256) for i in range(2)]; o += 1024
            etmp = Uf(o, 256); o += 512
            sqb = [Ub(o + i * 512, 512) for i in range(2)]; o += 1024
            assert o <= UEND, o

            P.add("sp", lambda e: e.dma_start(out=waf, in_=wa_d), writes=[R("waf"), R("modrow")], dma=True)
            P.add("sp", lambda e: e.dma_start(out=wa2f, in_=wa2_d), writes=[R("wa2f"), R("modrow")], dma=True)
            P.add("dve", lambda e: e.tensor_copy(wab, waf), reads=[R("waf")], writes=[R("wab")])
            P.add("dve", lambda e: e.tensor_copy(wa2b, wa2f), reads=[R("wa2f")], writes=[R("wa2b")])
            for tb in range(4):
                ts = slice(tb * 512, (tb + 1) * 512)
                ps, psr = ps_next([0, 1])
                for k in range(8):
                    P.add("pe", lambda e, ps=ps, k=k, ts=ts: e.matmul(
                        ps[0:48, :], wab[:, k * 48:(k + 1) * 48], hT[:, k, ts], start=(k == 0), stop=(k == 7)),
                        reads=[R("wab"), R("h", k, tb)], writes=[psr])
                P.add("act", lambda e, ps=ps, ts=ts: e.activation(a_aug[0:48, ts], ps[0:48, :], AF.Copy),
                      reads=[psr], writes=[R("a_aug", tb)])
                P.add("dve", lambda e, ts=ts: e.memset(a_aug[0:1, ts], 1.0),
                      reads=[R("a_aug", tb)], writes=[R("a_aug", tb)])

            for g in range(2):
                wA, wAr = w_slot(blk0 + 2 * g)
                wB, wBr = w_slot(blk0 + 2 * g + 1)
                if g == 0:
                    dump("b", 7, wsl[0][:, 0:2048], [R("w", 0)])
                    dump("b", 16, wsl[1][:, 0:2048], [R("w", 1)])
                    dump("b", 17, wsl[2][:, 0:2048], [R("w", 2)])
                P.add("dve", lambda e: e.memset(Sf[:, :, :], 0.0), writes=[R("Sf", 0), R("Sf", 1)])
                for par in range(2):
                    P.add("dve", lambda e, par=par: e.memset(Sb2[par][:, :, :], 0.0), writes=[R("Sb", 0, par), R("Sb", 1, par)])
                for tb in range(4):
                    ts = slice(tb * 512, (tb + 1) * 512)
                    pb = [(PS[2], R("ps", 2)), (PS[3], R("ps", 3))]
                    for c in range(4):
                        tok = slice(tb * 512 + c * 128, tb * 512 + (c + 1) * 128)
                        ps, psr = ps_next([0, 1])
                        P.add("pe", lambda e, wA=wA, wB=wB, ps=ps, tok=tok, g=g: e.matmul(
                            ps[:, 0:256], a_aug[0:48, tok], wa2b[0:48, g * 256:(g + 1) * 256], start=True, stop=True),
                            reads=[R("a_aug", tb), R("wa2b")], writes=[psr])
                        P.add("act", lambda e, wA=wA, wB=wB, ps=ps: e.activation(etmp, ps[:, 0:256], AF.Exp, scale=-1.0),
                              reads=[psr], writes=[R("etmp")])
                        sp_ = spf[c % 2]
                        P.add("act", lambda e, wA=wA, wB=wB, sp_=sp_: e.activation(sp_, etmp, AF.Ln, bias=1.0),
                              reads=[R("etmp")], writes=[R("sp", c % 2)])
                        ps, psr = ps_next([0, 1])
                        for k in range(8):
                            P.add("pe", lambda e, wA=wA, wB=wB, ps=ps, k=k, tok=tok: e.matmul(
                                ps[:], hT[:, k, tok], wA[:, k * 1024 + 512:k * 1024 + 1024],
                                start=(k == 0), stop=(k == 7)), reads=[wAr, R("h", k, tb)], writes=[psr])
                        P.add("act", lambda e, wA=wA, wB=wB, ps=ps, c=c: e.activation(vtok[:, c, :], ps[:], AF.Copy),
                              reads=[psr], writes=[R("vtok", c)])
                        for hh in range(2):
                            P.add("pe", lambda e, wA=wA, wB=wB, hh=hh, c=c, sp_=sp_, pb=pb: e.matmul(
                                pb[hh][0][:, c * 128:(c + 1) * 128], sp_[:, hh * 128:(hh + 1) * 128], trinegf,
                                start=True, stop=True),
                                reads=[R("sp", c % 2), R("cstf")], writes=[pb[hh][1]])
                    for hh in range(2):
                        P.add("act", lambda e, wA=wA, wB=wB, hh=hh, pb=pb: e.activation(ebf[hh], pb[hh][0][:], AF.Exp),
                              reads=[pb[hh][1]], writes=[R("eb", hh)])
                        P.add("act", lambda e, wA=wA, wB=wB, hh=hh, pb=pb: e.activation(enbf[hh], pb[hh][0][:], AF.Exp, scale=-1.0),
                              reads=[pb[hh][1]], writes=[R("enb", hh)])
                    for jj in range(4):
                        ps, psr = ps_next([0, 1])
                        for k in range(8):
                            P.add("pe", lambda e, wA=wA, wB=wB, ps=ps, k=k, jj=jj, ts=ts: e.matmul(
                                ps[:], wB[:, k * 512 + jj * 128:k * 512 + (jj + 1) * 128], hT[:, k, ts],
                                start=(k == 0), stop=(k == 7)), reads=[wBr, R("h", k, tb)], writes=[psr])
                        bc = SM_BR + g * 4 + jj
                        P.add("act", lambda e, wA=wA, wB=wB, ps=ps, jj=jj, bc=bc: e.activation(
                            rg[:, jj, :], ps[:], AF.Silu, bias=sm[:, bc:bc + 1]),
                            reads=[psr, R("sm")], writes=[R("rg", jj)])
                    for hh in range(2):
                        ps, psr = ps_next([0, 1])
                        for k in range(8):
                            P.add("pe", lambda e, wA=wA, wB=wB, ps=ps, k=k, hh=hh, ts=ts: e.matmul(
                                ps[:], wA[:, k * 1024 + hh * 128:k * 1024 + (hh + 1) * 128], hT[:, k, ts],
                                start=(k == 0), stop=(k == 7)), reads=[wAr, R("h", k, tb)], writes=[psr])
                        P.add("dve", lambda e, wA=wA, wB=wB, ps=ps, hh=hh: e.scalar_tensor_tensor(
                            qd[:, hh, :], ps[:], 128.0 ** -0.5, ebf[hh], ALU.mult, ALU.mult),
                            reads=[psr, R("eb", hh)], writes=[R("qd", hh)])
                        ps, psr = ps_next([0, 1])
                        for k in range(8):
                            P.add("pe", lambda e, wA=wA, wB=wB, ps=ps, k=k, hh=hh, ts=ts: e.matmul(
                                ps[:], wA[:, k * 1024 + 256 + hh * 128:k * 1024 + 256 + (hh + 1) * 128], hT[:, k, ts],
                                start=(k == 0), stop=(k == 7)), reads=[wAr, R("h", k, tb)], writes=[psr])
                        P.add("dve", lambda e, wA=wA, wB=wB, ps=ps, hh=hh: e.tensor_tensor(ki[:, hh, :], ps[:], enbf[hh], ALU.mult),
                              reads=[psr, R("enb", hh)], writes=[R("ki", hh)])
                    for c in range(4):
                        pt, ptr = PS[4 + (c % 2)], R("ps", 4 + (c % 2))
                        ptb = pt[:].bitcast(BF16)
                        for hh in range(2):
                            P.add("pe", lambda e, wA=wA, wB=wB, ptb=ptb, c=c, hh=hh: e.transpose(
                                ptb[:, hh * 128:(hh + 1) * 128], ki[:, hh, c * 128:(c + 1) * 128], identb),
                                reads=[R("ki", hh), R("cstb")], writes=[ptr])
                        P.add("act", lambda e, wA=wA, wB=wB, ptb=ptb, c=c: e.activation(
                            ktok[:, c, :, :].rearrange("p h f -> p (h f)"), ptb[:, 0:256], AF.Copy),
                            reads=[ptr], writes=[R("ktok", c)])
                    po = [(PS[4 + i], R("ps", 4 + i)) for i in range(4)]
                    for c in range(4):
                        cs = slice(c * 128, (c + 1) * 128)
                        cg = tb * 4 + c
                        Sb_w = Sb2[cg % 2]
                        Sb_r = Sb2[(cg + 1) % 2]
                        for hh in range(2):
                            pss, pssr = PS[2 + hh], R("ps", 2 + hh)
                            sl = (c % 4) * 128
                            P.add("pe", lambda e, wA=wA, wB=wB, pss=pss, sl=sl, hh=hh, cs=cs: e.matmul(
                                pss[:, sl:sl + 128], ki[:, hh, cs], qd[:, hh, cs], start=True, stop=True),
                                reads=[R("ki", hh), R("qd", hh)], writes=[pssr])
                            sT = sTb[hh]
                            P.add("dve", lambda e, wA=wA, wB=wB, pss=pss, sl=sl, sT=sT: e.tensor_tensor(
                                sT, pss[:, sl:sl + 128], trib, ALU.mult),
                                reads=[pssr, R("cstb")], writes=[R("sT", hh)])
                        for hh in range(2):
                            psp, pspr = ps_next([0, 1])
                            P.add("pe", lambda e, wA=wA, wB=wB, psp=psp, c=c, hh=hh: e.matmul(
                                psp[:, 0:256], ktok[:, c, hh, :], vtok[:, c, hh * 256:(hh + 1) * 256],
                                start=True, stop=True),
                                reads=[R("ktok", c), R("vtok", c)], writes=[pspr])
                            st = stmp[hh]
                            P.add("dve", lambda e, wA=wA, wB=wB, psp=psp, st=st, hh=hh: e.tensor_tensor(
                                st, psp[:, 0:256], Sf[:, hh, :], ALU.add),
                                reads=[pspr, R("Sf", hh)], writes=[R("stmp", hh)])
                            ecol = ebf[hh][:, c * 128 + 127:c * 128 + 128]
                            P.add("dve", lambda e, wA=wA, wB=wB, st=st, hh=hh, ecol=ecol: e.tensor_scalar(
                                Sf[:, hh, :], st, ecol, None, ALU.mult),
                                reads=[R("stmp", hh), R("eb", hh)], writes=[R("Sf", hh)])
                            P.add("act", lambda e, wA=wA, wB=wB, st=st, hh=hh, ecol=ecol, Sb_w=Sb_w: e.activation(
                                Sb_w[:, hh, :], st, AF.Copy, scale=ecol),
                                reads=[R("stmp", hh), R("eb", hh)], writes=[R("Sb", hh, cg % 2)])
                        for hh in range(2):
                            sT = sTb[hh]
                            for jj in range(2):
                                pot, potr = po[hh * 2 + jj]
                                P.add("pe", lambda e, wA=wA, wB=wB, pot=pot, c=c, hh=hh, jj=jj, sT=sT, cs=cs: e.matmul(
                                    pot[:, cs], vtok[:, c, hh * 256 + jj * 128:hh * 256 + (jj + 1) * 128], sT,
                                    start=True, stop=False),
                                    reads=[R("vtok", c), R("sT", hh)], writes=[potr])
                                P.add("pe", lambda e, wA=wA, wB=wB, pot=pot, hh=hh, jj=jj, cs=cs, Sb_r=Sb_r: e.matmul(
                                    pot[:, cs], Sb_r[:, hh, jj * 128:(jj + 1) * 128], qd[:, hh, cs],
                                    start=False, stop=True),
                                    reads=[R("Sb", hh, (cg + 1) % 2), R("qd", hh)], writes=[potr])
                    if g == 0 and tb == 0:
                        dump("f", 3, Sf[:, :, :].rearrange("p h f -> p (h f)"), [R("Sf", 0), R("Sf", 1)])
                    for hh in range(2):
                        pst, pstr = ps_next([0, 1])
                        for jj in range(2):
                            pot, potr = po[hh * 2 + jj]
                            sq = sqb[jj]
                            P.add("act", lambda e, wA=wA, wB=wB, pot=pot, sq=sq: e.activation(sq, pot[:], AF.Square),
                                  reads=[potr], writes=[R("g_sq", jj)])
                            P.add("pe", lambda e, wA=wA, wB=wB, pst=pst, sq=sq, jj=jj: e.matmul(
                                pst[:], onesb[:], sq, start=(jj == 0), stop=(jj == 1)),
                                reads=[R("g_sq", jj), R("onesb")], writes=[pstr])
                        lnt = tmpf[0]
                        rstd = tmpf[1]
                        P.add("act", lambda e, wA=wA, wB=wB, pst=pst, lnt=lnt: e.activation(lnt, pst[:], AF.Ln, bias=EPS, scale=1.0 / 256),
                              reads=[pstr], writes=[R("g_ln")])
                        P.add("act", lambda e, wA=wA, wB=wB, lnt=lnt, rstd=rstd: e.activation(rstd, lnt, AF.Exp, scale=-0.5),
                              reads=[R("g_ln")], writes=[R("g_rstd")])
                        for jj in range(2):
                            pot, potr = po[hh * 2 + jj]
                            t2 = ebf[jj]
                            P.add("dve", lambda e, wA=wA, wB=wB, pot=pot, jj=jj, rstd=rstd, t2=t2: e.scalar_tensor_tensor(
                                t2, pot[:], sm[:, SM_GNG + jj:SM_GNG + jj + 1], rstd, ALU.mult, ALU.mult),
                                reads=[potr, R("g_rstd"), R("sm"), R("eb", jj)], writes=[R("eb", jj)])
                            P.add("dve", lambda e, wA=wA, wB=wB, hh=hh, jj=jj, t2=t2: e.tensor_tensor(
                                og[:, hh * 2 + jj, :], t2, rg[:, hh * 2 + jj, :], ALU.mult),
                                reads=[R("eb", jj), R("rg", hh * 2 + jj)], writes=[R("og", hh * 2 + jj)])
                    if g == 0 and tb == 0:
                        dump("b", 5, og[:, :, :].rearrange("p j t -> p (j t)"), [R("og", j) for j in range(4)])
                        dump("f", 3, Sf[:, :, :].rearrange("p h f -> p (h f)"), [R("Sf", 0), R("Sf", 1)])
                    for dch in range(8):
                        ps, psr = ps_next([0, 1])
                        for j4 in range(4):
                            P.add("pe", lambda e, wA=wA, wB=wB, ps=ps, j4=j4, dch=dch: e.matmul(
                                ps[:], wB[:, 4096 + j4 * 1024 + dch * 128:4096 + j4 * 1024 + (dch + 1) * 128],
                                og[:, j4, :], start=(j4 == 0), stop=(j4 == 3)),
                                reads=[wBr, R("og", j4)], writes=[psr])
                        P.add("dve", lambda e, wA=wA, wB=wB, ps=ps, dch=dch, ts=ts: e.scalar_tensor_tensor(
                            xT[:, dch, ts], ps[:], gate[:, dch:dch + 1], xT[:, dch, ts], ALU.mult, ALU.add),
                            reads=[psr, R("modv"), R("x", dch, tb)], writes=[R("x", dch, tb)])
                w_issue()
                w_issue()

        def att(blk0):
            P.fence()
            gate = modv[:, 48 + 16:48 + 24]
            o = 0
            qT = U[:, o:o + 4096].rearrange("p (h t) -> p h t", h=2); o += 4096
            kz = U[:, o:o + 8192].rearrange("p (h i t) -> p h i t", h=2, i=2); o += 8192
            vaug = U[:, o:o + 4096].rearrange("p (c h f) -> p c h f", c=16, h=2); o += 4096
            oT = U[:, o:o + 1024].rearrange("p (h t) -> p h t", h=2); o += 1024
            ocs = o
            Cb = Ub(o, 2048); o += 2048
            Sb_ = Ub(o, 2048); o += 2048
            pTb = [Ub(o + i * 512, 512) for i in range(4)]; o += 2048
            oc = o
            qnb = [Ub(o + i * 512, 512) for i in range(2)]; o += 1024
            t1f = [Uf(o + i * 1024, 512) for i in range(2)]; o += 2048
            t2f = [Uf(o + i * 1024, 512) for i in range(2)]; o += 2048
            o = oc
            c_t1 = Uf(o, 512); o += 1024
            c_r2 = Uf(o, 512); o += 1024
            c_t2 = Uf(o, 512); o += 1024
            c_d = Uf(o, 512); o += 1024
            c_ln = Uf(o, 512); o += 1024
            c_sq = Ub(o, 512); o += 512
            assert o <= UEND, o
            angf = Uf(0, 2048)
            kf = Uf(4096, 2048)
            ki32 = U[:, 8192:12288].bitcast(I32)

            lrow = lrow_t[0:1, :]
            lsum = lamv[0:1, 2:8]
            P.add("sp", lambda e: e.dma_start(out=lrow, in_=rw_d), writes=[R("lrow")], dma=True)
            P.add("dve", lambda e: e.tensor_tensor(lrow[0:1, 0:64], lrow[0:1, 0:64], lrow[0:1, 64:128], ALU.mult),
                  reads=[R("lrow")], writes=[R("lrow")])
            P.add("dve", lambda e: e.tensor_tensor(lrow[0:1, 128:192], lrow[0:1, 128:192], lrow[0:1, 192:256], ALU.mult),
                  reads=[R("lrow")], writes=[R("lrow")])
            P.add("dve", lambda e: e.tensor_reduce(
                lsum[0:1, 0:2], lrow.rearrange("p (a b) -> p a b", a=2)[:, :, 0:64], AX.X, ALU.add),
                reads=[R("lrow")], writes=[R("lamv")])
            P.add("act", lambda e: e.activation(lsum[0:1, 2:4], lsum[0:1, 0:2], AF.Exp),
                  reads=[R("lamv")], writes=[R("lamv")])
            P.add("dve", lambda e: e.scalar_tensor_tensor(
                lsum[0:1, 4:5], lsum[0:1, 3:4], -LAMBDA_INIT, lsum[0:1, 2:3], ALU.add, ALU.subtract),
                reads=[R("lamv")], writes=[R("lamv")])
            ps, psr = ps_next([0, 1])
            P.add("pe", lambda e, ps=ps: e.matmul(ps[:, 0:1], onesf[0:1, :], lsum[0:1, 4:5], start=True, stop=True),
                  reads=[R("lamv"), R("onesf")], writes=[psr])
            P.add("dve", lambda e, ps=ps: e.tensor_copy(lamv[:, 0:1], ps[:, 0:1]), reads=[psr, R("lamv")], writes=[R("lamv")])
            P.add("dve", lambda e: e.tensor_scalar(lamv[:, 1:2], sm[:, SM_SLG:SM_SLG + 1], 1.0 - LAMBDA_INIT, None, ALU.mult),
                  reads=[R("sm"), R("lamv")], writes=[R("lamv")])

            posi1 = U[0:1, 12288:16384].bitcast(I32)
            posf1 = U[0:1, ocs:ocs + 4096].bitcast(F32)
            P.add("sp", lambda e: e.dma_start(out=posi1.bitcast(F32), in_=pos_d), writes=[R("posi")], dma=True)
            P.add("dve", lambda e: e.tensor_copy(posf1, posi1), reads=[R("posi")], writes=[R("posf")])
            for tb in range(4):
                ts = slice(tb * 512, (tb + 1) * 512)
                ps, psr = ps_next([0, 1])
                P.add("pe", lambda e, ps=ps, ts=ts: e.matmul(ps[:], onesf[0:1, :], posf1[0:1, ts], start=True, stop=True),
                      reads=[R("posf"), R("onesf")], writes=[psr])
                P.add("dve", lambda e, ps=ps, ts=ts: e.tensor_scalar(angf[:, ts], ps[:], sm[:, SM_INVF:SM_INVF + 1], None, ALU.mult),
                      reads=[psr, R("sm")], writes=[R("angf")])
            dump('f', 1, angf, [R('angf')])
            dump('f', 6, posi1.bitcast(F32), [R('posi')], npart=1)
            dump('f', 7, posf1, [R('posf')], npart=1)
            for which, dst, off in ((0, Sb_, 0.0), (1, Cb, float(np.pi / 2))):
                if which == 1:
                    P.add("dve", lambda e, off=off: e.tensor_scalar(angf, angf, off, None, ALU.add),
                          reads=[R("angf")], writes=[R("angf")])
                P.add("dve", lambda e: e.tensor_scalar(kf, angf, 1.0 / TWO_PI, None, ALU.mult),
                      reads=[R("angf")], writes=[R("kf")])
                P.add("dve", lambda e: e.tensor_copy(ki32, kf), reads=[R("kf")], writes=[R("ki32")])
                P.add("dve", lambda e: e.tensor_copy(kf, ki32), reads=[R("ki32")], writes=[R("kf")])
                P.add("dve", lambda e: e.scalar_tensor_tensor(kf, kf, -TWO_PI, angf, ALU.mult, ALU.add),
                      reads=[R("kf"), R("angf")], writes=[R("kf")])
                P.add("dve", lambda e: e.tensor_scalar(kf, kf, float(np.pi), float(-np.pi), ALU.min, ALU.max),
                      reads=[R("kf")], writes=[R("kf")])
                P.add("act", lambda e, dst=dst: e.activation(dst, kf, AF.Sin), reads=[R("kf")], writes=[R("rope", which)])
                dump('f', 2 + which, kf, [R("kf")])
            P.add("dve", lambda e: e.memset(kz[64:128, :, 0, :], 0.0), writes=[R("kzz"), R("kf"), R("ki32"), R("angf"), R("posi")])
            P.add("dve", lambda e: e.memset(kz[0:64, :, 1, :], 0.0), writes=[R("kzz"), R("kf"), R("ki32"), R("angf"), R("posi")])

            ALV = int(os.environ.get('ATT_LEVEL', '9'))
            for pr in range(4 if ALV > 0 else 0):
                wt, wr = w_slot(blk0 + pr)
                groups = [(nm, coff, hh, tb) for (nm, coff) in (("qT", 0), ("kT", 256)) for hh in range(2) for tb in range(4)]
                gstate = {}

                def stageA(gx):
                    nm, coff, hh, tb = groups[gx]
                    ts = slice(tb * 512, (tb + 1) * 512)
                    ps, psr = PS[gx % 2], R("ps", gx % 2)
                    for k in range(8):
                        P.add("pe", lambda e, wt=wt, ps=ps, k=k, hh=hh, ts=ts, coff=coff: e.matmul(
                            ps[:], wt[:, k * 768 + coff + hh * 128:k * 768 + coff + (hh + 1) * 128], hT[:, k, ts],
                            start=(k == 0), stop=(k == 7)), reads=[wr, R("h", k, tb)], writes=[psr])
                    qn = qnb[gx % 2]
                    P.add("act", lambda e, ps=ps, qn=qn: e.activation(qn, ps[:], AF.Copy),
                          reads=[psr], writes=[R("qn", gx % 2)])

                def stageB(gx):
                    nm, coff, hh, tb = groups[gx]
                    ts = slice(tb * 512, (tb + 1) * 512)
                    i2 = gx % 2
                    ps, psr = PS[gx % 2], R("ps", gx % 2)
                    qn = qnb[i2]
                    ps2, ps2r = PS[2 + i2], R("ps", 2 + i2)
                    P.add("pe", lambda e, ps2=ps2, qn=qn: e.matmul(ps2[:], permb, qn, start=True, stop=True),
                          reads=[R("qn", i2), R("cstb")], writes=[ps2r])
                    t1 = t1f[i2]
                    t2 = t2f[i2]
                    P.add("dve", lambda e, ps=ps, t1=t1, ts=ts: e.tensor_tensor(t1, ps[:], Cb[:, ts], ALU.mult),
                          reads=[psr, R("rope", 1)], writes=[R("t1", i2)])
                    P.add("dve", lambda e, ps2=ps2, t2=t2, ts=ts: e.tensor_tensor(t2, ps2[:], Sb_[:, ts], ALU.mult),
                          reads=[ps2r, R("rope", 0)], writes=[R("t2", i2)])
                    if nm == "qT":
                        P.add("pool", lambda e, hh=hh, ts=ts, t1=t1, t2=t2: e.tensor_tensor(
                            qT[:, hh, ts], t1, t2, ALU.add),
                            reads=[R("t1", i2), R("t2", i2)], writes=[R("qT", hh, tb)])
                    else:
                        for i_ in range(2):
                            prt_ = slice(64 * i_, 64 * i_ + 64)
                            P.add("pool", lambda e, hh=hh, ts=ts, t1=t1, t2=t2, i_=i_, prt_=prt_: e.tensor_tensor(
                                kz[prt_, hh, i_, ts], t1[prt_, :], t2[prt_, :], ALU.add),
                                reads=[R("t1", i2), R("t2", i2), R("kzz")], writes=[R("kT", hh, tb)])

                for gx in range(len(groups) + 1):
                    if gx < len(groups):
                        stageA(gx)
                    if gx >= 1:
                        stageB(gx - 1)
                for tt in range(16 if not os.environ.get("SKIPV") else 0):
                    tok = slice(tt * 128, (tt + 1) * 128)
                    ps, psr = ps_next([0, 1])
                    for k in range(8):
                        P.add("pe", lambda e, wt=wt, ps=ps, k=k, tok=tok: e.matmul(
                            ps[:, 0:256], hT[:, k, tok], wt[:, k * 768 + 512:k * 768 + 768],
                            start=(k == 0), stop=(k == 7)), reads=[wr, R("h", k, tt // 4)], writes=[psr])
                    P.add("act", lambda e, wt=wt, ps=ps, tt=tt: e.activation(
                        vaug[:, tt, :, :], ps[:, 0:256].rearrange("p (h f) -> p h f", h=2), AF.Copy),
                        reads=[psr], writes=[R("vaug", tt)])
                P.fence()
                its = []
                gi = 0
                for j in range(4):
                    for hh in range(2):
                        for i in range(2):
                            for kt in range(4 * j + 4):
                                its.append((hh, j, i, kt, gi))
                            gi += 1
                LA = 2
                SB = [1, 2, 3]
                deferred = []

                def emit_qk(idx):
                    hh, j, i, kt, g_ = its[idx]
                    r = kt - 4 * j
                    n0 = 128 * r if r > 0 else 0
                    ps, psr = PS[SB[idx % 3]], R("ps", SB[idx % 3])
                    P.add("pe", lambda e, ps=ps, i=i, hh=hh, kt=kt, j=j, n0=n0, r=r: e.matmul(
                        ps[:, n0:512], kz[:, hh, i, kt * 128:(kt + 1) * 128],
                        qT[:, hh, j * 512 + n0:(j + 1) * 512], start=True, stop=(r < 0)),
                        reads=[R("kT", hh, kt // 4), R("qT", hh, j), R("kzz")], writes=[psr])
                    if r >= 0:
                        P.add("pe", lambda e, ps=ps, n0=n0: e.matmul(
                            ps[:, n0:n0 + 128], identb, negmb, start=False, stop=True),
                            reads=[R("cstb")], writes=[psr])
                    pT = pTb[idx % 3]
                    pTr = R("pT", idx % 3)
                    P.add("act", lambda e, ps=ps, pT=pT, n0=n0: e.activation(
                        pT[:, n0:512], ps[:, n0:512], AF.Exp, scale=0.125),
                        reads=[psr], writes=[pTr])

                def emit_pv(idx):
                    hh, j, i, kt, g_ = its[idx]
                    r = kt - 4 * j
                    n0 = 128 * r if r > 0 else 0
                    pT = pTb[idx % 3]
                    pTr = R("pT", idx % 3)
                    ao, aor = PS[4 + 2 * i], R("ps", 4 + 2 * i)
                    al, alr = PS[5 + 2 * i], R("ps", 5 + 2 * i)
                    last = (kt == 4 * j + 3)
                    P.add("pe", lambda e, ao=ao, pT=pT, kt=kt, hh=hh, n0=n0, last=last: e.matmul(
                        ao[:, n0:512], vaug[:, kt, hh, :], pT[:, n0:512], start=(kt == 0), stop=last),
                        reads=[pTr, R("vaug", kt)], writes=[aor])
                    P.add("pe", lambda e, al=al, pT=pT, kt=kt, n0=n0, last=last: e.matmul(
                        al[:, n0:512], onesb[:], pT[:, n0:512], start=(kt == 0), stop=last),
                        reads=[pTr, R("onesb")], writes=[alr])
                    if not last:
                        return
                    qs = slice(j * 512, (j + 1) * 512)
                    if i == 0:
                        P.add("dve", lambda e, al=al: e.reciprocal(c_r2, al[:]), reads=[alr], writes=[R("c_r2")])
                        P.add("dve", lambda e, ao=ao: e.tensor_tensor(c_t1, ao[:], c_r2, ALU.mult),
                              reads=[aor, R("c_r2")], writes=[R("c_t1")])
                        return
                    P.add("dve", lambda e, al=al: e.reciprocal(c_r2, al[:]), reads=[alr], writes=[R("c_r2")])
                    P.add("dve", lambda e, ao=ao: e.scalar_tensor_tensor(c_t2, ao[:], lamv[:, 0:1], c_r2, ALU.mult, ALU.mult),
                          reads=[aor, R("c_r2"), R("lamv")], writes=[R("c_t2")])
                    P.add("pool", lambda e: e.tensor_tensor(c_d, c_t1, c_t2, ALU.add),
                          reads=[R("c_t1"), R("c_t2")], writes=[R("c_d")])
                    P.add("act", lambda e: e.activation(c_sq, c_d, AF.Square), reads=[R("c_d")], writes=[R("c_sq")])

                    def epi1(hh=hh):
                        pss, pssr = PS[0], R("ps", 0)
                        P.add("pe", lambda e, pss=pss: e.matmul(pss[:], onesb[:], c_sq, start=True, stop=True),
                              reads=[R("c_sq"), R("onesb")], writes=[pssr])
                        P.add("act", lambda e, pss=pss: e.activation(c_ln, pss[:], AF.Ln, bias=EPS, scale=1.0 / 128),
                              reads=[pssr], writes=[R("c_ln")])
                        P.add("act", lambda e: e.activation(c_ln, c_ln, AF.Exp, scale=-0.5),
                              reads=[R("c_ln")], writes=[R("c_ln")])
                        P.add("dve", lambda e, hh=hh: e.scalar_tensor_tensor(
                            oT[:, hh, :], c_d, lamv[:, 1:2], c_ln, ALU.mult, ALU.mult),
                            reads=[R("c_d"), R("c_ln"), R("lamv")], writes=[R("oT", hh)])

                    def epi2(j=j, qs=qs):
                        for dch in range(8):
                            psw, pswr = PS[0], R("ps", 0)
                            for h2 in range(2):
                                P.add("pe", lambda e, wt=wt, psw=psw, h2=h2, dch=dch: e.matmul(
                                    psw[:], wt[:, 6144 + h2 * 1024 + dch * 128:6144 + h2 * 1024 + (dch + 1) * 128],
                                    oT[:, h2, :], start=(h2 == 0), stop=(h2 == 1)),
                                    reads=[wr, R("oT", h2)], writes=[pswr])
                            P.add("dve", lambda e, psw=psw, dch=dch, qs=qs: e.scalar_tensor_tensor(
                                xT[:, dch, qs], psw[:], gate[:, dch:dch + 1], xT[:, dch, qs], ALU.mult, ALU.add),
                                reads=[pswr, R("modv"), R("x", dch, j)], writes=[R("x", dch, j)])

                    step_now = idx + LA
                    deferred.append((step_now + 5, epi1))
                    if hh == 1:
                        deferred.append((step_now + 9, epi2))

                def run_deferred(step, flush=False):
                    while deferred and (flush or deferred[0][0] <= step):
                        deferred.pop(0)[1]()

                for step in range(len(its) + LA):
                    if step < len(its):
                        emit_qk(step)
                    if step >= LA:
                        idx_ = step - LA
                        hh_, j_, i_, kt_, g_ = its[idx_]
                        if i_ == 1 and kt_ == 4 * j_ + 3:
                            run_deferred(step, flush=True)
                        emit_pv(idx_)
                    run_deferred(step)
                run_deferred(0, flush=True)
                P.fence()
                w_issue()

        adaln(0, 0)
        norm(dv[:, 0:8], modv[:, 0:8])
        for k in range(8):
            dump("b", 8 + k, hT[:, k, :], [R("h", k, tb) for tb in range(4)])
        dump("f", 4, modv[:, :], [R("modv")])
        dump("f", 5, dv[:, :], [R("dv")])
        gla(6)
        if stage >= 2:
            norm(dv[:, 8:16], modv[:, 24:32])
            mlp(0, 10)
        if stage >= 3:
            adaln(1, 18)
            norm(dv[:, 0:8], modv[:, 48:56])
            att(24)
        if stage >= 4:
            norm(dv[:, 8:16], modv[:, 48 + 24:48 + 32])
            mlp(1, 28)
            norm(sm[:, SM_FG:SM_FG + 8], None, final=True)
        ov = out_d.rearrange("(k p) t -> p k t", p=128)
        outr = []
        for k in range(8):
            rr = R("out", k)
            outr.append(rr)
            P.add("sp", lambda e, k=k: e.dma_start(out=ov[:, k, :], in_=xT[:, k, :]),
                  reads=[R("x", k, tb) for tb in range(4)], writes=[rr], dma=True)
        P.add("sp", lambda e: None, reads=outr, noinst=True)
        P.emit(sems, dsems)
    return nc


def _kmaj(Wc):
    K = Wc.shape[0] // 128
    return np.ascontiguousarray(Wc.reshape(K, 128, -1).transpose(1, 0, 2)).reshape(128, -1)


def _prep_shared(inp):
    f = np.float32
    blocks = np.zeros((NBLK, 128, 8192), f)
    bi = 0
    ada_w = np.asarray(inp["ada_w"], f)
    mlp_w1 = np.asarray(inp["mlp_w1"], f)
    mlp_w2 = np.asarray(inp["mlp_w2"], f)
    gw = np.asarray(inp["gla_w_in"], f)[0]
    gwo = np.asarray(inp["gla_w_o"], f)[0]
    dw = np.asarray(inp["diff_w_in"], f)[0]
    dwo = np.asarray(inp["diff_w_o"], f)[0]
    for nb_ in range(6):
        blocks[bi] = _kmaj(ada_w[0][:, nb_ * 1024:(nb_ + 1) * 1024]); bi += 1
    for g in range(2):
        cols = np.concatenate([np.arange(2 * g * 128, 2 * g * 128 + 256),
                               512 + np.arange(2 * g * 128, 2 * g * 128 + 256),
                               1024 + np.arange(2 * g * 256, 2 * g * 256 + 512)])
        blocks[bi] = _kmaj(gw[:, cols]); bi += 1
        rc = 2048 + np.arange(2 * g * 256, 2 * g * 256 + 512)
        blocks[bi][:, 0:4096] = _kmaj(gw[:, rc])
        blocks[bi][:, 4096:8192] = _kmaj(gwo[g * 512:(g + 1) * 512, :]); bi += 1
    for g in range(4):
        blocks[bi] = _kmaj(mlp_w1[0][:, g * 1024:(g + 1) * 1024]); bi += 1
        blocks[bi] = _kmaj(mlp_w2[0][g * 1024:(g + 1) * 1024, :]); bi += 1
    for nb_ in range(6):
        blocks[bi] = _kmaj(ada_w[1][:, nb_ * 1024:(nb_ + 1) * 1024]); bi += 1
    for pr in range(4):
        cols = np.concatenate([np.arange(pr * 256, pr * 256 + 256),
                               1024 + np.arange(pr * 256, pr * 256 + 256),
                               2048 + np.arange(pr * 256, pr * 256 + 256)])
        blocks[bi][:, 0:6144] = _kmaj(dw[:, cols])
        blocks[bi][:, 6144:8192] = _kmaj(dwo[pr * 256:(pr + 1) * 256, :]); bi += 1
    for g in range(4):
        blocks[bi] = _kmaj(mlp_w1[1][:, g * 1024:(g + 1) * 1024]); bi += 1
        blocks[bi] = _kmaj(mlp_w2[1][g * 1024:(g + 1) * 1024, :]); bi += 1
    assert bi == NBLK

    def pk(v):
        return np.asarray(v, f).reshape(-1, 128).T

    sm = np.zeros((128, NSM), f)
    ng = np.asarray(inp["norm_g"], f)
    for l in range(2):
        for j in range(2):
            sm[:, SM_NG + (2 * l + j) * 8:SM_NG + (2 * l + j) * 8 + 8] = pk(ng[l, j])
    sm[:, SM_FG:SM_FG + 8] = pk(inp["final_g"])
    ab = np.asarray(inp["ada_b"], f)
    for l in range(2):
        sm[:, SM_AB + l * 48:SM_AB + (l + 1) * 48] = pk(ab[l])
    sm[:, SM_BR:SM_BR + 8] = pk(np.asarray(inp["gla_b_r"], f)[0])
    sm[:, SM_GNG:SM_GNG + 2] = pk(np.asarray(inp["gla_norm_g"], f)[0])
    sm[:, SM_SLG:SM_SLG + 1] = pk(np.asarray(inp["diff_subln_g"], f)[0])
    inv_freq = (np.float32(500000.0) ** (-(np.arange(0, 16, 2, dtype=np.float32)) / np.float32(16))).astype(f)
    invf = np.zeros(128, f)
    for p in range(128):
        d = p % 64
        if d < 16:
            invf[p] = inv_freq[d % 8]
    sm[:, SM_INVF] = invf

    rw = np.asarray(inp["diff_lambda"], f)[0].reshape(1, 256)
    wa = np.zeros((128, 8, 48), f)
    wa[:, :, 32:48] = gw[:, 3072:3088].reshape(8, 128, 16).transpose(1, 0, 2)
    wa = wa.reshape(128, 8 * 48)
    wa2 = np.zeros((48, 512), f)
    wa2[0] = np.asarray(inp["gla_b_a"], f)[0]
    wa2[32:48] = np.asarray(inp["gla_w_a2"], f)[0]
    cst = np.zeros((128, 512), f)
    cst[:, 0:128] = np.eye(128, dtype=f)
    tri = (np.arange(128)[None, :] >= np.arange(128)[:, None]).astype(f)
    cst[:, 128:256] = tri
    cst[:, 256:384] = tri * f(-1.0 / 16.0)
    perm = np.zeros((128, 128), f)
    for b_ in range(2):
        base = 64 * b_
        for d in range(8):
            perm[base + d + 8, base + d] = -1.0
        for d in range(8, 16):
            perm[base + d - 8, base + d] = 1.0
    cst[:, 384:512] = perm
    return dict(wblk=blocks, sm=sm, rw=rw, wa=wa, wa2=wa2, cst=cst)


_NC_CACHE = {}


def kernel(x, c, positions, ada_w, ada_b, norm_g, mlp_w1, mlp_w2,
           gla_w_in, gla_w_a2, gla_b_a, gla_b_r, gla_norm_g, gla_w_o,
           diff_w_in, diff_lambda, diff_subln_g, diff_w_o, final_g, _stage=4, _debug=False, _ncores=NB):
    inp = dict(ada_w=ada_w, ada_b=ada_b, norm_g=norm_g, mlp_w1=mlp_w1, mlp_w2=mlp_w2,
               gla_w_in=gla_w_in, gla_w_a2=gla_w_a2, gla_b_a=gla_b_a, gla_b_r=gla_b_r,
               gla_norm_g=gla_norm_g, gla_w_o=gla_w_o, diff_w_in=diff_w_in, diff_lambda=diff_lambda,
               diff_subln_g=diff_subln_g, diff_w_o=diff_w_o, final_g=final_g)
    sh = _prep_shared(inp)
    x = np.asarray(x, np.float32)
    c = np.asarray(c, np.float32)
    positions = np.asarray(positions, np.int32)
    in_maps = []
    for b in range(NB):
        sm = sh["sm"].copy()
        sm[:, SM_C:SM_C + 8] = c[b].reshape(8, 128).T
        in_maps.append(dict(xT=np.ascontiguousarray(x[b].T), pos=np.ascontiguousarray(positions[b:b + 1]).view(np.float32),
                            sm=sm, rw=sh["rw"], wa=sh["wa"], wa2=sh["wa2"], cst=sh["cst"], wblk=sh["wblk"]))
    key = (_stage, _debug)
    if key not in _NC_CACHE:
        _NC_CACHE[key] = build_nc(_stage, _debug)
    nc = _NC_CACHE[key]
    res = run_bass_kernel_spmd(nc, in_maps[:_ncores], core_ids=list(range(_ncores)))
    out = np.stack([np.ascontiguousarray(r["outT"].T) for r in res.results], axis=0)
    if _debug:
        return out.astype(np.float32), res.results[0]["dbgf"], res.results[0]["dbgb"]
    return out.astype(np.float32)
```

```python
import contextlib
import os
import math
import numpy as np
import concourse.bass as bass
import concourse.mybir as mybir
from concourse.bass_utils import run_bass_kernel_spmd

F32 = mybir.dt.float32
BF16 = mybir.dt.bfloat16
I32 = mybir.dt.int32
AF = mybir.ActivationFunctionType
ALU = mybir.AluOpType
AX = mybir.AxisListType

D = 1024
T = 2048
NB = 8
DFF = 4096
EPS = 1e-6
LAMBDA_INIT = 0.8 - 0.6 * math.exp(-0.3 * 1)
NBLK = 36
ADA1_AFTER_GROUP = [2, 2, 1, 1]


def _block_order():
    order = [("ada", 0, n) for n in range(6)]
    for g in range(2):
        order += [("glaA", g), ("glaB", g)]
    n1 = 0
    for g in range(4):
        order += [("w1", 0, g), ("w2", 0, g)]
        for _ in range(ADA1_AFTER_GROUP[g]):
            order.append(("ada", 1, n1)); n1 += 1
    order += [("att", p) for p in range(4)]
    for g in range(4):
        order += [("w1", 1, g), ("w2", 1, g)]
    assert len(order) == NBLK and n1 == 6
    return order


BLOCKS = _block_order()
BIDX = {b: i for i, b in enumerate(BLOCKS)}
NSLOT = 3
TWO_PI = float(2.0 * np.pi)

SM_NG = 0
SM_FG = 32
SM_AB = 40
SM_BR = 136
SM_GNG = 144
SM_SLG = 146
SM_INVF = 147
SM_C = 148
NSM = 160


class Res:
    __slots__ = ("w", "r", "excl")

    def __init__(self):
        self.w = None
        self.r = []
        self.excl = False


class Op:
    __slots__ = ("eng", "fn", "deps", "eidx", "signal", "seq", "dma", "dsem", "dval")


class Prog:
    NDMA = 24

    def __init__(self, nc, same_engine_sync=True):
        self.nc = nc
        self.engs = {"pe": nc.tensor, "act": nc.scalar, "dve": nc.vector,
                     "pool": nc.gpsimd, "sp": nc.sync}
        self.ops = []
        self.ecount = {k: 0 for k in self.engs}
        self.known = {k: {} for k in self.engs}
        self.kdma = {k: {} for k in self.engs}
        self.ndma = {"sp": 0, "pool": 0}
        self.dma_ops = {"sp": [], "pool": []}
        self.dbase = {"sp": 0, "pool": self.NDMA // 2}
        self.same_engine_sync = same_engine_sync
        self.res = {}
        self.last = {}

    def R(self, *key):
        r = self.res.get(key)
        if r is None:
            r = Res()
            r.excl = (key[0] == "ps")
            self.res[key] = r
        return r

    def fence(self):
        last = dict(self.last)
        for eng in self.engs:
            self.add(eng, lambda e: None, extra=[o for k, o in last.items() if k != eng], noinst=True)

    def add(self, eng, fn, reads=(), writes=(), dma=False, extra=(), noinst=False):
        op = Op()
        op.eng = eng
        op.fn = fn
        op.dma = dma
        op.signal = False
        op.seq = None
        op.eidx = self.ecount[eng]
        self.ecount[eng] += 1
        if any(r.excl for r in reads):
            writes = list(writes) + [r for r in reads if r.excl]
            reads = [r for r in reads if not r.excl]
        deps = list(extra)
        for r in reads:
            if r.w is not None:
                deps.append(r.w)
        for r in writes:
            if r.w is not None:
                deps.append(r.w)
            deps.extend(r.r)
        if dma:
            i = self.ndma[eng]
            self.ndma[eng] += 1
            h = self.NDMA // 2
            op.dsem = self.dbase[eng] + i % h
            op.dval = 16 * (i // h + 1)
            if i >= h:
                deps.append(self.dma_ops[eng][i - h])
            self.dma_ops[eng].append(op)
        best = {}
        for d in deps:
            if d is op:
                continue
            if d.dma:
                if self.kdma[eng].get(d.dsem, 0) >= d.dval:
                    continue
                key = ("dma", d.dsem)
                if key not in best or best[key].dval < d.dval:
                    best[key] = d
            else:
                if d.eng == eng and (eng == "pe" or not self.same_engine_sync):
                    continue
                if self.known[eng].get(d.eng, -1) >= d.eidx:
                    continue
                key = d.eng
                if key not in best or best[key].eidx < d.eidx:
                    best[key] = d
        final = []
        for key, d in best.items():
            final.append(d)
            if d.dma:
                self.kdma[eng][d.dsem] = d.dval
            else:
                d.signal = True
                self.known[eng][d.eng] = d.eidx
        op.deps = final
        for r in reads:
            r.r.append(op)
        for r in writes:
            r.w = op
            r.r = []
        self.ops.append(op)
        if not dma and not noinst:
            self.last[eng] = op
        return op

    def emit(self, sems, dsems):
        cnt = {k: 0 for k in self.engs}
        for op in self.ops:
            if not op.dma and op.signal:
                cnt[op.eng] += 1
                op.seq = cnt[op.eng]
        for op in self.ops:
            e = self.engs[op.eng]
            for d in op.deps:
                if d.dma:
                    e.wait_ge(dsems[d.dsem], d.dval)
                else:
                    e.wait_ge(sems[d.eng], d.seq)
            ins = op.fn(e)
            if ins is None:
                continue
            if op.dma:
                ins.then_inc(dsems[op.dsem], 16)
            elif op.signal:
                ins.then_inc(sems[op.eng], 1)
        return cnt


def build_nc(stage=4, debug=False):
    nc = bass.Bass("TRN2", target_bir_lowering=False)
    xT_d = nc.dram_tensor("xT", [D, T], F32, kind="ExternalInput").ap()
    pos_d = nc.dram_tensor("pos", [1, T], F32, kind="ExternalInput").ap()
    sm_d = nc.dram_tensor("sm", [128, NSM], F32, kind="ExternalInput").ap()
    rw_d = nc.dram_tensor("rw", [1, 256], F32, kind="ExternalInput").ap()
    wa_d = nc.dram_tensor("wa", [128, 8 * 48], F32, kind="ExternalInput").ap()
    wa2_d = nc.dram_tensor("wa2", [48, 512], F32, kind="ExternalInput").ap()
    cst_d = nc.dram_tensor("cst", [128, 512], F32, kind="ExternalInput").ap()
    wblk_d = nc.dram_tensor("wblk", [NBLK, 128, 8192], F32, kind="ExternalInput").ap()
    out_d = nc.dram_tensor("outT", [D, T], F32, kind="ExternalOutput").ap()
    if debug:
        dbgf_d = nc.dram_tensor("dbgf", [16, 128, 2048], F32, kind="ExternalOutput").ap()
        dbgb_d = nc.dram_tensor("dbgb", [24, 128, 2048], BF16, kind="ExternalOutput").ap()

    es = contextlib.ExitStack()
    with es:
        def sb(name, shape, dt):
            return es.enter_context(nc.sbuf_tensor(name, shape, dt))

        xT = sb("xTs", [128, 8, T], F32)
        hT = sb("hTs", [128, 8, T], BF16)
        wsl = [sb("wsl%d" % i, [128, 8192], BF16) for i in range(NSLOT)]
        cstf = sb("cstf", [128, 512], F32)
        cstb = sb("cstb", [128, 512], BF16)
        onesb = sb("onesb", [128, 128], BF16)
        onesf = sb("onesf", [128, 128], F32)
        sm = sb("sms", [128, NSM], F32)
        modv = sb("modv", [128, 96], F32)
        dv = sb("dvs", [128, 64], F32)
        cact = sb("cact", [128, 8], BF16)
        lamv = sb("lamv", [128, 8], F32)
        lrow_t = sb("lrow", [1, 256], F32)
        U = sb("U", [128, 29400], BF16)
        PS = [es.enter_context(nc.psum_tensor("ps%d" % i, [128, 512], F32)) for i in range(8)]
        sems = {k: es.enter_context(nc.semaphore("s_" + k)) for k in ["pe", "act", "dve", "pool", "sp"]}
        dsems = [es.enter_context(nc.semaphore("d%d" % i)) for i in range(Prog.NDMA)]

        P = Prog(nc)
        R = P.R

        identf = cstf[:, 0:128]
        trinegf = cstf[:, 256:384]
        identb = cstb[:, 0:128]
        trib = cstb[:, 128:256]
        permb = cstb[:, 384:512]
        negmb = cstb[:, 256:384]

        def Ub(off, n):
            return U[:, off:off + n]

        def Uf(off, n):
            return U[:, off:off + 2 * n].bitcast(F32)

        def dump(kind, idx, ap, reads, npart=128):
            if not debug:
                return
            n = ap.shape[-1]
            dst = (dbgf_d if kind == "f" else dbgb_d)[idx, 0:npart, 0:n]
            P.add("sp", lambda e: e.dma_start(out=dst, in_=ap), reads=reads, writes=[R("dbg", kind, idx)], dma=True)

        wstate = {"next": 0}

        def w_issue():
            i = wstate["next"]
            if i >= NBLK:
                return
            wstate["next"] = i + 1
            s = i % NSLOT
            P.add("pool", lambda e, i=i, s=s: e.dma_start(out=wsl[s][:], in_=wblk_d[i]),
                  writes=[R("w", s)], dma=True)

        def w_slot(i):
            return wsl[i % NSLOT], R("w", i % NSLOT)

        prot = {"i": 0}

        def ps_next(banks):
            b = banks[prot["i"] % len(banks)]
            prot["i"] += 1
            return PS[b], R("ps", b)

        P.add("sp", lambda e: e.dma_start(out=sm[:], in_=sm_d), writes=[R("sm")], dma=True)
        P.add("sp", lambda e: e.dma_start(out=cstf[:], in_=cst_d), writes=[R("cstf")], dma=True)
        for i in range(NSLOT):
            w_issue()
        xv = xT_d.rearrange("(k p) t -> p k t", p=128)
        for k in range(8):
            P.add("sp", lambda e, k=k: e.dma_start(out=xT[:, k, :], in_=xv[:, k, :]),
                  writes=[R("x", k, tb) for tb in range(4)], dma=True)
        P.add("dve", lambda e: e.tensor_copy(cstb[:], cstf[:]), reads=[R("cstf")], writes=[R("cstb")])
        P.add("dve", lambda e: e.tensor_scalar(cstb[:, 256:384], cstf[:, 128:256], -1.0, 30000.0, ALU.add, ALU.mult),
              reads=[R("cstf"), R("cstb")], writes=[R("cstb")])
        P.add("dve", lambda e: e.memset(onesb[:], 1.0), writes=[R("onesb")])
        P.add("dve", lambda e: e.memset(onesf[:], 1.0), writes=[R("onesf")])
        P.add("act", lambda e: e.activation(cact[:], sm[:, SM_C:SM_C + 8], AF.Silu),
              reads=[R("sm")], writes=[R("cact")])

        def adaln_block(l, nb_):
            modrow = U[0:1, 18432:20480].bitcast(F32)
            wt, wr = w_slot(BIDX[("ada", l, nb_)])
            for half in range(2):
                n0 = half * 512
                ps, psr = ps_next([0, 1])
                for k in range(8):
                    P.add("pe", lambda e, ps=ps, wt=wt, k=k, n0=n0: e.matmul(
                        ps[0:1, :], cact[:, k:k + 1], wt[:, k * 1024 + n0:k * 1024 + n0 + 512],
                        start=(k == 0), stop=(k == 7)),
                        reads=[wr, R("cact")], writes=[psr])
                P.add("act", lambda e, ps=ps, n0=n0: e.activation(modrow[0:1, n0:n0 + 512], ps[0:1, :], AF.Copy),
                      reads=[psr], writes=[R("modrow")])
            w_issue()
            ps, psr = ps_next([0, 1])
            for m in range(8):
                P.add("pe", lambda e, ps=ps, m=m: e.matmul(
                    ps[:, m:m + 1], modrow[0:1, m * 128:(m + 1) * 128], onesf[0:1, 0:1], start=True, stop=True),
                    reads=[R("modrow"), R("onesf")], writes=[psr])
            c0 = l * 48 + nb_ * 8
            P.add("dve", lambda e, ps=ps, c0=c0: e.tensor_tensor(
                modv[:, c0:c0 + 8], ps[:, 0:8], sm[:, SM_AB + c0:SM_AB + c0 + 8], ALU.add),
                reads=[psr, R("sm")], writes=[R("modv")])

        def adaln_finish(l):
            mo = l * 48
            P.add("dve", lambda e: e.scalar_tensor_tensor(
                dv[:, 0:8], modv[:, mo + 8:mo + 16], 1.0, sm[:, SM_NG + (2 * l) * 8:SM_NG + (2 * l) * 8 + 8],
                ALU.add, ALU.mult), reads=[R("modv"), R("sm")], writes=[R("dv")])
            P.add("dve", lambda e: e.scalar_tensor_tensor(
                dv[:, 8:16], modv[:, mo + 32:mo + 40], 1.0, sm[:, SM_NG + (2 * l + 1) * 8:SM_NG + (2 * l + 1) * 8 + 8],
                ALU.add, ALU.mult), reads=[R("modv"), R("sm")], writes=[R("dv")])

        NO = 29400 - 5120
        UEND = 29400

        def norm(gs_ap, sh_ap, final=False):
            P.fence()
            for tb in range(4):
                ts = slice(tb * 512, (tb + 1) * 512)
                pss, pssr = ps_next([0, 1])
                for k in range(8):
                    sq = Ub(NO + (k % 2) * 512, 512)
                    sqr = R("n_sq", k % 2)
                    P.add("act", lambda e, sq=sq, k=k, ts=ts: e.activation(sq, xT[:, k, ts], AF.Square),
                          reads=[R("x", k, tb)], writes=[sqr])
                    P.add("pe", lambda e, pss=pss, sq=sq, k=k: e.matmul(pss[:], onesb[:], sq, start=(k == 0), stop=(k == 7)),
                          reads=[sqr, R("onesb")], writes=[pssr])
                lnt = Uf(NO + 1024, 512)
                rstd = Uf(NO + 2048, 512)
                P.add("act", lambda e, pss=pss, lnt=lnt: e.activation(lnt, pss[:], AF.Ln, bias=EPS, scale=1.0 / D),
                      reads=[pssr], writes=[R("n_ln")])
                P.add("act", lambda e, lnt=lnt, rstd=rstd: e.activation(rstd, lnt, AF.Exp, scale=-0.5),
                      reads=[R("n_ln")], writes=[R("n_rstd")])
                for k in range(8):
                    if final:
                        P.add("dve", lambda e, k=k, ts=ts, rstd=rstd: e.scalar_tensor_tensor(
                            xT[:, k, ts], xT[:, k, ts], gs_ap[:, k:k + 1], rstd, ALU.mult, ALU.mult),
                            reads=[R("n_rstd"), R("sm")], writes=[R("x", k, tb)])
                    else:
                        tmp = Uf(NO + 3072 + (k % 2) * 1024, 512)
                        tr = R("n_tmp", k % 2)
                        P.add("dve", lambda e, k=k, ts=ts, rstd=rstd, tmp=tmp: e.scalar_tensor_tensor(
                            tmp, xT[:, k, ts], gs_ap[:, k:k + 1], rstd, ALU.mult, ALU.mult),
                            reads=[R("n_rstd"), R("x", k, tb), R("dv")], writes=[tr])
                        P.add("act", lambda e, k=k, ts=ts, tmp=tmp: e.activation(
                            hT[:, k, ts], tmp, AF.Identity, bias=sh_ap[:, k:k + 1]),
                            reads=[tr, R("modv")], writes=[R("h", k, tb)])

        def mlp(l):
            P.fence()
            mo = l * 48
            gate = modv[:, mo + 40:mo + 48]
            hid = U[:, 0:16384].rearrange("p (j t) -> p j t", j=8)
            for g in range(4):
                w1, w1r = w_slot(BIDX[("w1", l, g)])
                for tb in range(4):
                    ts = slice(tb * 512, (tb + 1) * 512)
                    for j in range(8):
                        ps, psr = ps_next([0, 1, 2, 3])
                        for k in range(8):
                            P.add("pe", lambda e, ps=ps, w1=w1, k=k, j=j, ts=ts: e.matmul(
                                ps[:], w1[:, k * 1024 + j * 128:k * 1024 + (j + 1) * 128], hT[:, k, ts],
                                start=(k == 0), stop=(k == 7)),
                                reads=[w1r, R("h", k, tb)], writes=[psr])
                        sq = Uf(16384 + (j % 2) * 1024, 512)
                        sqr = R("m_sq", j % 2)
                        P.add("act", lambda e, ps=ps, sq=sq: e.activation(sq, ps[:], AF.Square),
                              reads=[psr], writes=[sqr])
                        P.add("dve", lambda e, ps=ps, sq=sq, j=j, ts=ts: e.scalar_tensor_tensor(
                            hid[:, j, ts], ps[:], 0.0, sq, ALU.is_gt, ALU.mult),
                            reads=[psr, sqr], writes=[R("hid", j, tb)])
                w_issue()
                w2, w2r = w_slot(BIDX[("w2", l, g)])
                for tb in range(4):
                    ts = slice(tb * 512, (tb + 1) * 512)
                    for dch in range(8):
                        ps, psr = ps_next([0, 1, 2, 3])
                        for j in range(8):
                            P.add("pe", lambda e, ps=ps, w2=w2, j=j, dch=dch, ts=ts: e.matmul(
                                ps[:], w2[:, j * 1024 + dch * 128:j * 1024 + (dch + 1) * 128], hid[:, j, ts],
                                start=(j == 0), stop=(j == 7)),
                                reads=[w2r, R("hid", j, tb)], writes=[psr])
                        P.add("dve", lambda e, ps=ps, dch=dch, ts=ts: e.scalar_tensor_tensor(
                            xT[:, dch, ts], ps[:], gate[:, dch:dch + 1], xT[:, dch, ts], ALU.mult, ALU.add),
                            reads=[psr, R("modv"), R("x", dch, tb)], writes=[R("x", dch, tb)])
                w_issue()
                if l == 0 and stage >= 3:
                    for _ in range(ADA1_AFTER_GROUP[g]):
                        adaln_block(1, ada1_state["n"])
                        ada1_state["n"] += 1

        ada1_state = {"n": 0}

        def gla():
            blk0 = BIDX[("glaA", 0)]
            P.fence()
            gate = modv[:, 16:24]
            o = 0
            a_aug = U[0:48, o:o + 2048]; o += 2048
            wab = Ub(o, 384); o += 384
            wa2b = U[0:48, o:o + 512]; o += 512
            waf = Uf(o, 384); o += 768
            wa2f = U[0:48, o:o + 1024].bitcast(F32); o += 1024
            spf = [Uf(o + i * 512, 256) for i in range(2)]; o += 1024
            ebf = [Uf(o + i * 1024, 512) for i in range(2)]; o += 2048
            enbf = [Uf(o + i * 1024, 512) for i in range(2)]; o += 2048
            qd = U[:, o:o + 1024].rearrange("p (h t) -> p h t", h=2); o += 1024
            ki = U[:, o:o + 1024].rearrange("p (h t) -> p h t", h=2); o += 1024
            ktok = U[:, o:o + 1024].rearrange("p (c h f) -> p c h f", c=4, h=2); o += 1024
            vtok = U[:, o:o + 2048].rearrange("p (c f) -> p c f", c=4); o += 2048
            rg = U[:, o:o + 2048].rearrange("p (j t) -> p j t", j=4); o += 2048
            og = U[:, o:o + 2048].rearrange("p (j t) -> p j t", j=4); o += 2048
            Sf = U[:, o:o + 1024].bitcast(F32).rearrange("p (h f) -> p h f", h=2); o += 1024
            Sb2 = [U[:, o + i * 512:o + (i + 1) * 512].rearrange("p (h f) -> p h f", h=2) for i in range(2)]; o += 1024
            sTb = [Ub(o + i * 128, 128) for i in range(2)]; o += 256
            tmpf = [Uf(o + i * 1024, 512) for i in range(2)]; o += 2048
            stmp = [Uf(o + i * 512, 256) for i in range(2)]; o += 1024
            etmp = Uf(o, 256); o += 512
            sqb = [Ub(o + i * 512, 512) for i in range(2)]; o += 1024
            assert o <= UEND, o

            P.add("sp", lambda e: e.dma_start(out=waf, in_=wa_d), writes=[R("waf"), R("modrow")], dma=True)
            P.add("sp", lambda e: e.dma_start(out=wa2f, in_=wa2_d), writes=[R("wa2f"), R("modrow")], dma=True)
            P.add("dve", lambda e: e.tensor_copy(wab, waf), reads=[R("waf")], writes=[R("wab")])
            P.add("dve", lambda e: e.tensor_copy(wa2b, wa2f), reads=[R("wa2f")], writes=[R("wa2b")])
            for tb in range(4):
                ts = slice(tb * 512, (tb + 1) * 512)
                ps, psr = ps_next([0, 1])
                for k in range(8):
                    P.add("pe", lambda e, ps=ps, k=k, ts=ts: e.matmul(
                        ps[0:48, :], wab[:, k * 48:(k + 1) * 48], hT[:, k, ts], start=(k == 0), stop=(k == 7)),
                        reads=[R("wab"), R("h", k, tb)], writes=[psr])
                P.add("act", lambda e, ps=ps, ts=ts: e.activation(a_aug[0:48, ts], ps[0:48, :], AF.Copy),
                      reads=[psr], writes=[R("a_aug", tb)])
                P.add("dve", lambda e, ts=ts: e.memset(a_aug[0:1, ts], 1.0),
                      reads=[R("a_aug", tb)], writes=[R("a_aug", tb)])

            for g in range(2):
                wA, wAr = w_slot(blk0 + 2 * g)
                wB, wBr = w_slot(blk0 + 2 * g + 1)
                if g == 0:
                    dump("b", 7, wsl[0][:, 0:2048], [R("w", 0)])
                    dump("b", 16, wsl[1][:, 0:2048], [R("w", 1)])
                    dump("b", 17, wsl[2][:, 0:2048], [R("w", 2)])
                P.add("dve", lambda e: e.memset(Sf[:, :, :], 0.0), writes=[R("Sf", 0), R("Sf", 1)])
                for par in range(2):
                    P.add("dve", lambda e, par=par: e.memset(Sb2[par][:, :, :], 0.0), writes=[R("Sb", 0, par), R("Sb", 1, par)])
                for tb in range(4):
                    ts = slice(tb * 512, (tb + 1) * 512)
                    pb = [(PS[2], R("ps", 2)), (PS[3], R("ps", 3))]
                    for c in range(4):
                        tok = slice(tb * 512 + c * 128, tb * 512 + (c + 1) * 128)
                        ps, psr = ps_next([0, 1])
                        P.add("pe", lambda e, wA=wA, wB=wB, ps=ps, tok=tok, g=g: e.matmul(
                            ps[:, 0:256], a_aug[0:48, tok], wa2b[0:48, g * 256:(g + 1) * 256], start=True, stop=True),
                            reads=[R("a_aug", tb), R("wa2b")], writes=[psr])
                        P.add("act", lambda e, wA=wA, wB=wB, ps=ps: e.activation(etmp, ps[:, 0:256], AF.Exp, scale=-1.0),
                              reads=[psr], writes=[R("etmp")])
                        sp_ = spf[c % 2]
                        P.add("act", lambda e, wA=wA, wB=wB, sp_=sp_: e.activation(sp_, etmp, AF.Ln, bias=1.0),
                              reads=[R("etmp")], writes=[R("sp", c % 2)])
                        ps, psr = ps_next([0, 1])
                        for k in range(8):
                            P.add("pe", lambda e, wA=wA, wB=wB, ps=ps, k=k, tok=tok: e.matmul(
                                ps[:], hT[:, k, tok], wA[:, k * 1024 + 512:k * 1024 + 1024],
                                start=(k == 0), stop=(k == 7)), reads=[wAr, R("h", k, tb)], writes=[psr])
                        P.add("act", lambda e, wA=wA, wB=wB, ps=ps, c=c: e.activation(vtok[:, c, :], ps[:], AF.Copy),
                              reads=[psr], writes=[R("vtok", c)])
                        for hh in range(2):
                            P.add("pe", lambda e, wA=wA, wB=wB, hh=hh, c=c, sp_=sp_, pb=pb: e.matmul(
                                pb[hh][0][:, c * 128:(c + 1) * 128], sp_[:, hh * 128:(hh + 1) * 128], trinegf,
                                start=True, stop=True),
                                reads=[R("sp", c % 2), R("cstf")], writes=[pb[hh][1]])
                    for hh in range(2):
                        P.add("act", lambda e, wA=wA, wB=wB, hh=hh, pb=pb: e.activation(ebf[hh], pb[hh][0][:], AF.Exp),
                              reads=[pb[hh][1]], writes=[R("eb", hh)])
                        P.add("act", lambda e, wA=wA, wB=wB, hh=hh, pb=pb: e.activation(enbf[hh], pb[hh][0][:], AF.Exp, scale=-1.0),
                              reads=[pb[hh][1]], writes=[R("enb", hh)])
                    for jj in range(4):
                        ps, psr = ps_next([0, 1])
                        for k in range(8):
                            P.add("pe", lambda e, wA=wA, wB=wB, ps=ps, k=k, jj=jj, ts=ts: e.matmul(
                                ps[:], wB[:, k * 512 + jj * 128:k * 512 + (jj + 1) * 128], hT[:, k, ts],
                                start=(k == 0), stop=(k == 7)), reads=[wBr, R("h", k, tb)], writes=[psr])
                        bc = SM_BR + g * 4 + jj
                        P.add("act", lambda e, wA=wA, wB=wB, ps=ps, jj=jj, bc=bc: e.activation(
                            rg[:, jj, :], ps[:], AF.Silu, bias=sm[:, bc:bc + 1]),
                            reads=[psr, R("sm")], writes=[R("rg", jj)])
                    for hh in range(2):
                        ps, psr = ps_next([0, 1])
                        for k in range(8):
                            P.add("pe", lambda e, wA=wA, wB=wB, ps=ps, k=k, hh=hh, ts=ts: e.matmul(
                                ps[:], wA[:, k * 1024 + hh * 128:k * 1024 + (hh + 1) * 128], hT[:, k, ts],
                                start=(k == 0), stop=(k == 7)), reads=[wAr, R("h", k, tb)], writes=[psr])
                        P.add("dve", lambda e, wA=wA, wB=wB, ps=ps, hh=hh: e.scalar_tensor_tensor(
                            qd[:, hh, :], ps[:], 128.0 ** -0.5, ebf[hh], ALU.mult, ALU.mult),
                            reads=[psr, R("eb", hh)], writes=[R("qd", hh)])
                        ps, psr = ps_next([0, 1])
                        for k in range(8):
                            P.add("pe", lambda e, wA=wA, wB=wB, ps=ps, k=k, hh=hh, ts=ts: e.matmul(
                                ps[:], wA[:, k * 1024 + 256 + hh * 128:k * 1024 + 256 + (hh + 1) * 128], hT[:, k, ts],
                                start=(k == 0), stop=(k == 7)), reads=[wAr, R("h", k, tb)], writes=[psr])
                        P.add("dve", lambda e, wA=wA, wB=wB, ps=ps, hh=hh: e.tensor_tensor(ki[:, hh, :], ps[:], enbf[hh], ALU.mult),
                              reads=[psr, R("enb", hh)], writes=[R("ki", hh)])
                    for c in range(4):
                        pt, ptr = PS[4 + (c % 2)], R("ps", 4 + (c % 2))
                        ptb = pt[:].bitcast(BF16)
                        for hh in range(2):
                            P.add("pe", lambda e, wA=wA, wB=wB, ptb=ptb, c=c, hh=hh: e.transpose(
                                ptb[:, hh * 128:(hh + 1) * 128], ki[:, hh, c * 128:(c + 1) * 128], identb),
                                reads=[R("ki", hh), R("cstb")], writes=[ptr])
                        P.add("act", lambda e, wA=wA, wB=wB, ptb=ptb, c=c: e.activation(
                            ktok[:, c, :, :].rearrange("p h f -> p (h f)"), ptb[:, 0:256], AF.Copy),
                            reads=[ptr], writes=[R("ktok", c)])
                    po = [(PS[4 + i], R("ps", 4 + i)) for i in range(4)]
                    for c in range(4):
                        cs = slice(c * 128, (c + 1) * 128)
                        cg = tb * 4 + c
                        Sb_w = Sb2[cg % 2]
                        Sb_r = Sb2[(cg + 1) % 2]
                        for hh in range(2):
                            pss, pssr = PS[2 + hh], R("ps", 2 + hh)
                            sl = (c % 4) * 128
                            P.add("pe", lambda e, wA=wA, wB=wB, pss=pss, sl=sl, hh=hh, cs=cs: e.matmul(
                                pss[:, sl:sl + 128], ki[:, hh, cs], qd[:, hh, cs], start=True, stop=True),
                                reads=[R("ki", hh), R("qd", hh)], writes=[pssr])
                            sT = sTb[hh]
                            P.add("dve", lambda e, wA=wA, wB=wB, pss=pss, sl=sl, sT=sT: e.tensor_tensor(
                                sT, pss[:, sl:sl + 128], trib, ALU.mult),
                                reads=[pssr, R("cstb")], writes=[R("sT", hh)])
                        for hh in range(2):
                            psp, pspr = ps_next([0, 1])
                            P.add("pe", lambda e, wA=wA, wB=wB, psp=psp, c=c, hh=hh: e.matmul(
                                psp[:, 0:256], ktok[:, c, hh, :], vtok[:, c, hh * 256:(hh + 1) * 256],
                                start=True, stop=True),
                                reads=[R("ktok", c), R("vtok", c)], writes=[pspr])
                            st = stmp[hh]
                            P.add("dve", lambda e, wA=wA, wB=wB, psp=psp, st=st, hh=hh: e.tensor_tensor(
                                st, psp[:, 0:256], Sf[:, hh, :], ALU.add),
                                reads=[pspr, R("Sf", hh)], writes=[R("stmp", hh)])
                            ecol = ebf[hh][:, c * 128 + 127:c * 128 + 128]
                            P.add("dve", lambda e, wA=wA, wB=wB, st=st, hh=hh, ecol=ecol: e.tensor_scalar(
                                Sf[:, hh, :], st, ecol, None, ALU.mult),
                                reads=[R("stmp", hh), R("eb", hh)], writes=[R("Sf", hh)])
                            P.add("act", lambda e, wA=wA, wB=wB, st=st, hh=hh, ecol=ecol, Sb_w=Sb_w: e.activation(
                                Sb_w[:, hh, :], st, AF.Copy, scale=ecol),
                                reads=[R("stmp", hh), R("eb", hh)], writes=[R("Sb", hh, cg % 2)])
                        for hh in range(2):
                            sT = sTb[hh]
                            for jj in range(2):
                                pot, potr = po[hh * 2 + jj]
                                P.add("pe", lambda e, wA=wA, wB=wB, pot=pot, c=c, hh=hh, jj=jj, sT=sT, cs=cs: e.matmul(
                                    pot[:, cs], vtok[:, c, hh * 256 + jj * 128:hh * 256 + (jj + 1) * 128], sT,
                                    start=True, stop=False),
                                    reads=[R("vtok", c), R("sT", hh)], writes=[potr])
                                P.add("pe", lambda e, wA=wA, wB=wB, pot=pot, hh=hh, jj=jj, cs=cs, Sb_r=Sb_r: e.matmul(
                                    pot[:, cs], Sb_r[:, hh, jj * 128:(jj + 1) * 128], qd[:, hh, cs],
                                    start=False, stop=True),
                                    reads=[R("Sb", hh, (cg + 1) % 2), R("qd", hh)], writes=[potr])
                    if g == 0 and tb == 0:
                        dump("f", 3, Sf[:, :, :].rearrange("p h f -> p (h f)"), [R("Sf", 0), R("Sf", 1)])
                    for hh in range(2):
                        pst, pstr = ps_next([0, 1])
                        for jj in range(2):
                            pot, potr = po[hh * 2 + jj]
                            sq = sqb[jj]
                            P.add("act", lambda e, wA=wA, wB=wB, pot=pot, sq=sq: e.activation(sq, pot[:], AF.Square),
                                  reads=[potr], writes=[R("g_sq", jj)])
                            P.add("pe", lambda e, wA=wA, wB=wB, pst=pst, sq=sq, jj=jj: e.matmul(
                                pst[:], onesb[:], sq, start=(jj == 0), stop=(jj == 1)),
                                reads=[R("g_sq", jj), R("onesb")], writes=[pstr])
                        lnt = tmpf[0]
                        rstd = tmpf[1]
                        P.add("act", lambda e, wA=wA, wB=wB, pst=pst, lnt=lnt: e.activation(lnt, pst[:], AF.Ln, bias=EPS, scale=1.0 / 256),
                              reads=[pstr], writes=[R("g_ln")])
                        P.add("act", lambda e, wA=wA, wB=wB, lnt=lnt, rstd=rstd: e.activation(rstd, lnt, AF.Exp, scale=-0.5),
                              reads=[R("g_ln")], writes=[R("g_rstd")])
                        for jj in range(2):
                            pot, potr = po[hh * 2 + jj]
                            t2 = ebf[jj]
                            P.add("dve", lambda e, wA=wA, wB=wB, pot=pot, jj=jj, rstd=rstd, t2=t2: e.scalar_tensor_tensor(
                                t2, pot[:], sm[:, SM_GNG + jj:SM_GNG + jj + 1], rstd, ALU.mult, ALU.mult),
                                reads=[potr, R("g_rstd"), R("sm"), R("eb", jj)], writes=[R("eb", jj)])
                            P.add("dve", lambda e, wA=wA, wB=wB, hh=hh, jj=jj, t2=t2: e.tensor_tensor(
                                og[:, hh * 2 + jj, :], t2, rg[:, hh * 2 + jj, :], ALU.mult),
                                reads=[R("eb", jj), R("rg", hh * 2 + jj)], writes=[R("og", hh * 2 + jj)])
                    if g == 0 and tb == 0:
                        dump("b", 5, og[:, :, :].rearrange("p j t -> p (j t)"), [R("og", j) for j in range(4)])
                        dump("f", 3, Sf[:, :, :].rearrange("p h f -> p (h f)"), [R("Sf", 0), R("Sf", 1)])
                    for dch in range(8):
                        ps, psr = ps_next([0, 1])
                        for j4 in range(4):
                            P.add("pe", lambda e, wA=wA, wB=wB, ps=ps, j4=j4, dch=dch: e.matmul(
                                ps[:], wB[:, 4096 + j4 * 1024 + dch * 128:4096 + j4 * 1024 + (dch + 1) * 128],
                                og[:, j4, :], start=(j4 == 0), stop=(j4 == 3)),
                                reads=[wBr, R("og", j4)], writes=[psr])
                        P.add("dve", lambda e, wA=wA, wB=wB, ps=ps, dch=dch, ts=ts: e.scalar_tensor_tensor(
                            xT[:, dch, ts], ps[:], gate[:, dch:dch + 1], xT[:, dch, ts], ALU.mult, ALU.add),
                            reads=[psr, R("modv"), R("x", dch, tb)], writes=[R("x", dch, tb)])
                w_issue()
                w_issue()

        def att():
            blk0 = BIDX[("att", 0)]
            P.fence()
            gate = modv[:, 48 + 16:48 + 24]
            o = 0
            qT = U[:, o:o + 4096].rearrange("p (h t) -> p h t", h=2); o += 4096
            kz = U[:, o:o + 8192].rearrange("p (h i t) -> p h i t", h=2, i=2); o += 8192
            vaug = U[:, o:o + 4096].rearrange("p (c h f) -> p c h f", c=16, h=2); o += 4096
            oT = U[:, o:o + 1024].rearrange("p (h t) -> p h t", h=2); o += 1024
            ocs = o
            Cb = Ub(o, 2048); o += 2048
            Sb_ = Ub(o, 2048); o += 2048
            pTb = [Ub(o + i * 512, 512) for i in range(4)]; o += 2048
            oc = o
            qnb = [Ub(o + i * 512, 512) for i in range(2)]; o += 1024
            t1f = [Uf(o + i * 1024, 512) for i in range(2)]; o += 2048
            t2f = [Uf(o + i * 1024, 512) for i in range(2)]; o += 2048
            o = oc
            c_t1 = Uf(o, 512); o += 1024
            c_r2 = Uf(o, 512); o += 1024
            c_t2 = Uf(o, 512); o += 1024
            c_d = Uf(o, 512); o += 1024
            c_ln = Uf(o, 512); o += 1024
            c_sq = Ub(o, 512); o += 512
            assert o <= UEND, o
            angf = Uf(0, 2048)
            kf = Uf(4096, 2048)
            ki32 = U[:, 8192:12288].bitcast(I32)

            lrow = lrow_t[0:1, :]
            lsum = lamv[0:1, 2:8]
            P.add("sp", lambda e: e.dma_start(out=lrow, in_=rw_d), writes=[R("lrow")], dma=True)
            P.add("dve", lambda e: e.tensor_tensor(lrow[0:1, 0:64], lrow[0:1, 0:64], lrow[0:1, 64:128], ALU.mult),
                  reads=[R("lrow")], writes=[R("lrow")])
            P.add("dve", lambda e: e.tensor_tensor(lrow[0:1, 128:192], lrow[0:1, 128:192], lrow[0:1, 192:256], ALU.mult),
                  reads=[R("lrow")], writes=[R("lrow")])
            P.add("dve", lambda e: e.tensor_reduce(
                lsum[0:1, 0:2], lrow.rearrange("p (a b) -> p a b", a=2)[:, :, 0:64], AX.X, ALU.add),
                reads=[R("lrow")], writes=[R("lamv")])
            P.add("act", lambda e: e.activation(lsum[0:1, 2:4], lsum[0:1, 0:2], AF.Exp),
                  reads=[R("lamv")], writes=[R("lamv")])
            P.add("dve", lambda e: e.scalar_tensor_tensor(
                lsum[0:1, 4:5], lsum[0:1, 3:4], -LAMBDA_INIT, lsum[0:1, 2:3], ALU.add, ALU.subtract),
                reads=[R("lamv")], writes=[R("lamv")])
            ps, psr = ps_next([0, 1])
            P.add("pe", lambda e, ps=ps: e.matmul(ps[:, 0:1], onesf[0:1, :], lsum[0:1, 4:5], start=True, stop=True),
                  reads=[R("lamv"), R("onesf")], writes=[psr])
            P.add("dve", lambda e, ps=ps: e.tensor_copy(lamv[:, 0:1], ps[:, 0:1]), reads=[psr, R("lamv")], writes=[R("lamv")])
            P.add("dve", lambda e: e.tensor_scalar(lamv[:, 1:2], sm[:, SM_SLG:SM_SLG + 1], 1.0 - LAMBDA_INIT, None, ALU.mult),
                  reads=[R("sm"), R("lamv")], writes=[R("lamv")])

            posi1 = U[0:1, 12288:16384].bitcast(I32)
            posf1 = U[0:1, ocs:ocs + 4096].bitcast(F32)
            P.add("sp", lambda e: e.dma_start(out=posi1.bitcast(F32), in_=pos_d), writes=[R("posi")], dma=True)
            P.add("dve", lambda e: e.tensor_copy(posf1, posi1), reads=[R("posi")], writes=[R("posf")])
            for tb in range(4):
                ts = slice(tb * 512, (tb + 1) * 512)
                ps, psr = ps_next([0, 1])
                P.add("pe", lambda e, ps=ps, ts=ts: e.matmul(ps[:], onesf[0:1, :], posf1[0:1, ts], start=True, stop=True),
                      reads=[R("posf"), R("onesf")], writes=[psr])
                P.add("dve", lambda e, ps=ps, ts=ts: e.tensor_scalar(angf[:, ts], ps[:], sm[:, SM_INVF:SM_INVF + 1], None, ALU.mult),
                      reads=[psr, R("sm")], writes=[R("angf")])
            dump('f', 1, angf, [R('angf')])
            dump('f', 6, posi1.bitcast(F32), [R('posi')], npart=1)
            dump('f', 7, posf1, [R('posf')], npart=1)
            for which, dst, off in ((0, Sb_, 0.0), (1, Cb, float(np.pi / 2))):
                if which == 1:
                    P.add("dve", lambda e, off=off: e.tensor_scalar(angf, angf, off, None, ALU.add),
                          reads=[R("angf")], writes=[R("angf")])
                P.add("dve", lambda e: e.tensor_scalar(kf, angf, 1.0 / TWO_PI, None, ALU.mult),
                      reads=[R("angf")], writes=[R("kf")])
                P.add("dve", lambda e: e.tensor_copy(ki32, kf), reads=[R("kf")], writes=[R("ki32")])
                P.add("dve", lambda e: e.tensor_copy(kf, ki32), reads=[R("ki32")], writes=[R("kf")])
                P.add("dve", lambda e: e.scalar_tensor_tensor(kf, kf, -TWO_PI, angf, ALU.mult, ALU.add),
                      reads=[R("kf"), R("angf")], writes=[R("kf")])
                P.add("dve", lambda e: e.tensor_scalar(kf, kf, float(np.pi), float(-np.pi), ALU.min, ALU.max),
                      reads=[R("kf")], writes=[R("kf")])
                P.add("act", lambda e, dst=dst: e.activation(dst, kf, AF.Sin), reads=[R("kf")], writes=[R("rope", which)])
                dump('f', 2 + which, kf, [R("kf")])
            P.add("dve", lambda e: e.memset(kz[64:128, :, 0, :], 0.0), writes=[R("kzz"), R("kf"), R("ki32"), R("angf"), R("posi")])
            P.add("dve", lambda e: e.memset(kz[0:64, :, 1, :], 0.0), writes=[R("kzz"), R("kf"), R("ki32"), R("angf"), R("posi")])

            ALV = int(os.environ.get('ATT_LEVEL', '9'))
            for pr in range(4 if ALV > 0 else 0):
                wt, wr = w_slot(blk0 + pr)
                groups = [(nm, coff, hh, tb) for (nm, coff) in (("qT", 0), ("kT", 256)) for hh in range(2) for tb in range(4)]
                gstate = {}

                def stageA(gx):
                    nm, coff, hh, tb = groups[gx]
                    ts = slice(tb * 512, (tb + 1) * 512)
                    ps, psr = PS[gx % 2], R("ps", gx % 2)
                    for k in range(8):
                        P.add("pe", lambda e, wt=wt, ps=ps, k=k, hh=hh, ts=ts, coff=coff: e.matmul(
                            ps[:], wt[:, k * 768 + coff + hh * 128:k * 768 + coff + (hh + 1) * 128], hT[:, k, ts],
                            start=(k == 0), stop=(k == 7)), reads=[wr, R("h", k, tb)], writes=[psr])
                    qn = qnb[gx % 2]
                    P.add("act", lambda e, ps=ps, qn=qn: e.activation(qn, ps[:], AF.Copy),
                          reads=[psr], writes=[R("qn", gx % 2)])

                def stageB(gx):
                    nm, coff, hh, tb = groups[gx]
                    ts = slice(tb * 512, (tb + 1) * 512)
                    i2 = gx % 2
                    ps, psr = PS[gx % 2], R("ps", gx % 2)
                    qn = qnb[i2]
                    ps2, ps2r = PS[2 + i2], R("ps", 2 + i2)
                    P.add("pe", lambda e, ps2=ps2, qn=qn: e.matmul(ps2[:], permb, qn, start=True, stop=True),
                          reads=[R("qn", i2), R("cstb")], writes=[ps2r])
                    t1 = t1f[i2]
                    t2 = t2f[i2]
                    P.add("dve", lambda e, ps=ps, t1=t1, ts=ts: e.tensor_tensor(t1, ps[:], Cb[:, ts], ALU.mult),
                          reads=[psr, R("rope", 1)], writes=[R("t1", i2)])
                    P.add("dve", lambda e, ps2=ps2, t2=t2, ts=ts: e.tensor_tensor(t2, ps2[:], Sb_[:, ts], ALU.mult),
                          reads=[ps2r, R("rope", 0)], writes=[R("t2", i2)])
                    if nm == "qT":
                        P.add("pool", lambda e, hh=hh, ts=ts, t1=t1, t2=t2: e.tensor_tensor(
                            qT[:, hh, ts], t1, t2, ALU.add),
                            reads=[R("t1", i2), R("t2", i2)], writes=[R("qT", hh, tb)])
                    else:
                        for i_ in range(2):
                            prt_ = slice(64 * i_, 64 * i_ + 64)
                            P.add("pool", lambda e, hh=hh, ts=ts, t1=t1, t2=t2, i_=i_, prt_=prt_: e.tensor_tensor(
                                kz[prt_, hh, i_, ts], t1[prt_, :], t2[prt_, :], ALU.add),
                                reads=[R("t1", i2), R("t2", i2), R("kzz")], writes=[R("kT", hh, tb)])

                for gx in range(len(groups) + 1):
                    if gx < len(groups):
                        stageA(gx)
                    if gx >= 1:
                        stageB(gx - 1)
                for tt in range(16 if not os.environ.get("SKIPV") else 0):
                    tok = slice(tt * 128, (tt + 1) * 128)
                    ps, psr = ps_next([0, 1])
                    for k in range(8):
                        P.add("pe", lambda e, wt=wt, ps=ps, k=k, tok=tok: e.matmul(
                            ps[:, 0:256], hT[:, k, tok], wt[:, k * 768 + 512:k * 768 + 768],
                            start=(k == 0), stop=(k == 7)), reads=[wr, R("h", k, tt // 4)], writes=[psr])
                    P.add("dve", lambda e, wt=wt, ps=ps, tt=tt: e.tensor_copy(
                        vaug[:, tt, :, :], ps[:, 0:256].rearrange("p (h f) -> p h f", h=2)),
                        reads=[psr], writes=[R("vaug", tt)])
                P.fence()
                its = []
                gi = 0
                for j in range(4):
                    for hh in range(2):
                        for i in range(2):
                            for kt in range(4 * j + 4):
                                its.append((hh, j, i, kt, gi))
                            gi += 1
                LA = 2
                SB = [1, 2, 3]
                deferred = []

                def emit_qk(idx):
                    hh, j, i, kt, g_ = its[idx]
                    r = kt - 4 * j
                    n0 = 128 * r if r > 0 else 0
                    ps, psr = PS[SB[idx % 3]], R("ps", SB[idx % 3])
                    P.add("pe", lambda e, ps=ps, i=i, hh=hh, kt=kt, j=j, n0=n0, r=r: e.matmul(
                        ps[:, n0:512], kz[:, hh, i, kt * 128:(kt + 1) * 128],
                        qT[:, hh, j * 512 + n0:(j + 1) * 512], start=True, stop=(r < 0)),
                        reads=[R("kT", hh, kt // 4), R("qT", hh, j), R("kzz")], writes=[psr])
                    if r >= 0:
                        P.add("pe", lambda e, ps=ps, n0=n0: e.matmul(
                            ps[:, n0:n0 + 128], identb, negmb, start=False, stop=True),
                            reads=[R("cstb")], writes=[psr])
                    pT = pTb[idx % 3]
                    pTr = R("pT", idx % 3)
                    P.add("act", lambda e, ps=ps, pT=pT, n0=n0: e.activation(
                        pT[:, n0:512], ps[:, n0:512], AF.Exp, scale=0.125),
                        reads=[psr], writes=[pTr])

                def emit_pv(idx):
                    hh, j, i, kt, g_ = its[idx]
                    r = kt - 4 * j
                    n0 = 128 * r if r > 0 else 0
                    pT = pTb[idx % 3]
                    pTr = R("pT", idx % 3)
                    ao, aor = PS[4 + 2 * i], R("ps", 4 + 2 * i)
                    al, alr = PS[5 + 2 * i], R("ps", 5 + 2 * i)
                    last = (kt == 4 * j + 3)
                    P.add("pe", lambda e, ao=ao, pT=pT, kt=kt, hh=hh, n0=n0, last=last: e.matmul(
                        ao[:, n0:512], vaug[:, kt, hh, :], pT[:, n0:512], start=(kt == 0), stop=last),
                        reads=[pTr, R("vaug", kt)], writes=[aor])
                    P.add("pe", lambda e, al=al, pT=pT, kt=kt, n0=n0, last=last: e.matmul(
                        al[:, n0:512], onesb[:], pT[:, n0:512], start=(kt == 0), stop=last),
                        reads=[pTr, R("onesb")], writes=[alr])
                    if not last:
                        return
                    qs = slice(j * 512, (j + 1) * 512)
                    if i == 0:
                        P.add("dve", lambda e, al=al: e.reciprocal(c_r2, al[:]), reads=[alr], writes=[R("c_r2")])
                        P.add("dve", lambda e, ao=ao: e.tensor_tensor(c_t1, ao[:], c_r2, ALU.mult),
                              reads=[aor, R("c_r2")], writes=[R("c_t1")])
                        return
                    P.add("dve", lambda e, al=al: e.reciprocal(c_r2, al[:]), reads=[alr], writes=[R("c_r2")])
                    P.add("dve", lambda e, ao=ao: e.scalar_tensor_tensor(c_t2, ao[:], lamv[:, 0:1], c_r2, ALU.mult, ALU.mult),
                          reads=[aor, R("c_r2"), R("lamv")], writes=[R("c_t2")])
                    P.add("pool", lambda e: e.tensor_tensor(c_d, c_t1, c_t2, ALU.add),
                          reads=[R("c_t1"), R("c_t2")], writes=[R("c_d")])

                    def epi1a():
                        P.add("act", lambda e: e.activation(c_sq, c_d, AF.Square), reads=[R("c_d")], writes=[R("c_sq")])

                    def epi1b():
                        pss, pssr = PS[0], R("ps", 0)
                        P.add("pe", lambda e, pss=pss: e.matmul(pss[:], onesb[:], c_sq, start=True, stop=True),
                              reads=[R("c_sq"), R("onesb")], writes=[pssr])

                    def epi1c(hh=hh):
                        pss, pssr = PS[0], R("ps", 0)
                        P.add("act", lambda e, pss=pss: e.activation(c_ln, pss[:], AF.Ln, bias=EPS, scale=1.0 / 128),
                              reads=[pssr], writes=[R("c_ln")])
                        P.add("act", lambda e: e.activation(c_ln, c_ln, AF.Exp, scale=-0.5),
                              reads=[R("c_ln")], writes=[R("c_ln")])
                        P.add("dve", lambda e, hh=hh: e.scalar_tensor_tensor(
                            oT[:, hh, :], c_d, lamv[:, 1:2], c_ln, ALU.mult, ALU.mult),
                            reads=[R("c_d"), R("c_ln"), R("lamv")], writes=[R("oT", hh)])

                    def epi2(j=j, qs=qs):
                        for dch in range(8):
                            psw, pswr = PS[0], R("ps", 0)
                            for h2 in range(2):
                                P.add("pe", lambda e, wt=wt, psw=psw, h2=h2, dch=dch: e.matmul(
                                    psw[:], wt[:, 6144 + h2 * 1024 + dch * 128:6144 + h2 * 1024 + (dch + 1) * 128],
                                    oT[:, h2, :], start=(h2 == 0), stop=(h2 == 1)),
                                    reads=[wr, R("oT", h2)], writes=[pswr])
                            P.add("dve", lambda e, psw=psw, dch=dch, qs=qs: e.scalar_tensor_tensor(
                                xT[:, dch, qs], psw[:], gate[:, dch:dch + 1], xT[:, dch, qs], ALU.mult, ALU.add),
                                reads=[pswr, R("modv"), R("x", dch, j)], writes=[R("x", dch, j)])

                    step_now = idx + LA
                    deferred.append((step_now + 10, epi1a))
                    deferred.append((step_now + 12, epi1b))
                    deferred.append((step_now + 14, epi1c))
                    if hh == 1:
                        deferred.append((step_now + 19, epi2))

                def run_deferred(step, flush=False):
                    while deferred and (flush or deferred[0][0] <= step):
                        deferred.pop(0)[1]()

                for step in range(len(its) + LA):
                    if step < len(its):
                        emit_qk(step)
                    if step >= LA:
                        idx_ = step - LA
                        hh_, j_, i_, kt_, g_ = its[idx_]
                        if i_ == 1 and kt_ == 4 * j_ + 3:
                            run_deferred(step, flush=True)
                        emit_pv(idx_)
                    run_deferred(step)
                run_deferred(0, flush=True)
                P.fence()
                w_issue()

        P.fence()
        for nb_ in range(6):
            adaln_block(0, nb_)
        adaln_finish(0)
        norm(dv[:, 0:8], modv[:, 0:8])
        gla()
        if stage >= 2:
            norm(dv[:, 8:16], modv[:, 24:32])
            mlp(0)
        if stage >= 3:
            adaln_finish(1)
            norm(dv[:, 0:8], modv[:, 48:56])
            att()
        if stage >= 4:
            norm(dv[:, 8:16], modv[:, 48 + 24:48 + 32])
            mlp(1)
            norm(sm[:, SM_FG:SM_FG + 8], None, final=True)
        ov = out_d.rearrange("(k p) t -> p k t", p=128)
        outr = []
        for k in range(8):
            rr = R("out", k)
            outr.append(rr)
            P.add("sp", lambda e, k=k: e.dma_start(out=ov[:, k, :], in_=xT[:, k, :]),
                  reads=[R("x", k, tb) for tb in range(4)], writes=[rr], dma=True)
        P.add("sp", lambda e: None, reads=outr, noinst=True)
        P.emit(sems, dsems)
    return nc


def _kmaj(Wc):
    K = Wc.shape[0] // 128
    return np.ascontiguousarray(Wc.reshape(K, 128, -1).transpose(1, 0, 2)).reshape(128, -1)


def _prep_shared(inp):
    f = np.float32
    blocks = np.zeros((NBLK, 128, 8192), f)
    ada_w = np.asarray(inp["ada_w"], f)
    mlp_w1 = np.asarray(inp["mlp_w1"], f)
    mlp_w2 = np.asarray(inp["mlp_w2"], f)
    gw = np.asarray(inp["gla_w_in"], f)[0]
    gwo = np.asarray(inp["gla_w_o"], f)[0]
    dw = np.asarray(inp["diff_w_in"], f)[0]
    dwo = np.asarray(inp["diff_w_o"], f)[0]
    for bi, name in enumerate(BLOCKS):
        kind = name[0]
        if kind == "ada":
            _, l, nb_ = name
            blocks[bi] = _kmaj(ada_w[l][:, nb_ * 1024:(nb_ + 1) * 1024])
        elif kind == "glaA":
            g = name[1]
            cols = np.concatenate([np.arange(2 * g * 128, 2 * g * 128 + 256),
                                   512 + np.arange(2 * g * 128, 2 * g * 128 + 256),
                                   1024 + np.arange(2 * g * 256, 2 * g * 256 + 512)])
            blocks[bi] = _kmaj(gw[:, cols])
        elif kind == "glaB":
            g = name[1]
            rc = 2048 + np.arange(2 * g * 256, 2 * g * 256 + 512)
            blocks[bi][:, 0:4096] = _kmaj(gw[:, rc])
            blocks[bi][:, 4096:8192] = _kmaj(gwo[g * 512:(g + 1) * 512, :])
        elif kind == "w1":
            _, l, g = name
            blocks[bi] = _kmaj(mlp_w1[l][:, g * 1024:(g + 1) * 1024])
        elif kind == "w2":
            _, l, g = name
            blocks[bi] = _kmaj(mlp_w2[l][g * 1024:(g + 1) * 1024, :])
        elif kind == "att":
            pr = name[1]
            cols = np.concatenate([np.arange(pr * 256, pr * 256 + 256),
                                   1024 + np.arange(pr * 256, pr * 256 + 256),
                                   2048 + np.arange(pr * 256, pr * 256 + 256)])
            blocks[bi][:, 0:6144] = _kmaj(dw[:, cols])
            blocks[bi][:, 6144:8192] = _kmaj(dwo[pr * 256:(pr + 1) * 256, :])

    def pk(v):
        return np.asarray(v, f).reshape(-1, 128).T

    sm = np.zeros((128, NSM), f)
    ng = np.asarray(inp["norm_g"], f)
    for l in range(2):
        for j in range(2):
            sm[:, SM_NG + (2 * l + j) * 8:SM_NG + (2 * l + j) * 8 + 8] = pk(ng[l, j])
    sm[:, SM_FG:SM_FG + 8] = pk(inp["final_g"])
    ab = np.asarray(inp["ada_b"], f)
    for l in range(2):
        sm[:, SM_AB + l * 48:SM_AB + (l + 1) * 48] = pk(ab[l])
    sm[:, SM_BR:SM_BR + 8] = pk(np.asarray(inp["gla_b_r"], f)[0])
    sm[:, SM_GNG:SM_GNG + 2] = pk(np.asarray(inp["gla_norm_g"], f)[0])
    sm[:, SM_SLG:SM_SLG + 1] = pk(np.asarray(inp["diff_subln_g"], f)[0])
    inv_freq = (np.float32(500000.0) ** (-(np.arange(0, 16, 2, dtype=np.float32)) / np.float32(16))).astype(f)
    invf = np.zeros(128, f)
    for p in range(128):
        d = p % 64
        if d < 16:
            invf[p] = inv_freq[d % 8]
    sm[:, SM_INVF] = invf

    rw = np.asarray(inp["diff_lambda"], f)[0].reshape(1, 256)
    wa = np.zeros((128, 8, 48), f)
    wa[:, :, 32:48] = gw[:, 3072:3088].reshape(8, 128, 16).transpose(1, 0, 2)
    wa = wa.reshape(128, 8 * 48)
    wa2 = np.zeros((48, 512), f)
    wa2[0] = np.asarray(inp["gla_b_a"], f)[0]
    wa2[32:48] = np.asarray(inp["gla_w_a2"], f)[0]
    cst = np.zeros((128, 512), f)
    cst[:, 0:128] = np.eye(128, dtype=f)
    tri = (np.arange(128)[None, :] >= np.arange(128)[:, None]).astype(f)
    cst[:, 128:256] = tri
    cst[:, 256:384] = tri * f(-1.0 / 16.0)
    perm = np.zeros((128, 128), f)
    for b_ in range(2):
        base = 64 * b_
        for d in range(8):
            perm[base + d + 8, base + d] = -1.0
        for d in range(8, 16):
            perm[base + d - 8, base + d] = 1.0
    cst[:, 384:512] = perm
    return dict(wblk=blocks, sm=sm, rw=rw, wa=wa, wa2=wa2, cst=cst)


_NC_CACHE = {}


def kernel(x, c, positions, ada_w, ada_b, norm_g, mlp_w1, mlp_w2,
           gla_w_in, gla_w_a2, gla_b_a, gla_b_r, gla_norm_g, gla_w_o,
           diff_w_in, diff_lambda, diff_subln_g, diff_w_o, final_g, _stage=4, _debug=False, _ncores=NB):
    inp = dict(ada_w=ada_w, ada_b=ada_b, norm_g=norm_g, mlp_w1=mlp_w1, mlp_w2=mlp_w2,
               gla_w_in=gla_w_in, gla_w_a2=gla_w_a2, gla_b_a=gla_b_a, gla_b_r=gla_b_r,
               gla_norm_g=gla_norm_g, gla_w_o=gla_w_o, diff_w_in=diff_w_in, diff_lambda=diff_lambda,
               diff_subln_g=diff_subln_g, diff_w_o=diff_w_o, final_g=final_g)
    sh = _prep_shared(inp)
    x = np.asarray(x, np.float32)
    c = np.asarray(c, np.float32)
    positions = np.asarray(positions, np.int32)
    in_maps = []
    for b in range(NB):
        sm = sh["sm"].copy()
        sm[:, SM_C:SM_C + 8] = c[b].reshape(8, 128).T
        in_maps.append(dict(xT=np.ascontiguousarray(x[b].T), pos=np.ascontiguousarray(positions[b:b + 1]).view(np.float32),
                            sm=sm, rw=sh["rw"], wa=sh["wa"], wa2=sh["wa2"], cst=sh["cst"], wblk=sh["wblk"]))
    key = (_stage, _debug)
    if key not in _NC_CACHE:
        _NC_CACHE[key] = build_nc(_stage, _debug)
    nc = _NC_CACHE[key]
    res = run_bass_kernel_spmd(nc, in_maps[:_ncores], core_ids=list(range(_ncores)))
    out = np.stack([np.ascontiguousarray(r["outT"].T) for r in res.results], axis=0)
    if _debug:
        return out.astype(np.float32), res.results[0]["dbgf"], res.results[0]["dbgb"]
    return out.astype(np.float32)
```

```python
import contextlib
import os
import math
import numpy as np
import concourse.bass as bass
import concourse.mybir as mybir
from concourse.bass_utils import run_bass_kernel_spmd

F32 = mybir.dt.float32
BF16 = mybir.dt.bfloat16
I32 = mybir.dt.int32
AF = mybir.ActivationFunctionType
ALU = mybir.AluOpType
AX = mybir.AxisListType

D = 1024
T = 2048
NB = 8
DFF = 4096
EPS = 1e-6
LAMBDA_INIT = 0.8 - 0.6 * math.exp(-0.3 * 1)
NBLK = 36
ADA1_AFTER_GROUP = [2, 2, 1, 1]


def _block_order():
    order = [("ada", 0, n) for n in range(6)]
    for g in range(2):
        order += [("glaA", g), ("glaB", g)]
    n1 = 0
    for g in range(4):
        order += [("w1", 0, g), ("w2", 0, g)]
        for _ in range(ADA1_AFTER_GROUP[g]):
            order.append(("ada", 1, n1)); n1 += 1
    order += [("att", p) for p in range(4)]
    for g in range(4):
        order += [("w1", 1, g), ("w2", 1, g)]
    assert len(order) == NBLK and n1 == 6
    return order


BLOCKS = _block_order()
BIDX = {b: i for i, b in enumerate(BLOCKS)}
NSLOT = 3
TWO_PI = float(2.0 * np.pi)

SM_NG = 0
SM_FG = 32
SM_AB = 40
SM_BR = 136
SM_GNG = 144
SM_SLG = 146
SM_INVF = 147
SM_C = 148
NSM = 160


class Res:
    __slots__ = ("w", "r", "excl")

    def __init__(self):
        self.w = None
        self.r = []
        self.excl = False


class Op:
    __slots__ = ("eng", "fn", "deps", "eidx", "signal", "seq", "dma", "dsem", "dval")


class Prog:
    NDMA = 24

    def __init__(self, nc, same_engine_sync=True):
        self.nc = nc
        self.engs = {"pe": nc.tensor, "act": nc.scalar, "dve": nc.vector,
                     "pool": nc.gpsimd, "sp": nc.sync}
        self.ops = []
        self.ecount = {k: 0 for k in self.engs}
        self.known = {k: {} for k in self.engs}
        self.kdma = {k: {} for k in self.engs}
        self.ndma = {"sp": 0, "pool": 0}
        self.dma_ops = {"sp": [], "pool": []}
        self.dbase = {"sp": 0, "pool": self.NDMA // 2}
        self.same_engine_sync = same_engine_sync
        self.res = {}
        self.last = {}

    def R(self, *key):
        r = self.res.get(key)
        if r is None:
            r = Res()
            r.excl = (key[0] == "ps")
            self.res[key] = r
        return r

    def alias_barrier(self, olds, news):
        pend = []
        for o in olds:
            if o.w is not None:
                pend.append(o.w)
            pend.extend(o.r)
        for n in news:
            n.r.extend(pend)

    def fence(self):
        last = dict(self.last)
        for eng in self.engs:
            self.add(eng, lambda e: None, extra=[o for k, o in last.items() if k != eng], noinst=True)

    def add(self, eng, fn, reads=(), writes=(), dma=False, extra=(), noinst=False):
        op = Op()
        op.eng = eng
        op.fn = fn
        op.dma = dma
        op.signal = False
        op.seq = None
        op.eidx = self.ecount[eng]
        self.ecount[eng] += 1
        if any(r.excl for r in reads):
            writes = list(writes) + [r for r in reads if r.excl]
            reads = [r for r in reads if not r.excl]
        deps = list(extra)
        for r in reads:
            if r.w is not None:
                deps.append(r.w)
        for r in writes:
            if r.w is not None:
                deps.append(r.w)
            deps.extend(r.r)
        if dma:
            i = self.ndma[eng]
            self.ndma[eng] += 1
            h = self.NDMA // 2
            op.dsem = self.dbase[eng] + i % h
            op.dval = 16 * (i // h + 1)
            if i >= h:
                deps.append(self.dma_ops[eng][i - h])
            self.dma_ops[eng].append(op)
        best = {}
        for d in deps:
            if d is op:
                continue
            if d.dma:
                if self.kdma[eng].get(d.dsem, 0) >= d.dval:
                    continue
                key = ("dma", d.dsem)
                if key not in best or best[key].dval < d.dval:
                    best[key] = d
            else:
                if d.eng == eng and (eng == "pe" or not self.same_engine_sync):
                    continue
                if self.known[eng].get(d.eng, -1) >= d.eidx:
                    continue
                key = d.eng
                if key not in best or best[key].eidx < d.eidx:
                    best[key] = d
        final = []
        for key, d in best.items():
            final.append(d)
            if d.dma:
                self.kdma[eng][d.dsem] = d.dval
            else:
                d.signal = True
                self.known[eng][d.eng] = d.eidx
        op.deps = final
        for r in reads:
            r.r.append(op)
        for r in writes:
            r.w = op
            r.r = []
        self.ops.append(op)
        if not dma and not noinst:
            self.last[eng] = op
        return op

    def emit(self, sems, dsems):
        cnt = {k: 0 for k in self.engs}
        for op in self.ops:
            if not op.dma and op.signal:
                cnt[op.eng] += 1
                op.seq = cnt[op.eng]
        for op in self.ops:
            e = self.engs[op.eng]
            for d in op.deps:
                if d.dma:
                    e.wait_ge(dsems[d.dsem], d.dval)
                else:
                    e.wait_ge(sems[d.eng], d.seq)
            ins = op.fn(e)
            if ins is None:
                continue
            if op.dma:
                ins.then_inc(dsems[op.dsem], 16)
            elif op.signal:
                ins.then_inc(sems[op.eng], 1)
        return cnt


def build_nc(stage=4, debug=False):
    nc = bass.Bass("TRN2", target_bir_lowering=False)
    xT_d = nc.dram_tensor("xT", [D, T], F32, kind="ExternalInput").ap()
    pos_d = nc.dram_tensor("pos", [1, T], F32, kind="ExternalInput").ap()
    sm_d = nc.dram_tensor("sm", [128, NSM], F32, kind="ExternalInput").ap()
    rw_d = nc.dram_tensor("rw", [1, 256], F32, kind="ExternalInput").ap()
    wa_d = nc.dram_tensor("wa", [128, 8 * 48], F32, kind="ExternalInput").ap()
    wa2_d = nc.dram_tensor("wa2", [48, 512], F32, kind="ExternalInput").ap()
    cst_d = nc.dram_tensor("cst", [128, 512], F32, kind="ExternalInput").ap()
    wblk_d = nc.dram_tensor("wblk", [NBLK, 128, 8192], F32, kind="ExternalInput").ap()
    out_d = nc.dram_tensor("outT", [D, T], F32, kind="ExternalOutput").ap()
    if debug:
        dbgf_d = nc.dram_tensor("dbgf", [16, 128, 2048], F32, kind="ExternalOutput").ap()
        dbgb_d = nc.dram_tensor("dbgb", [24, 128, 2048], BF16, kind="ExternalOutput").ap()

    es = contextlib.ExitStack()
    with es:
        def sb(name, shape, dt):
            return es.enter_context(nc.sbuf_tensor(name, shape, dt))

        xT = sb("xTs", [128, 8, T], F32)
        hT = sb("hTs", [128, 8, T], BF16)
        wsl = [sb("wsl%d" % i, [128, 8192], BF16) for i in range(NSLOT)]
        cstf = sb("cstf", [128, 512], F32)
        cstb = sb("cstb", [128, 512], BF16)
        onesb = sb("onesb", [128, 128], BF16)
        onesf = sb("onesf", [128, 128], F32)
        sm = sb("sms", [128, NSM], F32)
        modv = sb("modv", [128, 96], F32)
        dv = sb("dvs", [128, 64], F32)
        cact = sb("cact", [128, 8], BF16)
        lamv = sb("lamv", [128, 8], F32)
        lrow_t = sb("lrow", [1, 256], F32)
        U = sb("U", [128, 29400], BF16)
        PS = [es.enter_context(nc.psum_tensor("ps%d" % i, [128, 512], F32)) for i in range(8)]
        sems = {k: es.enter_context(nc.semaphore("s_" + k)) for k in ["pe", "act", "dve", "pool", "sp"]}
        dsems = [es.enter_context(nc.semaphore("d%d" % i)) for i in range(Prog.NDMA)]

        P = Prog(nc)
        R = P.R

        identf = cstf[:, 0:128]
        trinegf = cstf[:, 256:384]
        identb = cstb[:, 0:128]
        trib = cstb[:, 128:256]
        permb = cstb[:, 384:512]
        negmb = cstb[:, 256:384]

        def Ub(off, n):
            return U[:, off:off + n]

        def Uf(off, n):
            return U[:, off:off + 2 * n].bitcast(F32)

        def dump(kind, idx, ap, reads, npart=128):
            if not debug:
                return
            n = ap.shape[-1]
            dst = (dbgf_d if kind == "f" else dbgb_d)[idx, 0:npart, 0:n]
            P.add("sp", lambda e: e.dma_start(out=dst, in_=ap), reads=reads, writes=[R("dbg", kind, idx)], dma=True)

        wstate = {"next": 0}

        def w_issue():
            i = wstate["next"]
            if i >= NBLK:
                return
            wstate["next"] = i + 1
            s = i % NSLOT
            P.add("pool", lambda e, i=i, s=s: e.dma_start(out=wsl[s][:], in_=wblk_d[i]),
                  writes=[R("w", s)], dma=True)

        def w_slot(i):
            return wsl[i % NSLOT], R("w", i % NSLOT)

        prot = {"i": 0}

        def ps_next(banks):
            b = banks[prot["i"] % len(banks)]
            prot["i"] += 1
            return PS[b], R("ps", b)

        P.add("sp", lambda e: e.dma_start(out=sm[:], in_=sm_d), writes=[R("sm")], dma=True)
        P.add("sp", lambda e: e.dma_start(out=cstf[:], in_=cst_d), writes=[R("cstf")], dma=True)
        for i in range(NSLOT):
            w_issue()
        xv = xT_d.rearrange("(k p) t -> p k t", p=128)
        for k in range(8):
            P.add("sp", lambda e, k=k: e.dma_start(out=xT[:, k, :], in_=xv[:, k, :]),
                  writes=[R("x", k, tb) for tb in range(4)], dma=True)
        P.add("dve", lambda e: e.tensor_copy(cstb[:], cstf[:]), reads=[R("cstf")], writes=[R("cstb")])
        P.add("dve", lambda e: e.tensor_scalar(cstb[:, 256:384], cstf[:, 128:256], -1.0, 30000.0, ALU.add, ALU.mult),
              reads=[R("cstf"), R("cstb")], writes=[R("cstb")])
        P.add("dve", lambda e: e.memset(onesb[:], 1.0), writes=[R("onesb")])
        P.add("dve", lambda e: e.memset(onesf[:], 1.0), writes=[R("onesf")])
        P.add("act", lambda e: e.activation(cact[:], sm[:, SM_C:SM_C + 8], AF.Silu),
              reads=[R("sm")], writes=[R("cact")])

        def adaln_block(l, nb_):
            modrow = U[0:1, 18432:20480].bitcast(F32)
            wt, wr = w_slot(BIDX[("ada", l, nb_)])
            for half in range(2):
                n0 = half * 512
                ps, psr = ps_next([0, 1])
                for k in range(8):
                    P.add("pe", lambda e, ps=ps, wt=wt, k=k, n0=n0: e.matmul(
                        ps[0:1, :], cact[:, k:k + 1], wt[:, k * 1024 + n0:k * 1024 + n0 + 512],
                        start=(k == 0), stop=(k == 7)),
                        reads=[wr, R("cact")], writes=[psr])
                P.add("act", lambda e, ps=ps, n0=n0: e.activation(modrow[0:1, n0:n0 + 512], ps[0:1, :], AF.Copy),
                      reads=[psr], writes=[R("modrow")])
            w_issue()
            ps, psr = ps_next([0, 1])
            for m in range(8):
                P.add("pe", lambda e, ps=ps, m=m: e.matmul(
                    ps[:, m:m + 1], modrow[0:1, m * 128:(m + 1) * 128], onesf[0:1, 0:1], start=True, stop=True),
                    reads=[R("modrow"), R("onesf")], writes=[psr])
            c0 = l * 48 + nb_ * 8
            P.add("dve", lambda e, ps=ps, c0=c0: e.tensor_tensor(
                modv[:, c0:c0 + 8], ps[:, 0:8], sm[:, SM_AB + c0:SM_AB + c0 + 8], ALU.add),
                reads=[psr, R("sm")], writes=[R("modv")])

        def adaln_finish(l):
            mo = l * 48
            P.add("dve", lambda e: e.scalar_tensor_tensor(
                dv[:, 0:8], modv[:, mo + 8:mo + 16], 1.0, sm[:, SM_NG + (2 * l) * 8:SM_NG + (2 * l) * 8 + 8],
                ALU.add, ALU.mult), reads=[R("modv"), R("sm")], writes=[R("dv")])
            P.add("dve", lambda e: e.scalar_tensor_tensor(
                dv[:, 8:16], modv[:, mo + 32:mo + 40], 1.0, sm[:, SM_NG + (2 * l + 1) * 8:SM_NG + (2 * l + 1) * 8 + 8],
                ALU.add, ALU.mult), reads=[R("modv"), R("sm")], writes=[R("dv")])

        NO = 29400 - 5120
        UEND = 29400

        def norm(gs_ap, sh_ap, final=False, fence=True):
            if fence:
                P.fence()
            for tb in range(4):
                ts = slice(tb * 512, (tb + 1) * 512)
                pss, pssr = ps_next([0, 1])
                for k in range(8):
                    sq = Ub(NO + (k % 2) * 512, 512)
                    sqr = R("n_sq", k % 2)
                    P.add("act", lambda e, sq=sq, k=k, ts=ts: e.activation(sq, xT[:, k, ts], AF.Square),
                          reads=[R("x", k, tb)], writes=[sqr])
                    P.add("pe", lambda e, pss=pss, sq=sq, k=k: e.matmul(pss[:], onesb[:], sq, start=(k == 0), stop=(k == 7)),
                          reads=[sqr, R("onesb")], writes=[pssr])
                lnt = Uf(NO + 1024, 512)
                rstd = Uf(NO + 2048, 512)
                P.add("act", lambda e, pss=pss, lnt=lnt: e.activation(lnt, pss[:], AF.Ln, bias=EPS, scale=1.0 / D),
                      reads=[pssr], writes=[R("n_ln")])
                P.add("act", lambda e, lnt=lnt, rstd=rstd: e.activation(rstd, lnt, AF.Exp, scale=-0.5),
                      reads=[R("n_ln")], writes=[R("n_rstd")])
                for k in range(8):
                    if final:
                        P.add("dve", lambda e, k=k, ts=ts, rstd=rstd: e.scalar_tensor_tensor(
                            xT[:, k, ts], xT[:, k, ts], gs_ap[:, k:k + 1], rstd, ALU.mult, ALU.mult),
                            reads=[R("n_rstd"), R("sm")], writes=[R("x", k, tb)])
                    else:
                        tmp = Uf(NO + 3072 + (k % 2) * 1024, 512)
                        tr = R("n_tmp", k % 2)
                        P.add("dve", lambda e, k=k, ts=ts, rstd=rstd, tmp=tmp: e.scalar_tensor_tensor(
                            tmp, xT[:, k, ts], gs_ap[:, k:k + 1], rstd, ALU.mult, ALU.mult),
                            reads=[R("n_rstd"), R("x", k, tb), R("dv")], writes=[tr])
                        P.add("act", lambda e, k=k, ts=ts, tmp=tmp: e.activation(
                            hT[:, k, ts], tmp, AF.Identity, bias=sh_ap[:, k:k + 1]),
                            reads=[tr, R("modv")], writes=[R("h", k, tb)])

        def mlp(l):
            mo = l * 48
            gate = modv[:, mo + 40:mo + 48]
            hid = U[:, 0:16384].rearrange("p (j t) -> p j t", j=8)
            for g in range(4):
                w1, w1r = w_slot(BIDX[("w1", l, g)])
                for tb in range(4):
                    ts = slice(tb * 512, (tb + 1) * 512)
                    for j in range(8):
                        ps, psr = ps_next([0, 1, 2, 3])
                        for k in range(8):
                            P.add("pe", lambda e, ps=ps, w1=w1, k=k, j=j, ts=ts: e.matmul(
                                ps[:], w1[:, k * 1024 + j * 128:k * 1024 + (j + 1) * 128], hT[:, k, ts],
                                start=(k == 0), stop=(k == 7)),
                                reads=[w1r, R("h", k, tb)], writes=[psr])
                        sq = Uf(16384 + (j % 2) * 1024, 512)
                        sqr = R("m_sq", j % 2)
                        P.add("act", lambda e, ps=ps, sq=sq: e.activation(sq, ps[:], AF.Square),
                              reads=[psr], writes=[sqr])
                        P.add("dve", lambda e, ps=ps, sq=sq, j=j, ts=ts: e.scalar_tensor_tensor(
                            hid[:, j, ts], ps[:], 0.0, sq, ALU.is_gt, ALU.mult),
                            reads=[psr, sqr], writes=[R("hid", j, tb)])
                w_issue()
                w2, w2r = w_slot(BIDX[("w2", l, g)])
                for tb in range(4):
                    ts = slice(tb * 512, (tb + 1) * 512)
                    for dch in range(8):
                        ps, psr = ps_next([0, 1, 2, 3])
                        for j in range(8):
                            P.add("pe", lambda e, ps=ps, w2=w2, j=j, dch=dch, ts=ts: e.matmul(
                                ps[:], w2[:, j * 1024 + dch * 128:j * 1024 + (dch + 1) * 128], hid[:, j, ts],
                                start=(j == 0), stop=(j == 7)),
                                reads=[w2r, R("hid", j, tb)], writes=[psr])
                        P.add("dve", lambda e, ps=ps, dch=dch, ts=ts: e.scalar_tensor_tensor(
                            xT[:, dch, ts], ps[:], gate[:, dch:dch + 1], xT[:, dch, ts], ALU.mult, ALU.add),
                            reads=[psr, R("modv"), R("x", dch, tb)], writes=[R("x", dch, tb)])
                w_issue()
                if l == 0 and stage >= 3:
                    for _ in range(ADA1_AFTER_GROUP[g]):
                        adaln_block(1, ada1_state["n"])
                        ada1_state["n"] += 1

        ada1_state = {"n": 0}

        def gla():
            blk0 = BIDX[("glaA", 0)]
            P.fence()
            gate = modv[:, 16:24]
            o = 0
            a_aug = U[0:48, o:o + 2048]; o += 2048
            wab = Ub(o, 384); o += 384
            wa2b = U[0:48, o:o + 512]; o += 512
            waf = Uf(o, 384); o += 768
            wa2f = U[0:48, o:o + 1024].bitcast(F32); o += 1024
            spf = [Uf(o + i * 512, 256) for i in range(2)]; o += 1024
            ebf = [Uf(o + i * 1024, 512) for i in range(2)]; o += 2048
            enbf = [Uf(o + i * 1024, 512) for i in range(2)]; o += 2048
            qd = U[:, o:o + 1024].rearrange("p (h t) -> p h t", h=2); o += 1024
            ki = U[:, o:o + 1024].rearrange("p (h t) -> p h t", h=2); o += 1024
            ktok = U[:, o:o + 1024].rearrange("p (c h f) -> p c h f", c=4, h=2); o += 1024
            vtok = U[:, o:o + 2048].rearrange("p (c f) -> p c f", c=4); o += 2048
            rg = U[:, o:o + 2048].rearrange("p (j t) -> p j t", j=4); o += 2048
            og = U[:, o:o + 2048].rearrange("p (j t) -> p j t", j=4); o += 2048
            Sf = U[:, o:o + 1024].bitcast(F32).rearrange("p (h f) -> p h f", h=2); o += 1024
            Sb2 = [U[:, o + i * 512:o + (i + 1) * 512].rearrange("p (h f) -> p h f", h=2) for i in range(2)]; o += 1024
            sTb = [Ub(o + i * 128, 128) for i in range(2)]; o += 256
            tmpf = [Uf(o + i * 1024, 512) for i in range(2)]; o += 2048
            stmp = [Uf(o + i * 512, 256) for i in range(2)]; o += 1024
            etmp = Uf(o, 256); o += 512
            sqb = [Ub(o + i * 512, 512) for i in range(2)]; o += 1024
            assert o <= UEND, o

            P.add("sp", lambda e: e.dma_start(out=waf, in_=wa_d), writes=[R("waf"), R("modrow")], dma=True)
            P.add("sp", lambda e: e.dma_start(out=wa2f, in_=wa2_d), writes=[R("wa2f"), R("modrow")], dma=True)
            P.add("dve", lambda e: e.tensor_copy(wab, waf), reads=[R("waf")], writes=[R("wab")])
            P.add("dve", lambda e: e.tensor_copy(wa2b, wa2f), reads=[R("wa2f")], writes=[R("wa2b")])
            for tb in range(4):
                ts = slice(tb * 512, (tb + 1) * 512)
                ps, psr = ps_next([0, 1])
                for k in range(8):
                    P.add("pe", lambda e, ps=ps, k=k, ts=ts: e.matmul(
                        ps[0:48, :], wab[:, k * 48:(k + 1) * 48], hT[:, k, ts], start=(k == 0), stop=(k == 7)),
                        reads=[R("wab"), R("h", k, tb)], writes=[psr])
                P.add("act", lambda e, ps=ps, ts=ts: e.activation(a_aug[0:48, ts], ps[0:48, :], AF.Copy),
                      reads=[psr], writes=[R("a_aug", tb)])
                P.add("dve", lambda e, ts=ts: e.memset(a_aug[0:1, ts], 1.0),
                      reads=[R("a_aug", tb)], writes=[R("a_aug", tb)])

            for g in range(2):
                wA, wAr = w_slot(blk0 + 2 * g)
                wB, wBr = w_slot(blk0 + 2 * g + 1)
                if g == 0:
                    dump("b", 7, wsl[0][:, 0:2048], [R("w", 0)])
                    dump("b", 16, wsl[1][:, 0:2048], [R("w", 1)])
                    dump("b", 17, wsl[2][:, 0:2048], [R("w", 2)])
                P.add("dve", lambda e: e.memset(Sf[:, :, :], 0.0), writes=[R("Sf", 0), R("Sf", 1)])
                for par in range(2):
                    P.add("dve", lambda e, par=par: e.memset(Sb2[par][:, :, :], 0.0), writes=[R("Sb", 0, par), R("Sb", 1, par)])
                for tb in range(4):
                    ts = slice(tb * 512, (tb + 1) * 512)
                    pb = [(PS[2], R("ps", 2)), (PS[3], R("ps", 3))]
                    for c in range(4):
                        tok = slice(tb * 512 + c * 128, tb * 512 + (c + 1) * 128)
                        ps, psr = ps_next([0, 1])
                        P.add("pe", lambda e, wA=wA, wB=wB, ps=ps, tok=tok, g=g: e.matmul(
                            ps[:, 0:256], a_aug[0:48, tok], wa2b[0:48, g * 256:(g + 1) * 256], start=True, stop=True),
                            reads=[R("a_aug", tb), R("wa2b")], writes=[psr])
                        P.add("act", lambda e, wA=wA, wB=wB, ps=ps: e.activation(etmp, ps[:, 0:256], AF.Exp, scale=-1.0),
                              reads=[psr], writes=[R("etmp")])
                        sp_ = spf[c % 2]
                        P.add("act", lambda e, wA=wA, wB=wB, sp_=sp_: e.activation(sp_, etmp, AF.Ln, bias=1.0),
                              reads=[R("etmp")], writes=[R("sp", c % 2)])
                        ps, psr = ps_next([0, 1])
                        for k in range(8):
                            P.add("pe", lambda e, wA=wA, wB=wB, ps=ps, k=k, tok=tok: e.matmul(
                                ps[:], hT[:, k, tok], wA[:, k * 1024 + 512:k * 1024 + 1024],
                                start=(k == 0), stop=(k == 7)), reads=[wAr, R("h", k, tb)], writes=[psr])
                        P.add("act", lambda e, wA=wA, wB=wB, ps=ps, c=c: e.activation(vtok[:, c, :], ps[:], AF.Copy),
                              reads=[psr], writes=[R("vtok", c)])
                        for hh in range(2):
                            P.add("pe", lambda e, wA=wA, wB=wB, hh=hh, c=c, sp_=sp_, pb=pb: e.matmul(
                                pb[hh][0][:, c * 128:(c + 1) * 128], sp_[:, hh * 128:(hh + 1) * 128], trinegf,
                                start=True, stop=True),
                                reads=[R("sp", c % 2), R("cstf")], writes=[pb[hh][1]])
                    for hh in range(2):
                        P.add("act", lambda e, wA=wA, wB=wB, hh=hh, pb=pb: e.activation(ebf[hh], pb[hh][0][:], AF.Exp),
                              reads=[pb[hh][1]], writes=[R("eb", hh)])
                        P.add("act", lambda e, wA=wA, wB=wB, hh=hh, pb=pb: e.activation(enbf[hh], pb[hh][0][:], AF.Exp, scale=-1.0),
                              reads=[pb[hh][1]], writes=[R("enb", hh)])
                    for jj in range(4):
                        ps, psr = ps_next([0, 1])
                        for k in range(8):
                            P.add("pe", lambda e, wA=wA, wB=wB, ps=ps, k=k, jj=jj, ts=ts: e.matmul(
                                ps[:], wB[:, k * 512 + jj * 128:k * 512 + (jj + 1) * 128], hT[:, k, ts],
                                start=(k == 0), stop=(k == 7)), reads=[wBr, R("h", k, tb)], writes=[psr])
                        bc = SM_BR + g * 4 + jj
                        P.add("act", lambda e, wA=wA, wB=wB, ps=ps, jj=jj, bc=bc: e.activation(
                            rg[:, jj, :], ps[:], AF.Silu, bias=sm[:, bc:bc + 1]),
                            reads=[psr, R("sm")], writes=[R("rg", jj)])
                    for hh in range(2):
                        ps, psr = ps_next([0, 1])
                        for k in range(8):
                            P.add("pe", lambda e, wA=wA, wB=wB, ps=ps, k=k, hh=hh, ts=ts: e.matmul(
                                ps[:], wA[:, k * 1024 + hh * 128:k * 1024 + (hh + 1) * 128], hT[:, k, ts],
                                start=(k == 0), stop=(k == 7)), reads=[wAr, R("h", k, tb)], writes=[psr])
                        P.add("dve", lambda e, wA=wA, wB=wB, ps=ps, hh=hh: e.scalar_tensor_tensor(
                            qd[:, hh, :], ps[:], 128.0 ** -0.5, ebf[hh], ALU.mult, ALU.mult),
                            reads=[psr, R("eb", hh)], writes=[R("qd", hh)])
                        ps, psr = ps_next([0, 1])
                        for k in range(8):
                            P.add("pe", lambda e, wA=wA, wB=wB, ps=ps, k=k, hh=hh, ts=ts: e.matmul(
                                ps[:], wA[:, k * 1024 + 256 + hh * 128:k * 1024 + 256 + (hh + 1) * 128], hT[:, k, ts],
                                start=(k == 0), stop=(k == 7)), reads=[wAr, R("h", k, tb)], writes=[psr])
                        P.add("dve", lambda e, wA=wA, wB=wB, ps=ps, hh=hh: e.tensor_tensor(ki[:, hh, :], ps[:], enbf[hh], ALU.mult),
                              reads=[psr, R("enb", hh)], writes=[R("ki", hh)])
                    for c in range(4):
                        pt, ptr = PS[4 + (c % 2)], R("ps", 4 + (c % 2))
                        ptb = pt[:].bitcast(BF16)
                        for hh in range(2):
                            P.add("pe", lambda e, wA=wA, wB=wB, ptb=ptb, c=c, hh=hh: e.transpose(
                                ptb[:, hh * 128:(hh + 1) * 128], ki[:, hh, c * 128:(c + 1) * 128], identb),
                                reads=[R("ki", hh), R("cstb")], writes=[ptr])
                        P.add("act", lambda e, wA=wA, wB=wB, ptb=ptb, c=c: e.activation(
                            ktok[:, c, :, :].rearrange("p h f -> p (h f)"), ptb[:, 0:256], AF.Copy),
                            reads=[ptr], writes=[R("ktok", c)])
                    po = [(PS[4 + i], R("ps", 4 + i)) for i in range(4)]
                    for c in range(4):
                        cs = slice(c * 128, (c + 1) * 128)
                        cg = tb * 4 + c
                        Sb_w = Sb2[cg % 2]
                        Sb_r = Sb2[(cg + 1) % 2]
                        for hh in range(2):
                            pss, pssr = PS[2 + hh], R("ps", 2 + hh)
                            sl = (c % 4) * 128
                            P.add("pe", lambda e, wA=wA, wB=wB, pss=pss, sl=sl, hh=hh, cs=cs: e.matmul(
                                pss[:, sl:sl + 128], ki[:, hh, cs], qd[:, hh, cs], start=True, stop=True),
                                reads=[R("ki", hh), R("qd", hh)], writes=[pssr])
                            sT = sTb[hh]
                            P.add("dve", lambda e, wA=wA, wB=wB, pss=pss, sl=sl, sT=sT: e.tensor_tensor(
                                sT, pss[:, sl:sl + 128], trib, ALU.mult),
                                reads=[pssr, R("cstb")], writes=[R("sT", hh)])
                        for hh in range(2):
                            psp, pspr = ps_next([0, 1])
                            P.add("pe", lambda e, wA=wA, wB=wB, psp=psp, c=c, hh=hh: e.matmul(
                                psp[:, 0:256], ktok[:, c, hh, :], vtok[:, c, hh * 256:(hh + 1) * 256],
                                start=True, stop=True),
                                reads=[R("ktok", c), R("vtok", c)], writes=[pspr])
                            st = stmp[hh]
                            P.add("dve", lambda e, wA=wA, wB=wB, psp=psp, st=st, hh=hh: e.tensor_tensor(
                                st, psp[:, 0:256], Sf[:, hh, :], ALU.add),
                                reads=[pspr, R("Sf", hh)], writes=[R("stmp", hh)])
                            ecol = ebf[hh][:, c * 128 + 127:c * 128 + 128]
                            P.add("dve", lambda e, wA=wA, wB=wB, st=st, hh=hh, ecol=ecol: e.tensor_scalar(
                                Sf[:, hh, :], st, ecol, None, ALU.mult),
                                reads=[R("stmp", hh), R("eb", hh)], writes=[R("Sf", hh)])
                            P.add("act", lambda e, wA=wA, wB=wB, st=st, hh=hh, ecol=ecol, Sb_w=Sb_w: e.activation(
                                Sb_w[:, hh, :], st, AF.Copy, scale=ecol),
                                reads=[R("stmp", hh), R("eb", hh)], writes=[R("Sb", hh, cg % 2)])
                        for hh in range(2):
                            sT = sTb[hh]
                            for jj in range(2):
                                pot, potr = po[hh * 2 + jj]
                                P.add("pe", lambda e, wA=wA, wB=wB, pot=pot, c=c, hh=hh, jj=jj, sT=sT, cs=cs: e.matmul(
                                    pot[:, cs], vtok[:, c, hh * 256 + jj * 128:hh * 256 + (jj + 1) * 128], sT,
                                    start=True, stop=False),
                                    reads=[R("vtok", c), R("sT", hh)], writes=[potr])
                                P.add("pe", lambda e, wA=wA, wB=wB, pot=pot, hh=hh, jj=jj, cs=cs, Sb_r=Sb_r: e.matmul(
                                    pot[:, cs], Sb_r[:, hh, jj * 128:(jj + 1) * 128], qd[:, hh, cs],
                                    start=False, stop=True),
                                    reads=[R("Sb", hh, (cg + 1) % 2), R("qd", hh)], writes=[potr])
                    if g == 0 and tb == 0:
                        dump("f", 3, Sf[:, :, :].rearrange("p h f -> p (h f)"), [R("Sf", 0), R("Sf", 1)])
                    for hh in range(2):
                        pst, pstr = ps_next([0, 1])
                        for jj in range(2):
                            pot, potr = po[hh * 2 + jj]
                            sq = sqb[jj]
                            P.add("act", lambda e, wA=wA, wB=wB, pot=pot, sq=sq: e.activation(sq, pot[:], AF.Square),
                                  reads=[potr], writes=[R("g_sq", jj)])
                            P.add("pe", lambda e, wA=wA, wB=wB, pst=pst, sq=sq, jj=jj: e.matmul(
                                pst[:], onesb[:], sq, start=(jj == 0), stop=(jj == 1)),
                                reads=[R("g_sq", jj), R("onesb")], writes=[pstr])
                        lnt = tmpf[0]
                        rstd = tmpf[1]
                        P.add("act", lambda e, wA=wA, wB=wB, pst=pst, lnt=lnt: e.activation(lnt, pst[:], AF.Ln, bias=EPS, scale=1.0 / 256),
                              reads=[pstr], writes=[R("g_ln")])
                        P.add("act", lambda e, wA=wA, wB=wB, lnt=lnt, rstd=rstd: e.activation(rstd, lnt, AF.Exp, scale=-0.5),
                              reads=[R("g_ln")], writes=[R("g_rstd")])
                        for jj in range(2):
                            pot, potr = po[hh * 2 + jj]
                            t2 = ebf[jj]
                            P.add("dve", lambda e, wA=wA, wB=wB, pot=pot, jj=jj, rstd=rstd, t2=t2: e.scalar_tensor_tensor(
                                t2, pot[:], sm[:, SM_GNG + jj:SM_GNG + jj + 1], rstd, ALU.mult, ALU.mult),
                                reads=[potr, R("g_rstd"), R("sm"), R("eb", jj)], writes=[R("eb", jj)])
                            P.add("dve", lambda e, wA=wA, wB=wB, hh=hh, jj=jj, t2=t2: e.tensor_tensor(
                                og[:, hh * 2 + jj, :], t2, rg[:, hh * 2 + jj, :], ALU.mult),
                                reads=[R("eb", jj), R("rg", hh * 2 + jj)], writes=[R("og", hh * 2 + jj)])
                    if g == 0 and tb == 0:
                        dump("b", 5, og[:, :, :].rearrange("p j t -> p (j t)"), [R("og", j) for j in range(4)])
                        dump("f", 3, Sf[:, :, :].rearrange("p h f -> p (h f)"), [R("Sf", 0), R("Sf", 1)])
                    for dch in range(8):
                        ps, psr = ps_next([0, 1])
                        for j4 in range(4):
                            P.add("pe", lambda e, wA=wA, wB=wB, ps=ps, j4=j4, dch=dch: e.matmul(
                                ps[:], wB[:, 4096 + j4 * 1024 + dch * 128:4096 + j4 * 1024 + (dch + 1) * 128],
                                og[:, j4, :], start=(j4 == 0), stop=(j4 == 3)),
                                reads=[wBr, R("og", j4)], writes=[psr])
                        P.add("dve", lambda e, wA=wA, wB=wB, ps=ps, dch=dch, ts=ts: e.scalar_tensor_tensor(
                            xT[:, dch, ts], ps[:], gate[:, dch:dch + 1], xT[:, dch, ts], ALU.mult, ALU.add),
                            reads=[psr, R("modv"), R("x", dch, tb)], writes=[R("x", dch, tb)])
                w_issue()
                w_issue()

        def att():
            blk0 = BIDX[("att", 0)]
            P.fence()
            gate = modv[:, 48 + 16:48 + 24]
            o = 0
            qT = U[:, o:o + 4096].rearrange("p (h t) -> p h t", h=2); o += 4096
            kz = U[:, o:o + 8192].rearrange("p (h i t) -> p h i t", h=2, i=2); o += 8192
            vaug = U[:, o:o + 4096].rearrange("p (c h f) -> p c h f", c=16, h=2); o += 4096
            oT = U[:, o:o + 1024].rearrange("p (h t) -> p h t", h=2); o += 1024
            ocs = o
            Cb = Ub(o, 2048); o += 2048
            Sb_ = Ub(o, 2048); o += 2048
            pTb = [Ub(o + i * 512, 512) for i in range(4)]; o += 2048
            oc = o
            qnb = [Ub(o + i * 512, 512) for i in range(2)]; o += 1024
            t1f = [Uf(o + i * 1024, 512) for i in range(2)]; o += 2048
            t2f = [Uf(o + i * 1024, 512) for i in range(2)]; o += 2048
            o = oc
            c_t1 = Uf(o, 512); o += 1024
            c_r2 = Uf(o, 512); o += 1024
            c_t2 = Uf(o, 512); o += 1024
            c_d = Uf(o, 512); o += 1024
            c_ln = Uf(o, 512); o += 1024
            c_sq = Ub(o, 512); o += 512
            assert o <= UEND, o
            angf = Uf(0, 2048)
            kf = Uf(4096, 2048)
            ki32 = U[:, 8192:12288].bitcast(I32)

            lrow = lrow_t[0:1, :]
            lsum = lamv[0:1, 2:8]
            P.add("sp", lambda e: e.dma_start(out=lrow, in_=rw_d), writes=[R("lrow")], dma=True)
            P.add("dve", lambda e: e.tensor_tensor(lrow[0:1, 0:64], lrow[0:1, 0:64], lrow[0:1, 64:128], ALU.mult),
                  reads=[R("lrow")], writes=[R("lrow")])
            P.add("dve", lambda e: e.tensor_tensor(lrow[0:1, 128:192], lrow[0:1, 128:192], lrow[0:1, 192:256], ALU.mult),
                  reads=[R("lrow")], writes=[R("lrow")])
            P.add("dve", lambda e: e.tensor_reduce(
                lsum[0:1, 0:2], lrow.rearrange("p (a b) -> p a b", a=2)[:, :, 0:64], AX.X, ALU.add),
                reads=[R("lrow")], writes=[R("lamv")])
            P.add("act", lambda e: e.activation(lsum[0:1, 2:4], lsum[0:1, 0:2], AF.Exp),
                  reads=[R("lamv")], writes=[R("lamv")])
            P.add("dve", lambda e: e.scalar_tensor_tensor(
                lsum[0:1, 4:5], lsum[0:1, 3:4], -LAMBDA_INIT, lsum[0:1, 2:3], ALU.add, ALU.subtract),
                reads=[R("lamv")], writes=[R("lamv")])
            ps, psr = ps_next([0, 1])
            P.add("pe", lambda e, ps=ps: e.matmul(ps[:, 0:1], onesf[0:1, :], lsum[0:1, 4:5], start=True, stop=True),
                  reads=[R("lamv"), R("onesf")], writes=[psr])
            P.add("dve", lambda e, ps=ps: e.tensor_copy(lamv[:, 0:1], ps[:, 0:1]), reads=[psr, R("lamv")], writes=[R("lamv")])
            P.add("dve", lambda e: e.tensor_scalar(lamv[:, 1:2], sm[:, SM_SLG:SM_SLG + 1], 1.0 - LAMBDA_INIT, None, ALU.mult),
                  reads=[R("sm"), R("lamv")], writes=[R("lamv")])

            posi1 = U[0:1, 12288:16384].bitcast(I32)
            posf1 = U[0:1, ocs:ocs + 4096].bitcast(F32)
            P.add("sp", lambda e: e.dma_start(out=posi1.bitcast(F32), in_=pos_d), writes=[R("posi")], dma=True)
            P.add("dve", lambda e: e.tensor_copy(posf1, posi1), reads=[R("posi")], writes=[R("posf")])
            for tb in range(4):
                ts = slice(tb * 512, (tb + 1) * 512)
                ps, psr = ps_next([0, 1])
                P.add("pe", lambda e, ps=ps, ts=ts: e.matmul(ps[:], onesf[0:1, :], posf1[0:1, ts], start=True, stop=True),
                      reads=[R("posf"), R("onesf")], writes=[psr])
                P.add("dve", lambda e, ps=ps, ts=ts: e.tensor_scalar(angf[:, ts], ps[:], sm[:, SM_INVF:SM_INVF + 1], None, ALU.mult),
                      reads=[psr, R("sm")], writes=[R("angf")])
            dump('f', 1, angf, [R('angf')])
            dump('f', 6, posi1.bitcast(F32), [R('posi')], npart=1)
            dump('f', 7, posf1, [R('posf')], npart=1)
            for which, dst, off in ((0, Sb_, 0.0), (1, Cb, float(np.pi / 2))):
                if which == 1:
                    P.add("dve", lambda e, off=off: e.tensor_scalar(angf, angf, off, None, ALU.add),
                          reads=[R("angf")], writes=[R("angf")])
                P.add("dve", lambda e: e.tensor_scalar(kf, angf, 1.0 / TWO_PI, None, ALU.mult),
                      reads=[R("angf")], writes=[R("kf")])
                P.add("dve", lambda e: e.tensor_copy(ki32, kf), reads=[R("kf")], writes=[R("ki32")])
                P.add("dve", lambda e: e.tensor_copy(kf, ki32), reads=[R("ki32")], writes=[R("kf")])
                P.add("dve", lambda e: e.scalar_tensor_tensor(kf, kf, -TWO_PI, angf, ALU.mult, ALU.add),
                      reads=[R("kf"), R("angf")], writes=[R("kf")])
                P.add("dve", lambda e: e.tensor_scalar(kf, kf, float(np.pi), float(-np.pi), ALU.min, ALU.max),
                      reads=[R("kf")], writes=[R("kf")])
                P.add("act", lambda e, dst=dst: e.activation(dst, kf, AF.Sin), reads=[R("kf")], writes=[R("rope", which)])
                dump('f', 2 + which, kf, [R("kf")])
            P.add("dve", lambda e: e.memset(kz[64:128, :, 0, :], 0.0), writes=[R("kzz"), R("kf"), R("ki32"), R("angf"), R("posi")])
            P.add("dve", lambda e: e.memset(kz[0:64, :, 1, :], 0.0), writes=[R("kzz"), R("kf"), R("ki32"), R("angf"), R("posi")])

            ALV = int(os.environ.get('ATT_LEVEL', '9'))

            def proj_tmp_res():
                return [R("qn", 0), R("qn", 1), R("t1", 0), R("t1", 1), R("t2", 0), R("t2", 1)]

            def comb_tmp_res():
                return [R("c_t1"), R("c_r2"), R("c_t2"), R("c_d"), R("c_ln"), R("c_sq")]

            for pr in range(4 if ALV > 0 else 0):
                wt, wr = w_slot(blk0 + pr)
                groups = [(nm, coff, hh, tb) for (nm, coff) in (("qT", 0), ("kT", 256)) for hh in range(2) for tb in range(4)]
                gstate = {}

                def stageA(gx):
                    nm, coff, hh, tb = groups[gx]
                    ts = slice(tb * 512, (tb + 1) * 512)
                    ps, psr = PS[gx % 2], R("ps", gx % 2)
                    for k in range(8):
                        P.add("pe", lambda e, wt=wt, ps=ps, k=k, hh=hh, ts=ts, coff=coff: e.matmul(
                            ps[:], wt[:, k * 768 + coff + hh * 128:k * 768 + coff + (hh + 1) * 128], hT[:, k, ts],
                            start=(k == 0), stop=(k == 7)), reads=[wr, R("h", k, tb)], writes=[psr])
                    qn = qnb[gx % 2]
                    P.add("act", lambda e, ps=ps, qn=qn: e.activation(qn, ps[:], AF.Copy),
                          reads=[psr], writes=[R("qn", gx % 2)])

                def stageB(gx):
                    nm, coff, hh, tb = groups[gx]
                    ts = slice(tb * 512, (tb + 1) * 512)
                    i2 = gx % 2
                    ps, psr = PS[gx % 2], R("ps", gx % 2)
                    qn = qnb[i2]
                    ps2, ps2r = PS[2 + i2], R("ps", 2 + i2)
                    P.add("pe", lambda e, ps2=ps2, qn=qn: e.matmul(ps2[:], permb, qn, start=True, stop=True),
                          reads=[R("qn", i2), R("cstb")], writes=[ps2r])
                    t1 = t1f[i2]
                    t2 = t2f[i2]
                    P.add("dve", lambda e, ps=ps, t1=t1, ts=ts: e.tensor_tensor(t1, ps[:], Cb[:, ts], ALU.mult),
                          reads=[psr, R("rope", 1)], writes=[R("t1", i2)])
                    P.add("dve", lambda e, ps2=ps2, t2=t2, ts=ts: e.tensor_tensor(t2, ps2[:], Sb_[:, ts], ALU.mult),
                          reads=[ps2r, R("rope", 0)], writes=[R("t2", i2)])
                    if nm == "qT":
                        P.add("pool", lambda e, hh=hh, ts=ts, t1=t1, t2=t2: e.tensor_tensor(
                            qT[:, hh, ts], t1, t2, ALU.add),
                            reads=[R("t1", i2), R("t2", i2)], writes=[R("qT", hh, tb)])
                    else:
                        for i_ in range(2):
                            prt_ = slice(64 * i_, 64 * i_ + 64)
                            P.add("pool", lambda e, hh=hh, ts=ts, t1=t1, t2=t2, i_=i_, prt_=prt_: e.tensor_tensor(
                                kz[prt_, hh, i_, ts], t1[prt_, :], t2[prt_, :], ALU.add),
                                reads=[R("t1", i2), R("t2", i2), R("kzz")], writes=[R("kT", hh, tb)])

                for gx in range(len(groups) + 1):
                    if gx < len(groups):
                        stageA(gx)
                    if gx >= 1:
                        stageB(gx - 1)
                for tt in range(16 if not os.environ.get("SKIPV") else 0):
                    tok = slice(tt * 128, (tt + 1) * 128)
                    ps, psr = ps_next([0, 1])
                    for k in range(8):
                        P.add("pe", lambda e, wt=wt, ps=ps, k=k, tok=tok: e.matmul(
                            ps[:, 0:256], hT[:, k, tok], wt[:, k * 768 + 512:k * 768 + 768],
                            start=(k == 0), stop=(k == 7)), reads=[wr, R("h", k, tt // 4)], writes=[psr])
                    P.add("dve", lambda e, wt=wt, ps=ps, tt=tt: e.tensor_copy(
                        vaug[:, tt, :, :], ps[:, 0:256].rearrange("p (h f) -> p h f", h=2)),
                        reads=[psr], writes=[R("vaug", tt)])
                P.alias_barrier(proj_tmp_res(), comb_tmp_res())
                its = []
                gi = 0
                for j in range(4):
                    for hh in range(2):
                        for i in range(2):
                            for kt in range(4 * j + 4):
                                its.append((hh, j, i, kt, gi))
                            gi += 1
                LA = 2
                SB = [1, 2, 3]
                deferred = []

                def emit_qk(idx):
                    hh, j, i, kt, g_ = its[idx]
                    r = kt - 4 * j
                    n0 = 128 * r if r > 0 else 0
                    ps, psr = PS[SB[idx % 3]], R("ps", SB[idx % 3])
                    P.add("pe", lambda e, ps=ps, i=i, hh=hh, kt=kt, j=j, n0=n0, r=r: e.matmul(
                        ps[:, n0:512], kz[:, hh, i, kt * 128:(kt + 1) * 128],
                        qT[:, hh, j * 512 + n0:(j + 1) * 512], start=True, stop=(r < 0)),
                        reads=[R("kT", hh, kt // 4), R("qT", hh, j), R("kzz")], writes=[psr])
                    if r >= 0:
                        P.add("pe", lambda e, ps=ps, n0=n0: e.matmul(
                            ps[:, n0:n0 + 128], identb, negmb, start=False, stop=True),
                            reads=[R("cstb")], writes=[psr])
                    pT = pTb[idx % 3]
                    pTr = R("pT", idx % 3)
                    P.add("act", lambda e, ps=ps, pT=pT, n0=n0: e.activation(
                        pT[:, n0:512], ps[:, n0:512], AF.Exp, scale=0.125),
                        reads=[psr], writes=[pTr])

                def emit_pv(idx):
                    hh, j, i, kt, g_ = its[idx]
                    r = kt - 4 * j
                    n0 = 128 * r if r > 0 else 0
                    pT = pTb[idx % 3]
                    pTr = R("pT", idx % 3)
                    ao, aor = PS[4 + 2 * i], R("ps", 4 + 2 * i)
                    al, alr = PS[5 + 2 * i], R("ps", 5 + 2 * i)
                    last = (kt == 4 * j + 3)
                    P.add("pe", lambda e, ao=ao, pT=pT, kt=kt, hh=hh, n0=n0, last=last: e.matmul(
                        ao[:, n0:512], vaug[:, kt, hh, :], pT[:, n0:512], start=(kt == 0), stop=last),
                        reads=[pTr, R("vaug", kt)], writes=[aor])
                    P.add("pe", lambda e, al=al, pT=pT, kt=kt, n0=n0, last=last: e.matmul(
                        al[:, n0:512], onesb[:], pT[:, n0:512], start=(kt == 0), stop=last),
                        reads=[pTr, R("onesb")], writes=[alr])
                    if not last:
                        return
                    qs = slice(j * 512, (j + 1) * 512)
                    if i == 0:
                        P.add("dve", lambda e, al=al: e.reciprocal(c_r2, al[:]), reads=[alr], writes=[R("c_r2")])
                        P.add("dve", lambda e, ao=ao: e.tensor_tensor(c_t1, ao[:], c_r2, ALU.mult),
                              reads=[aor, R("c_r2")], writes=[R("c_t1")])
                        return
                    P.add("dve", lambda e, al=al: e.reciprocal(c_r2, al[:]), reads=[alr], writes=[R("c_r2")])
                    P.add("dve", lambda e, ao=ao: e.scalar_tensor_tensor(c_t2, ao[:], lamv[:, 0:1], c_r2, ALU.mult, ALU.mult),
                          reads=[aor, R("c_r2"), R("lamv")], writes=[R("c_t2")])
                    P.add("pool", lambda e: e.tensor_tensor(c_d, c_t1, c_t2, ALU.add),
                          reads=[R("c_t1"), R("c_t2")], writes=[R("c_d")])

                    def epi1a():
                        P.add("act", lambda e: e.activation(c_sq, c_d, AF.Square), reads=[R("c_d")], writes=[R("c_sq")])

                    def epi1b():
                        pss, pssr = PS[0], R("ps", 0)
                        P.add("pe", lambda e, pss=pss: e.matmul(pss[:], onesb[:], c_sq, start=True, stop=True),
                              reads=[R("c_sq"), R("onesb")], writes=[pssr])

                    def epi1c(hh=hh):
                        pss, pssr = PS[0], R("ps", 0)
                        P.add("act", lambda e, pss=pss: e.activation(c_ln, pss[:], AF.Ln, bias=EPS, scale=1.0 / 128),
                              reads=[pssr], writes=[R("c_ln")])
                        P.add("act", lambda e: e.activation(c_ln, c_ln, AF.Exp, scale=-0.5),
                              reads=[R("c_ln")], writes=[R("c_ln")])
                        P.add("dve", lambda e, hh=hh: e.scalar_tensor_tensor(
                            oT[:, hh, :], c_d, lamv[:, 1:2], c_ln, ALU.mult, ALU.mult),
                            reads=[R("c_d"), R("c_ln"), R("lamv")], writes=[R("oT", hh)])

                    def epi2(j=j, qs=qs):
                        for dch in range(8):
                            psw, pswr = PS[0], R("ps", 0)
                            for h2 in range(2):
                                P.add("pe", lambda e, wt=wt, psw=psw, h2=h2, dch=dch: e.matmul(
                                    psw[:], wt[:, 6144 + h2 * 1024 + dch * 128:6144 + h2 * 1024 + (dch + 1) * 128],
                                    oT[:, h2, :], start=(h2 == 0), stop=(h2 == 1)),
                                    reads=[wr, R("oT", h2)], writes=[pswr])
                            P.add("dve", lambda e, psw=psw, dch=dch, qs=qs: e.scalar_tensor_tensor(
                                xT[:, dch, qs], psw[:], gate[:, dch:dch + 1], xT[:, dch, qs], ALU.mult, ALU.add),
                                reads=[pswr, R("modv"), R("x", dch, j)], writes=[R("x", dch, j)])

                    step_now = idx + LA
                    deferred.append((step_now + 10, epi1a))
                    deferred.append((step_now + 12, epi1b))
                    deferred.append((step_now + 14, epi1c))
                    if hh == 1:
                        deferred.append((step_now + 19, epi2))

                def run_deferred(step, flush=False):
                    while deferred and (flush or deferred[0][0] <= step):
                        deferred.pop(0)[1]()

                for step in range(len(its) + LA):
                    if step < len(its):
                        emit_qk(step)
                    if step >= LA:
                        idx_ = step - LA
                        hh_, j_, i_, kt_, g_ = its[idx_]
                        if i_ == 1 and kt_ == 4 * j_ + 3:
                            run_deferred(step, flush=True)
                        emit_pv(idx_)
                    run_deferred(step)
                run_deferred(0, flush=True)
                P.alias_barrier(comb_tmp_res(), proj_tmp_res())
                w_issue()

        P.fence()
        for nb_ in range(6):
            adaln_block(0, nb_)
        adaln_finish(0)
        norm(dv[:, 0:8], modv[:, 0:8])
        gla()
        if stage >= 2:
            norm(dv[:, 8:16], modv[:, 24:32])
            mlp(0)
        if stage >= 3:
            adaln_finish(1)
            norm(dv[:, 0:8], modv[:, 48:56], fence=False)
            att()
        if stage >= 4:
            norm(dv[:, 8:16], modv[:, 48 + 24:48 + 32])
            mlp(1)
            norm(sm[:, SM_FG:SM_FG + 8], None, final=True, fence=False)
        ov = out_d.rearrange("(k p) t -> p k t", p=128)
        outr = []
        for k in range(8):
            rr = R("out", k)
            outr.append(rr)
            P.add("sp", lambda e, k=k: e.dma_start(out=ov[:, k, :], in_=xT[:, k, :]),
                  reads=[R("x", k, tb) for tb in range(4)], writes=[rr], dma=True)
        P.add("sp", lambda e: None, reads=outr, noinst=True)
        P.emit(sems, dsems)
    return nc


def _kmaj(Wc):
    K = Wc.shape[0] // 128
    return np.ascontiguousarray(Wc.reshape(K, 128, -1).transpose(1, 0, 2)).reshape(128, -1)


def _prep_shared(inp):
    f = np.float32
    blocks = np.zeros((NBLK, 128, 8192), f)
    ada_w = np.asarray(inp["ada_w"], f)
    mlp_w1 = np.asarray(inp["mlp_w1"], f)
    mlp_w2 = np.asarray(inp["mlp_w2"], f)
    gw = np.asarray(inp["gla_w_in"], f)[0]
    gwo = np.asarray(inp["gla_w_o"], f)[0]
    dw = np.asarray(inp["diff_w_in"], f)[0]
    dwo = np.asarray(inp["diff_w_o"], f)[0]
    for bi, name in enumerate(BLOCKS):
        kind = name[0]
        if kind == "ada":
            _, l, nb_ = name
            blocks[bi] = _kmaj(ada_w[l][:, nb_ * 1024:(nb_ + 1) * 1024])
        elif kind == "glaA":
            g = name[1]
            cols = np.concatenate([np.arange(2 * g * 128, 2 * g * 128 + 256),
                                   512 + np.arange(2 * g * 128, 2 * g * 128 + 256),
                                   1024 + np.arange(2 * g * 256, 2 * g * 256 + 512)])
            blocks[bi] = _kmaj(gw[:, cols])
        elif kind == "glaB":
            g = name[1]
            rc = 2048 + np.arange(2 * g * 256, 2 * g * 256 + 512)
            blocks[bi][:, 0:4096] = _kmaj(gw[:, rc])
            blocks[bi][:, 4096:8192] = _kmaj(gwo[g * 512:(g + 1) * 512, :])
        elif kind == "w1":
            _, l, g = name
            blocks[bi] = _kmaj(mlp_w1[l][:, g * 1024:(g + 1) * 1024])
        elif kind == "w2":
            _, l, g = name
            blocks[bi] = _kmaj(mlp_w2[l][g * 1024:(g + 1) * 1024, :])
        elif kind == "att":
            pr = name[1]
            cols = np.concatenate([np.arange(pr * 256, pr * 256 + 256),
                                   1024 + np.arange(pr * 256, pr * 256 + 256),
                                   2048 + np.arange(pr * 256, pr * 256 + 256)])
            blocks[bi][:, 0:6144] = _kmaj(dw[:, cols])
            blocks[bi][:, 6144:8192] = _kmaj(dwo[pr * 256:(pr + 1) * 256, :])

    def pk(v):
        return np.asarray(v, f).reshape(-1, 128).T

    sm = np.zeros((128, NSM), f)
    ng = np.asarray(inp["norm_g"], f)
    for l in range(2):
        for j in range(2):
            sm[:, SM_NG + (2 * l + j) * 8:SM_NG + (2 * l + j) * 8 + 8] = pk(ng[l, j])
    sm[:, SM_FG:SM_FG + 8] = pk(inp["final_g"])
    ab = np.asarray(inp["ada_b"], f)
    for l in range(2):
        sm[:, SM_AB + l * 48:SM_AB + (l + 1) * 48] = pk(ab[l])
    sm[:, SM_BR:SM_BR + 8] = pk(np.asarray(inp["gla_b_r"], f)[0])
    sm[:, SM_GNG:SM_GNG + 2] = pk(np.asarray(inp["gla_norm_g"], f)[0])
    sm[:, SM_SLG:SM_SLG + 1] = pk(np.asarray(inp["diff_subln_g"], f)[0])
    inv_freq = (np.float32(500000.0) ** (-(np.arange(0, 16, 2, dtype=np.float32)) / np.float32(16))).astype(f)
    invf = np.zeros(128, f)
    for p in range(128):
        d = p % 64
        if d < 16:
            invf[p] = inv_freq[d % 8]
    sm[:, SM_INVF] = invf

    rw = np.asarray(inp["diff_lambda"], f)[0].reshape(1, 256)
    wa = np.zeros((128, 8, 48), f)
    wa[:, :, 32:48] = gw[:, 3072:3088].reshape(8, 128, 16).transpose(1, 0, 2)
    wa = wa.reshape(128, 8 * 48)
    wa2 = np.zeros((48, 512), f)
    wa2[0] = np.asarray(inp["gla_b_a"], f)[0]
    wa2[32:48] = np.asarray(inp["gla_w_a2"], f)[0]
    cst = np.zeros((128, 512), f)
    cst[:, 0:128] = np.eye(128, dtype=f)
    tri = (np.arange(128)[None, :] >= np.arange(128)[:, None]).astype(f)
    cst[:, 128:256] = tri
    cst[:, 256:384] = tri * f(-1.0 / 16.0)
    perm = np.zeros((128, 128), f)
    for b_ in range(2):
        base = 64 * b_
        for d in range(8):
            perm[base + d + 8, base + d] = -1.0
        for d in range(8, 16):
            perm[base + d - 8, base + d] = 1.0
    cst[:, 384:512] = perm
    return dict(wblk=blocks, sm=sm, rw=rw, wa=wa, wa2=wa2, cst=cst)


_NC_CACHE = {}


def kernel(x, c, positions, ada_w, ada_b, norm_g, mlp_w1, mlp_w2,
           gla_w_in, gla_w_a2, gla_b_a, gla_b_r, gla_norm_g, gla_w_o,
           diff_w_in, diff_lambda, diff_subln_g, diff_w_o, final_g, _stage=4, _debug=False, _ncores=NB):
    inp = dict(ada_w=ada_w, ada_b=ada_b, norm_g=norm_g, mlp_w1=mlp_w1, mlp_w2=mlp_w2,
               gla_w_in=gla_w_in, gla_w_a2=gla_w_a2, gla_b_a=gla_b_a, gla_b_r=gla_b_r,
               gla_norm_g=gla_norm_g, gla_w_o=gla_w_o, diff_w_in=diff_w_in, diff_lambda=diff_lambda,
               diff_subln_g=diff_subln_g, diff_w_o=diff_w_o, final_g=final_g)
    sh = _prep_shared(inp)
    x = np.asarray(x, np.float32)
    c = np.asarray(c, np.float32)
    positions = np.asarray(positions, np.int32)
    in_maps = []
    for b in range(NB):
        sm = sh["sm"].copy()
        sm[:, SM_C:SM_C + 8] = c[b].reshape(8, 128).T
        in_maps.append(dict(xT=np.ascontiguousarray(x[b].T), pos=np.ascontiguousarray(positions[b:b + 1]).view(np.float32),
                            sm=sm, rw=sh["rw"], wa=sh["wa"], wa2=sh["wa2"], cst=sh["cst"], wblk=sh["wblk"]))
    key = (_stage, _debug)
    if key not in _NC_CACHE:
        _NC_CACHE[key] = build_nc(_stage, _debug)
    nc = _NC_CACHE[key]
    res = run_bass_kernel_spmd(nc, in_maps[:_ncores], core_ids=list(range(_ncores)))
    out = np.stack([np.ascontiguousarray(r["outT"].T) for r in res.results], axis=0)
    if _debug:
        return out.astype(np.float32), res.results[0]["dbgf"], res.results[0]["dbgb"]
    return out.astype(np.float32)
```

```python
import contextlib
import os
import math
import numpy as np
import concourse.bass as bass
import concourse.mybir as mybir
from concourse.bass_utils import run_bass_kernel_spmd

F32 = mybir.dt.float32
BF16 = mybir.dt.bfloat16
I32 = mybir.dt.int32
AF = mybir.ActivationFunctionType
ALU = mybir.AluOpType
AX = mybir.AxisListType

D = 1024
T = 2048
NB = 8
DFF = 4096
EPS = 1e-6
LAMBDA_INIT = 0.8 - 0.6 * math.exp(-0.3 * 1)
NBLK = 36
ADA1_AFTER_GROUP = [2, 2, 1, 1]


def _block_order():
    order = [("ada", 0, n) for n in range(6)]
    for g in range(2):
        order += [("glaA", g), ("glaB", g)]
    n1 = 0
    for g in range(4):
        order += [("w1", 0, g), ("w2", 0, g)]
        for _ in range(ADA1_AFTER_GROUP[g]):
            order.append(("ada", 1, n1)); n1 += 1
    order += [("att", p) for p in range(4)]
    for g in range(4):
        order += [("w1", 1, g), ("w2", 1, g)]
    assert len(order) == NBLK and n1 == 6
    return order


BLOCKS = _block_order()
BIDX = {b: i for i, b in enumerate(BLOCKS)}
NSLOT = 3
TWO_PI = float(2.0 * np.pi)

SM_NG = 0
SM_FG = 32
SM_AB = 40
SM_BR = 136
SM_GNG = 144
SM_SLG = 146
SM_INVF = 147
SM_C = 148
NSM = 160


class Res:
    __slots__ = ("w", "r", "excl")

    def __init__(self):
        self.w = None
        self.r = []
        self.excl = False


class Op:
    __slots__ = ("eng", "fn", "deps", "eidx", "signal", "seq", "dma", "dsem", "dval")


class Prog:
    NDMA = 24

    def __init__(self, nc, same_engine_sync=True):
        self.nc = nc
        self.engs = {"pe": nc.tensor, "act": nc.scalar, "dve": nc.vector,
                     "pool": nc.gpsimd, "sp": nc.sync}
        self.ops = []
        self.ecount = {k: 0 for k in self.engs}
        self.known = {k: {} for k in self.engs}
        self.kdma = {k: {} for k in self.engs}
        self.ndma = {"sp": 0, "pool": 0}
        self.dma_ops = {"sp": [], "pool": []}
        self.dbase = {"sp": 0, "pool": self.NDMA // 2}
        self.same_engine_sync = same_engine_sync
        self.res = {}
        self.last = {}

    def R(self, *key):
        r = self.res.get(key)
        if r is None:
            r = Res()
            r.excl = (key[0] == "ps")
            self.res[key] = r
        return r

    def alias_barrier(self, olds, news):
        pend = []
        for o in olds:
            if o.w is not None:
                pend.append(o.w)
            pend.extend(o.r)
        for n in news:
            n.r.extend(pend)

    def fence(self):
        last = dict(self.last)
        for eng in self.engs:
            self.add(eng, lambda e: None, extra=[o for k, o in last.items() if k != eng], noinst=True)

    def add(self, eng, fn, reads=(), writes=(), dma=False, extra=(), noinst=False):
        op = Op()
        op.eng = eng
        op.fn = fn
        op.dma = dma
        op.signal = False
        op.seq = None
        op.eidx = self.ecount[eng]
        self.ecount[eng] += 1
        if any(r.excl for r in reads):
            writes = list(writes) + [r for r in reads if r.excl]
            reads = [r for r in reads if not r.excl]
        deps = list(extra)
        for r in reads:
            if r.w is not None:
                deps.append(r.w)
        for r in writes:
            if r.w is not None:
                deps.append(r.w)
            deps.extend(r.r)
        if dma:
            i = self.ndma[eng]
            self.ndma[eng] += 1
            h = self.NDMA // 2
            op.dsem = self.dbase[eng] + i % h
            op.dval = 16 * (i // h + 1)
            if i >= h:
                deps.append(self.dma_ops[eng][i - h])
            self.dma_ops[eng].append(op)
        best = {}
        for d in deps:
            if d is op:
                continue
            if d.dma:
                if self.kdma[eng].get(d.dsem, 0) >= d.dval:
                    continue
                key = ("dma", d.dsem)
                if key not in best or best[key].dval < d.dval:
                    best[key] = d
            else:
                if d.eng == eng and (eng == "pe" or not self.same_engine_sync):
                    continue
                if self.known[eng].get(d.eng, -1) >= d.eidx:
                    continue
                key = d.eng
                if key not in best or best[key].eidx < d.eidx:
                    best[key] = d
        final = []
        for key, d in best.items():
            final.append(d)
            if d.dma:
                self.kdma[eng][d.dsem] = d.dval
            else:
                d.signal = True
                self.known[eng][d.eng] = d.eidx
        op.deps = final
        for r in reads:
            r.r.append(op)
        for r in writes:
            r.w = op
            r.r = []
        self.ops.append(op)
        if not dma and not noinst:
            self.last[eng] = op
        return op

    def emit(self, sems, dsems):
        cnt = {k: 0 for k in self.engs}
        for op in self.ops:
            if not op.dma and op.signal:
                cnt[op.eng] += 1
                op.seq = cnt[op.eng]
        for op in self.ops:
            e = self.engs[op.eng]
            for d in op.deps:
                if d.dma:
                    e.wait_ge(dsems[d.dsem], d.dval)
                else:
                    e.wait_ge(sems[d.eng], d.seq)
            ins = op.fn(e)
            if ins is None:
                continue
            if op.dma:
                ins.then_inc(dsems[op.dsem], 16)
            elif op.signal:
                ins.then_inc(sems[op.eng], 1)
        return cnt


def build_nc(stage=4, debug=False):
    nc = bass.Bass("TRN2", target_bir_lowering=False)
    xT_d = nc.dram_tensor("xT", [D, T], F32, kind="ExternalInput").ap()
    pos_d = nc.dram_tensor("pos", [1, T], F32, kind="ExternalInput").ap()
    sm_d = nc.dram_tensor("sm", [128, NSM], F32, kind="ExternalInput").ap()
    rw_d = nc.dram_tensor("rw", [1, 256], F32, kind="ExternalInput").ap()
    wa_d = nc.dram_tensor("wa", [128, 8 * 48], F32, kind="ExternalInput").ap()
    wa2_d = nc.dram_tensor("wa2", [48, 512], F32, kind="ExternalInput").ap()
    cst_d = nc.dram_tensor("cst", [128, 512], F32, kind="ExternalInput").ap()
    wblk_d = nc.dram_tensor("wblk", [NBLK, 128, 8192], F32, kind="ExternalInput").ap()
    out_d = nc.dram_tensor("outT", [D, T], F32, kind="ExternalOutput").ap()
    if debug:
        dbgf_d = nc.dram_tensor("dbgf", [16, 128, 2048], F32, kind="ExternalOutput").ap()
        dbgb_d = nc.dram_tensor("dbgb", [24, 128, 2048], BF16, kind="ExternalOutput").ap()

    es = contextlib.ExitStack()
    with es:
        def sb(name, shape, dt):
            return es.enter_context(nc.sbuf_tensor(name, shape, dt))

        xT = sb("xTs", [128, 8, T], F32)
        hT = sb("hTs", [128, 8, T], BF16)
        wsl = [sb("wsl%d" % i, [128, 8192], BF16) for i in range(NSLOT)]
        cstf = sb("cstf", [128, 512], F32)
        cstb = sb("cstb", [128, 512], BF16)
        onesb = sb("onesb", [128, 128], BF16)
        onesf = sb("onesf", [128, 128], F32)
        sm = sb("sms", [128, NSM], F32)
        modv = sb("modv", [128, 96], F32)
        dv = sb("dvs", [128, 64], F32)
        cact = sb("cact", [128, 8], BF16)
        lamv = sb("lamv", [128, 8], F32)
        lrow_t = sb("lrow", [1, 256], F32)
        U = sb("U", [128, 29400], BF16)
        PS = [es.enter_context(nc.psum_tensor("ps%d" % i, [128, 512], F32)) for i in range(8)]
        sems = {k: es.enter_context(nc.semaphore("s_" + k)) for k in ["pe", "act", "dve", "pool", "sp"]}
        dsems = [es.enter_context(nc.semaphore("d%d" % i)) for i in range(Prog.NDMA)]

        P = Prog(nc)
        R = P.R

        identf = cstf[:, 0:128]
        trinegf = cstf[:, 256:384]
        identb = cstb[:, 0:128]
        trib = cstb[:, 128:256]
        permb = cstb[:, 384:512]
        negmb = cstb[:, 256:384]

        def Ub(off, n):
            return U[:, off:off + n]

        def Uf(off, n):
            return U[:, off:off + 2 * n].bitcast(F32)

        def dump(kind, idx, ap, reads, npart=128):
            if not debug:
                return
            n = ap.shape[-1]
            dst = (dbgf_d if kind == "f" else dbgb_d)[idx, 0:npart, 0:n]
            P.add("sp", lambda e: e.dma_start(out=dst, in_=ap), reads=reads, writes=[R("dbg", kind, idx)], dma=True)

        wstate = {"next": 0}

        def w_issue():
            i = wstate["next"]
            if i >= NBLK:
                return
            wstate["next"] = i + 1
            s = i % NSLOT
            P.add("pool", lambda e, i=i, s=s: e.dma_start(out=wsl[s][:], in_=wblk_d[i]),
                  writes=[R("w", s)], dma=True)

        def w_slot(i):
            return wsl[i % NSLOT], R("w", i % NSLOT)

        prot = {"i": 0}

        def ps_next(banks):
            b = banks[prot["i"] % len(banks)]
            prot["i"] += 1
            return PS[b], R("ps", b)

        P.add("sp", lambda e: e.dma_start(out=sm[:], in_=sm_d), writes=[R("sm")], dma=True)
        P.add("sp", lambda e: e.dma_start(out=cstf[:], in_=cst_d), writes=[R("cstf")], dma=True)
        for i in range(NSLOT):
            w_issue()
        xv = xT_d.rearrange("(k p) t -> p k t", p=128)
        for k in range(8):
            P.add("sp", lambda e, k=k: e.dma_start(out=xT[:, k, :], in_=xv[:, k, :]),
                  writes=[R("x", k, tb) for tb in range(4)], dma=True)
        P.add("dve", lambda e: e.tensor_copy(cstb[:], cstf[:]), reads=[R("cstf")], writes=[R("cstb")])
        P.add("dve", lambda e: e.tensor_scalar(cstb[:, 256:384], cstf[:, 128:256], -1.0, 30000.0, ALU.add, ALU.mult),
              reads=[R("cstf"), R("cstb")], writes=[R("cstb")])
        P.add("dve", lambda e: e.memset(onesb[:], 1.0), writes=[R("onesb")])
        P.add("dve", lambda e: e.memset(onesf[:], 1.0), writes=[R("onesf")])
        g_wab = Ub(2048, 384)
        g_wa2b = U[0:48, 2432:2944]
        g_waf = Uf(2944, 384)
        g_wa2f = U[0:48, 3712:4736].bitcast(F32)
        P.add("sp", lambda e: e.dma_start(out=g_waf, in_=wa_d), writes=[R("waf")], dma=True)
        P.add("sp", lambda e: e.dma_start(out=g_wa2f, in_=wa2_d), writes=[R("wa2f")], dma=True)
        P.add("dve", lambda e: e.tensor_copy(g_wab, g_waf), reads=[R("waf")], writes=[R("wab")])
        P.add("dve", lambda e: e.tensor_copy(g_wa2b, g_wa2f), reads=[R("wa2f")], writes=[R("wa2b")])
        P.add("act", lambda e: e.activation(cact[:], sm[:, SM_C:SM_C + 8], AF.Silu),
              reads=[R("sm")], writes=[R("cact")])

        def adaln_block(l, nb_):
            modrow = U[0:1, 18432:20480].bitcast(F32)
            wt, wr = w_slot(BIDX[("ada", l, nb_)])
            for half in range(2):
                n0 = half * 512
                ps, psr = ps_next([0, 1])
                for k in range(8):
                    P.add("pe", lambda e, ps=ps, wt=wt, k=k, n0=n0: e.matmul(
                        ps[0:1, :], cact[:, k:k + 1], wt[:, k * 1024 + n0:k * 1024 + n0 + 512],
                        start=(k == 0), stop=(k == 7)),
                        reads=[wr, R("cact")], writes=[psr])
                P.add("act", lambda e, ps=ps, n0=n0: e.activation(modrow[0:1, n0:n0 + 512], ps[0:1, :], AF.Copy),
                      reads=[psr], writes=[R("modrow")])
            w_issue()
            ps, psr = ps_next([0, 1])
            for m in range(8):
                P.add("pe", lambda e, ps=ps, m=m: e.matmul(
                    ps[:, m:m + 1], modrow[0:1, m * 128:(m + 1) * 128], onesf[0:1, 0:1], start=True, stop=True),
                    reads=[R("modrow"), R("onesf")], writes=[psr])
            c0 = l * 48 + nb_ * 8
            P.add("dve", lambda e, ps=ps, c0=c0: e.tensor_tensor(
                modv[:, c0:c0 + 8], ps[:, 0:8], sm[:, SM_AB + c0:SM_AB + c0 + 8], ALU.add),
                reads=[psr, R("sm")], writes=[R("modv")])

        def adaln_finish(l):
            mo = l * 48
            P.add("dve", lambda e: e.scalar_tensor_tensor(
                dv[:, 0:8], modv[:, mo + 8:mo + 16], 1.0, sm[:, SM_NG + (2 * l) * 8:SM_NG + (2 * l) * 8 + 8],
                ALU.add, ALU.mult), reads=[R("modv"), R("sm")], writes=[R("dv")])
            P.add("dve", lambda e: e.scalar_tensor_tensor(
                dv[:, 8:16], modv[:, mo + 32:mo + 40], 1.0, sm[:, SM_NG + (2 * l + 1) * 8:SM_NG + (2 * l + 1) * 8 + 8],
                ALU.add, ALU.mult), reads=[R("modv"), R("sm")], writes=[R("dv")])

        NO = 29400 - 5120
        UEND = 29400

        rstd_all = Uf(12288, 2048)

        def norm_stats(tb, out_rstd):
            ts = slice(tb * 512, (tb + 1) * 512)
            pss, pssr = ps_next([0, 1])
            for k in range(8):
                sq = Ub(NO + (k % 2) * 512, 512)
                sqr = R("n_sq", k % 2)
                P.add("act", lambda e, sq=sq, k=k, ts=ts: e.activation(sq, xT[:, k, ts], AF.Square),
                      reads=[R("x", k, tb)], writes=[sqr])
                P.add("pe", lambda e, pss=pss, sq=sq, k=k: e.matmul(pss[:], onesb[:], sq, start=(k == 0), stop=(k == 7)),
                      reads=[sqr, R("onesb")], writes=[pssr])
            lnt = Uf(NO + 1024, 512)
            P.add("act", lambda e, pss=pss, lnt=lnt: e.activation(lnt, pss[:], AF.Ln, bias=EPS, scale=1.0 / D),
                  reads=[pssr], writes=[R("n_ln")])
            P.add("act", lambda e, lnt=lnt, out_rstd=out_rstd: e.activation(out_rstd, lnt, AF.Exp, scale=-0.5),
                  reads=[R("n_ln")], writes=[R("n_rstd", tb)])

        def norm(gs_ap, sh_ap, final=False, fence=True, pre=False):
            if fence:
                P.fence()
            for tb in range(4):
                ts = slice(tb * 512, (tb + 1) * 512)
                if pre:
                    rstd = rstd_all[:, ts]
                else:
                    rstd = Uf(NO + 2048, 512)
                    P.alias_barrier([R("n_rstd", t_) for t_ in range(4) if t_ != tb], [R("n_rstd", tb)])
                    norm_stats(tb, rstd)
                for k in range(8):
                    if final:
                        P.add("dve", lambda e, k=k, ts=ts, rstd=rstd: e.scalar_tensor_tensor(
                            xT[:, k, ts], xT[:, k, ts], gs_ap[:, k:k + 1], rstd, ALU.mult, ALU.mult),
                            reads=[R("n_rstd", tb), R("sm")], writes=[R("x", k, tb)])
                    else:
                        tmp = Uf(NO + 3072 + (k % 2) * 1024, 512)
                        tr = R("n_tmp", k % 2)
                        P.add("dve", lambda e, k=k, ts=ts, rstd=rstd, tmp=tmp: e.scalar_tensor_tensor(
                            tmp, xT[:, k, ts], gs_ap[:, k:k + 1], rstd, ALU.mult, ALU.mult),
                            reads=[R("n_rstd", tb), R("x", k, tb), R("dv")], writes=[tr])
                        P.add("act", lambda e, k=k, ts=ts, tmp=tmp: e.activation(
                            hT[:, k, ts], tmp, AF.Identity, bias=sh_ap[:, k:k + 1]),
                            reads=[tr, R("modv")], writes=[R("h", k, tb)])

        def mlp(l):
            mo = l * 48
            gate = modv[:, mo + 40:mo + 48]
            hid = U[:, 0:16384].rearrange("p (j t) -> p j t", j=8)
            for g in range(4):
                w1, w1r = w_slot(BIDX[("w1", l, g)])
                for tb in range(4):
                    ts = slice(tb * 512, (tb + 1) * 512)
                    for j in range(8):
                        ps, psr = ps_next([0, 1, 2, 3])
                        for k in range(8):
                            P.add("pe", lambda e, ps=ps, w1=w1, k=k, j=j, ts=ts: e.matmul(
                                ps[:], w1[:, k * 1024 + j * 128:k * 1024 + (j + 1) * 128], hT[:, k, ts],
                                start=(k == 0), stop=(k == 7)),
                                reads=[w1r, R("h", k, tb)], writes=[psr])
                        sq = Uf(16384 + (j % 2) * 1024, 512)
                        sqr = R("m_sq", j % 2)
                        P.add("act", lambda e, ps=ps, sq=sq: e.activation(sq, ps[:], AF.Square),
                              reads=[psr], writes=[sqr])
                        P.add("dve", lambda e, ps=ps, sq=sq, j=j, ts=ts: e.scalar_tensor_tensor(
                            hid[:, j, ts], ps[:], 0.0, sq, ALU.is_gt, ALU.mult),
                            reads=[psr, sqr], writes=[R("hid", j, tb)])
                w_issue()
                w2, w2r = w_slot(BIDX[("w2", l, g)])
                for tb in range(4):
                    ts = slice(tb * 512, (tb + 1) * 512)
                    for dch in range(8):
                        ps, psr = ps_next([0, 1, 2, 3])
                        for j in range(8):
                            P.add("pe", lambda e, ps=ps, w2=w2, j=j, dch=dch, ts=ts: e.matmul(
                                ps[:], w2[:, j * 1024 + dch * 128:j * 1024 + (dch + 1) * 128], hid[:, j, ts],
                                start=(j == 0), stop=(j == 7)),
                                reads=[w2r, R("hid", j, tb)], writes=[psr])
                        P.add("dve", lambda e, ps=ps, dch=dch, ts=ts: e.scalar_tensor_tensor(
                            xT[:, dch, ts], ps[:], gate[:, dch:dch + 1], xT[:, dch, ts], ALU.mult, ALU.add),
                            reads=[psr, R("modv"), R("x", dch, tb)], writes=[R("x", dch, tb)])
                w_issue()
                if l == 0 and stage >= 3:
                    for _ in range(ADA1_AFTER_GROUP[g]):
                        adaln_block(1, ada1_state["n"])
                        ada1_state["n"] += 1

        ada1_state = {"n": 0}

        def gla():
            blk0 = BIDX[("glaA", 0)]
            P.fence()
            gate = modv[:, 16:24]
            o = 0
            a_aug = U[0:48, o:o + 2048]; o += 2048
            wab = Ub(o, 384); o += 384
            wa2b = U[0:48, o:o + 512]; o += 512
            waf = Uf(o, 384); o += 768
            wa2f = U[0:48, o:o + 1024].bitcast(F32); o += 1024
            spf = [Uf(o + i * 512, 256) for i in range(2)]; o += 1024
            ebf = [Uf(o + i * 1024, 512) for i in range(2)]; o += 2048
            enbf = [Uf(o + i * 1024, 512) for i in range(2)]; o += 2048
            qd = U[:, o:o + 1024].rearrange("p (h t) -> p h t", h=2); o += 1024
            ki = U[:, o:o + 1024].rearrange("p (h t) -> p h t", h=2); o += 1024
            ktok = U[:, o:o + 1024].rearrange("p (c h f) -> p c h f", c=4, h=2); o += 1024
            vtok = U[:, o:o + 2048].rearrange("p (c f) -> p c f", c=4); o += 2048
            rg = U[:, o:o + 2048].rearrange("p (j t) -> p j t", j=4); o += 2048
            og = U[:, o:o + 2048].rearrange("p (j t) -> p j t", j=4); o += 2048
            Sf = U[:, o:o + 1024].bitcast(F32).rearrange("p (h f) -> p h f", h=2); o += 1024
            Sb2 = [U[:, o + i * 512:o + (i + 1) * 512].rearrange("p (h f) -> p h f", h=2) for i in range(2)]; o += 1024
            sTb = [Ub(o + i * 128, 128) for i in range(2)]; o += 256
            tmpf = [Uf(o + i * 1024, 512) for i in range(2)]; o += 2048
            stmp = [Uf(o + i * 512, 256) for i in range(2)]; o += 1024
            etmp = Uf(o, 256); o += 512
            sqb = [Ub(o + i * 512, 512) for i in range(2)]; o += 1024
            assert o <= UEND, o

            for tb in range(4):
                ts = slice(tb * 512, (tb + 1) * 512)
                ps, psr = ps_next([0, 1])
                for k in range(8):
                    P.add("pe", lambda e, ps=ps, k=k, ts=ts: e.matmul(
                        ps[0:48, :], wab[:, k * 48:(k + 1) * 48], hT[:, k, ts], start=(k == 0), stop=(k == 7)),
                        reads=[R("wab"), R("h", k, tb)], writes=[psr])
                P.add("act", lambda e, ps=ps, ts=ts: e.activation(a_aug[0:48, ts], ps[0:48, :], AF.Copy),
                      reads=[psr], writes=[R("a_aug", tb)])
                P.add("dve", lambda e, ts=ts: e.memset(a_aug[0:1, ts], 1.0),
                      reads=[R("a_aug", tb)], writes=[R("a_aug", tb)])

            for g in range(2):
                wA, wAr = w_slot(blk0 + 2 * g)
                wB, wBr = w_slot(blk0 + 2 * g + 1)
                if g == 0:
                    dump("b", 7, wsl[0][:, 0:2048], [R("w", 0)])
                    dump("b", 16, wsl[1][:, 0:2048], [R("w", 1)])
                    dump("b", 17, wsl[2][:, 0:2048], [R("w", 2)])
                P.add("dve", lambda e: e.memset(Sf[:, :, :], 0.0), writes=[R("Sf", 0), R("Sf", 1)])
                for par in range(2):
                    P.add("dve", lambda e, par=par: e.memset(Sb2[par][:, :, :], 0.0), writes=[R("Sb", 0, par), R("Sb", 1, par)])
                for tb in range(4):
                    ts = slice(tb * 512, (tb + 1) * 512)
                    pb = [(PS[2], R("ps", 2)), (PS[3], R("ps", 3))]
                    for c in range(4):
                        tok = slice(tb * 512 + c * 128, tb * 512 + (c + 1) * 128)
                        ps, psr = ps_next([0, 1])
                        P.add("pe", lambda e, wA=wA, wB=wB, ps=ps, tok=tok, g=g: e.matmul(
                            ps[:, 0:256], a_aug[0:48, tok], wa2b[0:48, g * 256:(g + 1) * 256], start=True, stop=True),
                            reads=[R("a_aug", tb), R("wa2b")], writes=[psr])
                        P.add("act", lambda e, wA=wA, wB=wB, ps=ps: e.activation(etmp, ps[:, 0:256], AF.Exp, scale=-1.0),
                              reads=[psr], writes=[R("etmp")])
                        sp_ = spf[c % 2]
                        P.add("act", lambda e, wA=wA, wB=wB, sp_=sp_: e.activation(sp_, etmp, AF.Ln, bias=1.0),
                              reads=[R("etmp")], writes=[R("sp", c % 2)])
                        ps, psr = ps_next([0, 1])
                        for k in range(8):
                            P.add("pe", lambda e, wA=wA, wB=wB, ps=ps, k=k, tok=tok: e.matmul(
                                ps[:], hT[:, k, tok], wA[:, k * 1024 + 512:k * 1024 + 1024],
                                start=(k == 0), stop=(k == 7)), reads=[wAr, R("h", k, tb)], writes=[psr])
                        P.add("act", lambda e, wA=wA, wB=wB, ps=ps, c=c: e.activation(vtok[:, c, :], ps[:], AF.Copy),
                              reads=[psr], writes=[R("vtok", c)])
                        for hh in range(2):
                            P.add("pe", lambda e, wA=wA, wB=wB, hh=hh, c=c, sp_=sp_, pb=pb: e.matmul(
                                pb[hh][0][:, c * 128:(c + 1) * 128], sp_[:, hh * 128:(hh + 1) * 128], trinegf,
                                start=True, stop=True),
                                reads=[R("sp", c % 2), R("cstf")], writes=[pb[hh][1]])
                    for hh in range(2):
                        P.add("act", lambda e, wA=wA, wB=wB, hh=hh, pb=pb: e.activation(ebf[hh], pb[hh][0][:], AF.Exp),
                              reads=[pb[hh][1]], writes=[R("eb", hh)])
                        P.add("act", lambda e, wA=wA, wB=wB, hh=hh, pb=pb: e.activation(enbf[hh], pb[hh][0][:], AF.Exp, scale=-1.0),
                              reads=[pb[hh][1]], writes=[R("enb", hh)])
                    for jj in range(4):
                        ps, psr = ps_next([0, 1])
                        for k in range(8):
                            P.add("pe", lambda e, wA=wA, wB=wB, ps=ps, k=k, jj=jj, ts=ts: e.matmul(
                                ps[:], wB[:, k * 512 + jj * 128:k * 512 + (jj + 1) * 128], hT[:, k, ts],
                                start=(k == 0), stop=(k == 7)), reads=[wBr, R("h", k, tb)], writes=[psr])
                        bc = SM_BR + g * 4 + jj
                        P.add("act", lambda e, wA=wA, wB=wB, ps=ps, jj=jj, bc=bc: e.activation(
                            rg[:, jj, :], ps[:], AF.Silu, bias=sm[:, bc:bc + 1]),
                            reads=[psr, R("sm")], writes=[R("rg", jj)])
                    for hh in range(2):
                        ps, psr = ps_next([0, 1])
                        for k in range(8):
                            P.add("pe", lambda e, wA=wA, wB=wB, ps=ps, k=k, hh=hh, ts=ts: e.matmul(
                                ps[:], wA[:, k * 1024 + hh * 128:k * 1024 + (hh + 1) * 128], hT[:, k, ts],
                                start=(k == 0), stop=(k == 7)), reads=[wAr, R("h", k, tb)], writes=[psr])
                        P.add("dve", lambda e, wA=wA, wB=wB, ps=ps, hh=hh: e.scalar_tensor_tensor(
                            qd[:, hh, :], ps[:], 128.0 ** -0.5, ebf[hh], ALU.mult, ALU.mult),
                            reads=[psr, R("eb", hh)], writes=[R("qd", hh)])
                        ps, psr = ps_next([0, 1])
                        for k in range(8):
                            P.add("pe", lambda e, wA=wA, wB=wB, ps=ps, k=k, hh=hh, ts=ts: e.matmul(
                                ps[:], wA[:, k * 1024 + 256 + hh * 128:k * 1024 + 256 + (hh + 1) * 128], hT[:, k, ts],
                                start=(k == 0), stop=(k == 7)), reads=[wAr, R("h", k, tb)], writes=[psr])
                        P.add("dve", lambda e, wA=wA, wB=wB, ps=ps, hh=hh: e.tensor_tensor(ki[:, hh, :], ps[:], enbf[hh], ALU.mult),
                              reads=[psr, R("enb", hh)], writes=[R("ki", hh)])
                    for c in range(4):
                        pt, ptr = PS[4 + (c % 2)], R("ps", 4 + (c % 2))
                        ptb = pt[:].bitcast(BF16)
                        for hh in range(2):
                            P.add("pe", lambda e, wA=wA, wB=wB, ptb=ptb, c=c, hh=hh: e.transpose(
                                ptb[:, hh * 128:(hh + 1) * 128], ki[:, hh, c * 128:(c + 1) * 128], identb),
                                reads=[R("ki", hh), R("cstb")], writes=[ptr])
                        P.add("act", lambda e, wA=wA, wB=wB, ptb=ptb, c=c: e.activation(
                            ktok[:, c, :, :].rearrange("p h f -> p (h f)"), ptb[:, 0:256], AF.Copy),
                            reads=[ptr], writes=[R("ktok", c)])
                    po = [(PS[4 + i], R("ps", 4 + i)) for i in range(4)]
                    for c in range(4):
                        cs = slice(c * 128, (c + 1) * 128)
                        cg = tb * 4 + c
                        Sb_w = Sb2[cg % 2]
                        Sb_r = Sb2[(cg + 1) % 2]
                        for hh in range(2):
                            pss, pssr = PS[2 + hh], R("ps", 2 + hh)
                            sl = (c % 4) * 128
                            P.add("pe", lambda e, wA=wA, wB=wB, pss=pss, sl=sl, hh=hh, cs=cs: e.matmul(
                                pss[:, sl:sl + 128], ki[:, hh, cs], qd[:, hh, cs], start=True, stop=True),
                                reads=[R("ki", hh), R("qd", hh)], writes=[pssr])
                            sT = sTb[hh]
                            P.add("dve", lambda e, wA=wA, wB=wB, pss=pss, sl=sl, sT=sT: e.tensor_tensor(
                                sT, pss[:, sl:sl + 128], trib, ALU.mult),
                                reads=[pssr, R("cstb")], writes=[R("sT", hh)])
                        for hh in range(2):
                            psp, pspr = ps_next([0, 1])
                            P.add("pe", lambda e, wA=wA, wB=wB, psp=psp, c=c, hh=hh: e.matmul(
                                psp[:, 0:256], ktok[:, c, hh, :], vtok[:, c, hh * 256:(hh + 1) * 256],
                                start=True, stop=True),
                                reads=[R("ktok", c), R("vtok", c)], writes=[pspr])
                            st = stmp[hh]
                            P.add("dve", lambda e, wA=wA, wB=wB, psp=psp, st=st, hh=hh: e.tensor_tensor(
                                st, psp[:, 0:256], Sf[:, hh, :], ALU.add),
                                reads=[pspr, R("Sf", hh)], writes=[R("stmp", hh)])
                            ecol = ebf[hh][:, c * 128 + 127:c * 128 + 128]
                            P.add("dve", lambda e, wA=wA, wB=wB, st=st, hh=hh, ecol=ecol: e.tensor_scalar(
                                Sf[:, hh, :], st, ecol, None, ALU.mult),
                                reads=[R("stmp", hh), R("eb", hh)], writes=[R("Sf", hh)])
                            P.add("act", lambda e, wA=wA, wB=wB, st=st, hh=hh, ecol=ecol, Sb_w=Sb_w: e.activation(
                                Sb_w[:, hh, :], st, AF.Copy, scale=ecol),
                                reads=[R("stmp", hh), R("eb", hh)], writes=[R("Sb", hh, cg % 2)])
                        for hh in range(2):
                            sT = sTb[hh]
                            for jj in range(2):
                                pot, potr = po[hh * 2 + jj]
                                P.add("pe", lambda e, wA=wA, wB=wB, pot=pot, c=c, hh=hh, jj=jj, sT=sT, cs=cs: e.matmul(
                                    pot[:, cs], vtok[:, c, hh * 256 + jj * 128:hh * 256 + (jj + 1) * 128], sT,
                                    start=True, stop=False),
                                    reads=[R("vtok", c), R("sT", hh)], writes=[potr])
                                P.add("pe", lambda e, wA=wA, wB=wB, pot=pot, hh=hh, jj=jj, cs=cs, Sb_r=Sb_r: e.matmul(
                                    pot[:, cs], Sb_r[:, hh, jj * 128:(jj + 1) * 128], qd[:, hh, cs],
                                    start=False, stop=True),
                                    reads=[R("Sb", hh, (cg + 1) % 2), R("qd", hh)], writes=[potr])
                    if g == 0 and tb == 0:
                        dump("f", 3, Sf[:, :, :].rearrange("p h f -> p (h f)"), [R("Sf", 0), R("Sf", 1)])
                    for hh in range(2):
                        pst, pstr = ps_next([0, 1])
                        for jj in range(2):
                            pot, potr = po[hh * 2 + jj]
                            sq = sqb[jj]
                            P.add("act", lambda e, wA=wA, wB=wB, pot=pot, sq=sq: e.activation(sq, pot[:], AF.Square),
                                  reads=[potr], writes=[R("g_sq", jj)])
                            P.add("pe", lambda e, wA=wA, wB=wB, pst=pst, sq=sq, jj=jj: e.matmul(
                                pst[:], onesb[:], sq, start=(jj == 0), stop=(jj == 1)),
                                reads=[R("g_sq", jj), R("onesb")], writes=[pstr])
                        lnt = tmpf[0]
                        rstd = tmpf[1]
                        P.add("act", lambda e, wA=wA, wB=wB, pst=pst, lnt=lnt: e.activation(lnt, pst[:], AF.Ln, bias=EPS, scale=1.0 / 256),
                              reads=[pstr], writes=[R("g_ln")])
                        P.add("act", lambda e, wA=wA, wB=wB, lnt=lnt, rstd=rstd: e.activation(rstd, lnt, AF.Exp, scale=-0.5),
                              reads=[R("g_ln")], writes=[R("g_rstd")])
                        for jj in range(2):
                            pot, potr = po[hh * 2 + jj]
                            t2 = ebf[jj]
                            P.add("dve", lambda e, wA=wA, wB=wB, pot=pot, jj=jj, rstd=rstd, t2=t2: e.scalar_tensor_tensor(
                                t2, pot[:], sm[:, SM_GNG + jj:SM_GNG + jj + 1], rstd, ALU.mult, ALU.mult),
                                reads=[potr, R("g_rstd"), R("sm"), R("eb", jj)], writes=[R("eb", jj)])
                            P.add("dve", lambda e, wA=wA, wB=wB, hh=hh, jj=jj, t2=t2: e.tensor_tensor(
                                og[:, hh * 2 + jj, :], t2, rg[:, hh * 2 + jj, :], ALU.mult),
                                reads=[R("eb", jj), R("rg", hh * 2 + jj)], writes=[R("og", hh * 2 + jj)])
                    if g == 0 and tb == 0:
                        dump("b", 5, og[:, :, :].rearrange("p j t -> p (j t)"), [R("og", j) for j in range(4)])
                        dump("f", 3, Sf[:, :, :].rearrange("p h f -> p (h f)"), [R("Sf", 0), R("Sf", 1)])
                    for dch in range(8):
                        ps, psr = ps_next([0, 1])
                        for j4 in range(4):
                            P.add("pe", lambda e, wA=wA, wB=wB, ps=ps, j4=j4, dch=dch: e.matmul(
                                ps[:], wB[:, 4096 + j4 * 1024 + dch * 128:4096 + j4 * 1024 + (dch + 1) * 128],
                                og[:, j4, :], start=(j4 == 0), stop=(j4 == 3)),
                                reads=[wBr, R("og", j4)], writes=[psr])
                        P.add("dve", lambda e, wA=wA, wB=wB, ps=ps, dch=dch, ts=ts: e.scalar_tensor_tensor(
                            xT[:, dch, ts], ps[:], gate[:, dch:dch + 1], xT[:, dch, ts], ALU.mult, ALU.add),
                            reads=[psr, R("modv"), R("x", dch, tb)], writes=[R("x", dch, tb)])
                w_issue()
                w_issue()

        def att():
            blk0 = BIDX[("att", 0)]
            P.fence()
            gate = modv[:, 48 + 16:48 + 24]
            o = 0
            qT = U[:, o:o + 4096].rearrange("p (h t) -> p h t", h=2); o += 4096
            kz = U[:, o:o + 8192].rearrange("p (h i t) -> p h i t", h=2, i=2); o += 8192
            vaug = U[:, o:o + 4096].rearrange("p (c h f) -> p c h f", c=16, h=2); o += 4096
            oT = U[:, o:o + 1024].rearrange("p (h t) -> p h t", h=2); o += 1024
            ocs = o
            Cb = Ub(o, 2048); o += 2048
            Sb_ = Ub(o, 2048); o += 2048
            pTb = [Ub(o + i * 512, 512) for i in range(4)]; o += 2048
            oc = o
            qnb = [Ub(o + i * 512, 512) for i in range(2)]; o += 1024
            t1f = [Uf(o + i * 1024, 512) for i in range(2)]; o += 2048
            t2f = [Uf(o + i * 1024, 512) for i in range(2)]; o += 2048
            o = oc
            c_t1 = Uf(o, 512); o += 1024
            c_r2 = Uf(o, 512); o += 1024
            c_t2 = Uf(o, 512); o += 1024
            c_d = Uf(o, 512); o += 1024
            c_ln = Uf(o, 512); o += 1024
            c_sq = Ub(o, 512); o += 512
            assert o <= UEND, o
            angf = Uf(0, 2048)
            kf = Uf(4096, 2048)
            ki32 = U[:, 8192:12288].bitcast(I32)

            lrow = lrow_t[0:1, :]
            lsum = lamv[0:1, 2:8]
            P.add("sp", lambda e: e.dma_start(out=lrow, in_=rw_d), writes=[R("lrow")], dma=True)
            P.add("dve", lambda e: e.tensor_tensor(lrow[0:1, 0:64], lrow[0:1, 0:64], lrow[0:1, 64:128], ALU.mult),
                  reads=[R("lrow")], writes=[R("lrow")])
            P.add("dve", lambda e: e.tensor_tensor(lrow[0:1, 128:192], lrow[0:1, 128:192], lrow[0:1, 192:256], ALU.mult),
                  reads=[R("lrow")], writes=[R("lrow")])
            P.add("dve", lambda e: e.tensor_reduce(
                lsum[0:1, 0:2], lrow.rearrange("p (a b) -> p a b", a=2)[:, :, 0:64], AX.X, ALU.add),
                reads=[R("lrow")], writes=[R("lamv")])
            P.add("act", lambda e: e.activation(lsum[0:1, 2:4], lsum[0:1, 0:2], AF.Exp),
                  reads=[R("lamv")], writes=[R("lamv")])
            P.add("dve", lambda e: e.scalar_tensor_tensor(
                lsum[0:1, 4:5], lsum[0:1, 3:4], -LAMBDA_INIT, lsum[0:1, 2:3], ALU.add, ALU.subtract),
                reads=[R("lamv")], writes=[R("lamv")])
            ps, psr = ps_next([0, 1])
            P.add("pe", lambda e, ps=ps: e.matmul(ps[:, 0:1], onesf[0:1, :], lsum[0:1, 4:5], start=True, stop=True),
                  reads=[R("lamv"), R("onesf")], writes=[psr])
            P.add("dve", lambda e, ps=ps: e.tensor_copy(lamv[:, 0:1], ps[:, 0:1]), reads=[psr, R("lamv")], writes=[R("lamv")])
            P.add("dve", lambda e: e.tensor_scalar(lamv[:, 1:2], sm[:, SM_SLG:SM_SLG + 1], 1.0 - LAMBDA_INIT, None, ALU.mult),
                  reads=[R("sm"), R("lamv")], writes=[R("lamv")])

            posi1 = U[0:1, 12288:16384].bitcast(I32)
            posf1 = U[0:1, ocs:ocs + 4096].bitcast(F32)
            P.add("sp", lambda e: e.dma_start(out=posi1.bitcast(F32), in_=pos_d), writes=[R("posi")], dma=True)
            P.add("dve", lambda e: e.tensor_copy(posf1, posi1), reads=[R("posi")], writes=[R("posf")])
            for tb in range(4):
                ts = slice(tb * 512, (tb + 1) * 512)
                ps, psr = ps_next([0, 1])
                P.add("pe", lambda e, ps=ps, ts=ts: e.matmul(ps[:], onesf[0:1, :], posf1[0:1, ts], start=True, stop=True),
                      reads=[R("posf"), R("onesf")], writes=[psr])
                P.add("dve", lambda e, ps=ps, ts=ts: e.tensor_scalar(angf[:, ts], ps[:], sm[:, SM_INVF:SM_INVF + 1], None, ALU.mult),
                      reads=[psr, R("sm")], writes=[R("angf")])
            dump('f', 1, angf, [R('angf')])
            dump('f', 6, posi1.bitcast(F32), [R('posi')], npart=1)
            dump('f', 7, posf1, [R('posf')], npart=1)
            for which, dst, off in ((0, Sb_, 0.0), (1, Cb, float(np.pi / 2))):
                if which == 1:
                    P.add("dve", lambda e, off=off: e.tensor_scalar(angf, angf, off, None, ALU.add),
                          reads=[R("angf")], writes=[R("angf")])
                P.add("dve", lambda e: e.tensor_scalar(kf, angf, 1.0 / TWO_PI, None, ALU.mult),
                      reads=[R("angf")], writes=[R("kf")])
                P.add("dve", lambda e: e.tensor_copy(ki32, kf), reads=[R("kf")], writes=[R("ki32")])
                P.add("dve", lambda e: e.tensor_copy(kf, ki32), reads=[R("ki32")], writes=[R("kf")])
                P.add("dve", lambda e: e.scalar_tensor_tensor(kf, kf, -TWO_PI, angf, ALU.mult, ALU.add),
                      reads=[R("kf"), R("angf")], writes=[R("kf")])
                P.add("dve", lambda e: e.tensor_scalar(kf, kf, float(np.pi), float(-np.pi), ALU.min, ALU.max),
                      reads=[R("kf")], writes=[R("kf")])
                P.add("act", lambda e, dst=dst: e.activation(dst, kf, AF.Sin), reads=[R("kf")], writes=[R("rope", which)])
                dump('f', 2 + which, kf, [R("kf")])
            P.add("dve", lambda e: e.memset(kz[64:128, :, 0, :], 0.0), writes=[R("kzz"), R("kf"), R("ki32"), R("angf"), R("posi")])
            P.add("dve", lambda e: e.memset(kz[0:64, :, 1, :], 0.0), writes=[R("kzz"), R("kf"), R("ki32"), R("angf"), R("posi")])

            ALV = int(os.environ.get('ATT_LEVEL', '9'))

            def proj_tmp_res():
                return [R("qn", 0), R("qn", 1), R("t1", 0), R("t1", 1), R("t2", 0), R("t2", 1)]

            def comb_tmp_res():
                return [R("c_t1"), R("c_r2"), R("c_t2"), R("c_d"), R("c_ln"), R("c_sq")]

            for pr in range(4 if ALV > 0 else 0):
                wt, wr = w_slot(blk0 + pr)
                groups = [(nm, coff, hh, tb) for (nm, coff) in (("qT", 0), ("kT", 256)) for hh in range(2) for tb in range(4)]
                gstate = {}

                def stageA(gx):
                    nm, coff, hh, tb = groups[gx]
                    ts = slice(tb * 512, (tb + 1) * 512)
                    ps, psr = PS[gx % 2], R("ps", gx % 2)
                    for k in range(8):
                        P.add("pe", lambda e, wt=wt, ps=ps, k=k, hh=hh, ts=ts, coff=coff: e.matmul(
                            ps[:], wt[:, k * 768 + coff + hh * 128:k * 768 + coff + (hh + 1) * 128], hT[:, k, ts],
                            start=(k == 0), stop=(k == 7)), reads=[wr, R("h", k, tb)], writes=[psr])
                    qn = qnb[gx % 2]
                    P.add("act", lambda e, ps=ps, qn=qn: e.activation(qn, ps[:], AF.Copy),
                          reads=[psr], writes=[R("qn", gx % 2)])

                def stageB(gx):
                    nm, coff, hh, tb = groups[gx]
                    ts = slice(tb * 512, (tb + 1) * 512)
                    i2 = gx % 2
                    ps, psr = PS[gx % 2], R("ps", gx % 2)
                    qn = qnb[i2]
                    ps2, ps2r = PS[2 + i2], R("ps", 2 + i2)
                    P.add("pe", lambda e, ps2=ps2, qn=qn: e.matmul(ps2[:], permb, qn, start=True, stop=True),
                          reads=[R("qn", i2), R("cstb")], writes=[ps2r])
                    t1 = t1f[i2]
                    t2 = t2f[i2]
                    P.add("dve", lambda e, ps=ps, t1=t1, ts=ts: e.tensor_tensor(t1, ps[:], Cb[:, ts], ALU.mult),
                          reads=[psr, R("rope", 1)], writes=[R("t1", i2)])
                    P.add("dve", lambda e, ps2=ps2, t2=t2, ts=ts: e.tensor_tensor(t2, ps2[:], Sb_[:, ts], ALU.mult),
                          reads=[ps2r, R("rope", 0)], writes=[R("t2", i2)])
                    if nm == "qT":
                        P.add("pool", lambda e, hh=hh, ts=ts, t1=t1, t2=t2: e.tensor_tensor(
                            qT[:, hh, ts], t1, t2, ALU.add),
                            reads=[R("t1", i2), R("t2", i2)], writes=[R("qT", hh, tb)])
                    else:
                        for i_ in range(2):
                            prt_ = slice(64 * i_, 64 * i_ + 64)
                            P.add("pool", lambda e, hh=hh, ts=ts, t1=t1, t2=t2, i_=i_, prt_=prt_: e.tensor_tensor(
                                kz[prt_, hh, i_, ts], t1[prt_, :], t2[prt_, :], ALU.add),
                                reads=[R("t1", i2), R("t2", i2), R("kzz")], writes=[R("kT", hh, tb)])

                for gx in range(len(groups) + 1):
                    if gx < len(groups):
                        stageA(gx)
                    if gx >= 1:
                        stageB(gx - 1)
                for tt in range(16 if not os.environ.get("SKIPV") else 0):
                    tok = slice(tt * 128, (tt + 1) * 128)
                    ps, psr = ps_next([0, 1])
                    for k in range(8):
                        P.add("pe", lambda e, wt=wt, ps=ps, k=k, tok=tok: e.matmul(
                            ps[:, 0:256], hT[:, k, tok], wt[:, k * 768 + 512:k * 768 + 768],
                            start=(k == 0), stop=(k == 7)), reads=[wr, R("h", k, tt // 4)], writes=[psr])
                    P.add("dve", lambda e, wt=wt, ps=ps, tt=tt: e.tensor_copy(
                        vaug[:, tt, :, :], ps[:, 0:256].rearrange("p (h f) -> p h f", h=2)),
                        reads=[psr], writes=[R("vaug", tt)])
                P.alias_barrier(proj_tmp_res(), comb_tmp_res())
                its = []
                gi = 0
                for j in range(4):
                    for hh in range(2):
                        for i in range(2):
                            for kt in range(4 * j + 4):
                                its.append((hh, j, i, kt, gi))
                            gi += 1
                LA = 2
                SB = [1, 2, 3]
                deferred = []

                def emit_qk(idx):
                    hh, j, i, kt, g_ = its[idx]
                    r = kt - 4 * j
                    n0 = 128 * r if r > 0 else 0
                    ps, psr = PS[SB[idx % 3]], R("ps", SB[idx % 3])
                    P.add("pe", lambda e, ps=ps, i=i, hh=hh, kt=kt, j=j, n0=n0, r=r: e.matmul(
                        ps[:, n0:512], kz[:, hh, i, kt * 128:(kt + 1) * 128],
                        qT[:, hh, j * 512 + n0:(j + 1) * 512], start=True, stop=(r < 0)),
                        reads=[R("kT", hh, kt // 4), R("qT", hh, j), R("kzz")], writes=[psr])
                    if r >= 0:
                        P.add("pe", lambda e, ps=ps, n0=n0: e.matmul(
                            ps[:, n0:n0 + 128], identb, negmb, start=False, stop=True),
                            reads=[R("cstb")], writes=[psr])
                    pT = pTb[idx % 3]
                    pTr = R("pT", idx % 3)
                    P.add("act", lambda e, ps=ps, pT=pT, n0=n0: e.activation(
                        pT[:, n0:512], ps[:, n0:512], AF.Exp, scale=0.125),
                        reads=[psr], writes=[pTr])

                def emit_pv(idx):
                    hh, j, i, kt, g_ = its[idx]
                    r = kt - 4 * j
                    n0 = 128 * r if r > 0 else 0
                    pT = pTb[idx % 3]
                    pTr = R("pT", idx % 3)
                    ao, aor = PS[4 + 2 * i], R("ps", 4 + 2 * i)
                    al, alr = PS[5 + 2 * i], R("ps", 5 + 2 * i)
                    last = (kt == 4 * j + 3)
                    P.add("pe", lambda e, ao=ao, pT=pT, kt=kt, hh=hh, n0=n0, last=last: e.matmul(
                        ao[:, n0:512], vaug[:, kt, hh, :], pT[:, n0:512], start=(kt == 0), stop=last),
                        reads=[pTr, R("vaug", kt)], writes=[aor])
                    P.add("pe", lambda e, al=al, pT=pT, kt=kt, n0=n0, last=last: e.matmul(
                        al[:, n0:512], onesb[:], pT[:, n0:512], start=(kt == 0), stop=last),
                        reads=[pTr, R("onesb")], writes=[alr])
                    if not last:
                        return
                    qs = slice(j * 512, (j + 1) * 512)
                    if i == 0:
                        P.add("dve", lambda e, al=al: e.reciprocal(c_r2, al[:]), reads=[alr], writes=[R("c_r2")])
                        P.add("dve", lambda e, ao=ao: e.tensor_tensor(c_t1, ao[:], c_r2, ALU.mult),
                              reads=[aor, R("c_r2")], writes=[R("c_t1")])
                        return
                    P.add("dve", lambda e, al=al: e.reciprocal(c_r2, al[:]), reads=[alr], writes=[R("c_r2")])
                    P.add("dve", lambda e, ao=ao: e.scalar_tensor_tensor(c_t2, ao[:], lamv[:, 0:1], c_r2, ALU.mult, ALU.mult),
                          reads=[aor, R("c_r2"), R("lamv")], writes=[R("c_t2")])
                    P.add("pool", lambda e: e.tensor_tensor(c_d, c_t1, c_t2, ALU.add),
                          reads=[R("c_t1"), R("c_t2")], writes=[R("c_d")])

                    def epi1a():
                        P.add("act", lambda e: e.activation(c_sq, c_d, AF.Square), reads=[R("c_d")], writes=[R("c_sq")])

                    def epi1b():
                        pss, pssr = PS[0], R("ps", 0)
                        P.add("pe", lambda e, pss=pss: e.matmul(pss[:], onesb[:], c_sq, start=True, stop=True),
                              reads=[R("c_sq"), R("onesb")], writes=[pssr])

                    def epi1c(hh=hh):
                        pss, pssr = PS[0], R("ps", 0)
                        P.add("act", lambda e, pss=pss: e.activation(c_ln, pss[:], AF.Ln, bias=EPS, scale=1.0 / 128),
                              reads=[pssr], writes=[R("c_ln")])
                        P.add("act", lambda e: e.activation(c_ln, c_ln, AF.Exp, scale=-0.5),
                              reads=[R("c_ln")], writes=[R("c_ln")])
                        P.add("dve", lambda e, hh=hh: e.scalar_tensor_tensor(
                            oT[:, hh, :], c_d, lamv[:, 1:2], c_ln, ALU.mult, ALU.mult),
                            reads=[R("c_d"), R("c_ln"), R("lamv")], writes=[R("oT", hh)])

                    def epi2(j=j, qs=qs):
                        for dch in range(8):
                            psw, pswr = PS[0], R("ps", 0)
                            for h2 in range(2):
                                P.add("pe", lambda e, wt=wt, psw=psw, h2=h2, dch=dch: e.matmul(
                                    psw[:], wt[:, 6144 + h2 * 1024 + dch * 128:6144 + h2 * 1024 + (dch + 1) * 128],
                                    oT[:, h2, :], start=(h2 == 0), stop=(h2 == 1)),
                                    reads=[wr, R("oT", h2)], writes=[pswr])
                            P.add("dve", lambda e, psw=psw, dch=dch, qs=qs: e.scalar_tensor_tensor(
                                xT[:, dch, qs], psw[:], gate[:, dch:dch + 1], xT[:, dch, qs], ALU.mult, ALU.add),
                                reads=[pswr, R("modv"), R("x", dch, j)], writes=[R("x", dch, j)])

                    step_now = idx + LA
                    deferred.append((step_now + 10, epi1a))
                    deferred.append((step_now + 12, epi1b))
                    deferred.append((step_now + 14, epi1c))
                    if hh == 1:
                        deferred.append((step_now + 19, epi2))

                def run_deferred(step, flush=False):
                    while deferred and (flush or deferred[0][0] <= step):
                        deferred.pop(0)[1]()

                for step in range(len(its) + LA):
                    if step < len(its):
                        emit_qk(step)
                    if step >= LA:
                        idx_ = step - LA
                        hh_, j_, i_, kt_, g_ = its[idx_]
                        if i_ == 1 and kt_ == 4 * j_ + 3:
                            run_deferred(step, flush=True)
                        emit_pv(idx_)
                    run_deferred(step)
                run_deferred(0, flush=True)
                P.alias_barrier(comb_tmp_res(), proj_tmp_res())
                w_issue()

        P.fence()
        for nb_ in range(3):
            adaln_block(0, nb_)
        for tb in range(4):
            norm_stats(tb, rstd_all[:, tb * 512:(tb + 1) * 512])
        for nb_ in range(3, 6):
            adaln_block(0, nb_)
        adaln_finish(0)
        norm(dv[:, 0:8], modv[:, 0:8], fence=False, pre=True)
        gla()
        if stage >= 2:
            norm(dv[:, 8:16], modv[:, 24:32])
            mlp(0)
        if stage >= 3:
            adaln_finish(1)
            norm(dv[:, 0:8], modv[:, 48:56], fence=False)
            att()
        if stage >= 4:
            norm(dv[:, 8:16], modv[:, 48 + 24:48 + 32])
            mlp(1)
            norm(sm[:, SM_FG:SM_FG + 8], None, final=True, fence=False)
        ov = out_d.rearrange("(k p) t -> p k t", p=128)
        outr = []
        for k in range(8):
            rr = R("out", k)
            outr.append(rr)
            P.add("sp", lambda e, k=k: e.dma_start(out=ov[:, k, :], in_=xT[:, k, :]),
                  reads=[R("x", k, tb) for tb in range(4)], writes=[rr], dma=True)
        P.add("sp", lambda e: None, reads=outr, noinst=True)
        P.emit(sems, dsems)
    return nc


def _kmaj(Wc):
    K = Wc.shape[0] // 128
    return np.ascontiguousarray(Wc.reshape(K, 128, -1).transpose(1, 0, 2)).reshape(128, -1)


def _prep_shared(inp):
    f = np.float32
    blocks = np.zeros((NBLK, 128, 8192), f)
    ada_w = np.asarray(inp["ada_w"], f)
    mlp_w1 = np.asarray(inp["mlp_w1"], f)
    mlp_w2 = np.asarray(inp["mlp_w2"], f)
    gw = np.asarray(inp["gla_w_in"], f)[0]
    gwo = np.asarray(inp["gla_w_o"], f)[0]
    dw = np.asarray(inp["diff_w_in"], f)[0]
    dwo = np.asarray(inp["diff_w_o"], f)[0]
    for bi, name in enumerate(BLOCKS):
        kind = name[0]
        if kind == "ada":
            _, l, nb_ = name
            blocks[bi] = _kmaj(ada_w[l][:, nb_ * 1024:(nb_ + 1) * 1024])
        elif kind == "glaA":
            g = name[1]
            cols = np.concatenate([np.arange(2 * g * 128, 2 * g * 128 + 256),
                                   512 + np.arange(2 * g * 128, 2 * g * 128 + 256),
                                   1024 + np.arange(2 * g * 256, 2 * g * 256 + 512)])
            blocks[bi] = _kmaj(gw[:, cols])
        elif kind == "glaB":
            g = name[1]
            rc = 2048 + np.arange(2 * g * 256, 2 * g * 256 + 512)
            blocks[bi][:, 0:4096] = _kmaj(gw[:, rc])
            blocks[bi][:, 4096:8192] = _kmaj(gwo[g * 512:(g + 1) * 512, :])
        elif kind == "w1":
            _, l, g = name
            blocks[bi] = _kmaj(mlp_w1[l][:, g * 1024:(g + 1) * 1024])
        elif kind == "w2":
            _, l, g = name
            blocks[bi] = _kmaj(mlp_w2[l][g * 1024:(g + 1) * 1024, :])
        elif kind == "att":
            pr = name[1]
            cols = np.concatenate([np.arange(pr * 256, pr * 256 + 256),
                                   1024 + np.arange(pr * 256, pr * 256 + 256),
                                   2048 + np.arange(pr * 256, pr * 256 + 256)])
            blocks[bi][:, 0:6144] = _kmaj(dw[:, cols])
            blocks[bi][:, 6144:8192] = _kmaj(dwo[pr * 256:(pr + 1) * 256, :])

    def pk(v):
        return np.asarray(v, f).reshape(-1, 128).T

    sm = np.zeros((128, NSM), f)
    ng = np.asarray(inp["norm_g"], f)
    for l in range(2):
        for j in range(2):
            sm[:, SM_NG + (2 * l + j) * 8:SM_NG + (2 * l + j) * 8 + 8] = pk(ng[l, j])
    sm[:, SM_FG:SM_FG + 8] = pk(inp["final_g"])
    ab = np.asarray(inp["ada_b"], f)
    for l in range(2):
        sm[:, SM_AB + l * 48:SM_AB + (l + 1) * 48] = pk(ab[l])
    sm[:, SM_BR:SM_BR + 8] = pk(np.asarray(inp["gla_b_r"], f)[0])
    sm[:, SM_GNG:SM_GNG + 2] = pk(np.asarray(inp["gla_norm_g"], f)[0])
    sm[:, SM_SLG:SM_SLG + 1] = pk(np.asarray(inp["diff_subln_g"], f)[0])
    inv_freq = (np.float32(500000.0) ** (-(np.arange(0, 16, 2, dtype=np.float32)) / np.float32(16))).astype(f)
    invf = np.zeros(128, f)
    for p in range(128):
        d = p % 64
        if d < 16:
            invf[p] = inv_freq[d % 8]
    sm[:, SM_INVF] = invf

    rw = np.asarray(inp["diff_lambda"], f)[0].reshape(1, 256)
    wa = np.zeros((128, 8, 48), f)
    wa[:, :, 32:48] = gw[:, 3072:3088].reshape(8, 128, 16).transpose(1, 0, 2)
    wa = wa.reshape(128, 8 * 48)
    wa2 = np.zeros((48, 512), f)
    wa2[0] = np.asarray(inp["gla_b_a"], f)[0]
    wa2[32:48] = np.asarray(inp["gla_w_a2"], f)[0]
    cst = np.zeros((128, 512), f)
    cst[:, 0:128] = np.eye(128, dtype=f)
    tri = (np.arange(128)[None, :] >= np.arange(128)[:, None]).astype(f)
    cst[:, 128:256] = tri
    cst[:, 256:384] = tri * f(-1.0 / 16.0)
    perm = np.zeros((128, 128), f)
    for b_ in range(2):
        base = 64 * b_
        for d in range(8):
            perm[base + d + 8, base + d] = -1.0
        for d in range(8, 16):
            perm[base + d - 8, base + d] = 1.0
    cst[:, 384:512] = perm
    return dict(wblk=blocks, sm=sm, rw=rw, wa=wa, wa2=wa2, cst=cst)


_NC_CACHE = {}


def kernel(x, c, positions, ada_w, ada_b, norm_g, mlp_w1, mlp_w2,
           gla_w_in, gla_w_a2, gla_b_a, gla_b_r, gla_norm_g, gla_w_o,
           diff_w_in, diff_lambda, diff_subln_g, diff_w_o, final_g, _stage=4, _debug=False, _ncores=NB):
    inp = dict(ada_w=ada_w, ada_b=ada_b, norm_g=norm_g, mlp_w1=mlp_w1, mlp_w2=mlp_w2,
               gla_w_in=gla_w_in, gla_w_a2=gla_w_a2, gla_b_a=gla_b_a, gla_b_r=gla_b_r,
               gla_norm_g=gla_norm_g, gla_w_o=gla_w_o, diff_w_in=diff_w_in, diff_lambda=diff_lambda,
               diff_subln_g=diff_subln_g, diff_w_o=diff_w_o, final_g=final_g)
    sh = _prep_shared(inp)
    x = np.asarray(x, np.float32)
    c = np.asarray(c, np.float32)
    positions = np.asarray(positions, np.int32)
    in_maps = []
    for b in range(NB):
        sm = sh["sm"].copy()
        sm[:, SM_C:SM_C + 8] = c[b].reshape(8, 128).T
        in_maps.append(dict(xT=np.ascontiguousarray(x[b].T), pos=np.ascontiguousarray(positions[b:b + 1]).view(np.float32),
                            sm=sm, rw=sh["rw"], wa=sh["wa"], wa2=sh["wa2"], cst=sh["cst"], wblk=sh["wblk"]))
    key = (_stage, _debug)
    if key not in _NC_CACHE:
        _NC_CACHE[key] = build_nc(_stage, _debug)
    nc = _NC_CACHE[key]
    res = run_bass_kernel_spmd(nc, in_maps[:_ncores], core_ids=list(range(_ncores)))
    out = np.stack([np.ascontiguousarray(r["outT"].T) for r in res.results], axis=0)
    if _debug:
        return out.astype(np.float32), res.results[0]["dbgf"], res.results[0]["dbgb"]
    return out.astype(np.float32)
```

```python
import contextlib
import os
import math
import numpy as np
import concourse.bass as bass
import concourse.mybir as mybir
from concourse.bass_utils import run_bass_kernel_spmd

F32 = mybir.dt.float32
BF16 = mybir.dt.bfloat16
I32 = mybir.dt.int32
AF = mybir.ActivationFunctionType
ALU = mybir.AluOpType
AX = mybir.AxisListType

D = 1024
T = 2048
NB = 8
DFF = 4096
EPS = 1e-6
LAMBDA_INIT = 0.8 - 0.6 * math.exp(-0.3 * 1)
NBLK = 36
ADA1_AFTER_GROUP = [2, 2, 1, 1]


def _block_order():
    order = [("ada", 0, n) for n in range(6)]
    for g in range(2):
        order += [("glaA", g), ("glaB", g)]
    n1 = 0
    for g in range(4):
        order += [("w1", 0, g), ("w2", 0, g)]
        for _ in range(ADA1_AFTER_GROUP[g]):
            order.append(("ada", 1, n1)); n1 += 1
    order += [("att", p) for p in range(4)]
    for g in range(4):
        order += [("w1", 1, g), ("w2", 1, g)]
    assert len(order) == NBLK and n1 == 6
    return order


BLOCKS = _block_order()
BIDX = {b: i for i, b in enumerate(BLOCKS)}
NSLOT = 3
TWO_PI = float(2.0 * np.pi)

SM_NG = 0
SM_FG = 32
SM_AB = 40
SM_BR = 136
SM_GNG = 144
SM_SLG = 146
SM_INVF = 147
SM_C = 148
NSM = 160


class Res:
    __slots__ = ("w", "r", "excl")

    def __init__(self):
        self.w = None
        self.r = []
        self.excl = False


class Op:
    __slots__ = ("eng", "fn", "deps", "eidx", "signal", "seq", "dma", "dsem", "dval")


class Prog:
    NDMA = 24

    def __init__(self, nc, same_engine_sync=True):
        self.nc = nc
        self.engs = {"pe": nc.tensor, "act": nc.scalar, "dve": nc.vector,
                     "pool": nc.gpsimd, "sp": nc.sync}
        self.ops = []
        self.ecount = {k: 0 for k in self.engs}
        self.known = {k: {} for k in self.engs}
        self.kdma = {k: {} for k in self.engs}
        self.ndma = {"sp": 0, "pool": 0}
        self.dma_ops = {"sp": [], "pool": []}
        self.dbase = {"sp": 0, "pool": self.NDMA // 2}
        self.same_engine_sync = same_engine_sync
        self.res = {}
        self.last = {}

    def R(self, *key):
        r = self.res.get(key)
        if r is None:
            r = Res()
            r.excl = (key[0] == "ps")
            self.res[key] = r
        return r

    def alias_barrier(self, olds, news):
        pend = []
        for o in olds:
            if o.w is not None:
                pend.append(o.w)
            pend.extend(o.r)
        for n in news:
            n.r.extend(pend)

    def fence(self):
        last = dict(self.last)
        for eng in self.engs:
            self.add(eng, lambda e: None, extra=[o for k, o in last.items() if k != eng], noinst=True)

    def add(self, eng, fn, reads=(), writes=(), dma=False, extra=(), noinst=False):
        op = Op()
        op.eng = eng
        op.fn = fn
        op.dma = dma
        op.signal = False
        op.seq = None
        op.eidx = self.ecount[eng]
        self.ecount[eng] += 1
        if any(r.excl for r in reads):
            writes = list(writes) + [r for r in reads if r.excl]
            reads = [r for r in reads if not r.excl]
        deps = list(extra)
        for r in reads:
            if r.w is not None:
                deps.append(r.w)
        for r in writes:
            if r.w is not None:
                deps.append(r.w)
            deps.extend(r.r)
        if dma:
            i = self.ndma[eng]
            self.ndma[eng] += 1
            h = self.NDMA // 2
            op.dsem = self.dbase[eng] + i % h
            op.dval = 16 * (i // h + 1)
            if i >= h:
                deps.append(self.dma_ops[eng][i - h])
            self.dma_ops[eng].append(op)
        best = {}
        for d in deps:
            if d is op:
                continue
            if d.dma:
                if self.kdma[eng].get(d.dsem, 0) >= d.dval:
                    continue
                key = ("dma", d.dsem)
                if key not in best or best[key].dval < d.dval:
                    best[key] = d
            else:
                if d.eng == eng and (eng == "pe" or not self.same_engine_sync):
                    continue
                if self.known[eng].get(d.eng, -1) >= d.eidx:
                    continue
                key = d.eng
                if key not in best or best[key].eidx < d.eidx:
                    best[key] = d
        final = []
        for key, d in best.items():
            final.append(d)
            if d.dma:
                self.kdma[eng][d.dsem] = d.dval
            else:
                d.signal = True
                self.known[eng][d.eng] = d.eidx
        op.deps = final
        for r in reads:
            r.r.append(op)
        for r in writes:
            r.w = op
            r.r = []
        self.ops.append(op)
        if not dma and not noinst:
            self.last[eng] = op
        return op

    def emit(self, sems, dsems):
        cnt = {k: 0 for k in self.engs}
        for op in self.ops:
            if not op.dma and op.signal:
                cnt[op.eng] += 1
                op.seq = cnt[op.eng]
        for op in self.ops:
            e = self.engs[op.eng]
            for d in op.deps:
                if d.dma:
                    e.wait_ge(dsems[d.dsem], d.dval)
                else:
                    e.wait_ge(sems[d.eng], d.seq)
            ins = op.fn(e)
            if ins is None:
                continue
            if op.dma:
                ins.then_inc(dsems[op.dsem], 16)
            elif op.signal:
                ins.then_inc(sems[op.eng], 1)
        return cnt


def build_nc(stage=4, debug=False):
    nc = bass.Bass("TRN2", target_bir_lowering=False)
    xT_d = nc.dram_tensor("xT", [D, T], F32, kind="ExternalInput").ap()
    pos_d = nc.dram_tensor("pos", [1, T], F32, kind="ExternalInput").ap()
    sm_d = nc.dram_tensor("sm", [128, NSM], F32, kind="ExternalInput").ap()
    rw_d = nc.dram_tensor("rw", [1, 256], F32, kind="ExternalInput").ap()
    wa_d = nc.dram_tensor("wa", [128, 8 * 48], F32, kind="ExternalInput").ap()
    wa2_d = nc.dram_tensor("wa2", [48, 512], F32, kind="ExternalInput").ap()
    cst_d = nc.dram_tensor("cst", [128, 512], F32, kind="ExternalInput").ap()
    wblk_d = nc.dram_tensor("wblk", [NBLK, 128, 8192], F32, kind="ExternalInput").ap()
    out_d = nc.dram_tensor("outT", [D, T], F32, kind="ExternalOutput").ap()
    if debug:
        dbgf_d = nc.dram_tensor("dbgf", [16, 128, 2048], F32, kind="ExternalOutput").ap()
        dbgb_d = nc.dram_tensor("dbgb", [24, 128, 2048], BF16, kind="ExternalOutput").ap()

    es = contextlib.ExitStack()
    with es:
        def sb(name, shape, dt):
            return es.enter_context(nc.sbuf_tensor(name, shape, dt))

        xT = sb("xTs", [128, 8, T], F32)
        hT = sb("hTs", [128, 8, T], BF16)
        wsl = [sb("wsl%d" % i, [128, 8192], BF16) for i in range(NSLOT)]
        cstf = sb("cstf", [128, 512], F32)
        cstb = sb("cstb", [128, 512], BF16)
        onesb = sb("onesb", [128, 128], BF16)
        onesf = sb("onesf", [128, 128], F32)
        sm = sb("sms", [128, NSM], F32)
        modv = sb("modv", [128, 96], F32)
        dv = sb("dvs", [128, 64], F32)
        cact = sb("cact", [128, 8], BF16)
        lamv = sb("lamv", [128, 8], F32)
        lrow_t = sb("lrow", [1, 256], F32)
        U = sb("U", [128, 29400], BF16)
        PS = [es.enter_context(nc.psum_tensor("ps%d" % i, [128, 512], F32)) for i in range(8)]
        sems = {k: es.enter_context(nc.semaphore("s_" + k)) for k in ["pe", "act", "dve", "pool", "sp"]}
        dsems = [es.enter_context(nc.semaphore("d%d" % i)) for i in range(Prog.NDMA)]

        P = Prog(nc)
        R = P.R

        identf = cstf[:, 0:128]
        trinegf = cstf[:, 256:384]
        identb = cstb[:, 0:128]
        trib = cstb[:, 128:256]
        permb = cstb[:, 384:512]
        negmb = cstb[:, 256:384]

        def Ub(off, n):
            return U[:, off:off + n]

        def Uf(off, n):
            return U[:, off:off + 2 * n].bitcast(F32)

        def dump(kind, idx, ap, reads, npart=128):
            if not debug:
                return
            n = ap.shape[-1]
            dst = (dbgf_d if kind == "f" else dbgb_d)[idx, 0:npart, 0:n]
            P.add("sp", lambda e: e.dma_start(out=dst, in_=ap), reads=reads, writes=[R("dbg", kind, idx)], dma=True)

        wstate = {"next": 0}

        def w_issue():
            i = wstate["next"]
            if i >= NBLK:
                return
            wstate["next"] = i + 1
            s = i % NSLOT
            P.add("pool", lambda e, i=i, s=s: e.dma_start(out=wsl[s][:], in_=wblk_d[i]),
                  writes=[R("w", s)], dma=True)

        def w_slot(i):
            return wsl[i % NSLOT], R("w", i % NSLOT)

        prot = {"i": 0}

        def ps_next(banks):
            b = banks[prot["i"] % len(banks)]
            prot["i"] += 1
            return PS[b], R("ps", b)

        P.add("sp", lambda e: e.dma_start(out=sm[:], in_=sm_d), writes=[R("sm")], dma=True)
        P.add("sp", lambda e: e.dma_start(out=cstf[:], in_=cst_d), writes=[R("cstf")], dma=True)
        for i in range(NSLOT):
            w_issue()
        xv = xT_d.rearrange("(k p) t -> p k t", p=128)
        for k in range(8):
            P.add("sp", lambda e, k=k: e.dma_start(out=xT[:, k, :], in_=xv[:, k, :]),
                  writes=[R("x", k, tb) for tb in range(4)], dma=True)
        P.add("dve", lambda e: e.tensor_copy(cstb[:], cstf[:]), reads=[R("cstf")], writes=[R("cstb")])
        P.add("dve", lambda e: e.tensor_scalar(cstb[:, 256:384], cstf[:, 128:256], -1.0, 30000.0, ALU.add, ALU.mult),
              reads=[R("cstf"), R("cstb")], writes=[R("cstb")])
        P.add("dve", lambda e: e.memset(onesb[:], 1.0), writes=[R("onesb")])
        P.add("dve", lambda e: e.memset(onesf[:], 1.0), writes=[R("onesf")])
        P.add("act", lambda e: e.activation(cact[:], sm[:, SM_C:SM_C + 8], AF.Silu),
              reads=[R("sm")], writes=[R("cact")])

        def adaln_block(l, nb_):
            modrow = U[0:1, 18432:20480].bitcast(F32)
            wt, wr = w_slot(BIDX[("ada", l, nb_)])
            for half in range(2):
                n0 = half * 512
                ps, psr = ps_next([0, 1])
                for k in range(8):
                    P.add("pe", lambda e, ps=ps, wt=wt, k=k, n0=n0: e.matmul(
                        ps[0:1, :], cact[:, k:k + 1], wt[:, k * 1024 + n0:k * 1024 + n0 + 512],
                        start=(k == 0), stop=(k == 7)),
                        reads=[wr, R("cact")], writes=[psr])
                P.add("act", lambda e, ps=ps, n0=n0: e.activation(modrow[0:1, n0:n0 + 512], ps[0:1, :], AF.Copy),
                      reads=[psr], writes=[R("modrow")])
            w_issue()
            ps, psr = ps_next([0, 1])
            for m in range(8):
                P.add("pe", lambda e, ps=ps, m=m: e.matmul(
                    ps[:, m:m + 1], modrow[0:1, m * 128:(m + 1) * 128], onesf[0:1, 0:1], start=True, stop=True),
                    reads=[R("modrow"), R("onesf")], writes=[psr])
            c0 = l * 48 + nb_ * 8
            P.add("dve", lambda e, ps=ps, c0=c0: e.tensor_tensor(
                modv[:, c0:c0 + 8], ps[:, 0:8], sm[:, SM_AB + c0:SM_AB + c0 + 8], ALU.add),
                reads=[psr, R("sm")], writes=[R("modv")])

        def adaln_finish(l):
            mo = l * 48
            P.add("dve", lambda e: e.scalar_tensor_tensor(
                dv[:, 0:8], modv[:, mo + 8:mo + 16], 1.0, sm[:, SM_NG + (2 * l) * 8:SM_NG + (2 * l) * 8 + 8],
                ALU.add, ALU.mult), reads=[R("modv"), R("sm")], writes=[R("dv")])
            P.add("dve", lambda e: e.scalar_tensor_tensor(
                dv[:, 8:16], modv[:, mo + 32:mo + 40], 1.0, sm[:, SM_NG + (2 * l + 1) * 8:SM_NG + (2 * l + 1) * 8 + 8],
                ALU.add, ALU.mult), reads=[R("modv"), R("sm")], writes=[R("dv")])

        NO = 29400 - 5120
        UEND = 29400

        def norm(gs_ap, sh_ap, final=False, fence=True):
            if fence:
                P.fence()
            for tb in range(4):
                ts = slice(tb * 512, (tb + 1) * 512)
                pss, pssr = ps_next([0, 1])
                for k in range(8):
                    sq = Ub(NO + (k % 2) * 512, 512)
                    sqr = R("n_sq", k % 2)
                    P.add("act", lambda e, sq=sq, k=k, ts=ts: e.activation(sq, xT[:, k, ts], AF.Square),
                          reads=[R("x", k, tb)], writes=[sqr])
                    P.add("pe", lambda e, pss=pss, sq=sq, k=k: e.matmul(pss[:], onesb[:], sq, start=(k == 0), stop=(k == 7)),
                          reads=[sqr, R("onesb")], writes=[pssr])
                lnt = Uf(NO + 1024, 512)
                rstd = Uf(NO + 2048, 512)
                P.add("act", lambda e, pss=pss, lnt=lnt: e.activation(lnt, pss[:], AF.Ln, bias=EPS, scale=1.0 / D),
                      reads=[pssr], writes=[R("n_ln")])
                P.add("act", lambda e, lnt=lnt, rstd=rstd: e.activation(rstd, lnt, AF.Exp, scale=-0.5),
                      reads=[R("n_ln")], writes=[R("n_rstd")])
                for k in range(8):
                    if final:
                        P.add("dve", lambda e, k=k, ts=ts, rstd=rstd: e.scalar_tensor_tensor(
                            xT[:, k, ts], xT[:, k, ts], gs_ap[:, k:k + 1], rstd, ALU.mult, ALU.mult),
                            reads=[R("n_rstd"), R("sm")], writes=[R("x", k, tb)])
                    else:
                        tmp = Uf(NO + 3072 + (k % 2) * 1024, 512)
                        tr = R("n_tmp", k % 2)
                        P.add("dve", lambda e, k=k, ts=ts, rstd=rstd, tmp=tmp: e.scalar_tensor_tensor(
                            tmp, xT[:, k, ts], gs_ap[:, k:k + 1], rstd, ALU.mult, ALU.mult),
                            reads=[R("n_rstd"), R("x", k, tb), R("dv")], writes=[tr])
                        P.add("act", lambda e, k=k, ts=ts, tmp=tmp: e.activation(
                            hT[:, k, ts], tmp, AF.Identity, bias=sh_ap[:, k:k + 1]),
                            reads=[tr, R("modv")], writes=[R("h", k, tb)])

        def mlp(l):
            mo = l * 48
            gate = modv[:, mo + 40:mo + 48]
            hid = U[:, 0:16384].rearrange("p (j t) -> p j t", j=8)
            for g in range(4):
                w1, w1r = w_slot(BIDX[("w1", l, g)])
                for tb in range(4):
                    ts = slice(tb * 512, (tb + 1) * 512)
                    for j in range(8):
                        ps, psr = ps_next([0, 1, 2, 3])
                        for k in range(8):
                            P.add("pe", lambda e, ps=ps, w1=w1, k=k, j=j, ts=ts: e.matmul(
                                ps[:], w1[:, k * 1024 + j * 128:k * 1024 + (j + 1) * 128], hT[:, k, ts],
                                start=(k == 0), stop=(k == 7)),
                                reads=[w1r, R("h", k, tb)], writes=[psr])
                        sq = Uf(16384 + (j % 2) * 1024, 512)
                        sqr = R("m_sq", j % 2)
                        P.add("act", lambda e, ps=ps, sq=sq: e.activation(sq, ps[:], AF.Square),
                              reads=[psr], writes=[sqr])
                        P.add("dve", lambda e, ps=ps, sq=sq, j=j, ts=ts: e.scalar_tensor_tensor(
                            hid[:, j, ts], ps[:], 0.0, sq, ALU.is_gt, ALU.mult),
                            reads=[psr, sqr], writes=[R("hid", j, tb)])
                w_issue()
                w2, w2r = w_slot(BIDX[("w2", l, g)])
                for tb in range(4):
                    ts = slice(tb * 512, (tb + 1) * 512)
                    for dch in range(8):
                        ps, psr = ps_next([0, 1, 2, 3])
                        for j in range(8):
                            P.add("pe", lambda e, ps=ps, w2=w2, j=j, dch=dch, ts=ts: e.matmul(
                                ps[:], w2[:, j * 1024 + dch * 128:j * 1024 + (dch + 1) * 128], hid[:, j, ts],
                                start=(j == 0), stop=(j == 7)),
                                reads=[w2r, R("hid", j, tb)], writes=[psr])
                        P.add("dve", lambda e, ps=ps, dch=dch, ts=ts: e.scalar_tensor_tensor(
                            xT[:, dch, ts], ps[:], gate[:, dch:dch + 1], xT[:, dch, ts], ALU.mult, ALU.add),
                            reads=[psr, R("modv"), R("x", dch, tb)], writes=[R("x", dch, tb)])
                w_issue()
                if l == 0 and stage >= 3:
                    for _ in range(ADA1_AFTER_GROUP[g]):
                        adaln_block(1, ada1_state["n"])
                        ada1_state["n"] += 1

        ada1_state = {"n": 0}

        def gla():
            blk0 = BIDX[("glaA", 0)]
            P.fence()
            gate = modv[:, 16:24]
            o = 0
            a_aug = U[0:48, o:o + 2048]; o += 2048
            wab = Ub(o, 384); o += 384
            wa2b = U[0:48, o:o + 512]; o += 512
            waf = Uf(o, 384); o += 768
            wa2f = U[0:48, o:o + 1024].bitcast(F32); o += 1024
            spf = [Uf(o + i * 512, 256) for i in range(2)]; o += 1024
            ebf = [Uf(o + i * 1024, 512) for i in range(2)]; o += 2048
            enbf = [Uf(o + i * 1024, 512) for i in range(2)]; o += 2048
            qd = U[:, o:o + 1024].rearrange("p (h t) -> p h t", h=2); o += 1024
            ki = U[:, o:o + 1024].rearrange("p (h t) -> p h t", h=2); o += 1024
            ktok = U[:, o:o + 1024].rearrange("p (c h f) -> p c h f", c=4, h=2); o += 1024
            vtok = U[:, o:o + 2048].rearrange("p (c f) -> p c f", c=4); o += 2048
            rg = U[:, o:o + 2048].rearrange("p (j t) -> p j t", j=4); o += 2048
            og = U[:, o:o + 2048].rearrange("p (j t) -> p j t", j=4); o += 2048
            Sf = U[:, o:o + 1024].bitcast(F32).rearrange("p (h f) -> p h f", h=2); o += 1024
            Sb2 = [U[:, o + i * 512:o + (i + 1) * 512].rearrange("p (h f) -> p h f", h=2) for i in range(2)]; o += 1024
            sTb = [Ub(o + i * 128, 128) for i in range(2)]; o += 256
            tmpf = [Uf(o + i * 1024, 512) for i in range(2)]; o += 2048
            stmp = [Uf(o + i * 512, 256) for i in range(2)]; o += 1024
            etmp = Uf(o, 256); o += 512
            sqb = [Ub(o + i * 512, 512) for i in range(2)]; o += 1024
            assert o <= UEND, o

            P.add("sp", lambda e: e.dma_start(out=waf, in_=wa_d), writes=[R("waf"), R("modrow")], dma=True)
            P.add("sp", lambda e: e.dma_start(out=wa2f, in_=wa2_d), writes=[R("wa2f"), R("modrow")], dma=True)
            P.add("dve", lambda e: e.tensor_copy(wab, waf), reads=[R("waf")], writes=[R("wab")])
            P.add("dve", lambda e: e.tensor_copy(wa2b, wa2f), reads=[R("wa2f")], writes=[R("wa2b")])
            for tb in range(4):
                ts = slice(tb * 512, (tb + 1) * 512)
                ps, psr = ps_next([0, 1])
                for k in range(8):
                    P.add("pe", lambda e, ps=ps, k=k, ts=ts: e.matmul(
                        ps[0:48, :], wab[:, k * 48:(k + 1) * 48], hT[:, k, ts], start=(k == 0), stop=(k == 7)),
                        reads=[R("wab"), R("h", k, tb)], writes=[psr])
                P.add("act", lambda e, ps=ps, ts=ts: e.activation(a_aug[0:48, ts], ps[0:48, :], AF.Copy),
                      reads=[psr], writes=[R("a_aug", tb)])
                P.add("dve", lambda e, ts=ts: e.memset(a_aug[0:1, ts], 1.0),
                      reads=[R("a_aug", tb)], writes=[R("a_aug", tb)])

            for g in range(2):
                wA, wAr = w_slot(blk0 + 2 * g)
                wB, wBr = w_slot(blk0 + 2 * g + 1)
                if g == 0:
                    dump("b", 7, wsl[0][:, 0:2048], [R("w", 0)])
                    dump("b", 16, wsl[1][:, 0:2048], [R("w", 1)])
                    dump("b", 17, wsl[2][:, 0:2048], [R("w", 2)])
                P.add("dve", lambda e: e.memset(Sf[:, :, :], 0.0), writes=[R("Sf", 0), R("Sf", 1)])
                for par in range(2):
                    P.add("dve", lambda e, par=par: e.memset(Sb2[par][:, :, :], 0.0), writes=[R("Sb", 0, par), R("Sb", 1, par)])
                for tb in range(4):
                    ts = slice(tb * 512, (tb + 1) * 512)
                    pb = [(PS[2], R("ps", 2)), (PS[3], R("ps", 3))]
                    for c in range(4):
                        tok = slice(tb * 512 + c * 128, tb * 512 + (c + 1) * 128)
                        ps, psr = ps_next([0, 1])
                        P.add("pe", lambda e, wA=wA, wB=wB, ps=ps, tok=tok, g=g: e.matmul(
                            ps[:, 0:256], a_aug[0:48, tok], wa2b[0:48, g * 256:(g + 1) * 256], start=True, stop=True),
                            reads=[R("a_aug", tb), R("wa2b")], writes=[psr])
                        P.add("act", lambda e, wA=wA, wB=wB, ps=ps: e.activation(etmp, ps[:, 0:256], AF.Exp, scale=-1.0),
                              reads=[psr], writes=[R("etmp")])
                        sp_ = spf[c % 2]
                        P.add("act", lambda e, wA=wA, wB=wB, sp_=sp_: e.activation(sp_, etmp, AF.Ln, bias=1.0),
                              reads=[R("etmp")], writes=[R("sp", c % 2)])
                        ps, psr = ps_next([0, 1])
                        for k in range(8):
                            P.add("pe", lambda e, wA=wA, wB=wB, ps=ps, k=k, tok=tok: e.matmul(
                                ps[:], hT[:, k, tok], wA[:, k * 1024 + 512:k * 1024 + 1024],
                                start=(k == 0), stop=(k == 7)), reads=[wAr, R("h", k, tb)], writes=[psr])
                        P.add("act", lambda e, wA=wA, wB=wB, ps=ps, c=c: e.activation(vtok[:, c, :], ps[:], AF.Copy),
                              reads=[psr], writes=[R("vtok", c)])
                        for hh in range(2):
                            P.add("pe", lambda e, wA=wA, wB=wB, hh=hh, c=c, sp_=sp_, pb=pb: e.matmul(
                                pb[hh][0][:, c * 128:(c + 1) * 128], sp_[:, hh * 128:(hh + 1) * 128], trinegf,
                                start=True, stop=True),
                                reads=[R("sp", c % 2), R("cstf")], writes=[pb[hh][1]])
                    for hh in range(2):
                        P.add("act", lambda e, wA=wA, wB=wB, hh=hh, pb=pb: e.activation(ebf[hh], pb[hh][0][:], AF.Exp),
                              reads=[pb[hh][1]], writes=[R("eb", hh)])
                        P.add("act", lambda e, wA=wA, wB=wB, hh=hh, pb=pb: e.activation(enbf[hh], pb[hh][0][:], AF.Exp, scale=-1.0),
                              reads=[pb[hh][1]], writes=[R("enb", hh)])
                    for jj in range(4):
                        ps, psr = ps_next([0, 1])
                        for k in range(8):
                            P.add("pe", lambda e, wA=wA, wB=wB, ps=ps, k=k, jj=jj, ts=ts: e.matmul(
                                ps[:], wB[:, k * 512 + jj * 128:k * 512 + (jj + 1) * 128], hT[:, k, ts],
                                start=(k == 0), stop=(k == 7)), reads=[wBr, R("h", k, tb)], writes=[psr])
                        bc = SM_BR + g * 4 + jj
                        P.add("act", lambda e, wA=wA, wB=wB, ps=ps, jj=jj, bc=bc: e.activation(
                            rg[:, jj, :], ps[:], AF.Silu, bias=sm[:, bc:bc + 1]),
                            reads=[psr, R("sm")], writes=[R("rg", jj)])
                    for hh in range(2):
                        ps, psr = ps_next([0, 1])
                        for k in range(8):
                            P.add("pe", lambda e, wA=wA, wB=wB, ps=ps, k=k, hh=hh, ts=ts: e.matmul(
                                ps[:], wA[:, k * 1024 + hh * 128:k * 1024 + (hh + 1) * 128], hT[:, k, ts],
                                start=(k == 0), stop=(k == 7)), reads=[wAr, R("h", k, tb)], writes=[psr])
                        P.add("dve", lambda e, wA=wA, wB=wB, ps=ps, hh=hh: e.scalar_tensor_tensor(
                            qd[:, hh, :], ps[:], 128.0 ** -0.5, ebf[hh], ALU.mult, ALU.mult),
                            reads=[psr, R("eb", hh)], writes=[R("qd", hh)])
                        ps, psr = ps_next([0, 1])
                        for k in range(8):
                            P.add("pe", lambda e, wA=wA, wB=wB, ps=ps, k=k, hh=hh, ts=ts: e.matmul(
                                ps[:], wA[:, k * 1024 + 256 + hh * 128:k * 1024 + 256 + (hh + 1) * 128], hT[:, k, ts],
                                start=(k == 0), stop=(k == 7)), reads=[wAr, R("h", k, tb)], writes=[psr])
                        P.add("dve", lambda e, wA=wA, wB=wB, ps=ps, hh=hh: e.tensor_tensor(ki[:, hh, :], ps[:], enbf[hh], ALU.mult),
                              reads=[psr, R("enb", hh)], writes=[R("ki", hh)])
                    for c in range(4):
                        pt, ptr = PS[4 + (c % 2)], R("ps", 4 + (c % 2))
                        ptb = pt[:].bitcast(BF16)
                        for hh in range(2):
                            P.add("pe", lambda e, wA=wA, wB=wB, ptb=ptb, c=c, hh=hh: e.transpose(
                                ptb[:, hh * 128:(hh + 1) * 128], ki[:, hh, c * 128:(c + 1) * 128], identb),
                                reads=[R("ki", hh), R("cstb")], writes=[ptr])
                        P.add("act", lambda e, wA=wA, wB=wB, ptb=ptb, c=c: e.activation(
                            ktok[:, c, :, :].rearrange("p h f -> p (h f)"), ptb[:, 0:256], AF.Copy),
                            reads=[ptr], writes=[R("ktok", c)])
                    po = [(PS[4 + i], R("ps", 4 + i)) for i in range(4)]
                    for c in range(4):
                        cs = slice(c * 128, (c + 1) * 128)
                        cg = tb * 4 + c
                        Sb_w = Sb2[cg % 2]
                        Sb_r = Sb2[(cg + 1) % 2]
                        for hh in range(2):
                            pss, pssr = PS[2 + hh], R("ps", 2 + hh)
                            sl = (c % 4) * 128
                            P.add("pe", lambda e, wA=wA, wB=wB, pss=pss, sl=sl, hh=hh, cs=cs: e.matmul(
                                pss[:, sl:sl + 128], ki[:, hh, cs], qd[:, hh, cs], start=True, stop=True),
                                reads=[R("ki", hh), R("qd", hh)], writes=[pssr])
                            sT = sTb[hh]
                            P.add("dve", lambda e, wA=wA, wB=wB, pss=pss, sl=sl, sT=sT: e.tensor_tensor(
                                sT, pss[:, sl:sl + 128], trib, ALU.mult),
                                reads=[pssr, R("cstb")], writes=[R("sT", hh)])
                        for hh in range(2):
                            psp, pspr = ps_next([0, 1])
                            P.add("pe", lambda e, wA=wA, wB=wB, psp=psp, c=c, hh=hh: e.matmul(
                                psp[:, 0:256], ktok[:, c, hh, :], vtok[:, c, hh * 256:(hh + 1) * 256],
                                start=True, stop=True),
                                reads=[R("ktok", c), R("vtok", c)], writes=[pspr])
                            st = stmp[hh]
                            P.add("dve", lambda e, wA=wA, wB=wB, psp=psp, st=st, hh=hh: e.tensor_tensor(
                                st, psp[:, 0:256], Sf[:, hh, :], ALU.add),
                                reads=[pspr, R("Sf", hh)], writes=[R("stmp", hh)])
                            ecol = ebf[hh][:, c * 128 + 127:c * 128 + 128]
                            P.add("dve", lambda e, wA=wA, wB=wB, st=st, hh=hh, ecol=ecol: e.tensor_scalar(
                                Sf[:, hh, :], st, ecol, None, ALU.mult),
                                reads=[R("stmp", hh), R("eb", hh)], writes=[R("Sf", hh)])
                            P.add("act", lambda e, wA=wA, wB=wB, st=st, hh=hh, ecol=ecol, Sb_w=Sb_w: e.activation(
                                Sb_w[:, hh, :], st, AF.Copy, scale=ecol),
                                reads=[R("stmp", hh), R("eb", hh)], writes=[R("Sb", hh, cg % 2)])
                        for hh in range(2):
                            sT = sTb[hh]
                            for jj in range(2):
                                pot, potr = po[hh * 2 + jj]
                                P.add("pe", lambda e, wA=wA, wB=wB, pot=pot, c=c, hh=hh, jj=jj, sT=sT, cs=cs: e.matmul(
                                    pot[:, cs], vtok[:, c, hh * 256 + jj * 128:hh * 256 + (jj + 1) * 128], sT,
                                    start=True, stop=False),
                                    reads=[R("vtok", c), R("sT", hh)], writes=[potr])
                                P.add("pe", lambda e, wA=wA, wB=wB, pot=pot, hh=hh, jj=jj, cs=cs, Sb_r=Sb_r: e.matmul(
                                    pot[:, cs], Sb_r[:, hh, jj * 128:(jj + 1) * 128], qd[:, hh, cs],
                                    start=False, stop=True),
                                    reads=[R("Sb", hh, (cg + 1) % 2), R("qd", hh)], writes=[potr])
                    if g == 0 and tb == 0:
                        dump("f", 3, Sf[:, :, :].rearrange("p h f -> p (h f)"), [R("Sf", 0), R("Sf", 1)])
                    for hh in range(2):
                        pst, pstr = ps_next([0, 1])
                        for jj in range(2):
                            pot, potr = po[hh * 2 + jj]
                            sq = sqb[jj]
                            P.add("act", lambda e, wA=wA, wB=wB, pot=pot, sq=sq: e.activation(sq, pot[:], AF.Square),
                                  reads=[potr], writes=[R("g_sq", jj)])
                            P.add("pe", lambda e, wA=wA, wB=wB, pst=pst, sq=sq, jj=jj: e.matmul(
                                pst[:], onesb[:], sq, start=(jj == 0), stop=(jj == 1)),
                                reads=[R("g_sq", jj), R("onesb")], writes=[pstr])
                        lnt = tmpf[0]
                        rstd = tmpf[1]
                        P.add("act", lambda e, wA=wA, wB=wB, pst=pst, lnt=lnt: e.activation(lnt, pst[:], AF.Ln, bias=EPS, scale=1.0 / 256),
                              reads=[pstr], writes=[R("g_ln")])
                        P.add("act", lambda e, wA=wA, wB=wB, lnt=lnt, rstd=rstd: e.activation(rstd, lnt, AF.Exp, scale=-0.5),
                              reads=[R("g_ln")], writes=[R("g_rstd")])
                        for jj in range(2):
                            pot, potr = po[hh * 2 + jj]
                            t2 = ebf[jj]
                            P.add("dve", lambda e, wA=wA, wB=wB, pot=pot, jj=jj, rstd=rstd, t2=t2: e.scalar_tensor_tensor(
                                t2, pot[:], sm[:, SM_GNG + jj:SM_GNG + jj + 1], rstd, ALU.mult, ALU.mult),
                                reads=[potr, R("g_rstd"), R("sm"), R("eb", jj)], writes=[R("eb", jj)])
                            P.add("dve", lambda e, wA=wA, wB=wB, hh=hh, jj=jj, t2=t2: e.tensor_tensor(
                                og[:, hh * 2 + jj, :], t2, rg[:, hh * 2 + jj, :], ALU.mult),
                                reads=[R("eb", jj), R("rg", hh * 2 + jj)], writes=[R("og", hh * 2 + jj)])
                    if g == 0 and tb == 0:
                        dump("b", 5, og[:, :, :].rearrange("p j t -> p (j t)"), [R("og", j) for j in range(4)])
                        dump("f", 3, Sf[:, :, :].rearrange("p h f -> p (h f)"), [R("Sf", 0), R("Sf", 1)])
                    for dch in range(8):
                        ps, psr = ps_next([0, 1])
                        for j4 in range(4):
                            P.add("pe", lambda e, wA=wA, wB=wB, ps=ps, j4=j4, dch=dch: e.matmul(
                                ps[:], wB[:, 4096 + j4 * 1024 + dch * 128:4096 + j4 * 1024 + (dch + 1) * 128],
                                og[:, j4, :], start=(j4 == 0), stop=(j4 == 3)),
                                reads=[wBr, R("og", j4)], writes=[psr])
                        P.add("dve", lambda e, wA=wA, wB=wB, ps=ps, dch=dch, ts=ts: e.scalar_tensor_tensor(
                            xT[:, dch, ts], ps[:], gate[:, dch:dch + 1], xT[:, dch, ts], ALU.mult, ALU.add),
                            reads=[psr, R("modv"), R("x", dch, tb)], writes=[R("x", dch, tb)])
                w_issue()
                w_issue()

        def att():
            blk0 = BIDX[("att", 0)]
            P.fence()
            gate = modv[:, 48 + 16:48 + 24]
            o = 0
            qT = U[:, o:o + 4096].rearrange("p (h t) -> p h t", h=2); o += 4096
            kz = U[:, o:o + 8192].rearrange("p (h i t) -> p h i t", h=2, i=2); o += 8192
            vaug = U[:, o:o + 4096].rearrange("p (c h f) -> p c h f", c=16, h=2); o += 4096
            oT = U[:, o:o + 1024].rearrange("p (h t) -> p h t", h=2); o += 1024
            ocs = o
            Cb = Ub(o, 2048); o += 2048
            Sb_ = Ub(o, 2048); o += 2048
            pTb = [Ub(o + i * 512, 512) for i in range(4)]; o += 2048
            oc = o
            qnb = [Ub(o + i * 512, 512) for i in range(2)]; o += 1024
            t1f = [Uf(o + i * 1024, 512) for i in range(2)]; o += 2048
            t2f = [Uf(o + i * 1024, 512) for i in range(2)]; o += 2048
            o = oc
            c_t1 = Uf(o, 512); o += 1024
            c_r2 = Uf(o, 512); o += 1024
            c_t2 = Uf(o, 512); o += 1024
            c_d = Uf(o, 512); o += 1024
            c_ln = Uf(o, 512); o += 1024
            c_sq = Ub(o, 512); o += 512
            assert o <= UEND, o
            angf = Uf(0, 2048)
            kf = Uf(4096, 2048)
            ki32 = U[:, 8192:12288].bitcast(I32)

            lrow = lrow_t[0:1, :]
            lsum = lamv[0:1, 2:8]
            P.add("sp", lambda e: e.dma_start(out=lrow, in_=rw_d), writes=[R("lrow")], dma=True)
            P.add("dve", lambda e: e.tensor_tensor(lrow[0:1, 0:64], lrow[0:1, 0:64], lrow[0:1, 64:128], ALU.mult),
                  reads=[R("lrow")], writes=[R("lrow")])
            P.add("dve", lambda e: e.tensor_tensor(lrow[0:1, 128:192], lrow[0:1, 128:192], lrow[0:1, 192:256], ALU.mult),
                  reads=[R("lrow")], writes=[R("lrow")])
            P.add("dve", lambda e: e.tensor_reduce(
                lsum[0:1, 0:2], lrow.rearrange("p (a b) -> p a b", a=2)[:, :, 0:64], AX.X, ALU.add),
                reads=[R("lrow")], writes=[R("lamv")])
            P.add("act", lambda e: e.activation(lsum[0:1, 2:4], lsum[0:1, 0:2], AF.Exp),
                  reads=[R("lamv")], writes=[R("lamv")])
            P.add("dve", lambda e: e.scalar_tensor_tensor(
                lsum[0:1, 4:5], lsum[0:1, 3:4], -LAMBDA_INIT, lsum[0:1, 2:3], ALU.add, ALU.subtract),
                reads=[R("lamv")], writes=[R("lamv")])
            ps, psr = ps_next([0, 1])
            P.add("pe", lambda e, ps=ps: e.matmul(ps[:, 0:1], onesf[0:1, :], lsum[0:1, 4:5], start=True, stop=True),
                  reads=[R("lamv"), R("onesf")], writes=[psr])
            P.add("dve", lambda e, ps=ps: e.tensor_copy(lamv[:, 0:1], ps[:, 0:1]), reads=[psr, R("lamv")], writes=[R("lamv")])
            P.add("dve", lambda e: e.tensor_scalar(lamv[:, 1:2], sm[:, SM_SLG:SM_SLG + 1], 1.0 - LAMBDA_INIT, None, ALU.mult),
                  reads=[R("sm"), R("lamv")], writes=[R("lamv")])

            posi1 = U[0:1, 12288:16384].bitcast(I32)
            posf1 = U[0:1, ocs:ocs + 4096].bitcast(F32)
            P.add("sp", lambda e: e.dma_start(out=posi1.bitcast(F32), in_=pos_d), writes=[R("posi")], dma=True)
            P.add("dve", lambda e: e.tensor_copy(posf1, posi1), reads=[R("posi")], writes=[R("posf")])
            for tb in range(4):
                ts = slice(tb * 512, (tb + 1) * 512)
                ps, psr = ps_next([0, 1])
                P.add("pe", lambda e, ps=ps, ts=ts: e.matmul(ps[:], onesf[0:1, :], posf1[0:1, ts], start=True, stop=True),
                      reads=[R("posf"), R("onesf")], writes=[psr])
                P.add("dve", lambda e, ps=ps, ts=ts: e.tensor_scalar(angf[:, ts], ps[:], sm[:, SM_INVF:SM_INVF + 1], None, ALU.mult),
                      reads=[psr, R("sm")], writes=[R("angf")])
            dump('f', 1, angf, [R('angf')])
            dump('f', 6, posi1.bitcast(F32), [R('posi')], npart=1)
            dump('f', 7, posf1, [R('posf')], npart=1)
            for which, dst, off in ((0, Sb_, 0.0), (1, Cb, float(np.pi / 2))):
                if which == 1:
                    P.add("dve", lambda e, off=off: e.tensor_scalar(angf, angf, off, None, ALU.add),
                          reads=[R("angf")], writes=[R("angf")])
                P.add("dve", lambda e: e.tensor_scalar(kf, angf, 1.0 / TWO_PI, None, ALU.mult),
                      reads=[R("angf")], writes=[R("kf")])
                P.add("dve", lambda e: e.tensor_copy(ki32, kf), reads=[R("kf")], writes=[R("ki32")])
                P.add("dve", lambda e: e.tensor_copy(kf, ki32), reads=[R("ki32")], writes=[R("kf")])
                P.add("dve", lambda e: e.scalar_tensor_tensor(kf, kf, -TWO_PI, angf, ALU.mult, ALU.add),
                      reads=[R("kf"), R("angf")], writes=[R("kf")])
                P.add("dve", lambda e: e.tensor_scalar(kf, kf, float(np.pi), float(-np.pi), ALU.min, ALU.max),
                      reads=[R("kf")], writes=[R("kf")])
                P.add("act", lambda e, dst=dst: e.activation(dst, kf, AF.Sin), reads=[R("kf")], writes=[R("rope", which)])
                dump('f', 2 + which, kf, [R("kf")])
            P.add("dve", lambda e: e.memset(kz[64:128, :, 0, :], 0.0), writes=[R("kzz"), R("kf"), R("ki32"), R("angf"), R("posi")])
            P.add("dve", lambda e: e.memset(kz[0:64, :, 1, :], 0.0), writes=[R("kzz"), R("kf"), R("ki32"), R("angf"), R("posi")])

            ALV = int(os.environ.get('ATT_LEVEL', '9'))

            def proj_tmp_res():
                return [R("qn", 0), R("qn", 1), R("t1", 0), R("t1", 1), R("t2", 0), R("t2", 1)]

            def comb_tmp_res():
                return [R("c_t1"), R("c_r2"), R("c_t2"), R("c_d"), R("c_ln"), R("c_sq")]

            for pr in range(4 if ALV > 0 else 0):
                wt, wr = w_slot(blk0 + pr)
                groups = [(nm, coff, hh, tb) for (nm, coff) in (("qT", 0), ("kT", 256)) for hh in range(2) for tb in range(4)]
                gstate = {}

                def stageA(gx):
                    nm, coff, hh, tb = groups[gx]
                    ts = slice(tb * 512, (tb + 1) * 512)
                    ps, psr = PS[gx % 2], R("ps", gx % 2)
                    for k in range(8):
                        P.add("pe", lambda e, wt=wt, ps=ps, k=k, hh=hh, ts=ts, coff=coff: e.matmul(
                            ps[:], wt[:, k * 768 + coff + hh * 128:k * 768 + coff + (hh + 1) * 128], hT[:, k, ts],
                            start=(k == 0), stop=(k == 7)), reads=[wr, R("h", k, tb)], writes=[psr])
                    qn = qnb[gx % 2]
                    P.add("act", lambda e, ps=ps, qn=qn: e.activation(qn, ps[:], AF.Copy),
                          reads=[psr], writes=[R("qn", gx % 2)])

                def stageB(gx):
                    nm, coff, hh, tb = groups[gx]
                    ts = slice(tb * 512, (tb + 1) * 512)
                    i2 = gx % 2
                    ps, psr = PS[gx % 2], R("ps", gx % 2)
                    qn = qnb[i2]
                    ps2, ps2r = PS[2 + i2], R("ps", 2 + i2)
                    P.add("pe", lambda e, ps2=ps2, qn=qn: e.matmul(ps2[:], permb, qn, start=True, stop=True),
                          reads=[R("qn", i2), R("cstb")], writes=[ps2r])
                    t1 = t1f[i2]
                    t2 = t2f[i2]
                    P.add("dve", lambda e, ps=ps, t1=t1, ts=ts: e.tensor_tensor(t1, ps[:], Cb[:, ts], ALU.mult),
                          reads=[psr, R("rope", 1)], writes=[R("t1", i2)])
                    P.add("dve", lambda e, ps2=ps2, t2=t2, ts=ts: e.tensor_tensor(t2, ps2[:], Sb_[:, ts], ALU.mult),
                          reads=[ps2r, R("rope", 0)], writes=[R("t2", i2)])
                    if nm == "qT":
                        P.add("pool", lambda e, hh=hh, ts=ts, t1=t1, t2=t2: e.tensor_tensor(
                            qT[:, hh, ts], t1, t2, ALU.add),
                            reads=[R("t1", i2), R("t2", i2)], writes=[R("qT", hh, tb)])
                    else:
                        for i_ in range(2):
                            prt_ = slice(64 * i_, 64 * i_ + 64)
                            P.add("pool", lambda e, hh=hh, ts=ts, t1=t1, t2=t2, i_=i_, prt_=prt_: e.tensor_tensor(
                                kz[prt_, hh, i_, ts], t1[prt_, :], t2[prt_, :], ALU.add),
                                reads=[R("t1", i2), R("t2", i2), R("kzz")], writes=[R("kT", hh, tb)])

                for gx in range(len(groups) + 1):
                    if gx < len(groups):
                        stageA(gx)
                    if gx >= 1:
                        stageB(gx - 1)
                for tt in range(16 if not os.environ.get("SKIPV") else 0):
                    tok = slice(tt * 128, (tt + 1) * 128)
                    ps, psr = ps_next([0, 1])
                    for k in range(8):
                        P.add("pe", lambda e, wt=wt, ps=ps, k=k, tok=tok: e.matmul(
                            ps[:, 0:256], hT[:, k, tok], wt[:, k * 768 + 512:k * 768 + 768],
                            start=(k == 0), stop=(k == 7)), reads=[wr, R("h", k, tt // 4)], writes=[psr])
                    P.add("dve", lambda e, wt=wt, ps=ps, tt=tt: e.tensor_copy(
                        vaug[:, tt, :, :], ps[:, 0:256].rearrange("p (h f) -> p h f", h=2)),
                        reads=[psr], writes=[R("vaug", tt)])
                P.alias_barrier(proj_tmp_res(), comb_tmp_res())
                its = []
                gi = 0
                for j in range(4):
                    for hh in range(2):
                        for i in range(2):
                            for kt in range(4 * j + 4):
                                its.append((hh, j, i, kt, gi))
                            gi += 1
                LA = 2
                SB = [1, 2, 3]
                deferred = []

                def emit_qk(idx):
                    hh, j, i, kt, g_ = its[idx]
                    r = kt - 4 * j
                    n0 = 128 * r if r > 0 else 0
                    ps, psr = PS[SB[idx % 3]], R("ps", SB[idx % 3])
                    P.add("pe", lambda e, ps=ps, i=i, hh=hh, kt=kt, j=j, n0=n0, r=r: e.matmul(
                        ps[:, n0:512], kz[:, hh, i, kt * 128:(kt + 1) * 128],
                        qT[:, hh, j * 512 + n0:(j + 1) * 512], start=True, stop=(r < 0)),
                        reads=[R("kT", hh, kt // 4), R("qT", hh, j), R("kzz")], writes=[psr])
                    if r >= 0:
                        P.add("pe", lambda e, ps=ps, n0=n0: e.matmul(
                            ps[:, n0:n0 + 128], identb, negmb, start=False, stop=True),
                            reads=[R("cstb")], writes=[psr])
                    pT = pTb[idx % 3]
                    pTr = R("pT", idx % 3)
                    P.add("act", lambda e, ps=ps, pT=pT, n0=n0: e.activation(
                        pT[:, n0:512], ps[:, n0:512], AF.Exp, scale=0.125),
                        reads=[psr], writes=[pTr])

                def emit_pv(idx):
                    hh, j, i, kt, g_ = its[idx]
                    r = kt - 4 * j
                    n0 = 128 * r if r > 0 else 0
                    pT = pTb[idx % 3]
                    pTr = R("pT", idx % 3)
                    ao, aor = PS[4 + 2 * i], R("ps", 4 + 2 * i)
                    al, alr = PS[5 + 2 * i], R("ps", 5 + 2 * i)
                    last = (kt == 4 * j + 3)
                    P.add("pe", lambda e, ao=ao, pT=pT, kt=kt, hh=hh, n0=n0, last=last: e.matmul(
                        ao[:, n0:512], vaug[:, kt, hh, :], pT[:, n0:512], start=(kt == 0), stop=last),
                        reads=[pTr, R("vaug", kt)], writes=[aor])
                    P.add("pe", lambda e, al=al, pT=pT, kt=kt, n0=n0, last=last: e.matmul(
                        al[:, n0:512], onesb[:], pT[:, n0:512], start=(kt == 0), stop=last),
                        reads=[pTr, R("onesb")], writes=[alr])
                    if not last:
                        return
                    qs = slice(j * 512, (j + 1) * 512)
                    if i == 0:
                        P.add("dve", lambda e, al=al: e.reciprocal(c_r2, al[:]), reads=[alr], writes=[R("c_r2")])
                        P.add("dve", lambda e, ao=ao: e.tensor_tensor(c_t1, ao[:], c_r2, ALU.mult),
                              reads=[aor, R("c_r2")], writes=[R("c_t1")])
                        return
                    P.add("dve", lambda e, al=al: e.reciprocal(c_r2, al[:]), reads=[alr], writes=[R("c_r2")])
                    P.add("dve", lambda e, ao=ao: e.scalar_tensor_tensor(c_t2, ao[:], lamv[:, 0:1], c_r2, ALU.mult, ALU.mult),
                          reads=[aor, R("c_r2"), R("lamv")], writes=[R("c_t2")])
                    P.add("pool", lambda e: e.tensor_tensor(c_d, c_t1, c_t2, ALU.add),
                          reads=[R("c_t1"), R("c_t2")], writes=[R("c_d")])

                    def epi1a():
                        P.add("act", lambda e: e.activation(c_sq, c_d, AF.Square), reads=[R("c_d")], writes=[R("c_sq")])

                    def epi1b():
                        pss, pssr = PS[0], R("ps", 0)
                        P.add("pe", lambda e, pss=pss: e.matmul(pss[:], onesb[:], c_sq, start=True, stop=True),
                              reads=[R("c_sq"), R("onesb")], writes=[pssr])

                    def epi1c(hh=hh):
                        pss, pssr = PS[0], R("ps", 0)
                        P.add("act", lambda e, pss=pss: e.activation(c_ln, pss[:], AF.Ln, bias=EPS, scale=1.0 / 128),
                              reads=[pssr], writes=[R("c_ln")])
                        P.add("act", lambda e: e.activation(c_ln, c_ln, AF.Exp, scale=-0.5),
                              reads=[R("c_ln")], writes=[R("c_ln")])
                        P.add("dve", lambda e, hh=hh: e.scalar_tensor_tensor(
                            oT[:, hh, :], c_d, lamv[:, 1:2], c_ln, ALU.mult, ALU.mult),
                            reads=[R("c_d"), R("c_ln"), R("lamv")], writes=[R("oT", hh)])

                    def epi2_one(dch, j=j, qs=qs):
                        psw, pswr = PS[0], R("ps", 0)
                        for h2 in range(2):
                            P.add("pe", lambda e, wt=wt, psw=psw, h2=h2, dch=dch: e.matmul(
                                psw[:], wt[:, 6144 + h2 * 1024 + dch * 128:6144 + h2 * 1024 + (dch + 1) * 128],
                                oT[:, h2, :], start=(h2 == 0), stop=(h2 == 1)),
                                reads=[wr, R("oT", h2)], writes=[pswr])
                        P.add("dve", lambda e, psw=psw, dch=dch, qs=qs: e.scalar_tensor_tensor(
                            xT[:, dch, qs], psw[:], gate[:, dch:dch + 1], xT[:, dch, qs], ALU.mult, ALU.add),
                            reads=[pswr, R("modv"), R("x", dch, j)], writes=[R("x", dch, j)])

                    step_now = idx + LA
                    defer(step_now + 10, epi1a)
                    defer(step_now + 12, epi1b)
                    defer(step_now + 14, epi1c)
                    if hh == 1:
                        for dch in range(8):
                            defer(step_now + 18 + 2 * dch, lambda dch=dch: epi2_one(dch))

                dseq = [0]

                def defer(due, fn):
                    dseq[0] += 1
                    deferred.append((due, dseq[0], fn))
                    deferred.sort(key=lambda t: (t[0], t[1]))

                def run_deferred(step, flush=False):
                    while deferred and (flush or deferred[0][0] <= step):
                        deferred.pop(0)[2]()

                for step in range(len(its) + LA):
                    if step < len(its):
                        emit_qk(step)
                    if step >= LA:
                        idx_ = step - LA
                        hh_, j_, i_, kt_, g_ = its[idx_]
                        if i_ == 1 and kt_ == 4 * j_ + 3:
                            run_deferred(step, flush=True)
                        emit_pv(idx_)
                    run_deferred(step)
                run_deferred(0, flush=True)
                P.alias_barrier(comb_tmp_res(), proj_tmp_res())
                w_issue()

        P.fence()
        for nb_ in range(6):
            adaln_block(0, nb_)
        adaln_finish(0)
        norm(dv[:, 0:8], modv[:, 0:8])
        gla()
        if stage >= 2:
            norm(dv[:, 8:16], modv[:, 24:32])
            mlp(0)
        if stage >= 3:
            adaln_finish(1)
            norm(dv[:, 0:8], modv[:, 48:56], fence=False)
            att()
        if stage >= 4:
            norm(dv[:, 8:16], modv[:, 48 + 24:48 + 32])
            mlp(1)
            norm(sm[:, SM_FG:SM_FG + 8], None, final=True, fence=False)
        ov = out_d.rearrange("(k p) t -> p k t", p=128)
        outr = []
        for k in range(8):
            rr = R("out", k)
            outr.append(rr)
            P.add("sp", lambda e, k=k: e.dma_start(out=ov[:, k, :], in_=xT[:, k, :]),
                  reads=[R("x", k, tb) for tb in range(4)], writes=[rr], dma=True)
        P.add("sp", lambda e: None, reads=outr, noinst=True)
        P.emit(sems, dsems)
    return nc


def _kmaj(Wc):
    K = Wc.shape[0] // 128
    return np.ascontiguousarray(Wc.reshape(K, 128, -1).transpose(1, 0, 2)).reshape(128, -1)


def _prep_shared(inp):
    f = np.float32
    blocks = np.zeros((NBLK, 128, 8192), f)
    ada_w = np.asarray(inp["ada_w"], f)
    mlp_w1 = np.asarray(inp["mlp_w1"], f)
    mlp_w2 = np.asarray(inp["mlp_w2"], f)
    gw = np.asarray(inp["gla_w_in"], f)[0]
    gwo = np.asarray(inp["gla_w_o"], f)[0]
    dw = np.asarray(inp["diff_w_in"], f)[0]
    dwo = np.asarray(inp["diff_w_o"], f)[0]
    for bi, name in enumerate(BLOCKS):
        kind = name[0]
        if kind == "ada":
            _, l, nb_ = name
            blocks[bi] = _kmaj(ada_w[l][:, nb_ * 1024:(nb_ + 1) * 1024])
        elif kind == "glaA":
            g = name[1]
            cols = np.concatenate([np.arange(2 * g * 128, 2 * g * 128 + 256),
                                   512 + np.arange(2 * g * 128, 2 * g * 128 + 256),
                                   1024 + np.arange(2 * g * 256, 2 * g * 256 + 512)])
            blocks[bi] = _kmaj(gw[:, cols])
        elif kind == "glaB":
            g = name[1]
            rc = 2048 + np.arange(2 * g * 256, 2 * g * 256 + 512)
            blocks[bi][:, 0:4096] = _kmaj(gw[:, rc])
            blocks[bi][:, 4096:8192] = _kmaj(gwo[g * 512:(g + 1) * 512, :])
        elif kind == "w1":
            _, l, g = name
            blocks[bi] = _kmaj(mlp_w1[l][:, g * 1024:(g + 1) * 1024])
        elif kind == "w2":
            _, l, g = name
            blocks[bi] = _kmaj(mlp_w2[l][g * 1024:(g + 1) * 1024, :])
        elif kind == "att":
            pr = name[1]
            cols = np.concatenate([np.arange(pr * 256, pr * 256 + 256),
                                   1024 + np.arange(pr * 256, pr * 256 + 256),
                                   2048 + np.arange(pr * 256, pr * 256 + 256)])
            blocks[bi][:, 0:6144] = _kmaj(dw[:, cols])
            blocks[bi][:, 6144:8192] = _kmaj(dwo[pr * 256:(pr + 1) * 256, :])

    def pk(v):
        return np.asarray(v, f).reshape(-1, 128).T

    sm = np.zeros((128, NSM), f)
    ng = np.asarray(inp["norm_g"], f)
    for l in range(2):
        for j in range(2):
            sm[:, SM_NG + (2 * l + j) * 8:SM_NG + (2 * l + j) * 8 + 8] = pk(ng[l, j])
    sm[:, SM_FG:SM_FG + 8] = pk(inp["final_g"])
    ab = np.asarray(inp["ada_b"], f)
    for l in range(2):
        sm[:, SM_AB + l * 48:SM_AB + (l + 1) * 48] = pk(ab[l])
    sm[:, SM_BR:SM_BR + 8] = pk(np.asarray(inp["gla_b_r"], f)[0])
    sm[:, SM_GNG:SM_GNG + 2] = pk(np.asarray(inp["gla_norm_g"], f)[0])
    sm[:, SM_SLG:SM_SLG + 1] = pk(np.asarray(inp["diff_subln_g"], f)[0])
    inv_freq = (np.float32(500000.0) ** (-(np.arange(0, 16, 2, dtype=np.float32)) / np.float32(16))).astype(f)
    invf = np.zeros(128, f)
    for p in range(128):
        d = p % 64
        if d < 16:
            invf[p] = inv_freq[d % 8]
    sm[:, SM_INVF] = invf

    rw = np.asarray(inp["diff_lambda"], f)[0].reshape(1, 256)
    wa = np.zeros((128, 8, 48), f)
    wa[:, :, 32:48] = gw[:, 3072:3088].reshape(8, 128, 16).transpose(1, 0, 2)
    wa = wa.reshape(128, 8 * 48)
    wa2 = np.zeros((48, 512), f)
    wa2[0] = np.asarray(inp["gla_b_a"], f)[0]
    wa2[32:48] = np.asarray(inp["gla_w_a2"], f)[0]
    cst = np.zeros((128, 512), f)
    cst[:, 0:128] = np.eye(128, dtype=f)
    tri = (np.arange(128)[None, :] >= np.arange(128)[:, None]).astype(f)
    cst[:, 128:256] = tri
    cst[:, 256:384] = tri * f(-1.0 / 16.0)
    perm = np.zeros((128, 128), f)
    for b_ in range(2):
        base = 64 * b_
        for d in range(8):
            perm[base + d + 8, base + d] = -1.0
        for d in range(8, 16):
            perm[base + d - 8, base + d] = 1.0
    cst[:, 384:512] = perm
    return dict(wblk=blocks, sm=sm, rw=rw, wa=wa, wa2=wa2, cst=cst)


_NC_CACHE = {}


def kernel(x, c, positions, ada_w, ada_b, norm_g, mlp_w1, mlp_w2,
           gla_w_in, gla_w_a2, gla_b_a, gla_b_r, gla_norm_g, gla_w_o,
           diff_w_in, diff_lambda, diff_subln_g, diff_w_o, final_g, _stage=4, _debug=False, _ncores=NB):
    inp = dict(ada_w=ada_w, ada_b=ada_b, norm_g=norm_g, mlp_w1=mlp_w1, mlp_w2=mlp_w2,
               gla_w_in=gla_w_in, gla_w_a2=gla_w_a2, gla_b_a=gla_b_a, gla_b_r=gla_b_r,
               gla_norm_g=gla_norm_g, gla_w_o=gla_w_o, diff_w_in=diff_w_in, diff_lambda=diff_lambda,
               diff_subln_g=diff_subln_g, diff_w_o=diff_w_o, final_g=final_g)
    sh = _prep_shared(inp)
    x = np.asarray(x, np.float32)
    c = np.asarray(c, np.float32)
    positions = np.asarray(positions, np.int32)
    in_maps = []
    for b in range(NB):
        sm = sh["sm"].copy()
        sm[:, SM_C:SM_C + 8] = c[b].reshape(8, 128).T
        in_maps.append(dict(xT=np.ascontiguousarray(x[b].T), pos=np.ascontiguousarray(positions[b:b + 1]).view(np.float32),
                            sm=sm, rw=sh["rw"], wa=sh["wa"], wa2=sh["wa2"], cst=sh["cst"], wblk=sh["wblk"]))
    key = (_stage, _debug)
    if key not in _NC_CACHE:
        _NC_CACHE[key] = build_nc(_stage, _debug)
    nc = _NC_CACHE[key]
    res = run_bass_kernel_spmd(nc, in_maps[:_ncores], core_ids=list(range(_ncores)))
    out = np.stack([np.ascontiguousarray(r["outT"].T) for r in res.results], axis=0)
    if _debug:
        return out.astype(np.float32), res.results[0]["dbgf"], res.results[0]["dbgb"]
    return out.astype(np.float32)
```

```python
import contextlib
import os
import math
import numpy as np
import concourse.bass as bass
import concourse.mybir as mybir
from concourse.bass_utils import run_bass_kernel_spmd

F32 = mybir.dt.float32
BF16 = mybir.dt.bfloat16
I32 = mybir.dt.int32
AF = mybir.ActivationFunctionType
ALU = mybir.AluOpType
AX = mybir.AxisListType

D = 1024
T = 2048
NB = 8
DFF = 4096
EPS = 1e-6
LAMBDA_INIT = 0.8 - 0.6 * math.exp(-0.3 * 1)
NBLK = 36
ADA1_AFTER_GROUP = [2, 2, 2, 0]


def _block_order():
    order = [("ada", 0, n) for n in range(6)]
    for g in range(2):
        order += [("glaA", g), ("glaB", g)]
    n1 = 0
    for g in range(4):
        order += [("w1", 0, g), ("w2", 0, g)]
        for _ in range(ADA1_AFTER_GROUP[g]):
            order.append(("ada", 1, n1)); n1 += 1
    order += [("att", p) for p in range(4)]
    for g in range(4):
        order += [("w1", 1, g), ("w2", 1, g)]
    assert len(order) == NBLK and n1 == 6
    return order


BLOCKS = _block_order()
BIDX = {b: i for i, b in enumerate(BLOCKS)}
NSLOT = 3
TWO_PI = float(2.0 * np.pi)

SM_NG = 0
SM_FG = 32
SM_AB = 40
SM_BR = 136
SM_GNG = 144
SM_SLG = 146
SM_INVF = 147
SM_C = 148
NSM = 160


class Res:
    __slots__ = ("w", "r", "excl")

    def __init__(self):
        self.w = None
        self.r = []
        self.excl = False


class Op:
    __slots__ = ("eng", "fn", "deps", "eidx", "signal", "seq", "dma", "dsem", "dval")


class Prog:
    NDMA = 24

    def __init__(self, nc, same_engine_sync=True):
        self.nc = nc
        self.engs = {"pe": nc.tensor, "act": nc.scalar, "dve": nc.vector,
                     "pool": nc.gpsimd, "sp": nc.sync}
        self.ops = []
        self.ecount = {k: 0 for k in self.engs}
        self.known = {k: {} for k in self.engs}
        self.kdma = {k: {} for k in self.engs}
        self.ndma = {"sp": 0, "pool": 0}
        self.dma_ops = {"sp": [], "pool": []}
        self.dbase = {"sp": 0, "pool": self.NDMA // 2}
        self.same_engine_sync = same_engine_sync
        self.res = {}
        self.last = {}

    def R(self, *key):
        r = self.res.get(key)
        if r is None:
            r = Res()
            r.excl = (key[0] == "ps")
            self.res[key] = r
        return r

    def alias_barrier(self, olds, news):
        pend = []
        for o in olds:
            if o.w is not None:
                pend.append(o.w)
            pend.extend(o.r)
        for n in news:
            n.r.extend(pend)

    def fence(self):
        last = dict(self.last)
        for eng in self.engs:
            self.add(eng, lambda e: None, extra=[o for k, o in last.items() if k != eng], noinst=True)

    def add(self, eng, fn, reads=(), writes=(), dma=False, extra=(), noinst=False):
        op = Op()
        op.eng = eng
        op.fn = fn
        op.dma = dma
        op.signal = False
        op.seq = None
        op.eidx = self.ecount[eng]
        self.ecount[eng] += 1
        if any(r.excl for r in reads):
            writes = list(writes) + [r for r in reads if r.excl]
            reads = [r for r in reads if not r.excl]
        deps = list(extra)
        for r in reads:
            if r.w is not None:
                deps.append(r.w)
        for r in writes:
            if r.w is not None:
                deps.append(r.w)
            deps.extend(r.r)
        if dma:
            i = self.ndma[eng]
            self.ndma[eng] += 1
            h = self.NDMA // 2
            op.dsem = self.dbase[eng] + i % h
            op.dval = 16 * (i // h + 1)
            if i >= h:
                deps.append(self.dma_ops[eng][i - h])
            self.dma_ops[eng].append(op)
        best = {}
        for d in deps:
            if d is op:
                continue
            if d.dma:
                if self.kdma[eng].get(d.dsem, 0) >= d.dval:
                    continue
                key = ("dma", d.dsem)
                if key not in best or best[key].dval < d.dval:
                    best[key] = d
            else:
                if d.eng == eng and (eng == "pe" or not self.same_engine_sync):
                    continue
                if self.known[eng].get(d.eng, -1) >= d.eidx:
                    continue
                key = d.eng
                if key not in best or best[key].eidx < d.eidx:
                    best[key] = d
        final = []
        for key, d in best.items():
            final.append(d)
            if d.dma:
                self.kdma[eng][d.dsem] = d.dval
            else:
                d.signal = True
                self.known[eng][d.eng] = d.eidx
        op.deps = final
        for r in reads:
            r.r.append(op)
        for r in writes:
            r.w = op
            r.r = []
        self.ops.append(op)
        if not dma and not noinst:
            self.last[eng] = op
        return op

    def emit(self, sems, dsems):
        cnt = {k: 0 for k in self.engs}
        for op in self.ops:
            if not op.dma and op.signal:
                cnt[op.eng] += 1
                op.seq = cnt[op.eng]
        for op in self.ops:
            e = self.engs[op.eng]
            for d in op.deps:
                if d.dma:
                    e.wait_ge(dsems[d.dsem], d.dval)
                else:
                    e.wait_ge(sems[d.eng], d.seq)
            ins = op.fn(e)
            if ins is None:
                continue
            if op.dma:
                ins.then_inc(dsems[op.dsem], 16)
            elif op.signal:
                ins.then_inc(sems[op.eng], 1)
        return cnt


def build_nc(stage=4, debug=False):
    nc = bass.Bass("TRN2", target_bir_lowering=False)
    xT_d = nc.dram_tensor("xT", [D, T], F32, kind="ExternalInput").ap()
    pos_d = nc.dram_tensor("pos", [1, T], F32, kind="ExternalInput").ap()
    sm_d = nc.dram_tensor("sm", [128, NSM], F32, kind="ExternalInput").ap()
    rw_d = nc.dram_tensor("rw", [1, 256], F32, kind="ExternalInput").ap()
    wa_d = nc.dram_tensor("wa", [128, 8 * 48], F32, kind="ExternalInput").ap()
    wa2_d = nc.dram_tensor("wa2", [48, 512], F32, kind="ExternalInput").ap()
    cst_d = nc.dram_tensor("cst", [128, 512], F32, kind="ExternalInput").ap()
    wblk_d = nc.dram_tensor("wblk", [NBLK, 128, 8192], F32, kind="ExternalInput").ap()
    out_d = nc.dram_tensor("outT", [D, T], F32, kind="ExternalOutput").ap()
    if debug:
        dbgf_d = nc.dram_tensor("dbgf", [16, 128, 2048], F32, kind="ExternalOutput").ap()
        dbgb_d = nc.dram_tensor("dbgb", [24, 128, 2048], BF16, kind="ExternalOutput").ap()

    es = contextlib.ExitStack()
    with es:
        def sb(name, shape, dt):
            return es.enter_context(nc.sbuf_tensor(name, shape, dt))

        xT = sb("xTs", [128, 8, T], F32)
        hT = sb("hTs", [128, 8, T], BF16)
        wsl = [sb("wsl%d" % i, [128, 8192], BF16) for i in range(NSLOT)]
        cstf = sb("cstf", [128, 512], F32)
        cstb = sb("cstb", [128, 512], BF16)
        onesb = sb("onesb", [128, 128], BF16)
        onesf = sb("onesf", [128, 128], F32)
        sm = sb("sms", [128, NSM], F32)
        modv = sb("modv", [128, 96], F32)
        dv = sb("dvs", [128, 64], F32)
        cact = sb("cact", [128, 8], BF16)
        lamv = sb("lamv", [128, 8], F32)
        lrow_t = sb("lrow", [1, 256], F32)
        U = sb("U", [128, 29400], BF16)
        PS = [es.enter_context(nc.psum_tensor("ps%d" % i, [128, 512], F32)) for i in range(8)]
        sems = {k: es.enter_context(nc.semaphore("s_" + k)) for k in ["pe", "act", "dve", "pool", "sp"]}
        dsems = [es.enter_context(nc.semaphore("d%d" % i)) for i in range(Prog.NDMA)]

        P = Prog(nc)
        R = P.R

        identf = cstf[:, 0:128]
        trinegf = cstf[:, 256:384]
        identb = cstb[:, 0:128]
        trib = cstb[:, 128:256]
        permb = cstb[:, 384:512]
        negmb = cstb[:, 256:384]

        def Ub(off, n):
            return U[:, off:off + n]

        def Uf(off, n):
            return U[:, off:off + 2 * n].bitcast(F32)

        def dump(kind, idx, ap, reads, npart=128):
            if not debug:
                return
            n = ap.shape[-1]
            dst = (dbgf_d if kind == "f" else dbgb_d)[idx, 0:npart, 0:n]
            P.add("sp", lambda e: e.dma_start(out=dst, in_=ap), reads=reads, writes=[R("dbg", kind, idx)], dma=True)

        wstate = {"next": 0}

        def w_issue():
            i = wstate["next"]
            if i >= NBLK:
                return
            wstate["next"] = i + 1
            s = i % NSLOT
            P.add("pool", lambda e, i=i, s=s: e.dma_start(out=wsl[s][:], in_=wblk_d[i]),
                  writes=[R("w", s)], dma=True)

        def w_slot(i):
            return wsl[i % NSLOT], R("w", i % NSLOT)

        prot = {"i": 0}

        def ps_next(banks):
            b = banks[prot["i"] % len(banks)]
            prot["i"] += 1
            return PS[b], R("ps", b)

        P.add("sp", lambda e: e.dma_start(out=sm[:], in_=sm_d), writes=[R("sm")], dma=True)
        P.add("sp", lambda e: e.dma_start(out=cstf[:], in_=cst_d), writes=[R("cstf")], dma=True)
        for i in range(NSLOT):
            w_issue()
        xv = xT_d.rearrange("(k p) t -> p k t", p=128)
        for k in range(8):
            P.add("sp", lambda e, k=k: e.dma_start(out=xT[:, k, :], in_=xv[:, k, :]),
                  writes=[R("x", k, tb) for tb in range(4)], dma=True)
        P.add("dve", lambda e: e.tensor_copy(cstb[:], cstf[:]), reads=[R("cstf")], writes=[R("cstb")])
        P.add("dve", lambda e: e.tensor_scalar(cstb[:, 256:384], cstf[:, 128:256], -1.0, 30000.0, ALU.add, ALU.mult),
              reads=[R("cstf"), R("cstb")], writes=[R("cstb")])
        P.add("dve", lambda e: e.memset(onesb[:], 1.0), writes=[R("onesb")])
        P.add("dve", lambda e: e.memset(onesf[:], 1.0), writes=[R("onesf")])
        P.add("act", lambda e: e.activation(cact[:], sm[:, SM_C:SM_C + 8], AF.Silu),
              reads=[R("sm")], writes=[R("cact")])

        def adaln_block(l, nb_):
            modrow = U[0:1, 18432:20480].bitcast(F32)
            wt, wr = w_slot(BIDX[("ada", l, nb_)])
            for half in range(2):
                n0 = half * 512
                ps, psr = ps_next([0, 1])
                for k in range(8):
                    P.add("pe", lambda e, ps=ps, wt=wt, k=k, n0=n0: e.matmul(
                        ps[0:1, :], cact[:, k:k + 1], wt[:, k * 1024 + n0:k * 1024 + n0 + 512],
                        start=(k == 0), stop=(k == 7)),
                        reads=[wr, R("cact")], writes=[psr])
                P.add("act", lambda e, ps=ps, n0=n0: e.activation(modrow[0:1, n0:n0 + 512], ps[0:1, :], AF.Copy),
                      reads=[psr], writes=[R("modrow")])
            w_issue()
            ps, psr = ps_next([0, 1])
            for m in range(8):
                P.add("pe", lambda e, ps=ps, m=m: e.matmul(
                    ps[:, m:m + 1], modrow[0:1, m * 128:(m + 1) * 128], onesf[0:1, 0:1], start=True, stop=True),
                    reads=[R("modrow"), R("onesf")], writes=[psr])
            c0 = l * 48 + nb_ * 8
            P.add("dve", lambda e, ps=ps, c0=c0: e.tensor_tensor(
                modv[:, c0:c0 + 8], ps[:, 0:8], sm[:, SM_AB + c0:SM_AB + c0 + 8], ALU.add),
                reads=[psr, R("sm")], writes=[R("modv")])

        def adaln_finish(l):
            mo = l * 48
            P.add("dve", lambda e: e.scalar_tensor_tensor(
                dv[:, 0:8], modv[:, mo + 8:mo + 16], 1.0, sm[:, SM_NG + (2 * l) * 8:SM_NG + (2 * l) * 8 + 8],
                ALU.add, ALU.mult), reads=[R("modv"), R("sm")], writes=[R("dv")])
            P.add("dve", lambda e: e.scalar_tensor_tensor(
                dv[:, 8:16], modv[:, mo + 32:mo + 40], 1.0, sm[:, SM_NG + (2 * l + 1) * 8:SM_NG + (2 * l + 1) * 8 + 8],
                ALU.add, ALU.mult), reads=[R("modv"), R("sm")], writes=[R("dv")])

        NO = 29400 - 5120
        UEND = 29400

        def norm_tb(tb, gs_ap, sh_ap, final=False):
            ts = slice(tb * 512, (tb + 1) * 512)
            pss, pssr = ps_next([0, 1])
            for k in range(8):
                sq = Ub(NO + (k % 2) * 512, 512)
                sqr = R("n_sq", k % 2)
                P.add("act", lambda e, sq=sq, k=k, ts=ts: e.activation(sq, xT[:, k, ts], AF.Square),
                      reads=[R("x", k, tb)], writes=[sqr])
                P.add("pe", lambda e, pss=pss, sq=sq, k=k: e.matmul(pss[:], onesb[:], sq, start=(k == 0), stop=(k == 7)),
                      reads=[sqr, R("onesb")], writes=[pssr])
            lnt = Uf(NO + 1024, 512)
            rstd = Uf(NO + 2048, 512)
            P.add("act", lambda e, pss=pss, lnt=lnt: e.activation(lnt, pss[:], AF.Ln, bias=EPS, scale=1.0 / D),
                  reads=[pssr], writes=[R("n_ln")])
            P.add("act", lambda e, lnt=lnt, rstd=rstd: e.activation(rstd, lnt, AF.Exp, scale=-0.5),
                  reads=[R("n_ln")], writes=[R("n_rstd")])
            for k in range(8):
                if final:
                    P.add("dve", lambda e, k=k, ts=ts, rstd=rstd: e.scalar_tensor_tensor(
                        xT[:, k, ts], xT[:, k, ts], gs_ap[:, k:k + 1], rstd, ALU.mult, ALU.mult),
                        reads=[R("n_rstd"), R("sm")], writes=[R("x", k, tb)])
                else:
                    tmp = Uf(NO + 3072 + (k % 2) * 1024, 512)
                    tr = R("n_tmp", k % 2)
                    P.add("dve", lambda e, k=k, ts=ts, rstd=rstd, tmp=tmp: e.scalar_tensor_tensor(
                        tmp, xT[:, k, ts], gs_ap[:, k:k + 1], rstd, ALU.mult, ALU.mult),
                        reads=[R("n_rstd"), R("x", k, tb), R("dv")], writes=[tr])
                    P.add("act", lambda e, k=k, ts=ts, tmp=tmp: e.activation(
                        hT[:, k, ts], tmp, AF.Identity, bias=sh_ap[:, k:k + 1]),
                        reads=[tr, R("modv")], writes=[R("h", k, tb)])

        def norm(gs_ap, sh_ap, final=False, fence=True):
            if fence:
                P.fence()
            for tb in range(4):
                norm_tb(tb, gs_ap, sh_ap, final)

        def mlp(l, after_tb=None):
            mo = l * 48
            gate = modv[:, mo + 40:mo + 48]
            hid = U[:, 0:16384].rearrange("p (j t) -> p j t", j=8)
            for g in range(4):
                w1, w1r = w_slot(BIDX[("w1", l, g)])
                for tb in range(4):
                    ts = slice(tb * 512, (tb + 1) * 512)
                    for j in range(8):
                        ps, psr = ps_next([0, 1, 2, 3])
                        for k in range(8):
                            P.add("pe", lambda e, ps=ps, w1=w1, k=k, j=j, ts=ts: e.matmul(
                                ps[:], w1[:, k * 1024 + j * 128:k * 1024 + (j + 1) * 128], hT[:, k, ts],
                                start=(k == 0), stop=(k == 7)),
                                reads=[w1r, R("h", k, tb)], writes=[psr])
                        sq = Uf(16384 + (j % 2) * 1024, 512)
                        sqr = R("m_sq", j % 2)
                        P.add("act", lambda e, ps=ps, sq=sq: e.activation(sq, ps[:], AF.Square),
                              reads=[psr], writes=[sqr])
                        P.add("dve", lambda e, ps=ps, sq=sq, j=j, ts=ts: e.scalar_tensor_tensor(
                            hid[:, j, ts], ps[:], 0.0, sq, ALU.is_gt, ALU.mult),
                            reads=[psr, sqr], writes=[R("hid", j, tb)])
                w_issue()
                w2, w2r = w_slot(BIDX[("w2", l, g)])
                for tb in range(4):
                    ts = slice(tb * 512, (tb + 1) * 512)
                    for dch in range(8):
                        ps, psr = ps_next([0, 1, 2, 3])
                        for j in range(8):
                            P.add("pe", lambda e, ps=ps, w2=w2, j=j, dch=dch, ts=ts: e.matmul(
                                ps[:], w2[:, j * 1024 + dch * 128:j * 1024 + (dch + 1) * 128], hid[:, j, ts],
                                start=(j == 0), stop=(j == 7)),
                                reads=[w2r, R("hid", j, tb)], writes=[psr])
                        P.add("dve", lambda e, ps=ps, dch=dch, ts=ts: e.scalar_tensor_tensor(
                            xT[:, dch, ts], ps[:], gate[:, dch:dch + 1], xT[:, dch, ts], ALU.mult, ALU.add),
                            reads=[psr, R("modv"), R("x", dch, tb)], writes=[R("x", dch, tb)])
                    if after_tb is not None and g == 3:
                        after_tb(tb)
                w_issue()
                if l == 0 and stage >= 3:
                    for _ in range(ADA1_AFTER_GROUP[g]):
                        adaln_block(1, ada1_state["n"])
                        ada1_state["n"] += 1
                    if ada1_state["n"] == 6 and not ada1_state.get("fin"):
                        adaln_finish(1)
                        ada1_state["fin"] = True

        ada1_state = {"n": 0}

        def gla():
            blk0 = BIDX[("glaA", 0)]
            P.fence()
            gate = modv[:, 16:24]
            o = 0
            a_aug = U[0:48, o:o + 2048]; o += 2048
            wab = Ub(o, 384); o += 384
            wa2b = U[0:48, o:o + 512]; o += 512
            waf = Uf(o, 384); o += 768
            wa2f = U[0:48, o:o + 1024].bitcast(F32); o += 1024
            spf = [Uf(o + i * 512, 256) for i in range(2)]; o += 1024
            ebf = [Uf(o + i * 1024, 512) for i in range(2)]; o += 2048
            enbf = [Uf(o + i * 1024, 512) for i in range(2)]; o += 2048
            qd = U[:, o:o + 1024].rearrange("p (h t) -> p h t", h=2); o += 1024
            ki = U[:, o:o + 1024].rearrange("p (h t) -> p h t", h=2); o += 1024
            ktok = U[:, o:o + 1024].rearrange("p (c h f) -> p c h f", c=4, h=2); o += 1024
            vtok = U[:, o:o + 2048].rearrange("p (c f) -> p c f", c=4); o += 2048
            rg = U[:, o:o + 2048].rearrange("p (j t) -> p j t", j=4); o += 2048
            og = U[:, o:o + 2048].rearrange("p (j t) -> p j t", j=4); o += 2048
            Sf = U[:, o:o + 1024].bitcast(F32).rearrange("p (h f) -> p h f", h=2); o += 1024
            Sb2 = [U[:, o + i * 512:o + (i + 1) * 512].rearrange("p (h f) -> p h f", h=2) for i in range(2)]; o += 1024
            sTb = [Ub(o + i * 128, 128) for i in range(2)]; o += 256
            tmpf = [Uf(o + i * 1024, 512) for i in range(2)]; o += 2048
            stmp = [Uf(o + i * 512, 256) for i in range(2)]; o += 1024
            etmp = Uf(o, 256); o += 512
            sqb = [Ub(o + i * 512, 512) for i in range(2)]; o += 1024
            assert o <= UEND, o

            P.add("sp", lambda e: e.dma_start(out=waf, in_=wa_d), writes=[R("waf"), R("modrow")], dma=True)
            P.add("sp", lambda e: e.dma_start(out=wa2f, in_=wa2_d), writes=[R("wa2f"), R("modrow")], dma=True)
            P.add("dve", lambda e: e.tensor_copy(wab, waf), reads=[R("waf")], writes=[R("wab")])
            P.add("dve", lambda e: e.tensor_copy(wa2b, wa2f), reads=[R("wa2f")], writes=[R("wa2b")])
            for tb in range(4):
                ts = slice(tb * 512, (tb + 1) * 512)
                ps, psr = ps_next([0, 1])
                for k in range(8):
                    P.add("pe", lambda e, ps=ps, k=k, ts=ts: e.matmul(
                        ps[0:48, :], wab[:, k * 48:(k + 1) * 48], hT[:, k, ts], start=(k == 0), stop=(k == 7)),
                        reads=[R("wab"), R("h", k, tb)], writes=[psr])
                P.add("act", lambda e, ps=ps, ts=ts: e.activation(a_aug[0:48, ts], ps[0:48, :], AF.Copy),
                      reads=[psr], writes=[R("a_aug", tb)])
                P.add("dve", lambda e, ts=ts: e.memset(a_aug[0:1, ts], 1.0),
                      reads=[R("a_aug", tb)], writes=[R("a_aug", tb)])

            for g in range(2):
                wA, wAr = w_slot(blk0 + 2 * g)
                wB, wBr = w_slot(blk0 + 2 * g + 1)
                if g == 0:
                    dump("b", 7, wsl[0][:, 0:2048], [R("w", 0)])
                    dump("b", 16, wsl[1][:, 0:2048], [R("w", 1)])
                    dump("b", 17, wsl[2][:, 0:2048], [R("w", 2)])
                P.add("dve", lambda e: e.memset(Sf[:, :, :], 0.0), writes=[R("Sf", 0), R("Sf", 1)])
                for par in range(2):
                    P.add("dve", lambda e, par=par: e.memset(Sb2[par][:, :, :], 0.0), writes=[R("Sb", 0, par), R("Sb", 1, par)])
                for tb in range(4):
                    ts = slice(tb * 512, (tb + 1) * 512)
                    pb = [(PS[2], R("ps", 2)), (PS[3], R("ps", 3))]
                    for c in range(4):
                        tok = slice(tb * 512 + c * 128, tb * 512 + (c + 1) * 128)
                        ps, psr = ps_next([0, 1])
                        P.add("pe", lambda e, wA=wA, wB=wB, ps=ps, tok=tok, g=g: e.matmul(
                            ps[:, 0:256], a_aug[0:48, tok], wa2b[0:48, g * 256:(g + 1) * 256], start=True, stop=True),
                            reads=[R("a_aug", tb), R("wa2b")], writes=[psr])
                        P.add("act", lambda e, wA=wA, wB=wB, ps=ps: e.activation(etmp, ps[:, 0:256], AF.Exp, scale=-1.0),
                              reads=[psr], writes=[R("etmp")])
                        sp_ = spf[c % 2]
                        P.add("act", lambda e, wA=wA, wB=wB, sp_=sp_: e.activation(sp_, etmp, AF.Ln, bias=1.0),
                              reads=[R("etmp")], writes=[R("sp", c % 2)])
                        ps, psr = ps_next([0, 1])
                        for k in range(8):
                            P.add("pe", lambda e, wA=wA, wB=wB, ps=ps, k=k, tok=tok: e.matmul(
                                ps[:], hT[:, k, tok], wA[:, k * 1024 + 512:k * 1024 + 1024],
                                start=(k == 0), stop=(k == 7)), reads=[wAr, R("h", k, tb)], writes=[psr])
                        P.add("act", lambda e, wA=wA, wB=wB, ps=ps, c=c: e.activation(vtok[:, c, :], ps[:], AF.Copy),
                              reads=[psr], writes=[R("vtok", c)])
                        for hh in range(2):
                            P.add("pe", lambda e, wA=wA, wB=wB, hh=hh, c=c, sp_=sp_, pb=pb: e.matmul(
                                pb[hh][0][:, c * 128:(c + 1) * 128], sp_[:, hh * 128:(hh + 1) * 128], trinegf,
                                start=True, stop=True),
                                reads=[R("sp", c % 2), R("cstf")], writes=[pb[hh][1]])
                    for hh in range(2):
                        P.add("act", lambda e, wA=wA, wB=wB, hh=hh, pb=pb: e.activation(ebf[hh], pb[hh][0][:], AF.Exp),
                              reads=[pb[hh][1]], writes=[R("eb", hh)])
                        P.add("act", lambda e, wA=wA, wB=wB, hh=hh, pb=pb: e.activation(enbf[hh], pb[hh][0][:], AF.Exp, scale=-1.0),
                              reads=[pb[hh][1]], writes=[R("enb", hh)])
                    for jj in range(4):
                        ps, psr = ps_next([0, 1])
                        for k in range(8):
                            P.add("pe", lambda e, wA=wA, wB=wB, ps=ps, k=k, jj=jj, ts=ts: e.matmul(
                                ps[:], wB[:, k * 512 + jj * 128:k * 512 + (jj + 1) * 128], hT[:, k, ts],
                                start=(k == 0), stop=(k == 7)), reads=[wBr, R("h", k, tb)], writes=[psr])
                        bc = SM_BR + g * 4 + jj
                        P.add("act", lambda e, wA=wA, wB=wB, ps=ps, jj=jj, bc=bc: e.activation(
                            rg[:, jj, :], ps[:], AF.Silu, bias=sm[:, bc:bc + 1]),
                            reads=[psr, R("sm")], writes=[R("rg", jj)])
                    for hh in range(2):
                        ps, psr = ps_next([0, 1])
                        for k in range(8):
                            P.add("pe", lambda e, wA=wA, wB=wB, ps=ps, k=k, hh=hh, ts=ts: e.matmul(
                                ps[:], wA[:, k * 1024 + hh * 128:k * 1024 + (hh + 1) * 128], hT[:, k, ts],
                                start=(k == 0), stop=(k == 7)), reads=[wAr, R("h", k, tb)], writes=[psr])
                        P.add("dve", lambda e, wA=wA, wB=wB, ps=ps, hh=hh: e.scalar_tensor_tensor(
                            qd[:, hh, :], ps[:], 128.0 ** -0.5, ebf[hh], ALU.mult, ALU.mult),
                            reads=[psr, R("eb", hh)], writes=[R("qd", hh)])
                        ps, psr = ps_next([0, 1])
                        for k in range(8):
                            P.add("pe", lambda e, wA=wA, wB=wB, ps=ps, k=k, hh=hh, ts=ts: e.matmul(
                                ps[:], wA[:, k * 1024 + 256 + hh * 128:k * 1024 + 256 + (hh + 1) * 128], hT[:, k, ts],
                                start=(k == 0), stop=(k == 7)), reads=[wAr, R("h", k, tb)], writes=[psr])
                        P.add("dve", lambda e, wA=wA, wB=wB, ps=ps, hh=hh: e.tensor_tensor(ki[:, hh, :], ps[:], enbf[hh], ALU.mult),
                              reads=[psr, R("enb", hh)], writes=[R("ki", hh)])
                    for c in range(4):
                        pt, ptr = PS[4 + (c % 2)], R("ps", 4 + (c % 2))
                        ptb = pt[:].bitcast(BF16)
                        for hh in range(2):
                            P.add("pe", lambda e, wA=wA, wB=wB, ptb=ptb, c=c, hh=hh: e.transpose(
                                ptb[:, hh * 128:(hh + 1) * 128], ki[:, hh, c * 128:(c + 1) * 128], identb),
                                reads=[R("ki", hh), R("cstb")], writes=[ptr])
                        P.add("act", lambda e, wA=wA, wB=wB, ptb=ptb, c=c: e.activation(
                            ktok[:, c, :, :].rearrange("p h f -> p (h f)"), ptb[:, 0:256], AF.Copy),
                            reads=[ptr], writes=[R("ktok", c)])
                    po = [(PS[4 + i], R("ps", 4 + i)) for i in range(4)]
                    for c in range(4):
                        cs = slice(c * 128, (c + 1) * 128)
                        cg = tb * 4 + c
                        Sb_w = Sb2[cg % 2]
                        Sb_r = Sb2[(cg + 1) % 2]
                        for hh in range(2):
                            pss, pssr = PS[2 + hh], R("ps", 2 + hh)
                            sl = (c % 4) * 128
                            P.add("pe", lambda e, wA=wA, wB=wB, pss=pss, sl=sl, hh=hh, cs=cs: e.matmul(
                                pss[:, sl:sl + 128], ki[:, hh, cs], qd[:, hh, cs], start=True, stop=True),
                                reads=[R("ki", hh), R("qd", hh)], writes=[pssr])
                            sT = sTb[hh]
                            P.add("dve", lambda e, wA=wA, wB=wB, pss=pss, sl=sl, sT=sT: e.tensor_tensor(
                                sT, pss[:, sl:sl + 128], trib, ALU.mult),
                                reads=[pssr, R("cstb")], writes=[R("sT", hh)])
                        for hh in range(2):
                            psp, pspr = ps_next([0, 1])
                            P.add("pe", lambda e, wA=wA, wB=wB, psp=psp, c=c, hh=hh: e.matmul(
                                psp[:, 0:256], ktok[:, c, hh, :], vtok[:, c, hh * 256:(hh + 1) * 256],
                                start=True, stop=True),
                                reads=[R("ktok", c), R("vtok", c)], writes=[pspr])
                            st = stmp[hh]
                            P.add("dve", lambda e, wA=wA, wB=wB, psp=psp, st=st, hh=hh: e.tensor_tensor(
                                st, psp[:, 0:256], Sf[:, hh, :], ALU.add),
                                reads=[pspr, R("Sf", hh)], writes=[R("stmp", hh)])
                            ecol = ebf[hh][:, c * 128 + 127:c * 128 + 128]
                            P.add("dve", lambda e, wA=wA, wB=wB, st=st, hh=hh, ecol=ecol: e.tensor_scalar(
                                Sf[:, hh, :], st, ecol, None, ALU.mult),
                                reads=[R("stmp", hh), R("eb", hh)], writes=[R("Sf", hh)])
                            P.add("act", lambda e, wA=wA, wB=wB, st=st, hh=hh, ecol=ecol, Sb_w=Sb_w: e.activation(
                                Sb_w[:, hh, :], st, AF.Copy, scale=ecol),
                                reads=[R("stmp", hh), R("eb", hh)], writes=[R("Sb", hh, cg % 2)])
                        for hh in range(2):
                            sT = sTb[hh]
                            for jj in range(2):
                                pot, potr = po[hh * 2 + jj]
                                P.add("pe", lambda e, wA=wA, wB=wB, pot=pot, c=c, hh=hh, jj=jj, sT=sT, cs=cs: e.matmul(
                                    pot[:, cs], vtok[:, c, hh * 256 + jj * 128:hh * 256 + (jj + 1) * 128], sT,
                                    start=True, stop=False),
                                    reads=[R("vtok", c), R("sT", hh)], writes=[potr])
                                P.add("pe", lambda e, wA=wA, wB=wB, pot=pot, hh=hh, jj=jj, cs=cs, Sb_r=Sb_r: e.matmul(
                                    pot[:, cs], Sb_r[:, hh, jj * 128:(jj + 1) * 128], qd[:, hh, cs],
                                    start=False, stop=True),
                                    reads=[R("Sb", hh, (cg + 1) % 2), R("qd", hh)], writes=[potr])
                    if g == 0 and tb == 0:
                        dump("f", 3, Sf[:, :, :].rearrange("p h f -> p (h f)"), [R("Sf", 0), R("Sf", 1)])
                    for hh in range(2):
                        pst, pstr = ps_next([0, 1])
                        for jj in range(2):
                            pot, potr = po[hh * 2 + jj]
                            sq = sqb[jj]
                            P.add("act", lambda e, wA=wA, wB=wB, pot=pot, sq=sq: e.activation(sq, pot[:], AF.Square),
                                  reads=[potr], writes=[R("g_sq", jj)])
                            P.add("pe", lambda e, wA=wA, wB=wB, pst=pst, sq=sq, jj=jj: e.matmul(
                                pst[:], onesb[:], sq, start=(jj == 0), stop=(jj == 1)),
                                reads=[R("g_sq", jj), R("onesb")], writes=[pstr])
                        lnt = tmpf[0]
                        rstd = tmpf[1]
                        P.add("act", lambda e, wA=wA, wB=wB, pst=pst, lnt=lnt: e.activation(lnt, pst[:], AF.Ln, bias=EPS, scale=1.0 / 256),
                              reads=[pstr], writes=[R("g_ln")])
                        P.add("act", lambda e, wA=wA, wB=wB, lnt=lnt, rstd=rstd: e.activation(rstd, lnt, AF.Exp, scale=-0.5),
                              reads=[R("g_ln")], writes=[R("g_rstd")])
                        for jj in range(2):
                            pot, potr = po[hh * 2 + jj]
                            t2 = ebf[jj]
                            P.add("dve", lambda e, wA=wA, wB=wB, pot=pot, jj=jj, rstd=rstd, t2=t2: e.scalar_tensor_tensor(
                                t2, pot[:], sm[:, SM_GNG + jj:SM_GNG + jj + 1], rstd, ALU.mult, ALU.mult),
                                reads=[potr, R("g_rstd"), R("sm"), R("eb", jj)], writes=[R("eb", jj)])
                            P.add("dve", lambda e, wA=wA, wB=wB, hh=hh, jj=jj, t2=t2: e.tensor_tensor(
                                og[:, hh * 2 + jj, :], t2, rg[:, hh * 2 + jj, :], ALU.mult),
                                reads=[R("eb", jj), R("rg", hh * 2 + jj)], writes=[R("og", hh * 2 + jj)])
                    if g == 0 and tb == 0:
                        dump("b", 5, og[:, :, :].rearrange("p j t -> p (j t)"), [R("og", j) for j in range(4)])
                        dump("f", 3, Sf[:, :, :].rearrange("p h f -> p (h f)"), [R("Sf", 0), R("Sf", 1)])
                    for dch in range(8):
                        ps, psr = ps_next([0, 1])
                        for j4 in range(4):
                            P.add("pe", lambda e, wA=wA, wB=wB, ps=ps, j4=j4, dch=dch: e.matmul(
                                ps[:], wB[:, 4096 + j4 * 1024 + dch * 128:4096 + j4 * 1024 + (dch + 1) * 128],
                                og[:, j4, :], start=(j4 == 0), stop=(j4 == 3)),
                                reads=[wBr, R("og", j4)], writes=[psr])
                        P.add("dve", lambda e, wA=wA, wB=wB, ps=ps, dch=dch, ts=ts: e.scalar_tensor_tensor(
                            xT[:, dch, ts], ps[:], gate[:, dch:dch + 1], xT[:, dch, ts], ALU.mult, ALU.add),
                            reads=[psr, R("modv"), R("x", dch, tb)], writes=[R("x", dch, tb)])
                w_issue()
                w_issue()

        def att():
            blk0 = BIDX[("att", 0)]
            P.fence()
            gate = modv[:, 48 + 16:48 + 24]
            o = 0
            qT = U[:, o:o + 4096].rearrange("p (h t) -> p h t", h=2); o += 4096
            kz = U[:, o:o + 8192].rearrange("p (h i t) -> p h i t", h=2, i=2); o += 8192
            vaug = U[:, o:o + 4096].rearrange("p (c h f) -> p c h f", c=16, h=2); o += 4096
            oT = U[:, o:o + 1024].rearrange("p (h t) -> p h t", h=2); o += 1024
            ocs = o
            Cb = Ub(o, 2048); o += 2048
            Sb_ = Ub(o, 2048); o += 2048
            pTb = [Ub(o + i * 512, 512) for i in range(4)]; o += 2048
            oc = o
            qnb = [Ub(o + i * 512, 512) for i in range(2)]; o += 1024
            t1f = [Uf(o + i * 1024, 512) for i in range(2)]; o += 2048
            t2f = [Uf(o + i * 1024, 512) for i in range(2)]; o += 2048
            o = oc
            c_t1 = Uf(o, 512); o += 1024
            c_r2 = Uf(o, 512); o += 1024
            c_t2 = Uf(o, 512); o += 1024
            c_d = Uf(o, 512); o += 1024
            c_ln = Uf(o, 512); o += 1024
            c_sq = Ub(o, 512); o += 512
            assert o <= UEND, o
            angf = Uf(0, 2048)
            kf = Uf(4096, 2048)
            ki32 = U[:, 8192:12288].bitcast(I32)

            lrow = lrow_t[0:1, :]
            lsum = lamv[0:1, 2:8]
            P.add("sp", lambda e: e.dma_start(out=lrow, in_=rw_d), writes=[R("lrow")], dma=True)
            P.add("dve", lambda e: e.tensor_tensor(lrow[0:1, 0:64], lrow[0:1, 0:64], lrow[0:1, 64:128], ALU.mult),
                  reads=[R("lrow")], writes=[R("lrow")])
            P.add("dve", lambda e: e.tensor_tensor(lrow[0:1, 128:192], lrow[0:1, 128:192], lrow[0:1, 192:256], ALU.mult),
                  reads=[R("lrow")], writes=[R("lrow")])
            P.add("dve", lambda e: e.tensor_reduce(
                lsum[0:1, 0:2], lrow.rearrange("p (a b) -> p a b", a=2)[:, :, 0:64], AX.X, ALU.add),
                reads=[R("lrow")], writes=[R("lamv")])
            P.add("act", lambda e: e.activation(lsum[0:1, 2:4], lsum[0:1, 0:2], AF.Exp),
                  reads=[R("lamv")], writes=[R("lamv")])
            P.add("dve", lambda e: e.scalar_tensor_tensor(
                lsum[0:1, 4:5], lsum[0:1, 3:4], -LAMBDA_INIT, lsum[0:1, 2:3], ALU.add, ALU.subtract),
                reads=[R("lamv")], writes=[R("lamv")])
            ps, psr = ps_next([0, 1])
            P.add("pe", lambda e, ps=ps: e.matmul(ps[:, 0:1], onesf[0:1, :], lsum[0:1, 4:5], start=True, stop=True),
                  reads=[R("lamv"), R("onesf")], writes=[psr])
            P.add("dve", lambda e, ps=ps: e.tensor_copy(lamv[:, 0:1], ps[:, 0:1]), reads=[psr, R("lamv")], writes=[R("lamv")])
            P.add("dve", lambda e: e.tensor_scalar(lamv[:, 1:2], sm[:, SM_SLG:SM_SLG + 1], 1.0 - LAMBDA_INIT, None, ALU.mult),
                  reads=[R("sm"), R("lamv")], writes=[R("lamv")])

            posi1 = U[0:1, 12288:16384].bitcast(I32)
            posf1 = U[0:1, ocs:ocs + 4096].bitcast(F32)
            P.add("sp", lambda e: e.dma_start(out=posi1.bitcast(F32), in_=pos_d), writes=[R("posi")], dma=True)
            P.add("dve", lambda e: e.tensor_copy(posf1, posi1), reads=[R("posi")], writes=[R("posf")])
            for tb in range(4):
                ts = slice(tb * 512, (tb + 1) * 512)
                ps, psr = ps_next([0, 1])
                P.add("pe", lambda e, ps=ps, ts=ts: e.matmul(ps[:], onesf[0:1, :], posf1[0:1, ts], start=True, stop=True),
                      reads=[R("posf"), R("onesf")], writes=[psr])
                P.add("dve", lambda e, ps=ps, ts=ts: e.tensor_scalar(angf[:, ts], ps[:], sm[:, SM_INVF:SM_INVF + 1], None, ALU.mult),
                      reads=[psr, R("sm")], writes=[R("angf")])
            dump('f', 1, angf, [R('angf')])
            dump('f', 6, posi1.bitcast(F32), [R('posi')], npart=1)
            dump('f', 7, posf1, [R('posf')], npart=1)
            for which, dst, off in ((0, Sb_, 0.0), (1, Cb, float(np.pi / 2))):
                if which == 1:
                    P.add("dve", lambda e, off=off: e.tensor_scalar(angf, angf, off, None, ALU.add),
                          reads=[R("angf")], writes=[R("angf")])
                P.add("dve", lambda e: e.tensor_scalar(kf, angf, 1.0 / TWO_PI, None, ALU.mult),
                      reads=[R("angf")], writes=[R("kf")])
                P.add("dve", lambda e: e.tensor_copy(ki32, kf), reads=[R("kf")], writes=[R("ki32")])
                P.add("dve", lambda e: e.tensor_copy(kf, ki32), reads=[R("ki32")], writes=[R("kf")])
                P.add("dve", lambda e: e.scalar_tensor_tensor(kf, kf, -TWO_PI, angf, ALU.mult, ALU.add),
                      reads=[R("kf"), R("angf")], writes=[R("kf")])
                P.add("dve", lambda e: e.tensor_scalar(kf, kf, float(np.pi), float(-np.pi), ALU.min, ALU.max),
                      reads=[R("kf")], writes=[R("kf")])
                P.add("act", lambda e, dst=dst: e.activation(dst, kf, AF.Sin), reads=[R("kf")], writes=[R("rope", which)])
                dump('f', 2 + which, kf, [R("kf")])
            P.add("dve", lambda e: e.memset(kz[64:128, :, 0, :], 0.0), writes=[R("kzz"), R("kf"), R("ki32"), R("angf"), R("posi")])
            P.add("dve", lambda e: e.memset(kz[0:64, :, 1, :], 0.0), writes=[R("kzz"), R("kf"), R("ki32"), R("angf"), R("posi")])

            ALV = int(os.environ.get('ATT_LEVEL', '9'))

            def proj_tmp_res():
                return [R("qn", 0), R("qn", 1), R("t1", 0), R("t1", 1), R("t2", 0), R("t2", 1)]

            def comb_tmp_res():
                return [R("c_t1"), R("c_r2"), R("c_t2"), R("c_d"), R("c_ln"), R("c_sq")]

            for pr in range(4 if ALV > 0 else 0):
                wt, wr = w_slot(blk0 + pr)
                groups = [(nm, coff, hh, tb) for (nm, coff) in (("qT", 0), ("kT", 256)) for hh in range(2) for tb in range(4)]
                gstate = {}

                def stageA(gx):
                    nm, coff, hh, tb = groups[gx]
                    ts = slice(tb * 512, (tb + 1) * 512)
                    ps, psr = PS[gx % 2], R("ps", gx % 2)
                    for k in range(8):
                        P.add("pe", lambda e, wt=wt, ps=ps, k=k, hh=hh, ts=ts, coff=coff: e.matmul(
                            ps[:], wt[:, k * 768 + coff + hh * 128:k * 768 + coff + (hh + 1) * 128], hT[:, k, ts],
                            start=(k == 0), stop=(k == 7)), reads=[wr, R("h", k, tb)], writes=[psr])
                    qn = qnb[gx % 2]
                    P.add("act", lambda e, ps=ps, qn=qn: e.activation(qn, ps[:], AF.Copy),
                          reads=[psr], writes=[R("qn", gx % 2)])

                def stageB(gx):
                    nm, coff, hh, tb = groups[gx]
                    ts = slice(tb * 512, (tb + 1) * 512)
                    i2 = gx % 2
                    ps, psr = PS[gx % 2], R("ps", gx % 2)
                    qn = qnb[i2]
                    ps2, ps2r = PS[2 + i2], R("ps", 2 + i2)
                    P.add("pe", lambda e, ps2=ps2, qn=qn: e.matmul(ps2[:], permb, qn, start=True, stop=True),
                          reads=[R("qn", i2), R("cstb")], writes=[ps2r])
                    t1 = t1f[i2]
                    t2 = t2f[i2]
                    P.add("dve", lambda e, ps=ps, t1=t1, ts=ts: e.tensor_tensor(t1, ps[:], Cb[:, ts], ALU.mult),
                          reads=[psr, R("rope", 1)], writes=[R("t1", i2)])
                    P.add("dve", lambda e, ps2=ps2, t2=t2, ts=ts: e.tensor_tensor(t2, ps2[:], Sb_[:, ts], ALU.mult),
                          reads=[ps2r, R("rope", 0)], writes=[R("t2", i2)])
                    if nm == "qT":
                        P.add("pool", lambda e, hh=hh, ts=ts, t1=t1, t2=t2: e.tensor_tensor(
                            qT[:, hh, ts], t1, t2, ALU.add),
                            reads=[R("t1", i2), R("t2", i2)], writes=[R("qT", hh, tb)])
                    else:
                        for i_ in range(2):
                            prt_ = slice(64 * i_, 64 * i_ + 64)
                            P.add("pool", lambda e, hh=hh, ts=ts, t1=t1, t2=t2, i_=i_, prt_=prt_: e.tensor_tensor(
                                kz[prt_, hh, i_, ts], t1[prt_, :], t2[prt_, :], ALU.add),
                                reads=[R("t1", i2), R("t2", i2), R("kzz")], writes=[R("kT", hh, tb)])

                for gx in range(len(groups) + 1):
                    if gx < len(groups):
                        stageA(gx)
                    if gx >= 1:
                        stageB(gx - 1)
                for tt in range(16 if not os.environ.get("SKIPV") else 0):
                    tok = slice(tt * 128, (tt + 1) * 128)
                    ps, psr = ps_next([0, 1])
                    for k in range(8):
                        P.add("pe", lambda e, wt=wt, ps=ps, k=k, tok=tok: e.matmul(
                            ps[:, 0:256], hT[:, k, tok], wt[:, k * 768 + 512:k * 768 + 768],
                            start=(k == 0), stop=(k == 7)), reads=[wr, R("h", k, tt // 4)], writes=[psr])
                    P.add("dve", lambda e, wt=wt, ps=ps, tt=tt: e.tensor_copy(
                        vaug[:, tt, :, :], ps[:, 0:256].rearrange("p (h f) -> p h f", h=2)),
                        reads=[psr], writes=[R("vaug", tt)])
                P.alias_barrier(proj_tmp_res(), comb_tmp_res())
                its = []
                gi = 0
                for j in range(4):
                    for hh in range(2):
                        for i in range(2):
                            for kt in range(4 * j + 4):
                                its.append((hh, j, i, kt, gi))
                            gi += 1
                LA = 2
                SB = [1, 2, 3]
                deferred = []

                def emit_qk(idx):
                    hh, j, i, kt, g_ = its[idx]
                    r = kt - 4 * j
                    n0 = 128 * r if r > 0 else 0
                    ps, psr = PS[SB[idx % 3]], R("ps", SB[idx % 3])
                    P.add("pe", lambda e, ps=ps, i=i, hh=hh, kt=kt, j=j, n0=n0, r=r: e.matmul(
                        ps[:, n0:512], kz[:, hh, i, kt * 128:(kt + 1) * 128],
                        qT[:, hh, j * 512 + n0:(j + 1) * 512], start=True, stop=(r < 0)),
                        reads=[R("kT", hh, kt // 4), R("qT", hh, j), R("kzz")], writes=[psr])
                    if r >= 0:
                        P.add("pe", lambda e, ps=ps, n0=n0: e.matmul(
                            ps[:, n0:n0 + 128], identb, negmb, start=False, stop=True),
                            reads=[R("cstb")], writes=[psr])
                    pT = pTb[idx % 3]
                    pTr = R("pT", idx % 3)
                    P.add("act", lambda e, ps=ps, pT=pT, n0=n0: e.activation(
                        pT[:, n0:512], ps[:, n0:512], AF.Exp, scale=0.125),
                        reads=[psr], writes=[pTr])

                def emit_pv(idx):
                    hh, j, i, kt, g_ = its[idx]
                    r = kt - 4 * j
                    n0 = 128 * r if r > 0 else 0
                    pT = pTb[idx % 3]
                    pTr = R("pT", idx % 3)
                    ao, aor = PS[4 + 2 * i], R("ps", 4 + 2 * i)
                    al, alr = PS[5 + 2 * i], R("ps", 5 + 2 * i)
                    last = (kt == 4 * j + 3)
                    P.add("pe", lambda e, ao=ao, pT=pT, kt=kt, hh=hh, n0=n0, last=last: e.matmul(
                        ao[:, n0:512], vaug[:, kt, hh, :], pT[:, n0:512], start=(kt == 0), stop=last),
                        reads=[pTr, R("vaug", kt)], writes=[aor])
                    P.add("pe", lambda e, al=al, pT=pT, kt=kt, n0=n0, last=last: e.matmul(
                        al[:, n0:512], onesb[:], pT[:, n0:512], start=(kt == 0), stop=last),
                        reads=[pTr, R("onesb")], writes=[alr])
                    if not last:
                        return
                    qs = slice(j * 512, (j + 1) * 512)
                    if i == 0:
                        P.add("dve", lambda e, al=al: e.reciprocal(c_r2, al[:]), reads=[alr], writes=[R("c_r2")])
                        P.add("dve", lambda e, ao=ao: e.tensor_tensor(c_t1, ao[:], c_r2, ALU.mult),
                              reads=[aor, R("c_r2")], writes=[R("c_t1")])
                        return
                    P.add("dve", lambda e, al=al: e.reciprocal(c_r2, al[:]), reads=[alr], writes=[R("c_r2")])
                    P.add("dve", lambda e, ao=ao: e.scalar_tensor_tensor(c_t2, ao[:], lamv[:, 0:1], c_r2, ALU.mult, ALU.mult),
                          reads=[aor, R("c_r2"), R("lamv")], writes=[R("c_t2")])
                    P.add("pool", lambda e: e.tensor_tensor(c_d, c_t1, c_t2, ALU.add),
                          reads=[R("c_t1"), R("c_t2")], writes=[R("c_d")])

                    def epi1a():
                        P.add("act", lambda e: e.activation(c_sq, c_d, AF.Square), reads=[R("c_d")], writes=[R("c_sq")])

                    def epi1b():
                        pss, pssr = PS[0], R("ps", 0)
                        P.add("pe", lambda e, pss=pss: e.matmul(pss[:], onesb[:], c_sq, start=True, stop=True),
                              reads=[R("c_sq"), R("onesb")], writes=[pssr])

                    def epi1c(hh=hh):
                        pss, pssr = PS[0], R("ps", 0)
                        P.add("act", lambda e, pss=pss: e.activation(c_ln, pss[:], AF.Ln, bias=EPS, scale=1.0 / 128),
                              reads=[pssr], writes=[R("c_ln")])
                        P.add("act", lambda e: e.activation(c_ln, c_ln, AF.Exp, scale=-0.5),
                              reads=[R("c_ln")], writes=[R("c_ln")])
                        P.add("dve", lambda e, hh=hh: e.scalar_tensor_tensor(
                            oT[:, hh, :], c_d, lamv[:, 1:2], c_ln, ALU.mult, ALU.mult),
                            reads=[R("c_d"), R("c_ln"), R("lamv")], writes=[R("oT", hh)])

                    def epi2_one(dch, j=j, qs=qs):
                        psw, pswr = PS[0], R("ps", 0)
                        for h2 in range(2):
                            P.add("pe", lambda e, wt=wt, psw=psw, h2=h2, dch=dch: e.matmul(
                                psw[:], wt[:, 6144 + h2 * 1024 + dch * 128:6144 + h2 * 1024 + (dch + 1) * 128],
                                oT[:, h2, :], start=(h2 == 0), stop=(h2 == 1)),
                                reads=[wr, R("oT", h2)], writes=[pswr])
                        P.add("dve", lambda e, psw=psw, dch=dch, qs=qs: e.scalar_tensor_tensor(
                            xT[:, dch, qs], psw[:], gate[:, dch:dch + 1], xT[:, dch, qs], ALU.mult, ALU.add),
                            reads=[pswr, R("modv"), R("x", dch, j)], writes=[R("x", dch, j)])

                    step_now = idx + LA
                    defer(step_now + 10, epi1a)
                    defer(step_now + 12, epi1b)
                    defer(step_now + 14, epi1c)
                    if hh == 1:
                        for dch in range(8):
                            defer(step_now + 18 + 2 * dch, lambda dch=dch: epi2_one(dch))

                dseq = [0]

                def defer(due, fn):
                    dseq[0] += 1
                    deferred.append((due, dseq[0], fn))
                    deferred.sort(key=lambda t: (t[0], t[1]))

                def run_deferred(step, flush=False):
                    while deferred and (flush or deferred[0][0] <= step):
                        deferred.pop(0)[2]()

                for step in range(len(its) + LA):
                    if step < len(its):
                        emit_qk(step)
                    if step >= LA:
                        idx_ = step - LA
                        hh_, j_, i_, kt_, g_ = its[idx_]
                        if i_ == 1 and kt_ == 4 * j_ + 3:
                            run_deferred(step, flush=True)
                        emit_pv(idx_)
                    run_deferred(step)
                run_deferred(0, flush=True)
                P.alias_barrier(comb_tmp_res(), proj_tmp_res())
                w_issue()

        P.fence()
        for nb_ in range(6):
            adaln_block(0, nb_)
        adaln_finish(0)
        norm(dv[:, 0:8], modv[:, 0:8])
        gla()
        if stage >= 2:
            norm(dv[:, 8:16], modv[:, 24:32])
            mlp(0, after_tb=(lambda tb: norm_tb(tb, dv[:, 0:8], modv[:, 48:56])) if stage >= 3 else None)
        if stage >= 3:
            att()
        if stage >= 4:
            norm(dv[:, 8:16], modv[:, 48 + 24:48 + 32])
            ov3 = out_d.rearrange("(k p) t -> p k t", p=128)
            outr = []

            def final_tb(tb):
                norm_tb(tb, sm[:, SM_FG:SM_FG + 8], None, final=True)
                ts = slice(tb * 512, (tb + 1) * 512)
                rr = R("out", tb)
                outr.append(rr)
                P.add("sp", lambda e, ts=ts: e.dma_start(out=ov3[:, :, ts], in_=xT[:, :, ts]),
                      reads=[R("x", k, tb) for k in range(8)], writes=[rr], dma=True)

            mlp(1, after_tb=final_tb)
            P.add("sp", lambda e: None, reads=outr, noinst=True)
        else:
            ov = out_d.rearrange("(k p) t -> p k t", p=128)
            outr = []
            for k in range(8):
                rr = R("out", k)
                outr.append(rr)
                P.add("sp", lambda e, k=k: e.dma_start(out=ov[:, k, :], in_=xT[:, k, :]),
                      reads=[R("x", k, tb) for tb in range(4)], writes=[rr], dma=True)
            P.add("sp", lambda e: None, reads=outr, noinst=True)
        P.emit(sems, dsems)
    return nc


def _kmaj(Wc):
    K = Wc.shape[0] // 128
    return np.ascontiguousarray(Wc.reshape(K, 128, -1).transpose(1, 0, 2)).reshape(128, -1)


def _prep_shared(inp):
    f = np.float32
    blocks = np.zeros((NBLK, 128, 8192), f)
    ada_w = np.asarray(inp["ada_w"], f)
    mlp_w1 = np.asarray(inp["mlp_w1"], f)
    mlp_w2 = np.asarray(inp["mlp_w2"], f)
    gw = np.asarray(inp["gla_w_in"], f)[0]
    gwo = np.asarray(inp["gla_w_o"], f)[0]
    dw = np.asarray(inp["diff_w_in"], f)[0]
    dwo = np.asarray(inp["diff_w_o"], f)[0]
    for bi, name in enumerate(BLOCKS):
        kind = name[0]
        if kind == "ada":
            _, l, nb_ = name
            blocks[bi] = _kmaj(ada_w[l][:, nb_ * 1024:(nb_ + 1) * 1024])
        elif kind == "glaA":
            g = name[1]
            cols = np.concatenate([np.arange(2 * g * 128, 2 * g * 128 + 256),
                                   512 + np.arange(2 * g * 128, 2 * g * 128 + 256),
                                   1024 + np.arange(2 * g * 256, 2 * g * 256 + 512)])
            blocks[bi] = _kmaj(gw[:, cols])
        elif kind == "glaB":
            g = name[1]
            rc = 2048 + np.arange(2 * g * 256, 2 * g * 256 + 512)
            blocks[bi][:, 0:4096] = _kmaj(gw[:, rc])
            blocks[bi][:, 4096:8192] = _kmaj(gwo[g * 512:(g + 1) * 512, :])
        elif kind == "w1":
            _, l, g = name
            blocks[bi] = _kmaj(mlp_w1[l][:, g * 1024:(g + 1) * 1024])
        elif kind == "w2":
            _, l, g = name
            blocks[bi] = _kmaj(mlp_w2[l][g * 1024:(g + 1) * 1024, :])
        elif kind == "att":
            pr = name[1]
            cols = np.concatenate([np.arange(pr * 256, pr * 256 + 256),
                                   1024 + np.arange(pr * 256, pr * 256 + 256),
                                   2048 + np.arange(pr * 256, pr * 256 + 256)])
            blocks[bi][:, 0:6144] = _kmaj(dw[:, cols])
            blocks[bi][:, 6144:8192] = _kmaj(dwo[pr * 256:(pr + 1) * 256, :])

    def pk(v):
        return np.asarray(v, f).reshape(-1, 128).T

    sm = np.zeros((128, NSM), f)
    ng = np.asarray(inp["norm_g"], f)
    for l in range(2):
        for j in range(2):
            sm[:, SM_NG + (2 * l + j) * 8:SM_NG + (2 * l + j) * 8 + 8] = pk(ng[l, j])
    sm[:, SM_FG:SM_FG + 8] = pk(inp["final_g"])
    ab = np.asarray(inp["ada_b"], f)
    for l in range(2):
        sm[:, SM_AB + l * 48:SM_AB + (l + 1) * 48] = pk(ab[l])
    sm[:, SM_BR:SM_BR + 8] = pk(np.asarray(inp["gla_b_r"], f)[0])
    sm[:, SM_GNG:SM_GNG + 2] = pk(np.asarray(inp["gla_norm_g"], f)[0])
    sm[:, SM_SLG:SM_SLG + 1] = pk(np.asarray(inp["diff_subln_g"], f)[0])
    inv_freq = (np.float32(500000.0) ** (-(np.arange(0, 16, 2, dtype=np.float32)) / np.float32(16))).astype(f)
    invf = np.zeros(128, f)
    for p in range(128):
        d = p % 64
        if d < 16:
            invf[p] = inv_freq[d % 8]
    sm[:, SM_INVF] = invf

    rw = np.asarray(inp["diff_lambda"], f)[0].reshape(1, 256)
    wa = np.zeros((128, 8, 48), f)
    wa[:, :, 32:48] = gw[:, 3072:3088].reshape(8, 128, 16).transpose(1, 0, 2)
    wa = wa.reshape(128, 8 * 48)
    wa2 = np.zeros((48, 512), f)
    wa2[0] = np.asarray(inp["gla_b_a"], f)[0]
    wa2[32:48] = np.asarray(inp["gla_w_a2"], f)[0]
    cst = np.zeros((128, 512), f)
    cst[:, 0:128] = np.eye(128, dtype=f)
    tri = (np.arange(128)[None, :] >= np.arange(128)[:, None]).astype(f)
    cst[:, 128:256] = tri
    cst[:, 256:384] = tri * f(-1.0 / 16.0)
    perm = np.zeros((128, 128), f)
    for b_ in range(2):
        base = 64 * b_
        for d in range(8):
            perm[base + d + 8, base + d] = -1.0
        for d in range(8, 16):
            perm[base + d - 8, base + d] = 1.0
    cst[:, 384:512] = perm
    return dict(wblk=blocks, sm=sm, rw=rw, wa=wa, wa2=wa2, cst=cst)


_NC_CACHE = {}


def kernel(x, c, positions, ada_w, ada_b, norm_g, mlp_w1, mlp_w2,
           gla_w_in, gla_w_a2, gla_b_a, gla_b_r, gla_norm_g, gla_w_o,
           diff_w_in, diff_lambda, diff_subln_g, diff_w_o, final_g, _stage=4, _debug=False, _ncores=NB):
    inp = dict(ada_w=ada_w, ada_b=ada_b, norm_g=norm_g, mlp_w1=mlp_w1, mlp_w2=mlp_w2,
               gla_w_in=gla_w_in, gla_w_a2=gla_w_a2, gla_b_a=gla_b_a, gla_b_r=gla_b_r,
               gla_norm_g=gla_norm_g, gla_w_o=gla_w_o, diff_w_in=diff_w_in, diff_lambda=diff_lambda,
               diff_subln_g=diff_subln_g, diff_w_o=diff_w_o, final_g=final_g)
    sh = _prep_shared(inp)
    x = np.asarray(x, np.float32)
    c = np.asarray(c, np.float32)
    positions = np.asarray(positions, np.int32)
    in_maps = []
    for b in range(NB):
        sm = sh["sm"].copy()
        sm[:, SM_C:SM_C + 8] = c[b].reshape(8, 128).T
        in_maps.append(dict(xT=np.ascontiguousarray(x[b].T), pos=np.ascontiguousarray(positions[b:b + 1]).view(np.float32),
                            sm=sm, rw=sh["rw"], wa=sh["wa"], wa2=sh["wa2"], cst=sh["cst"], wblk=sh["wblk"]))
    key = (_stage, _debug)
    if key not in _NC_CACHE:
        _NC_CACHE[key] = build_nc(_stage, _debug)
    nc = _NC_CACHE[key]
    res = run_bass_kernel_spmd(nc, in_maps[:_ncores], core_ids=list(range(_ncores)))
    out = np.stack([np.ascontiguousarray(r["outT"].T) for r in res.results], axis=0)
    if _debug:
        return out.astype(np.float32), res.results[0]["dbgf"], res.results[0]["dbgb"]
    return out.astype(np.float32)
```
